# Optimizing a Trainium2 kernel written in Bass

```python
import math
import jax
import jax.numpy as jnp
from jax import lax
import numpy as np

D_MODEL = 1024
BATCH = 4
SEQ = 4096
DEPTH = 4

CTX_LEN = 256
GRID_W = 64
N_MIXERS = 3
N_S5 = (DEPTH + 2) // 3
N_GDN = (DEPTH + 1) // 3
N_NA = DEPTH // 3
EPS = 1e-6
F32 = jnp.float32

S5_WIDTH = D_MODEL
S5_GROUP = 16
S5_GROUPS = S5_WIDTH // S5_GROUP
S5_STATE = 64
S5_DT_MIN = 1e-3
S5_DT_MAX = 1e-1

GDN_HEAD_DIM = 128
GDN_HEADS = D_MODEL // GDN_HEAD_DIM
GDN_WIDTH = GDN_HEADS * GDN_HEAD_DIM
GDN_CONV = 5
GDN_CHUNK = 64

NA_HEAD_DIM = 64
NA_HEADS = D_MODEL // NA_HEAD_DIM
NA_WIDTH = NA_HEADS * NA_HEAD_DIM
NA_KH_MAX = 8
NA_KW = 16

kernel_name = 'hybrid_s5_gdn_natten_dit_block'


def rmsnorm(x, g):
    xf = x.astype(F32)
    y = xf * lax.rsqrt(jnp.mean(xf * xf, axis=-1, keepdims=True) + EPS)
    return (y * g.astype(F32)).astype(x.dtype)


def l2norm(t):
    return t * lax.rsqrt(jnp.sum(t * t, axis=-1, keepdims=True) + EPS)


def split_heads(t, n_heads):
    b, n, _ = t.shape
    return t.reshape(b, n, n_heads, -1).transpose(0, 2, 1, 3)


def merge_heads(t):
    b, h, n, d = t.shape
    return t.transpose(0, 2, 1, 3).reshape(b, n, h * d)


def adaln(cond, w, b):
    m = jax.nn.silu(cond) @ w + b
    return jnp.split(m, 3, axis=-1)


def s5_discretize(lam_re, lam_im, log_dt, b_re, b_im):
    lr = lam_re.astype(F32)
    li = lam_im.astype(F32)
    dt = jnp.exp(log_dt.astype(F32))[:, None]
    mag = jnp.exp(lr * dt)
    a_re = mag * jnp.cos(li * dt)
    a_im = mag * jnp.sin(li * dt)
    den = lr * lr + li * li
    f_re = ((a_re - 1.0) * lr + a_im * li) / den
    f_im = (a_im * lr - (a_re - 1.0) * li) / den
    br = b_re.astype(F32)
    bi = b_im.astype(F32)
    bb_re = f_re[..., None] * br - f_im[..., None] * bi
    bb_im = f_re[..., None] * bi + f_im[..., None] * br
    return a_re, a_im, bb_re, bb_im


def _complex_affine_combine(e1, e2):
    a1r, a1i, b1r, b1i = e1
    a2r, a2i, b2r, b2i = e2
    ar = a1r * a2r - a1i * a2i
    ai = a1r * a2i + a1i * a2r
    br = a2r * b1r - a2i * b1i + b2r
    bi = a2r * b1i + a2i * b1r + b2i
    return ar, ai, br, bi


def s5_scan(a_re, a_im, bu_re, bu_im, s0_re=None, s0_im=None, reverse=False):
    n = bu_re.shape[0]
    ar = jnp.broadcast_to(a_re[None, None], (n, 1) + a_re.shape)
    ai = jnp.broadcast_to(a_im[None, None], (n, 1) + a_im.shape)
    cum_re, cum_im, s_re, s_im = lax.associative_scan(
        _complex_affine_combine, (ar, ai, bu_re, bu_im), reverse=reverse, axis=0)
    if s0_re is not None:
        s_re, s_im = (s_re + cum_re * s0_re - cum_im * s0_im,
                      s_im + cum_re * s0_im + cum_im * s0_re)
    return s_re, s_im


def s5_group_time_major(u):
    b, n, _ = u.shape
    return u.astype(F32).transpose(1, 0, 2).reshape(n, b, S5_GROUPS, S5_GROUP)


def s5_drive(ug, bb_re, bb_im):
    return (jnp.einsum('tbgc,gpc->tbgp', ug, bb_re),
            jnp.einsum('tbgc,gpc->tbgp', ug, bb_im))


def s5_readout(s_re, s_im, cr, ci):
    return jnp.einsum('tbgp,gcp->tbgc', s_re, cr) - jnp.einsum('tbgp,gcp->tbgc', s_im, ci)


def s5_output(yg, z, glu_w, glu_b, out_w):
    n, b = yg.shape[:2]
    y = yg.reshape(n, b, S5_WIDTH).transpose(1, 0, 2)
    gl = jax.nn.gelu(y)
    ga, gb = jnp.split(gl @ glu_w.astype(F32) + glu_b.astype(F32), 2, axis=-1)
    y = ga * jax.nn.sigmoid(gb) * jax.nn.silu(z.astype(F32))
    return y.astype(z.dtype) @ out_w


def s5_mixer(a_lat, a_ctx, in_w, lam_re, lam_im, log_dt, b_re, b_im, c_re, c_im,
             d_skip, glu_w, glu_b, out_w, need_ctx_out):
    u_l, z_l = jnp.split(a_lat @ in_w, 2, axis=-1)
    u_c, z_c = jnp.split(a_ctx @ in_w, 2, axis=-1)
    ug_l = s5_group_time_major(u_l)
    ug_c = s5_group_time_major(u_c)
    d_g = d_skip.astype(F32).reshape(S5_GROUPS, S5_GROUP)
    y_l = ug_l * d_g
    y_c = ug_c * d_g
    for dr in range(2):
        rev = dr == 1
        a_re, a_im, bb_re, bb_im = s5_discretize(lam_re[dr], lam_im[dr], log_dt[dr], b_re[dr], b_im[dr])
        cr = c_re[dr].astype(F32)
        ci = c_im[dr].astype(F32)
        bc_re, bc_im = s5_drive(ug_c, bb_re, bb_im)
        sc_re, sc_im = s5_scan(a_re, a_im, bc_re, bc_im, reverse=rev)
        last = 0 if rev else -1
        bl_re, bl_im = s5_drive(ug_l, bb_re, bb_im)
        sl_re, sl_im = s5_scan(a_re, a_im, bl_re, bl_im, sc_re[last], sc_im[last], reverse=rev)
        y_l = y_l + s5_readout(sl_re, sl_im, cr, ci)
        if need_ctx_out:
            y_c = y_c + s5_readout(sc_re, sc_im, cr, ci)
    out_l = s5_output(y_l, z_l, glu_w, glu_b, out_w)
    out_c = s5_output(y_c, z_c, glu_w, glu_b, out_w) if need_ctx_out else None
    return out_l, out_c


def short_conv_silu(t, w):
    ch = t.shape[-1]
    k = w.shape[0]
    y = lax.conv_general_dilated(t, w[:, None, :].astype(t.dtype), window_strides=(1,),
                                 padding=[(k // 2, k // 2)],
                                 dimension_numbers=('NWC', 'WIO', 'NWC'),
                                 feature_group_count=ch)
    return jax.nn.silu(y)


def gated_delta_chunked(q, k, v, g, beta, s0):
    bsz, nh, n, dk = q.shape
    dv = v.shape[-1]
    cs = GDN_CHUNK
    nc = n // cs

    def chunks(t):
        return t.reshape((bsz, nh, nc, cs) + t.shape[3:])

    q = chunks(q * dk ** -0.5)
    k = chunks(k)
    v = chunks(v)
    g = jnp.cumsum(chunks(g), axis=-1)
    beta = chunks(beta)
    causal = jnp.tril(jnp.ones((cs, cs), dtype=bool))
    strict = jnp.tril(jnp.ones((cs, cs), dtype=bool), -1)
    decay = jnp.exp(jnp.where(causal, g[..., :, None] - g[..., None, :], -jnp.inf))
    kb = k * beta[..., None]
    lower = jnp.where(strict, jnp.einsum('bhncd,bhnsd->bhncs', kb, k) * decay, 0.0)
    eye = jnp.eye(cs, dtype=F32)
    rhs = jnp.concatenate([v * beta[..., None], kb * jnp.exp(g)[..., None]], axis=-1)
    uw = lax.linalg.triangular_solve(lower + eye, rhs, left_side=True, lower=True,
                                     unit_diagonal=True)
    u = uw[..., :dv]
    w = uw[..., dv:]
    a_qk = jnp.einsum('bhncd,bhnsd->bhncs', q, k) * decay

    def step(state, blk):
        q_i, k_i, u_i, w_i, g_i, a_i = blk
        v_new = u_i - jnp.einsum('bhcd,bhde->bhce', w_i, state)
        o_i = (jnp.einsum('bhcd,bhde->bhce', q_i * jnp.exp(g_i)[..., None], state)
               + jnp.einsum('bhcs,bhse->bhce', a_i, v_new))
        g_last = g_i[..., -1]
        k_dec = k_i * jnp.exp(g_last[..., None] - g_i)[..., None]
        state = state * jnp.exp(g_last)[..., None, None] + jnp.einsum('bhcd,bhce->bhde', k_dec, v_new)
        return state, o_i

    blocks = tuple(jnp.moveaxis(t, 2, 0) for t in (q, k, u, w, g, a_qk))
    state, o = lax.scan(step, s0, blocks)
    o = jnp.moveaxis(o, 0, 2).reshape(bsz, nh, n, dv)
    return state, o


def gdn_project(h, in_w, conv_w, a_log, dt_bias):
    b, n, _ = h.shape
    wd, nh = GDN_WIDTH, GDN_HEADS
    p = h @ in_w
    qkv = short_conv_silu(p[..., :3 * wd], conv_w)
    q, k, v = [split_heads(t, nh).astype(F32) for t in jnp.split(qkv, 3, axis=-1)]
    z = p[..., 3 * wd:4 * wd]
    ab = p[..., 4 * wd:].astype(F32)
    a_raw = ab[..., :2 * nh].reshape(b, n, 2, nh).transpose(2, 0, 3, 1)
    b_raw = ab[..., 2 * nh:].reshape(b, n, 2, nh).transpose(2, 0, 3, 1)
    g = -jnp.exp(a_log.astype(F32))[:, None, :, None] * jax.nn.softplus(
        a_raw + dt_bias.astype(F32)[:, None, :, None])
    beta = jax.nn.sigmoid(b_raw)
    return l2norm(q), l2norm(k), v, z, g, beta


def tflip(t, rev):
    return jnp.flip(t, axis=2) if rev else t


def gdn_output(o, z, norm_g, out_w):
    y = rmsnorm(o.transpose(0, 2, 1, 3), norm_g)
    b, n = z.shape[:2]
    y = y.reshape(b, n, GDN_WIDTH) * jax.nn.silu(z.astype(F32))
    return y.astype(z.dtype) @ out_w


def gdn_mixer(a_lat, a_ctx, in_w, conv_w, a_log, dt_bias, norm_g, out_w, need_ctx_out):
    ql, kl, vl, zl, gl, bl = gdn_project(a_lat, in_w, conv_w, a_log, dt_bias)
    qc, kc, vc, zc, gc, bc = gdn_project(a_ctx, in_w, conv_w, a_log, dt_bias)
    bsz = a_lat.shape[0]
    s0 = jnp.zeros((bsz, GDN_HEADS, GDN_HEAD_DIM, GDN_HEAD_DIM), F32)
    o_l = jnp.zeros_like(vl)
    o_c = jnp.zeros_like(vc)
    for dr in range(2):
        rev = dr == 1
        s_c, oc = gated_delta_chunked(tflip(qc, rev), tflip(kc, rev), tflip(vc, rev),
                                      tflip(gc[dr], rev), tflip(bc[dr], rev), s0)
        _, ol = gated_delta_chunked(tflip(ql, rev), tflip(kl, rev), tflip(vl, rev),
                                    tflip(gl[dr], rev), tflip(bl[dr], rev), s_c)
        o_l = o_l + tflip(ol, rev)
        if need_ctx_out:
            o_c = o_c + tflip(oc, rev)
    out_l = gdn_output(o_l, zl, norm_g, out_w)
    out_c = gdn_output(o_c, zc, norm_g, out_w) if need_ctx_out else None
    return out_l, out_c


def na_mixer(a_lat, a_ctx, in_w, rpb, out_w, need_ctx_out):
    bsz, n, _ = a_lat.shape
    rows = n // GRID_W
    kh = min(NA_KH_MAX, rows)
    scale = NA_HEAD_DIM ** -0.5
    q, k, v, z = jnp.split(a_lat @ in_w, 4, axis=-1)
    qc, kc, vc, zc = jnp.split(a_ctx @ in_w, 4, axis=-1)
    q, k, v = [split_heads(t, NA_HEADS) for t in (q, k, v)]
    qc, kc, vc = [split_heads(t, NA_HEADS) for t in (qc, kc, vc)]
    grid_shape = (bsz, NA_HEADS, rows, GRID_W, NA_HEAD_DIM)
    q_g, k_g, v_g = q.reshape(grid_shape), k.reshape(grid_shape), v.reshape(grid_shape)
    cols = np.arange(GRID_W)
    col_start = np.clip(cols - NA_KW // 2, 0, GRID_W - NA_KW)
    col_idx = col_start[:, None] + np.arange(NA_KW)[None, :]
    col_off = col_idx - cols[:, None] + (NA_KW - 1)
    rpb_cols = rpb.astype(F32)[:, :, col_off]
    n_win = kh * NA_KW

    def row_block(r):
        r0 = jnp.clip(r - kh // 2, 0, rows - kh)
        k_win = lax.dynamic_slice_in_dim(k_g, r0, kh, axis=2)[:, :, :, col_idx]
        v_win = lax.dynamic_slice_in_dim(v_g, r0, kh, axis=2)[:, :, :, col_idx]
        q_r = lax.dynamic_index_in_dim(q_g, r, axis=2, keepdims=False)
        row_off = r0 + jnp.arange(kh) - r + (NA_KH_MAX - 1)
        bias = rpb_cols[:, row_off].transpose(0, 2, 1, 3)
        s_win = jnp.einsum('bhwd,bhrwjd->bhwrj', q_r, k_win).astype(F32) * scale + bias[None]
        s_ctx = jnp.einsum('bhwd,bhnd->bhwn', q_r, kc).astype(F32) * scale
        s = jnp.concatenate([s_win.reshape(bsz, NA_HEADS, GRID_W, n_win), s_ctx], axis=-1)
        p = jax.nn.softmax(s, axis=-1).astype(v.dtype)
        p_win = p[..., :n_win].reshape(bsz, NA_HEADS, GRID_W, kh, NA_KW)
        return (jnp.einsum('bhwrj,bhrwjd->bhwd', p_win, v_win)
                + jnp.einsum('bhwn,bhnd->bhwd', p[..., n_win:], vc))

    o = lax.map(row_block, jnp.arange(rows))
    o = o.transpose(1, 0, 3, 2, 4).reshape(bsz, n, NA_WIDTH)
    out_l = (o * jax.nn.silu(z)) @ out_w
    out_c = None
    if need_ctx_out:
        s = jnp.einsum('bhqd,bhkd->bhqk', qc, kc).astype(F32) * scale
        oc = merge_heads(jnp.einsum('bhqk,bhkd->bhqd', jax.nn.softmax(s, axis=-1).astype(vc.dtype), vc))
        out_c = (oc * jax.nn.silu(zc)) @ out_w
    return out_l, out_c


def setup_inputs(seed: int = 0) -> dict:
    key = jax.random.key(seed)
    keys = iter(jax.random.split(key, 32))

    def nrm(shape, std):
        return jax.random.normal(next(keys), shape, F32) * std

    def unif(shape, lo, hi):
        return jax.random.uniform(next(keys), shape, F32, lo, hi)

    D = D_MODEL
    G, P, Cg, E = S5_GROUPS, S5_STATE, S5_GROUP, S5_WIDTH
    Wg, Hg = GDN_WIDTH, GDN_HEADS
    x = nrm((BATCH, SEQ, D), 1.0)
    c = nrm((BATCH, D), 1.0)
    ctx = nrm((BATCH, CTX_LEN, D), 1.0)
    c_ctx = nrm((D,), 1.0)
    ada_w = nrm((DEPTH, D, 3 * D), 0.5 * D ** -0.5)
    ada_b = nrm((DEPTH, 3 * D), 0.01)
    pre_g = 1.0 + nrm((DEPTH, D), 0.02)
    post_g = 1.0 + nrm((DEPTH, D), 0.02)
    s5_in_w = nrm((N_S5, D, 2 * E), D ** -0.5)
    s5_lam_re = -0.5 + nrm((N_S5, 2, G, P), 0.01)
    s5_lam_im = math.pi * jnp.arange(P, dtype=F32) + nrm((N_S5, 2, G, P), 0.01)
    s5_log_dt = unif((N_S5, 2, G), math.log(S5_DT_MIN), math.log(S5_DT_MAX))
    s5_b_re = nrm((N_S5, 2, G, P, Cg), (2 * Cg) ** -0.5)
    s5_b_im = nrm((N_S5, 2, G, P, Cg), (2 * Cg) ** -0.5)
    s5_c_re = nrm((N_S5, 2, G, Cg, P), P ** -0.5)
    s5_c_im = nrm((N_S5, 2, G, Cg, P), P ** -0.5)
    s5_d = nrm((N_S5, E), 1.0)
    s5_glu_w = nrm((N_S5, E, 2 * E), E ** -0.5)
    s5_glu_b = nrm((N_S5, 2 * E), 0.01)
    s5_out_w = nrm((N_S5, E, D), E ** -0.5)
    gdn_in_w = nrm((N_GDN, D, 4 * Wg + 4 * Hg), D ** -0.5)
    gdn_conv_w = nrm((N_GDN, GDN_CONV, 3 * Wg), GDN_CONV ** -0.5)
    gdn_a_log = jnp.log(unif((N_GDN, 2, Hg), 1.0, 16.0))
    dt = jnp.exp(unif((N_GDN, 2, Hg), math.log(1e-3), math.log(1e-1)))
    gdn_dt_bias = dt + jnp.log(-jnp.expm1(-dt))
    gdn_norm_g = 1.0 + nrm((N_GDN, GDN_HEAD_DIM), 0.02)
    gdn_out_w = nrm((N_GDN, Wg, D), Wg ** -0.5)
    na_in_w = nrm((N_NA, D, 4 * NA_WIDTH), D ** -0.5)
    na_rpb = nrm((N_NA, NA_HEADS, 2 * NA_KH_MAX - 1, 2 * NA_KW - 1), 0.1)
    na_out_w = nrm((N_NA, NA_WIDTH, D), NA_WIDTH ** -0.5)
    return {'x': x, 'c': c, 'ctx': ctx, 'c_ctx': c_ctx,
            'ada_w': ada_w, 'ada_b': ada_b, 'pre_g': pre_g, 'post_g': post_g,
            's5_in_w': s5_in_w, 's5_lam_re': s5_lam_re, 's5_lam_im': s5_lam_im,
            's5_log_dt': s5_log_dt, 's5_b_re': s5_b_re, 's5_b_im': s5_b_im,
            's5_c_re': s5_c_re, 's5_c_im': s5_c_im, 's5_d': s5_d,
            's5_glu_w': s5_glu_w, 's5_glu_b': s5_glu_b, 's5_out_w': s5_out_w,
            'gdn_in_w': gdn_in_w, 'gdn_conv_w': gdn_conv_w, 'gdn_a_log': gdn_a_log,
            'gdn_dt_bias': gdn_dt_bias, 'gdn_norm_g': gdn_norm_g, 'gdn_out_w': gdn_out_w,
            'na_in_w': na_in_w, 'na_rpb': na_rpb, 'na_out_w': na_out_w}


def reference(x, c, ctx, c_ctx, ada_w, ada_b, pre_g, post_g,
              s5_in_w, s5_lam_re, s5_lam_im, s5_log_dt, s5_b_re, s5_b_im,
              s5_c_re, s5_c_im, s5_d, s5_glu_w, s5_glu_b, s5_out_w,
              gdn_in_w, gdn_conv_w, gdn_a_log, gdn_dt_bias, gdn_norm_g, gdn_out_w,
              na_in_w, na_rpb, na_out_w):
    h_lat, h_ctx = x, ctx
    for i in range(DEPTH):
        kind, j = i % N_MIXERS, i // N_MIXERS
        need_ctx_out = i < DEPTH - 1
        sh_l, sc_l, gt_l = adaln(c, ada_w[i], ada_b[i])
        sh_c, sc_c, gt_c = adaln(c_ctx, ada_w[i], ada_b[i])
        a_lat = rmsnorm(h_lat, pre_g[i]) * (1.0 + sc_l[:, None]) + sh_l[:, None]
        a_ctx = rmsnorm(h_ctx, pre_g[i]) * (1.0 + sc_c) + sh_c
        if kind == 0:
            o_lat, o_ctx = s5_mixer(a_lat, a_ctx, s5_in_w[j], s5_lam_re[j], s5_lam_im[j],
                                    s5_log_dt[j], s5_b_re[j], s5_b_im[j], s5_c_re[j], s5_c_im[j],
                                    s5_d[j], s5_glu_w[j], s5_glu_b[j], s5_out_w[j], need_ctx_out)
        elif kind == 1:
            o_lat, o_ctx = gdn_mixer(a_lat, a_ctx, gdn_in_w[j], gdn_conv_w[j], gdn_a_log[j],
                                     gdn_dt_bias[j], gdn_norm_g[j], gdn_out_w[j], need_ctx_out)
        else:
            o_lat, o_ctx = na_mixer(a_lat, a_ctx, na_in_w[j], na_rpb[j], na_out_w[j], need_ctx_out)
        h_lat = h_lat + gt_l[:, None] * rmsnorm(o_lat, post_g[i])
        if need_ctx_out:
            h_ctx = h_ctx + gt_c * rmsnorm(o_ctx, post_g[i])
    return h_lat
```

```python
import contextlib
import math
import numpy as np
import concourse.bass as bass
import concourse.mybir as mybir
from concourse.bass_utils import run_bass_kernel_spmd

F32 = mybir.dt.float32
BF16 = mybir.dt.bfloat16
AF = mybir.ActivationFunctionType
ALU = mybir.AluOpType
AX = mybir.AxisListType

D = 1024
SEQ = 4096
CTX = 256
T = SEQ + CTX
NT = T // 128
DEPTH = 4
EPS = 1e-6
L = 8
NK = T // L
TBLK = [(i * 512, min(512, T - i * 512)) for i in range((T + 511) // 512)]


DBG = set()
ONLY_GLU = False
OUT_CUT = 99


class Tok:
    __slots__ = ("w", "r")

    def __init__(self):
        self.w = None
        self.r = {}


class KB:
    def __init__(self):
        self.nc = bass.Bass("TRN2", target_bir_lowering=False)
        self.es = contextlib.ExitStack()
        nc = self.nc
        self.eng = {}
        for name, obj in (("pe", nc.tensor), ("act", nc.scalar), ("dve", nc.vector), ("pool", nc.gpsimd), ("sp", nc.sync)):
            sem = self.es.enter_context(nc.semaphore("sem_" + name))
            self.eng[name] = dict(e=obj, sem=sem, cnt=0, waited={}, name=name)
        self.ndma = 48
        self.dsem = [self.es.enter_context(nc.semaphore("dsem%d" % i)) for i in range(self.ndma)]
        self.dval = [0] * self.ndma
        self.dnext = 0
        self.uid = 0
        self.banks = []
        for i in range(8):
            t = self.es.enter_context(nc.psum_tensor("bank%d" % i, [128, 512], F32))
            self.banks.append((t, Tok()))
        self.bnext = 0
        self.ninst = 0

    def sb(self, shape, dtype, name=None, stack=None):
        self.uid += 1
        t = (stack or self.es).enter_context(self.nc.sbuf_tensor("%s_%d" % (name or "t", self.uid), list(shape), dtype))
        return t

    def dram(self, name, shape, dtype, kind="Internal"):
        if name in DBG:
            kind = "ExternalOutput"
        return self.nc.dram_tensor(name, list(shape), dtype, kind=kind).ap()

    def bank(self):
        b = self.banks[self.bnext % 8]
        self.bnext += 1
        return b

    def _wait(self, E, sem, val):
        key = id(sem)
        if E["waited"].get(key, 0) < val:
            E["e"].wait_ge(sem, val)
            E["waited"][key] = val

    def _deps(self, E, reads, writes):
        own = id(E["sem"])
        for t in reads:
            if t.w is not None:
                if E["name"] == "pe" and id(t.w[0]) == own:
                    continue
                self._wait(E, *t.w)
        for t in writes:
            if t.w is not None and not (E["name"] == "pe" and id(t.w[0]) == own):
                self._wait(E, *t.w)
            for sem, val in t.r.values():
                if E["name"] == "pe" and id(sem) == own:
                    continue
                self._wait(E, sem, val)

    def _mark(self, sem, val, reads, writes):
        for t in writes:
            t.w = (sem, val)
            t.r = {}
        for t in reads:
            t.r[id(sem)] = (sem, val)

    def op(self, eng, fn, r=(), w=()):
        E = self.eng[eng]
        self._deps(E, r, w)
        inst = fn(E["e"])
        E["cnt"] += 1
        inst.then_inc(E["sem"], 1)
        self._mark(E["sem"], E["cnt"], r, w)
        self.ninst += 1
        return inst

    def dma(self, q, out, in_, r=(), w=(), **kw):
        E = self.eng[q]
        i = self.dnext % self.ndma
        self.dnext += 1
        sem = self.dsem[i]
        self._wait(E, sem, self.dval[i])
        self._deps(E, r, w)
        inst = E["e"].dma_start(out=out, in_=in_, **kw)
        self.dval[i] += 16
        inst.then_inc(sem, 16)
        self._mark(sem, self.dval[i], r, w)
        self.ninst += 1

    def barrier(self):
        for E in self.eng.values():
            for Fn in self.eng.values():
                if Fn["cnt"] > 0:
                    self._wait(E, Fn["sem"], Fn["cnt"])
            for i in range(self.ndma):
                if self.dval[i] > 0:
                    self._wait(E, self.dsem[i], self.dval[i])

    @contextlib.contextmanager
    def scope(self):
        st = contextlib.ExitStack()
        yield st
        self.barrier()
        st.close()


def bc(ap, shape):
    return ap.to_broadcast(list(shape))


class StopBuild(Exception):
    pass


def build(n_layers=DEPTH, stop=None):
    layers = list(range(n_layers)) if isinstance(n_layers, int) else list(n_layers)
    n_layers = layers[-1] + 1
    first_layer = [True]
    kb = KB()
    nc = kb.nc

    def stage(name):
        if stop == name:
            raise StopBuild()

    def din(name, shape, dt=F32):
        return kb.dram(name, shape, dt, kind="ExternalInput")

    hx = din("hx", [T, D])
    condT = din("condT", [128, 8, 2])
    ada_w = din("ada_w", [DEPTH, D, 3 * D])
    ada_bT = din("ada_bT", [DEPTH, 128, 24])
    ada_bg = din("ada_bg", [DEPTH, 128, D])
    pre_gT = din("pre_gT", [DEPTH, 128, 8])
    post_gb = din("post_gb", [DEPTH, 128, D])
    s5_in_w = din("s5_in_w", [2, D, 2 * D])
    s5_glu_w = din("s5_glu_w", [2, D, 2 * D])
    s5_out_w = din("s5_out_w", [2, D, D])
    s5_lamT = din("s5_lamT", [2, 2, 2, 64, 64])
    s5_ldt = din("s5_ldt", [2, 2, 64, 64])
    s5_bT = din("s5_bT", [2, 2, 2, 64, 64 * 16])
    s5_cT = din("s5_cT", [2, 2, 2, 64, 64 * 16])
    s5_dT = din("s5_dT", [2, 128, 8])
    s5_gbT = din("s5_gbT", [2, 128, 16])
    gdn_in_w = din("gdn_in_w", [1, D, 4 * D + 32])
    gdn_out_w = din("gdn_out_w", [1, D, D])
    gdn_prm = din("gdn_prm", [1, 16, 2])
    gdn_cwT = din("gdn_cwT", [1, 128, 24, 5])
    gdn_ngb = din("gdn_ngb", [1, 128, 128])
    c_gmask = din("c_gmask", [128, 4, 128])
    c_bmask = din("c_bmask", [128, 3, 128])
    na_in_w = din("na_in_w", [1, D, 4 * D])
    na_out_w = din("na_out_w", [1, D, D])
    na_biasT = din("na_biasT", [1, 128, 8, 15, 64])
    c_namask = din("c_namask", [128, 64])
    c_ident = din("c_ident", [128, 128])
    c_bd = din("c_bd", [128, 128])
    c_par = din("c_par", [128, 4])
    c_colpar = din("c_colpar", [64, 4, 128])
    out = kb.dram("out", [SEQ, D], F32, kind="ExternalOutput")
    h_d = kb.dram("h_d", [T, D], F32)
    sz_d = kb.dram("sz_d", [8, 128, T], F32)
    u_d = kb.dram("u_d", [8, 128, T], BF16)
    u_tok = Tok()
    gl_d = kb.dram("gl_d", [8, 128, T], BF16)
    dx_d = kb.dram("dx_d", [2, 64, 2, 64, NK], F32)
    xin_d = kb.dram("xin_d", [2, 64, 2, 64, NK], BF16)
    kb_d = kb.dram("kb_d", [2, 128, 8, L, 128], BF16)
    q_d = kb.dram("q_d", [8, 128, T], BF16)
    k_d = kb.dram("k_d", [8, 128, T], BF16)
    v_d = kb.dram("v_d", [T, D], BF16)
    y2_d = kb.dram("y2_d", [8, 128, T], BF16)
    q_tok, k_tok, v_tok, y2_tok = Tok(), Tok(), Tok(), Tok()
    qkvT_d = kb.dram("qkvT_d", [24, 128, T], BF16)
    gb_d = kb.dram("gb_d", [T, 32], F32)
    o_d = kb.dram("o_d", [T, D], F32)
    qkv_tok, gb_tok = Tok(), Tok()
    o_tok = [Tok() for _ in range(NT)]
    hsnap_d = kb.dram("hsnap_d", [3, T, D], F32) if "hsnap_d" in DBG else None
    snap_tok = Tok()
    h_tok = [Tok() for _ in range(NT)]
    sz_tok = Tok()
    gl_tok = Tok()
    dx_tok = [Tok(), Tok()]
    xin_tok = [Tok(), Tok()]
    kbd_tok = Tok()

    ident_f = kb.sb([128, 128], F32, "identf")
    ident_b = kb.sb([128, 128], BF16, "identb")
    bdmask = kb.sb([128, 128], F32, "bdmask")
    parmask = kb.sb([128, 4], F32, "parmask")
    colpar = kb.sb([64, 4, 128], F32, "colpar")
    halfpi = kb.sb([128, 1], F32, "halfpi")
    scT = kb.sb([128, 8, 2], F32, "scT")
    ctok = Tok()
    kb.dma("sp", ident_f[:], c_ident[:, :], w=[ctok])
    kb.dma("pool", ident_b[:], c_ident[:, :], w=[ctok])
    kb.dma("sp", bdmask[:], c_bd[:, :], w=[ctok])
    kb.dma("sp", parmask[:], c_par[:, :], w=[ctok])
    kb.dma("sp", colpar[:], c_colpar[:, :, :], w=[ctok])
    kb.op("dve", lambda e: e.memset(halfpi[:], math.pi / 2), w=[ctok])
    kb.dma("sp", scT[:], condT[:, :, :], w=[ctok])
    kb.op("act", lambda e: e.activation(out=scT[:], in_=scT[:], func=AF.Silu), r=[ctok], w=[ctok])
    kb.barrier()

    modT = kb.sb([128, 24, 2], F32, "modT")
    S1 = kb.sb([128, 8, 2], F32, "S1")
    GPG = kb.sb([128, 2, D], F32, "GPG")
    mod_tok = Tok()

    def adaln(i, st):
        wbuf = kb.sb([128, 8, 512], F32, "adaw", st)
        scRep = kb.sb([128, 8, 2, 128], F32, "scRep", st)
        kb.op("dve", lambda e: e.tensor_copy(out=scRep[:], in_=bc(scT[:].unsqueeze(3), [128, 8, 2, 128])), r=[ctok], w=[ctok])
        abT = kb.sb([128, 24], F32, "abT", st)
        abg = kb.sb([128, D], F32, "abg", st)
        pgb = kb.sb([128, D], F32, "pgb", st)
        pgT = kb.sb([128, 8], F32, "pgT", st)
        wt = Tok()
        ct = Tok()
        kb.dma("sp", abT[:], ada_bT[i], w=[ct])
        kb.dma("sp", abg[:], ada_bg[i], w=[ct])
        kb.dma("sp", pgb[:], post_gb[i], w=[ct])
        kb.dma("sp", pgT[:], pre_gT[i], w=[ct])
        wv = ada_w[i].rearrange("(kc p) n -> p kc n", p=128)
        for grp in range(6):
            kb.dma("sp", wbuf[:], wv[:, :, grp * 512:(grp + 1) * 512], w=[wt])
            for mm in range(4):
                m = grp * 4 + mm
                ps, pt = kb.bank()
                for kc in range(8):
                    kb.op("pe", lambda e, kc=kc, mm=mm, ps=ps: e.matmul(ps[:, 0:2], lhsT=wbuf[:, kc, mm * 128:(mm + 1) * 128],
                                                                          rhs=scT[:, kc, :], start=(kc == 0), stop=(kc == 7)),
                          r=[wt, ctok], w=[pt])
                kb.op("dve", lambda e, m=m, ps=ps: e.tensor_scalar(out=modT[:, m, :], in0=ps[:, 0:2], scalar1=abT[:, m:m + 1],
                                                                     scalar2=None, op0=ALU.add), r=[pt, ct], w=[mod_tok])
            if grp >= 4:
                for rr in range(2):
                    ps, pt = kb.bank()
                    for kc in range(8):
                        kb.op("pe", lambda e, kc=kc, rr=rr, ps=ps: e.matmul(ps[:, :], lhsT=scRep[:, kc, rr, :], rhs=wbuf[:, kc, :],
                                                                              start=(kc == 0), stop=(kc == 7)), r=[wt, ctok], w=[pt])
                    sl = slice((grp - 4) * 512, (grp - 3) * 512)
                    kb.op("dve", lambda e, rr=rr, ps=ps, sl=sl: e.tensor_tensor(out=GPG[:, rr, sl], in0=ps[:, :], in1=abg[:, sl], op=ALU.add),
                          r=[pt, ct], w=[mod_tok])
                    kb.op("dve", lambda e, rr=rr, sl=sl: e.tensor_tensor(out=GPG[:, rr, sl], in0=GPG[:, rr, sl], in1=pgb[:, sl], op=ALU.mult),
                          r=[ct, mod_tok], w=[mod_tok])
        kb.op("dve", lambda e: e.scalar_tensor_tensor(out=S1[:], in0=modT[:, 8:16, :], scalar=1.0,
                                                      in1=bc(pgT[:].unsqueeze(2), [128, 8, 2]), op0=ALU.add, op1=ALU.mult),
              r=[mod_tok, ct], w=[mod_tok])

    def prenorm(i, aT, a_toks, st):
        src = hx if first_layer[0] else h_d
        hts = [kb.sb([128, D], F32, "ht", st) for _ in range(2)]
        htk = [Tok(), Tok()]
        junk = kb.sb([128, D], BF16, "junk", st)
        ybs = [kb.sb([128, D], F32, "yb", st) for _ in range(2)]
        tmf = [kb.sb([128, 4, 128], F32, "tmf", st) for _ in range(2)]
        tmk = [Tok(), Tok()]
        ybk = [Tok(), Tok()]
        stat = kb.sb([128, 4], F32, "stat", st)
        jt = Tok()
        stt = Tok()
        for ti in range(NT):
            rr = 1 if ti < 2 else 0
            ht, hk = hts[ti % 2], htk[ti % 2]
            yb, yk = ybs[ti % 2], ybk[ti % 2]
            kb.dma("sp", ht[:], src[ti * 128:(ti + 1) * 128, :], r=[h_tok[ti]], w=[hk])
            kb.op("act", lambda e, ht=ht: e.activation(out=junk[:], in_=ht[:], func=AF.Square, accum_out=stat[:, 0:1]), r=[hk], w=[jt, stt])
            kb.op("dve", lambda e: e.tensor_scalar(out=stat[:, 1:2], in0=stat[:, 0:1], scalar1=1.0 / D, scalar2=EPS, op0=ALU.mult, op1=ALU.add),
                  r=[stt], w=[stt])
            kb.op("act", lambda e: e.activation(out=stat[:, 2:3], in_=stat[:, 1:2], func=AF.Sqrt), r=[stt], w=[stt])
            kb.op("dve", lambda e: e.reciprocal(out=stat[:, 3:4], in_=stat[:, 2:3]), r=[stt], w=[stt])
            kb.op("act", lambda e, ht=ht, yb=yb: e.activation(out=yb[:], in_=ht[:], func=AF.Copy, scale=stat[:, 3:4]), r=[hk, stt], w=[yk])
            for hf in range(2):
                ps, pt = kb.bank()
                for c4 in range(4):
                    c = hf * 4 + c4
                    kb.op("pe", lambda e, c=c, c4=c4, yb=yb, ps=ps: e.transpose(out=ps[:, c4 * 128:(c4 + 1) * 128], in_=yb[:, c * 128:(c + 1) * 128],
                                                                             identity=ident_f[:]), r=[yk, ctok], w=[pt])
                for c4 in range(4):
                    c = hf * 4 + c4
                    kb.op("act", lambda e, c=c, c4=c4, ps=ps, rr=rr: e.activation(out=aT[:, c, ti * 128:(ti + 1) * 128], in_=ps[:, c4 * 128:(c4 + 1) * 128],
                                                                               func=AF.Identity, scale=S1[:, c, rr:rr + 1], bias=modT[:, c, rr:rr + 1]),
                          r=[pt, mod_tok], w=[a_toks[ti]])

    def proj_bufs(st):
        return ([kb.sb([128, 8, 512], BF16, "wproj", st) for _ in range(2)], [Tok(), Tok()])

    def proj_fm(w_ap, ncols, aT, a_toks, consume, st, wb_=None, gsz=512):
        wv = w_ap.rearrange("(kc p) n -> p kc n", p=128)
        wbs, wks = wb_ if wb_ is not None else proj_bufs(st)
        ng = (ncols + gsz - 1) // gsz
        for g in range(ng):
            wb, wk = wbs[g % 2], wks[g % 2]
            c0 = g * gsz
            cn = min(gsz, ncols - c0)
            kb.dma("pool", wb[:, :, 0:cn], wv[:, :, c0:c0 + cn], w=[wk])
            for tb, (t0, n) in enumerate(TBLK):
                trs = a_toks[t0 // 128:(t0 + n) // 128]
                for mm in range((cn + 127) // 128):
                    mw = min(128, cn - mm * 128)
                    ps, pt = kb.bank()
                    for kc in range(8):
                        kb.op("pe", lambda e, kc=kc, mm=mm, mw=mw, ps=ps, wb=wb, t0=t0, n=n: e.matmul(
                            ps[0:mw, 0:n], lhsT=wb[:, kc, mm * 128:mm * 128 + mw], rhs=aT[:, kc, t0:t0 + n], start=(kc == 0), stop=(kc == 7)),
                            r=[wk] + trs, w=[pt])
                    consume(g * (gsz // 128) + mm, tb, t0, n, ps, pt)

    def outproj_block(i, y2T, y2k, t0, n, wout, wok, st_bufs, last):
        src = hx if first_layer[0] else h_d
        hold, holdk, tmp, tmpk, stat, stt, junk, jt = st_bufs
        for tt in range(n // 128):
            ti = (t0 // 128) + tt
            rr = 1 if ti < 2 else 0
            if last and rr == 1:
                continue
            p0, k0 = kb.bank()
            p1, k1 = kb.bank()
            for nh, (ps, pk) in enumerate(((p0, k0), (p1, k1))):
                for kc in range(8):
                    kb.op("pe", lambda e, kc=kc, ps=ps, nh=nh, tt=tt: e.matmul(ps[:, :], lhsT=y2T[:, kc, tt * 128:(tt + 1) * 128],
                                                                                 rhs=wout[:, kc, nh * 512:(nh + 1) * 512],
                                                                                 start=(kc == 0), stop=(kc == 7)), r=[y2k, wok], w=[pk])
            b = ti % 2
            if OUT_CUT < 2:
                continue
            kb.dma("sp", hold[b][:], src[ti * 128:(ti + 1) * 128, :], r=[h_tok[ti]], w=[holdk[b]])
            if OUT_CUT < 3:
                continue
            kb.op("act", lambda e: e.activation(out=junk[:, 0:512], in_=p0[:, :], func=AF.Square, accum_out=stat[:, 0:1]), r=[k0], w=[jt, stt])
            kb.op("act", lambda e: e.activation(out=junk[:, 512:1024], in_=p1[:, :], func=AF.Square, accum_out=stat[:, 1:2]), r=[k1], w=[jt, stt])
            if OUT_CUT < 4:
                continue
            kb.op("dve", lambda e: e.tensor_tensor(out=stat[:, 2:3], in0=stat[:, 0:1], in1=stat[:, 1:2], op=ALU.add), r=[stt], w=[stt])
            kb.op("dve", lambda e: e.tensor_scalar(out=stat[:, 3:4], in0=stat[:, 2:3], scalar1=1.0 / D, scalar2=EPS, op0=ALU.mult, op1=ALU.add),
                  r=[stt], w=[stt])
            kb.op("act", lambda e: e.activation(out=stat[:, 4:5], in_=stat[:, 3:4], func=AF.Sqrt), r=[stt], w=[stt])
            kb.op("dve", lambda e: e.reciprocal(out=stat[:, 5:6], in_=stat[:, 4:5]), r=[stt], w=[stt])
            if OUT_CUT < 5:
                continue
            for nh, (ps, pk) in enumerate(((p0, k0), (p1, k1))):
                sl = slice(nh * 512, (nh + 1) * 512)
                kb.op("dve", lambda e, ps=ps, sl=sl, b=b, rr=rr: e.scalar_tensor_tensor(out=tmp[b][:, sl], in0=ps[:, :], scalar=stat[:, 5:6],
                                                                                         in1=GPG[:, rr, sl], op0=ALU.mult, op1=ALU.mult),
                      r=[pk, stt, mod_tok], w=[tmpk[b]])
            if OUT_CUT < 6:
                continue
            kb.op("dve", lambda e, b=b: e.tensor_tensor(out=tmp[b][:], in0=tmp[b][:], in1=hold[b][:], op=ALU.add), r=[holdk[b]], w=[tmpk[b]])
            if OUT_CUT < 7:
                continue
            if last:
                kb.dma("sp", out[(ti - 2) * 128:(ti - 1) * 128, :], tmp[b][:], r=[tmpk[b]], w=[h_tok[ti]])
            else:
                kb.dma("sp", h_d[ti * 128:(ti + 1) * 128, :], tmp[b][:], r=[tmpk[b]], w=[h_tok[ti]])

    def outproj_bufs(st):
        hold = [kb.sb([128, D], F32, "hold", st) for _ in range(2)]
        tmp = [kb.sb([128, D], F32, "otmp", st) for _ in range(2)]
        return (hold, [Tok(), Tok()], tmp, [Tok(), Tok()], kb.sb([128, 8], F32, "ostat", st), Tok(), kb.sb([128, D], BF16, "ojunk", st), Tok())

    def s5_layer(i, j, last):
        with kb.scope() as st:
            adaln(i, st)
        if not ONLY_GLU:
            stage("adaln")
            with kb.scope() as st:
                aT = kb.sb([128, 8, T], BF16, "aT", st)
                a_toks = [Tok() for _ in range(NT)]
                prenorm(i, aT, a_toks, st)
                stage("prenorm")
                szs = [kb.sb([128, 512], BF16, "szs", st) for _ in range(2)]
                szf = [kb.sb([128, 512], F32, "szf", st) for _ in range(2)]
                szk = [Tok(), Tok()]
                cnt = [0]

                def consume(m, tb, t0, n, ps, pt):
                    b = cnt[0] % 2
                    cnt[0] += 1
                    if m < 8:
                        kb.op("dve", lambda e: e.tensor_copy(out=szs[b][:, 0:n], in_=ps[:, 0:n]), r=[pt], w=[szk[b]])
                        kb.dma("sp", u_d[m, :, t0:t0 + n], szs[b][:, 0:n], r=[szk[b]], w=[u_tok])
                    else:
                        kb.op("act", lambda e: e.activation(out=szf[b][:, 0:n], in_=ps[:, 0:n], func=AF.Silu), r=[pt], w=[szk[b]])
                        kb.dma("sp", sz_d[m - 8, :, t0:t0 + n], szf[b][:, 0:n], r=[szk[b]], w=[sz_tok])

                proj_fm(s5_in_w[j], 2 * D, aT, a_toks, consume, st)

            stage("proj")
            s5_scan(j)
            stage("scan")
        with kb.scope() as st:
            gw = kb.sb([128, 8, 2 * D], BF16, "gw", st)
            wout = kb.sb([128, 8, D], BF16, "wout", st)
            gbT = kb.sb([128, 16], F32, "gbT", st)
            wk = Tok()
            kb.dma("pool", gw[:], s5_glu_w[j].rearrange("(kc p) n -> p kc n", p=128), w=[wk])
            kb.dma("pool", wout[:], s5_out_w[j].rearrange("(kc p) n -> p kc n", p=128), w=[wk])
            kb.dma("sp", gbT[:], s5_gbT[j], w=[wk])
            stage("gluw")
            glb = [kb.sb([128, 8, 512], BF16, "glb", st) for _ in range(2)]
            szb = [kb.sb([128, 8, 512], F32, "szb", st) for _ in range(2)]
            y2b = [kb.sb([128, 8, 512], BF16, "y2b", st) for _ in range(2)]
            glk, szk2, y2k = [Tok(), Tok()], [Tok(), Tok()], [Tok(), Tok()]
            sg = [kb.sb([128, 512], F32, "sg", st) for _ in range(2)]
            sgk = [Tok(), Tok()]
            tt_ = [kb.sb([128, 512], F32, "gt", st) for _ in range(2)]
            ttk = [Tok(), Tok()]
            ob = outproj_bufs(st)
            for tb, (t0, n) in enumerate(TBLK):
                b = tb % 2
                kb.dma("sp", glb[b][:, :, 0:n], gl_d.rearrange("q p t -> p q t")[:, :, t0:t0 + n], r=[gl_tok], w=[glk[b]])
                kb.dma("sp", szb[b][:, :, 0:n], sz_d.rearrange("q p t -> p q t")[:, :, t0:t0 + n], r=[sz_tok], w=[szk2[b]])
                for m in range(8):
                    pa, ka = kb.bank()
                    pb, kbk = kb.bank()
                    for (ps, pk, off) in ((pa, ka, 0), (pb, kbk, D)):
                        for kc in range(8):
                            kb.op("pe", lambda e, ps=ps, kc=kc, off=off, m=m, b=b, n=n: e.matmul(
                                ps[:, 0:n], lhsT=gw[:, kc, off + m * 128:off + (m + 1) * 128], rhs=glb[b][:, kc, 0:n],
                                start=(kc == 0), stop=(kc == 7)), r=[wk, glk[b]], w=[pk])
                    s = m % 2
                    kb.op("act", lambda e, s=s, pb=pb, m=m, n=n: e.activation(out=sg[s][:, 0:n], in_=pb[:, 0:n], func=AF.Sigmoid,
                                                                              bias=gbT[:, 8 + m:9 + m]), r=[kbk, wk], w=[sgk[s]])
                    kb.op("dve", lambda e, s=s, pa=pa, m=m, n=n: e.scalar_tensor_tensor(out=tt_[s][:, 0:n], in0=pa[:, 0:n], scalar=gbT[:, m:m + 1],
                                                                                       in1=sg[s][:, 0:n], op0=ALU.add, op1=ALU.mult),
                          r=[ka, sgk[s], wk], w=[ttk[s]])
                    kb.op("pool", lambda e, s=s, m=m, b=b, n=n: e.tensor_tensor(out=y2b[b][:, m, 0:n], in0=tt_[s][:, 0:n], in1=szb[b][:, m, 0:n],
                                                                                 op=ALU.mult), r=[ttk[s], szk2[b]], w=[y2k[b]])
                if tb == 0:
                    stage("glu0")
                outproj_block(i, y2b[b], y2k[b], t0, n, wout, wk, ob, last)
                if tb == 0:
                    stage("out0")

    def s5_scan(j):
        NH = NK // 2
        with kb.scope() as st1:
            PR = [kb.sb([64, L + 1, 64], F32, "PR", st1) for _ in range(2)]
            PI = [kb.sb([64, L + 1, 64], F32, "PI", st1) for _ in range(2)]
            ptk = [Tok(), Tok()]
            for d in range(2):
                with kb.scope() as st:
                    g = Tok()

                    def t64(name):
                        return kb.sb([64, 64], F32, name, st)
                    lr, li, dt, mag, th, c, s, cc, ss, cs = [t64(x) for x in "lr li dt mag th c s cc ss cs".split()]
                    ar, ai, den, fr, fi, t1, t2, am1 = [t64(x) for x in "ar ai den fr fi t1 t2 am1".split()]
                    kb.dma("sp", lr[:], s5_lamT[j, d, 0], w=[g])
                    kb.dma("sp", li[:], s5_lamT[j, d, 1], w=[g])
                    kb.dma("sp", dt[:], s5_ldt[j, d], w=[g])
                    V = lambda fn: kb.op("dve", fn, r=[g], w=[g])
                    A = lambda fn: kb.op("act", fn, r=[g], w=[g])
                    TT = lambda o, a, b_, op: V(lambda e: e.tensor_tensor(out=o, in0=a, in1=b_, op=op))
                    A(lambda e: e.activation(out=dt[:], in_=dt[:], func=AF.Exp))
                    TT(mag[:], lr[:], dt[:], ALU.mult)
                    A(lambda e: e.activation(out=mag[:], in_=mag[:], func=AF.Exp))
                    TT(th[:], li[:], dt[:], ALU.mult)
                    A(lambda e: e.activation(out=s[:], in_=th[:], func=AF.Sin, scale=0.125))
                    A(lambda e: e.activation(out=c[:], in_=th[:], func=AF.Sin, scale=-0.125, bias=halfpi[0:64, 0:1]))
                    for _ in range(3):
                        TT(cc[:], c[:], c[:], ALU.mult)
                        TT(ss[:], s[:], s[:], ALU.mult)
                        TT(cs[:], c[:], s[:], ALU.mult)
                        TT(c[:], cc[:], ss[:], ALU.subtract)
                        V(lambda e: e.tensor_scalar(out=s[:], in0=cs[:], scalar1=2.0, scalar2=None, op0=ALU.mult))
                    TT(ar[:], mag[:], c[:], ALU.mult)
                    TT(ai[:], mag[:], s[:], ALU.mult)
                    TT(t1[:], lr[:], lr[:], ALU.mult)
                    TT(t2[:], li[:], li[:], ALU.mult)
                    TT(den[:], t1[:], t2[:], ALU.add)
                    V(lambda e: e.reciprocal(out=den[:], in_=den[:]))
                    V(lambda e: e.tensor_scalar(out=am1[:], in0=ar[:], scalar1=-1.0, scalar2=None, op0=ALU.add))
                    TT(t1[:], am1[:], lr[:], ALU.mult)
                    TT(t2[:], ai[:], li[:], ALU.mult)
                    TT(t1[:], t1[:], t2[:], ALU.add)
                    TT(fr[:], t1[:], den[:], ALU.mult)
                    TT(t1[:], ai[:], lr[:], ALU.mult)
                    TT(t2[:], am1[:], li[:], ALU.mult)
                    TT(t1[:], t1[:], t2[:], ALU.subtract)
                    TT(fi[:], t1[:], den[:], ALU.mult)

                    def cmul(orr, oi, xr, xi, yr, yi):
                        TT(t1[:], xr, yr, ALU.mult)
                        TT(t2[:], xi, yi, ALU.mult)
                        TT(orr, t1[:], t2[:], ALU.subtract)
                        TT(t1[:], xr, yi, ALU.mult)
                        TT(t2[:], xi, yr, ALU.mult)
                        TT(oi, t1[:], t2[:], ALU.add)
                    pr, pi = PR[d], PI[d]
                    V(lambda e: e.memset(pr[:, 0, :], 1.0))
                    V(lambda e: e.memset(pi[:, 0, :], 0.0))
                    for n_ in range(1, L + 1):
                        cmul(pr[:, n_, :], pi[:, n_, :], pr[:, n_ - 1, :], pi[:, n_ - 1, :], ar[:], ai[:])
                    PFR = kb.sb([64, L, 64], F32, "PFR", st)
                    PFI = kb.sb([64, L, 64], F32, "PFI", st)
                    for n_ in range(L):
                        cmul(PFR[:, n_, :], PFI[:, n_, :], pr[:, n_, :], pi[:, n_, :], fr[:], fi[:])
                    WTr = kb.sb([64, L, 64, 16], BF16, "WTr", st)
                    WTi = kb.sb([64, L, 64, 16], BF16, "WTi", st)
                    stA = contextlib.ExitStack()
                    br = kb.sb([64, 64, 16], F32, "br", stA)
                    bi = kb.sb([64, 64, 16], F32, "bi", stA)
                    kb.dma("sp", br[:], s5_bT[j, d, 0].rearrange("p (g c) -> p g c", c=16), w=[g])
                    kb.dma("sp", bi[:], s5_bT[j, d, 1].rearrange("p (g c) -> p g c", c=16), w=[g])
                    w1 = kb.sb([64, 64, 16], F32, "w1", stA)
                    w2 = kb.sb([64, 64, 16], F32, "w2", stA)
                    for jp in range(L):
                        n_ = (L - 1 - jp) if d == 0 else jp
                        pfr = bc(PFR[:, n_, :].unsqueeze(2), [64, 64, 16])
                        pfi = bc(PFI[:, n_, :].unsqueeze(2), [64, 64, 16])
                        TT(w1[:], br[:], pfr, ALU.mult)
                        TT(w2[:], bi[:], pfi, ALU.mult)
                        TT(WTr[:, jp], w1[:], w2[:], ALU.subtract)
                        TT(w1[:], bi[:], pfr, ALU.mult)
                        TT(w2[:], br[:], pfi, ALU.mult)
                        TT(WTi[:, jp], w1[:], w2[:], ALU.add)
                    stage("gen%d" % d)
                    kb.barrier()
                    stA.close()
                    stB = contextlib.ExitStack()
                    crb = kb.sb([64, 64 * 16], BF16, "crb", stB)
                    cib = kb.sb([64, 64 * 16], BF16, "cib", stB)
                    kb.dma("pool", crb[:], s5_cT[j, d, 0], w=[g])
                    kb.dma("pool", cib[:], s5_cT[j, d, 1], w=[g])
                    A(lambda e: e.mul(out=cib[:], in_=cib[:], mul=-1.0))
                    KBs = kb.sb([128, 8, L, 128], BF16, "KBs", stB)
                    dT = kb.sb([128, 8], F32, "dT", stB)
                    kb.dma("sp", dT[:], s5_dT[j], w=[g])
                    kt = Tok()
                    for q in range(8):
                        for tau in range(L):
                            jp = (L - 1 - tau) if d == 0 else tau
                            ps, pt = kb.bank()
                            kb.op("pe", lambda e, ps=ps, jp=jp, q=q: e.matmul(ps[:, 0:128], lhsT=WTr[:, jp, q * 8:(q + 1) * 8, :].rearrange("p g c -> p (g c)"), rhs=crb[:, q * 128:(q + 1) * 128],
                                                                               start=True, stop=False), r=[g], w=[pt])
                            kb.op("pe", lambda e, ps=ps, jp=jp, q=q: e.matmul(ps[:, 0:128], lhsT=WTi[:, jp, q * 8:(q + 1) * 8, :].rearrange("p g c -> p (g c)"), rhs=cib[:, q * 128:(q + 1) * 128],
                                                                               start=False, stop=True), r=[g], w=[pt])
                            kb.op("dve", lambda e, ps=ps, q=q, tau=tau: e.tensor_tensor(out=KBs[:, q, tau, :], in0=ps[:, 0:128], in1=bdmask[:], op=ALU.mult),
                                  r=[pt, ctok], w=[kt])
                        if d == 0:
                            kb.op("dve", lambda e, q=q: e.scalar_tensor_tensor(out=KBs[:, q, 0, :], in0=ident_f[:], scalar=dT[:, q:q + 1], in1=KBs[:, q, 0, :],
                                                                               op0=ALU.mult, op1=ALU.add), r=[g, ctok, kt], w=[kt])
                    kb.dma("sp", kb_d[d], KBs[:], r=[kt], w=[kbd_tok])
                    kb.barrier()
                    stB.close()
                    stage("kblk%d" % d)
                    WPs = [kb.sb([128, 4, 2, L, 64], BF16, "WP", st) for _ in range(2)]
                    WPk = [Tok(), Tok()]
                    dXs = [kb.sb([64, 2, 8, NK], F32, "dXs", st) for _ in range(2)]
                    dXk = [Tok(), Tok()]
                    ev = 0
                    uqs = [kb.sb([128, T], BF16, "uq", st) for _ in range(2)]
                    uqk = [Tok(), Tok()]
                    for q in range(8):
                        WP, wpk = WPs[q % 2], WPk[q % 2]
                        dX, dxk = dXs[q % 2], dXk[q % 2]
                        uq, uk = uqs[q % 2], uqk[q % 2]
                        kb.dma("sp", uq[:], u_d[q], r=[u_tok], w=[uk])
                        uv = uq[:].rearrange("p (k j) -> p k j", j=L)
                        ps, pt = kb.bank()
                        psb = ps[:].bitcast(BF16)
                        for ri, WTx in enumerate((WTr, WTi)):
                            for jp in range(L):
                                o = (ri * L + jp) * 64
                                kb.op("pe", lambda e, psb=psb, o=o, WTx=WTx, jp=jp, q=q: e.transpose(out=psb[:, o:o + 64], in_=WTx[:, jp, q * 8:(q + 1) * 8, :].rearrange("p g c -> p (g c)"),
                                                                                                     identity=ident_b[0:64, 0:64]), r=[g, ctok], w=[pt])
                        for par in range(4):
                            kb.op("dve", lambda e, psb=psb, par=par, WP=WP: e.tensor_scalar(out=WP[:, par].rearrange("p a b c -> p (a b c)"), in0=psb[:, 0:2 * L * 64],
                                                                                             scalar1=parmask[:, par:par + 1], scalar2=None, op0=ALU.mult),
                                  r=[pt, ctok], w=[wpk])
                        for pp in range(4):
                            for par in range(2):
                                gl = 2 * pp + par
                                for ri in range(2):
                                    for hf in range(2):
                                        ps, pt = kb.bank()
                                        for jp in range(L):
                                            kb.op("pe", lambda e, ps=ps, WP=WP, par=par, ri=ri, jp=jp, pp=pp, q=q, hf=hf: e.matmul(
                                                ps[0:64, 0:NH], lhsT=(WP[32 * pp:32 * pp + 32, par, ri, jp, :] if pp < 3 else WP[64:128, 2 + par, ri, jp, :]),
                                                rhs=(uv[32 * pp:32 * pp + 32, hf * NH:(hf + 1) * NH, jp] if pp < 3 else uv[64:128, hf * NH:(hf + 1) * NH, jp]),
                                                start=(jp == 0), stop=(jp == L - 1)),
                                                r=[wpk, uk], w=[pt])
                                        dst = dX[:, ri, gl, hf * NH:(hf + 1) * NH]
                                        if ev % 2 == 0:
                                            kb.op("act", lambda e, ps=ps, dst=dst: e.copy(out=dst, in_=ps[0:64, 0:NH]), r=[pt], w=[dxk])
                                        else:
                                            kb.op("dve", lambda e, ps=ps, dst=dst: e.tensor_copy(out=dst, in_=ps[0:64, 0:NH]), r=[pt], w=[dxk])
                                        ev += 1
                        kb.dma("sp", dx_d[d, :, :, q * 8:(q + 1) * 8, :], dX[:], r=[dxk], w=[dx_tok[d]])
            stage("dx")
            SEG = 32
            ctx_k = CTX // L
            segs_f = [(0, ctx_k)] + [(k0, min(SEG, NK - k0)) for k0 in range(ctx_k, NK, SEG)]
            with kb.scope() as st:
                def rec(d):
                    E = "dve" if d == 0 else "pool"
                    X = kb.sb([64, 2, 64], F32, "X", st)
                    t1 = kb.sb([64, 2, 64], F32, "rt1", st)
                    t2 = kb.sb([64, 2, 64], F32, "rt2", st)
                    AR2 = kb.sb([64, 2, 64], F32, "AR2", st)
                    AIn = kb.sb([64, 64], F32, "AIn", st)
                    AIp = kb.sb([64, 64], F32, "AIp", st)
                    xk, tk1, tk2, ak = Tok(), Tok(), Tok(), Tok()
                    kb.op(E, lambda e, X=X: e.memset(X[:], 0.0), w=[xk])
                    for h_ in range(2):
                        kb.op(E, lambda e, h_=h_, AR2=AR2, d=d: e.tensor_copy(out=AR2[:, h_, :], in_=PR[d][:, L, :]), w=[ak])
                    kb.op(E, lambda e, AIp=AIp, d=d: e.tensor_copy(out=AIp[:], in_=PI[d][:, L, :]), w=[ak])
                    kb.op(E, lambda e, AIn=AIn, d=d: e.tensor_scalar(out=AIn[:], in0=PI[d][:, L, :], scalar1=-1.0, scalar2=None, op0=ALU.mult), w=[ak])
                    dsegs = [kb.sb([64, 2, 64, SEG], F32, "dseg", st) for _ in range(2)]
                    xsegs = [kb.sb([64, 2, 64, SEG], BF16, "xseg", st) for _ in range(2)]
                    dsk, xsk = [Tok(), Tok()], [Tok(), Tok()]
                    if d == 0:
                        order = [(k0, n, False) for (k0, n) in segs_f]
                    else:
                        csegs = [(k0, n) for (k0, n) in segs_f if k0 < ctx_k]
                        lsegs = [(k0, n) for (k0, n) in segs_f if k0 >= ctx_k]
                        order = [(k0, n, True) for (k0, n) in reversed(csegs)] + [(k0, n, True) for (k0, n) in reversed(lsegs)]
                    for si, (k0, n, rev) in enumerate(order):
                        b = si % 2
                        ds, xs = dsegs[b], xsegs[b]
                        kb.dma("sp", ds[:, :, :, 0:n], dx_d[d, :, :, :, k0:k0 + n], r=[dx_tok[d]], w=[dsk[b]])
                        ks = range(n - 1, -1, -1) if rev else range(n)
                        for kk in ks:
                            kb.op("act", lambda e, xs=xs, kk=kk, X=X: e.copy(out=xs[:, :, :, kk], in_=X[:]), r=[xk], w=[xsk[b]])
                            kb.op(E, lambda e, t1=t1, X=X, AR2=AR2: e.tensor_tensor(out=t1[:], in0=X[:], in1=AR2[:], op=ALU.mult), r=[xk, ak], w=[tk1])
                            kb.op(E, lambda e, t2=t2, X=X, AIn=AIn: e.tensor_tensor(out=t2[:, 0, :], in0=X[:, 1, :], in1=AIn[:], op=ALU.mult), r=[xk, ak], w=[tk2])
                            kb.op(E, lambda e, t2=t2, X=X, AIp=AIp: e.tensor_tensor(out=t2[:, 1, :], in0=X[:, 0, :], in1=AIp[:], op=ALU.mult), r=[xk, ak], w=[tk2])
                            kb.op(E, lambda e, t1=t1, t2=t2: e.tensor_tensor(out=t1[:], in0=t1[:], in1=t2[:], op=ALU.add), r=[tk2], w=[tk1])
                            kb.op(E, lambda e, t1=t1, X=X, ds=ds, kk=kk: e.tensor_tensor(out=X[:], in0=t1[:], in1=ds[:, :, :, kk], op=ALU.add),
                                  r=[tk1, dsk[b]], w=[xk])
                            yield
                        kb.dma("sp", xin_d[d, :, :, :, k0:k0 + n], xs[:, :, :, 0:n], r=[xsk[b]], w=[xin_tok[d]])
                alive = [rec(0), rec(1)]
                while alive:
                    for g_ in list(alive):
                        try:
                            next(g_)
                        except StopIteration:
                            alive.remove(g_)
            stage("rec")
            with kb.scope() as st:
                PRm = [kb.sb([64, L, 64], F32, "PRm", st) for _ in range(2)]
                PIm = [kb.sb([64, L, 64], F32, "PIm", st) for _ in range(2)]
                pmk = Tok()
                for d in range(2):
                    for r_ in range(L):
                        m_ = r_ + 1 if d == 0 else L - r_
                        kb.op("dve", lambda e, d=d, r_=r_, m_=m_: e.tensor_copy(out=PRm[d][:, r_, :], in_=PR[d][:, m_, :]), w=[pmk])
                        kb.op("dve", lambda e, d=d, r_=r_, m_=m_: e.tensor_copy(out=PIm[d][:, r_, :], in_=PI[d][:, m_, :]), w=[pmk])
                cr = [kb.sb([64, 64, 16], F32, "cr", st) for _ in range(2)]
                ci = [kb.sb([64, 64, 16], F32, "ci", st) for _ in range(2)]
                for d in range(2):
                    kb.dma("sp", cr[d][:], s5_cT[j, d, 0].rearrange("p (g c) -> p g c", c=16), w=[pmk])
                    kb.dma("sp", ci[d][:], s5_cT[j, d, 1].rearrange("p (g c) -> p g c", c=16), w=[pmk])
                m1 = kb.sb([64, L, 8, 16], F32, "m1", st)
                m2 = kb.sb([64, L, 8, 16], F32, "m2", st)
                mr = kb.sb([64, L, 8, 16], F32, "mr", st)
                mi = kb.sb([64, L, 8, 16], F32, "mi", st)
                mk = Tok()
                MX = [kb.sb([64, 2, 2, 4, L, 128], BF16, "MX", st) for _ in range(1)]
                mxk = [Tok(), Tok()]
                XQ = [kb.sb([64, 2, 2, 8, NH], BF16, "XQ", st) for _ in range(2)]
                xqk = [Tok(), Tok()]
                KQ = [kb.sb([128, 2, L, 128], BF16, "KQ", st) for _ in range(2)]
                kqk = [Tok(), Tok()]
                glq = [kb.sb([128, NK, L], BF16, "glq", st) for _ in range(2)]
                glk = [Tok(), Tok()]
                yx = [kb.sb([128, NH], F32, "yx", st) for _ in range(2)]
                y2 = [kb.sb([128, NH], F32, "yy", st) for _ in range(2)]
                ysg = [kb.sb([128, NH], F32, "ysg", st) for _ in range(2)]
                yk = [Tok(), Tok()]
                uqs = [kb.sb([128, T], BF16, "uq3", st) for _ in range(2)]
                uqk = [Tok(), Tok()]
                it = 0
                for q in range(8):
                    MXq, mxq = MX[0], mxk[0]
                    KQq, kqq = KQ[q % 2], kqk[q % 2]
                    gq, gqk = glq[q % 2], glk[q % 2]
                    for d in range(2):
                        kb.dma("sp", KQq[:, d], kb_d[d, :, q], r=[kbd_tok], w=[kqq])
                    uq, uk = uqs[q % 2], uqk[q % 2]
                    kb.dma("sp", uq[:], u_d[q], r=[u_tok], w=[uk])
                    uv = uq[:].rearrange("p (k j) -> p k j", j=L)
                    for d in range(2):
                        crq = bc(cr[d][:, q * 8:(q + 1) * 8, :].unsqueeze(1), [64, L, 8, 16])
                        ciq = bc(ci[d][:, q * 8:(q + 1) * 8, :].unsqueeze(1), [64, L, 8, 16])
                        prq = bc(PRm[d][:, :, q * 8:(q + 1) * 8].unsqueeze(3), [64, L, 8, 16])
                        piq = bc(PIm[d][:, :, q * 8:(q + 1) * 8].unsqueeze(3), [64, L, 8, 16])
                        V = lambda fn: kb.op("dve", fn, r=[pmk, mk], w=[mk])
                        V(lambda e: e.tensor_tensor(out=m1[:], in0=crq, in1=prq, op=ALU.mult))
                        V(lambda e: e.tensor_tensor(out=m2[:], in0=ciq, in1=piq, op=ALU.mult))
                        V(lambda e: e.tensor_tensor(out=mr[:], in0=m1[:], in1=m2[:], op=ALU.subtract))
                        V(lambda e: e.tensor_tensor(out=m1[:], in0=crq, in1=piq, op=ALU.mult))
                        V(lambda e: e.tensor_tensor(out=m2[:], in0=ciq, in1=prq, op=ALU.mult))
                        V(lambda e: e.scalar_tensor_tensor(out=mi[:], in0=m1[:], scalar=-1.0, in1=m2[:], op0=ALU.mult, op1=ALU.subtract))
                        for ri, src in enumerate((mr, mi)):
                            for par in range(4):
                                kb.op("dve", lambda e, d=d, ri=ri, par=par, src=src, MXq=MXq: e.tensor_tensor(
                                    out=MXq[:, d, ri, par], in0=src[:].rearrange("p r g c -> p r (g c)"),
                                    in1=bc(colpar[:, par:par + 1, :], [64, L, 128]), op=ALU.mult), r=[mk, ctok], w=[mxq])
                    for hf in range(2):
                        XQh, xqh = XQ[hf], xqk[hf]
                        for d in range(2):
                            kb.dma("sp", XQh[:, d], xin_d[d, :, :, q * 8:(q + 1) * 8, hf * NH:(hf + 1) * NH], r=xin_tok, w=[xqh])
                        for r_ in range(L):
                            ps, pt = kb.bank()
                            first = [True]

                            def MM(lhsT, rhs, outp, extra_r):
                                stt_ = first[0]
                                first[0] = False
                                kb.op("pe", lambda e: e.matmul(outp, lhsT=lhsT, rhs=rhs, start=stt_, stop=False, skip_group_check=True), r=extra_r, w=[pt])
                            for tau in range(0, r_ + 1):
                                MM(KQq[:, 0, tau, :], uv[:, hf * NH:(hf + 1) * NH, r_ - tau], ps[:, 0:NH], [kqq, uk])
                            for tau in range(0, L - r_):
                                MM(KQq[:, 1, tau, :], uv[:, hf * NH:(hf + 1) * NH, r_ + tau], ps[:, 0:NH], [kqq, uk])
                            for d in range(2):
                                for pp in range(4):
                                    for par in range(2):
                                        for ri in range(2):
                                            if pp < 3:
                                                MM(MXq[:, d, ri, par, r_, 32 * pp:32 * pp + 32], XQh[:, d, ri, 2 * pp + par, :], ps[32 * pp:32 * pp + 32, 0:NH], [mxq, xqh])
                                            else:
                                                MM(MXq[:, d, ri, 2 + par, r_, 64:128], XQh[:, d, ri, 2 * pp + par, :], ps[64:128, 0:NH], [mxq, xqh])
                            b = it % 2
                            it += 1
                            kb.op("act", lambda e, b=b, ps=ps: e.copy(out=yx[b][:], in_=ps[:, 0:NH]), r=[pt], w=[yk[b]])
                            kb.op("pool", lambda e, b=b: e.tensor_tensor(out=y2[b][:], in0=yx[b][:], in1=yx[b][:], op=ALU.mult), r=[yk[b]], w=[yk[b]])
                            kb.op("dve", lambda e, b=b: e.tensor_scalar(out=y2[b][:], in0=y2[b][:], scalar1=0.044715, scalar2=1.0, op0=ALU.mult, op1=ALU.add),
                                  r=[yk[b]], w=[yk[b]])
                            kb.op("pool", lambda e, b=b: e.tensor_tensor(out=y2[b][:], in0=y2[b][:], in1=yx[b][:], op=ALU.mult), r=[yk[b]], w=[yk[b]])
                            kb.op("act", lambda e, b=b: e.activation(out=ysg[b][:], in_=y2[b][:], func=AF.Sigmoid, scale=1.5957691216057308), r=[yk[b]], w=[yk[b]])
                            kb.op("dve", lambda e, b=b, gq=gq, hf=hf, r_=r_: e.tensor_tensor(out=gq[:, hf * NH:(hf + 1) * NH, r_], in0=yx[b][:], in1=ysg[b][:], op=ALU.mult),
                                  r=[yk[b]], w=[gqk])
                    kb.dma("sp", gl_d[q], gq[:].rearrange("p k j -> p (k j)"), r=[gqk], w=[gl_tok])


    def stash_consume(dst_list, st):
        szs = [kb.sb([128, 512], BF16, "stg", st) for _ in range(3)]
        szf = [kb.sb([128, 512], F32, "stgf", st) for _ in range(3)]
        szk = [Tok() for _ in range(3)]
        cnt = [0]

        def consume(m, tb, t0, n, ps, pt):
            dst, dtok, func = dst_list[m]
            b = cnt[0] % 3
            cnt[0] += 1
            if func is None:
                kb.op("dve", lambda e: e.tensor_copy(out=szs[b][0:ps_rows(m), 0:n], in_=ps[0:ps_rows(m), 0:n]), r=[pt], w=[szk[b]])
            else:
                kb.op("act", lambda e: e.activation(out=szf[b][0:ps_rows(m), 0:n], in_=ps[0:ps_rows(m), 0:n], func=func), r=[pt], w=[szk[b]])
                kb.dma("sp", dst[0:ps_rows(m), t0:t0 + n], szf[b][0:ps_rows(m), 0:n], r=[szk[b]], w=[dtok])
                return
            kb.dma("sp", dst[0:ps_rows(m), t0:t0 + n], szs[b][0:ps_rows(m), 0:n], r=[szk[b]], w=[dtok])

        def ps_rows(m):
            return dst_list[m][0].shape[0]
        return consume

    def final_stage(i, last, wout_ap, st):
        wout = kb.sb([128, 8, D], BF16, "wout", st)
        wk = Tok()
        kb.dma("pool", wout[:], wout_ap.rearrange("(kc p) n -> p kc n", p=128), w=[wk])
        y2b = [kb.sb([128, 8, 512], BF16, "y2b", st) for _ in range(2)]
        y2k = [Tok(), Tok()]
        ob = outproj_bufs(st)
        for tb, (t0, n) in enumerate(TBLK):
            b = tb % 2
            kb.dma("sp", y2b[b][:, :, 0:n], y2_d.rearrange("q p t -> p q t")[:, :, t0:t0 + n], r=[y2_tok], w=[y2k[b]])
            outproj_block(i, y2b[b], y2k[b], t0, n, wout, wk, ob, last)

    def na_layer(i, j, last):
        with kb.scope() as st:
            adaln(i, st)
        with kb.scope() as st:
            aT = kb.sb([128, 8, T], BF16, "aT", st)
            a_toks = [Tok() for _ in range(NT)]
            prenorm(i, aT, a_toks, st)
            dl = [(q_d[m], q_tok, None) for m in range(8)] + [(k_d[m], k_tok, None) for m in range(8)]
            dl += [None] * 8 + [(sz_d[m], sz_tok, AF.Silu) for m in range(8)]
            cons = stash_consume(dl, st)
            proj_fm(na_in_w[j][:, 0:2 * D], 2 * D, aT, a_toks, cons, st)
            proj_fm(na_in_w[j][:, 3 * D:4 * D], D, aT, a_toks, lambda m, *a: cons(m + 24, *a), st)
            wv = kb.sb([128, 8, D], BF16, "wv", st)
            wvk = Tok()
            kb.dma("pool", wv[:], na_in_w[j].rearrange("(kc p) n -> p kc n", p=128)[:, :, 2 * D:3 * D], w=[wvk])
            vst = [kb.sb([128, D], BF16, "vst", st) for _ in range(2)]
            vsk = [Tok(), Tok()]
            for ti in range(NT):
                b = ti % 2
                for nh in range(2):
                    ps, pt = kb.bank()
                    for kc in range(8):
                        kb.op("pe", lambda e, ps=ps, kc=kc, ti=ti, nh=nh: e.matmul(ps[:, :], lhsT=aT[:, kc, ti * 128:(ti + 1) * 128],
                                                                                     rhs=wv[:, kc, nh * 512:(nh + 1) * 512], start=(kc == 0), stop=(kc == 7)),
                              r=[wvk, a_toks[ti]], w=[pt])
                    if nh == 0:
                        kb.op("act", lambda e, ps=ps, b=b: e.copy(out=vst[b][:, 0:512], in_=ps[:, :]), r=[pt], w=[vsk[b]])
                    else:
                        kb.op("dve", lambda e, ps=ps, b=b: e.tensor_copy(out=vst[b][:, 512:1024], in_=ps[:, :]), r=[pt], w=[vsk[b]])
                kb.dma("sp", v_d[ti * 128:(ti + 1) * 128, :], vst[b][:], r=[vsk[b]], w=[v_tok])
        stage("na_proj")
        with kb.scope() as st:
            biasm = kb.sb([128, 8, 15, 64], BF16, "biasm", st)
            bstage = kb.sb([128, 15, 64], F32, "bstage", st)
            mask = kb.sb([128, 64], F32, "namask", st)
            bk = Tok()
            kb.dma("sp", mask[:], c_namask[:, :], w=[bk])
            for ch in range(8):
                kb.dma("sp", bstage[:], na_biasT[j, :, ch], w=[bk])
                kb.op("dve", lambda e, ch=ch: e.tensor_tensor(out=biasm[:, ch], in0=bstage[:], in1=bc(mask[:].unsqueeze(1), [128, 15, 64]), op=ALU.add),
                      r=[bk], w=[bk])
            qs = [kb.sb([128, T], BF16, "qs", st) for _ in range(2)]
            ks = [kb.sb([128, T], BF16, "ks", st) for _ in range(2)]
            szc = [kb.sb([128, T], F32, "szc", st) for _ in range(2)]
            vA = [kb.sb([128, 32, 128], BF16, "vA", st) for _ in range(2)]
            vB = [kb.sb([128, 32, 128], BF16, "vB", st) for _ in range(2)]
            vC = [kb.sb([128, 2, 128], BF16, "vC", st) for _ in range(2)]
            y2c = [kb.sb([128, T], BF16, "y2c", st) for _ in range(2)]
            lk = [Tok(), Tok()]
            y2k = [Tok(), Tok()]
            NB = 3
            sc = [kb.sb([128, 768], F32, "sc", st) for _ in range(NB)]
            pb = [kb.sb([128, 768], BF16, "pb", st) for _ in range(NB)]
            pT = [kb.sb([128, 768], BF16, "pT", st) for _ in range(NB)]
            sts = [kb.sb([128, 4], F32, "nst", st) for _ in range(NB)]
            sck = [Tok() for _ in range(NB)]
            pbk = [Tok() for _ in range(NB)]
            ptk = [Tok() for _ in range(NB)]
            stk = [Tok() for _ in range(NB)]
            it = [0]
            vd3 = v_d.rearrange("t (c d) -> t c d", d=128)

            def softmax_pv(ps_list, width, vtiles, out_ps, out_ap_rows, ncol, y2dst, szsrc, deps):
                b = it[0] % NB
                it[0] += 1
                for (pap, ptok, c0, w_, bias) in ps_list:
                    if bias is not None:
                        kb.op("dve", lambda e, pap=pap, c0=c0, w_=w_, bias=bias: e.scalar_tensor_tensor(
                            out=sc[b][:, c0:c0 + w_], in0=pap, scalar=0.125, in1=bias, op0=ALU.mult, op1=ALU.add), r=[ptok, bk], w=[sck[b]])
                    else:
                        kb.op("act", lambda e, pap=pap, c0=c0, w_=w_: e.activation(out=sc[b][:, c0:c0 + w_], in_=pap, func=AF.Copy, scale=0.125),
                              r=[ptok], w=[sck[b]])
                kb.op("dve", lambda e: e.reduce_max(out=sts[b][:, 0:1], in_=sc[b][:, 0:width], axis=AX.X), r=[sck[b]], w=[stk[b]])
                kb.op("dve", lambda e: e.tensor_scalar(out=sts[b][:, 1:2], in0=sts[b][:, 0:1], scalar1=-1.0, scalar2=None, op0=ALU.mult), r=[stk[b]], w=[stk[b]])
                kb.op("act", lambda e: e.activation(out=pb[b][:, 0:width], in_=sc[b][:, 0:width], func=AF.Exp, bias=sts[b][:, 1:2], accum_out=sts[b][:, 2:3]),
                      r=[sck[b], stk[b]], w=[pbk[b], stk[b]])
                kb.op("dve", lambda e: e.reciprocal(out=sts[b][:, 3:4], in_=sts[b][:, 2:3]), r=[stk[b]], w=[stk[b]])
                kb.op("dve", lambda e: e.tensor_scalar(out=pb[b][:, 0:width], in0=pb[b][:, 0:width], scalar1=sts[b][:, 3:4], scalar2=None, op0=ALU.mult),
                      r=[stk[b]], w=[pbk[b]])
                ps, pt = kb.bank()
                psb = ps[:].bitcast(BF16)
                nkt = width // 128
                for kt in range(nkt):
                    kb.op("pe", lambda e, kt=kt: e.transpose(out=psb[:, kt * 128:(kt + 1) * 128], in_=pb[b][:, kt * 128:(kt + 1) * 128], identity=ident_b[:]),
                          r=[pbk[b], ctok], w=[pt])
                kb.op("act", lambda e: e.copy(out=pT[b][:, 0:width], in_=psb[:, 0:width]), r=[pt], w=[ptk[b]])
                ops, opt = out_ps
                first = {}
                for kt in range(nkt):
                    for (hb, c0q) in out_ap_rows:
                        kb.op("pe", lambda e, kt=kt, hb=hb, c0q=c0q: e.matmul(ops[hb:hb + 64, 0:ncol], lhsT=vtiles[kt][:, hb:hb + 64],
                                                                               rhs=pT[b][:, kt * 128 + c0q:kt * 128 + c0q + ncol],
                                                                               start=(kt == 0), stop=(kt == nkt - 1)), r=[ptk[b]] + deps, w=[opt])
                rows = slice(min(h_ for h_, _ in out_ap_rows), max(h_ for h_, _ in out_ap_rows) + 64)
                kb.op("dve", lambda e: e.tensor_tensor(out=y2dst[rows], in0=ops[rows, 0:ncol], in1=szsrc[rows], op=ALU.mult), r=[opt] + deps, w=[y2k[cb[0]]])

            cb = [0]
            for ch in range(8):
                b2 = ch % 2
                cb[0] = b2
                kb.dma("sp", qs[b2][:], q_d[ch], r=[q_tok], w=[lk[b2]])
                kb.dma("sp", ks[b2][:], k_d[ch], r=[k_tok], w=[lk[b2]])
                kb.dma("sp", szc[b2][:], sz_d[ch], r=[sz_tok], w=[lk[b2]])
                kb.dma("sp", vA[b2][:], vd3[CTX:T, ch, :].rearrange("(m p) d -> p m d", p=128), r=[v_tok], w=[lk[b2]])
                kb.dma("sp", vB[b2][:, 0:31, :], vd3[CTX + 64:T - 64, ch, :].rearrange("(m p) d -> p m d", p=128), r=[v_tok], w=[lk[b2]])
                kb.dma("sp", vC[b2][:], vd3[0:CTX, ch, :].rearrange("(m p) d -> p m d", p=128), r=[v_tok], w=[lk[b2]])
                q_, k_, sz_, y2_ = qs[b2], ks[b2], szc[b2], y2c[b2]
                for hp in range(2):
                    hb = 64 * hp
                    for qt in range(2):
                        ps, pt = kb.bank()
                        kb.op("pe", lambda e, ps=ps, hb=hb, qt=qt: e.matmul(ps[:, 0:256], lhsT=q_[hb:hb + 64, qt * 128:(qt + 1) * 128], rhs=k_[hb:hb + 64, 0:256],
                                                                             start=True, stop=True), r=[lk[b2]], w=[pt])
                        ops = kb.bank()
                        t0 = qt * 128
                        softmax_pv([(ps[:, 0:256], pt, 0, 256, None)], 256, [vC[b2][:, 0, :], vC[b2][:, 1, :]], ops, [(hb, 0)], 128,
                                   y2_[:, t0:t0 + 128], sz_[:, t0:t0 + 128], [lk[b2]])
                for r_ in range(64):
                    r0 = min(max(r_ - 4, 0), 56)
                    ro0 = r0 - r_ + 7
                    tq = CTX + 64 * r_
                    tk = CTX + 64 * r0
                    pw, ptw = kb.bank()
                    pc, ptc = kb.bank()
                    for hp in range(2):
                        hb = 64 * hp
                        kb.op("pe", lambda e, hb=hb, pw=pw: e.matmul(pw[hb:hb + 64, :], lhsT=q_[hb:hb + 64, tq:tq + 64], rhs=k_[hb:hb + 64, tk:tk + 512],
                                                                      start=True, stop=True), r=[lk[b2]], w=[ptw])
                        kb.op("pe", lambda e, hb=hb, pc=pc: e.matmul(pc[hb:hb + 64, 0:256], lhsT=q_[hb:hb + 64, tq:tq + 64], rhs=k_[hb:hb + 64, 0:256],
                                                                      start=True, stop=True), r=[lk[b2]], w=[ptc])
                    bias = biasm[:, ch, ro0:ro0 + 8, :].rearrange("p a b -> p (a b)")
                    if r0 % 2 == 0:
                        vt = [vA[b2][:, r0 // 2 + kt, :] for kt in range(4)]
                    else:
                        vt = [vB[b2][:, (r0 - 1) // 2 + kt, :] for kt in range(4)]
                    vt += [vC[b2][:, 0, :], vC[b2][:, 1, :]]
                    ops = kb.bank()
                    softmax_pv([(pw[:, :], ptw, 0, 512, bias), (pc[:, 0:256], ptc, 512, 256, None)], 768, vt, ops, [(0, 0), (64, 64)], 64,
                               y2_[:, tq:tq + 64], sz_[:, tq:tq + 64], [lk[b2]])
                kb.dma("sp", y2_d[ch], y2_[:], r=[y2k[b2]], w=[y2_tok])
        stage("na_attn")
        with kb.scope() as st:
            final_stage(i, last, na_out_w[j], st)

    def gdn_layer(i, j, last):
        HD = 128
        NH_ = 8
        with kb.scope() as st:
            adaln(i, st)
        with kb.scope() as st:
            aT = kb.sb([128, 8, T], BF16, "aT", st)
            a_toks = [Tok() for _ in range(NT)]
            prenorm(i, aT, a_toks, st)
            cons = stash_consume([None] * 24 + [(sz_d[m], sz_tok, AF.Silu) for m in range(8)], st)
            wb_ = proj_bufs(st)
            proj_fm(gdn_in_w[j][:, 3 * D:4 * D], D, aT, a_toks, lambda m, *a: cons(m + 24, *a), st, wb_)
            stG = contextlib.ExitStack()
            grow = kb.sb([16, T], F32, "grow", stG)
            brow = kb.sb([16, T], F32, "brow", stG)
            gk, bk_ = Tok(), Tok()
            proj_fm(gdn_in_w[j][:, 4 * D:4 * D + 16], 16, aT, a_toks,
                    lambda m, tb, t0, n, ps, pt: kb.op("act", lambda e: e.copy(out=grow[:, t0:t0 + n], in_=ps[0:16, 0:n]), r=[pt], w=[gk]), st, wb_)
            proj_fm(gdn_in_w[j][:, 4 * D + 16:4 * D + 32], 16, aT, a_toks,
                    lambda m, tb, t0, n, ps, pt: kb.op("act", lambda e: e.activation(out=brow[:, t0:t0 + n], in_=ps[0:16, 0:n], func=AF.Sigmoid), r=[pt], w=[bk_]), st, wb_)
            st = stG
            prm = kb.sb([16, 4], F32, "gprm", st)
            kb.dma("sp", prm[:, 0:2], gdn_prm[j], w=[gk])
            kb.op("dve", lambda e: e.memset(prm[:, 3:4], 1.0), w=[gk])
            kb.op("act", lambda e: e.activation(out=prm[:, 2:3], in_=prm[:, 0:1], func=AF.Exp), r=[gk], w=[gk])
            kb.op("dve", lambda e: e.tensor_scalar(out=prm[:, 2:3], in0=prm[:, 2:3], scalar1=-1.0, scalar2=None, op0=ALU.mult), r=[gk], w=[gk])
            kb.op("act", lambda e: e.activation(out=grow[:], in_=grow[:], func=AF.Exp, bias=prm[:, 1:2]), r=[gk], w=[gk])
            kb.op("act", lambda e: e.activation(out=grow[:], in_=grow[:], func=AF.Ln, bias=prm[:, 3:4]), r=[gk], w=[gk])
            kb.op("dve", lambda e: e.tensor_scalar(out=grow[:], in0=grow[:], scalar1=prm[:, 2:3], scalar2=None, op0=ALU.mult), r=[gk], w=[gk])
            gbs = kb.sb([128, NT, 32], F32, "gbs", st)
            gbk = Tok()
            for ti in range(NT):
                ps, pt = kb.bank()
                kb.op("pe", lambda e, ps=ps, ti=ti: e.transpose(out=ps[:, 0:16], in_=grow[:, ti * 128:(ti + 1) * 128], identity=ident_f[0:16, 0:16]), r=[gk, ctok], w=[pt])
                kb.op("pe", lambda e, ps=ps, ti=ti: e.transpose(out=ps[:, 16:32], in_=brow[:, ti * 128:(ti + 1) * 128], identity=ident_f[0:16, 0:16]), r=[bk_, ctok], w=[pt])
                kb.op("dve", lambda e, ps=ps, ti=ti: e.tensor_copy(out=gbs[:, ti, :], in_=ps[:, 0:32]), r=[pt], w=[gbk])
            kb.dma("sp", gb_d.rearrange("(n p) c -> p n c", p=128), gbs[:], r=[gbk], w=[gb_tok])
            kb.barrier()
            stG.close()
            st = contextlib.ExitStack()
            xr = [kb.sb([128, T], F32, "xrow", st) for _ in range(2)]
            xk = [Tok() for _ in range(2)]
            yr = kb.sb([128, T], F32, "yrow", st)
            sq = kb.sb([128, T], BF16, "sqrow", st)
            ykk = Tok()
            cw = kb.sb([128, 24, 5], F32, "convw", st)
            cwk = Tok()
            kb.dma("sp", cw[:], gdn_cwT[j], w=[cwk])
            ones_b = kb.sb([128, 128], BF16, "ones_b", st)
            epsc = kb.sb([128, 1], F32, "epsc", st)
            kb.op("dve", lambda e: e.memset(ones_b[:], 1.0), w=[cwk])
            kb.op("dve", lambda e: e.memset(epsc[:], EPS), w=[cwk])
            stg = [kb.sb([128, 512], BF16, "cstg", st) for _ in range(2)]
            stgk = [Tok(), Tok()]
            rnb = [kb.sb([128, 512], F32, "rnb", st) for _ in range(2)]
            rnk = [Tok(), Tok()]
            sc_ = [0]

            def qkv_consume(m, tb, t0, n, ps, pt):
                mm = m % 2
                if (tb + m) % 2 == 0:
                    kb.op("act", lambda e: e.copy(out=xr[mm][:, t0:t0 + n], in_=ps[:, 0:n]), r=[pt], w=[xk[mm]])
                else:
                    kb.op("dve", lambda e: e.tensor_copy(out=xr[mm][:, t0:t0 + n], in_=ps[:, 0:n]), r=[pt], w=[xk[mm]])
                if tb != len(TBLK) - 1:
                    return
                x = xr[mm]
                for (a0, a1) in ((0, CTX), (CTX, T)):
                    kb.op("dve", lambda e: e.tensor_scalar(out=yr[:, a0:a1], in0=x[:, a0:a1], scalar1=cw[:, m, 2:3], scalar2=None, op0=ALU.mult),
                          r=[xk[mm], cwk], w=[ykk])
                    for s_ in (-2, -1, 1, 2):
                        lo, hi = max(a0, a0 - s_), min(a1, a1 - s_)
                        kb.op("dve", lambda e, lo=lo, hi=hi, s_=s_: e.scalar_tensor_tensor(out=yr[:, lo:hi], in0=x[:, lo + s_:hi + s_], scalar=cw[:, m, 2 + s_:3 + s_],
                                                                                          in1=yr[:, lo:hi], op0=ALU.mult, op1=ALU.add), r=[xk[mm], cwk, ykk], w=[ykk])
                kb.op("act", lambda e: e.activation(out=yr[:], in_=yr[:], func=AF.Silu), r=[ykk], w=[ykk])
                dstd = qkvT_d[m]
                if m < 16:
                    kb.op("act", lambda e: e.activation(out=sq[:], in_=yr[:], func=AF.Square), r=[ykk], w=[ykk])
                for tb2, (u0, n2) in enumerate(TBLK):
                    b = sc_[0] % 2
                    sc_[0] += 1
                    if m < 16:
                        p2, pt2 = kb.bank()
                        kb.op("pe", lambda e, p2=p2, u0=u0, n2=n2: e.matmul(p2[:, 0:n2], lhsT=ones_b[:], rhs=sq[:, u0:u0 + n2], start=True, stop=True), r=[ykk, cwk], w=[pt2])
                        kb.op("act", lambda e, p2=p2, n2=n2, b=b: e.activation(out=rnb[b][:, 0:n2], in_=p2[:, 0:n2], func=AF.Sqrt, bias=epsc[:, 0:1]), r=[pt2, cwk], w=[rnk[b]])
                        kb.op("dve", lambda e, n2=n2, b=b: e.reciprocal(out=rnb[b][:, 0:n2], in_=rnb[b][:, 0:n2]), r=[rnk[b]], w=[rnk[b]])
                        kb.op("dve", lambda e, u0=u0, n2=n2, b=b: e.tensor_tensor(out=stg[b][:, 0:n2], in0=yr[:, u0:u0 + n2], in1=rnb[b][:, 0:n2], op=ALU.mult),
                              r=[rnk[b], ykk], w=[stgk[b]])
                    else:
                        kb.op("act", lambda e, u0=u0, n2=n2, b=b: e.copy(out=stg[b][:, 0:n2], in_=yr[:, u0:u0 + n2]), r=[ykk], w=[stgk[b]])
                    kb.dma("sp", dstd[:, u0:u0 + n2], stg[b][:, 0:n2], r=[stgk[b]], w=[qkv_tok])

            proj_fm(gdn_in_w[j][:, 0:3 * D], 3 * D, aT, a_toks, qkv_consume, st, wb_, gsz=256)
            kb.barrier()
            st.close()
        stage("gdn_proj")
        with kb.scope() as st:
            msk = kb.sb([128, 4, 128], F32, "gmsk", st)
            ones_f = kb.sb([128, 1], F32, "ones_f", st)
            ngb = kb.sb([128, 128], F32, "ngb", st)
            mk_ = Tok()
            kb.dma("sp", msk[:], c_gmask[:, :, :], w=[mk_])
            kb.dma("sp", ngb[:], gdn_ngb[j], w=[mk_])
            kb.op("dve", lambda e: e.memset(ones_f[:], 1.0), w=[mk_])
            gbs = kb.sb([128, NT, 32], F32, "gbs2", st)
            kb.dma("sp", gbs[:], gb_d.rearrange("(n p) c -> p n c", p=128), r=[gb_tok], w=[mk_])
            qT = [kb.sb([128, T], BF16, "gqT", st) for _ in range(2)]
            kT = [kb.sb([128, T], BF16, "gkT", st) for _ in range(2)]
            vT = [kb.sb([128, T], BF16, "gvT", st) for _ in range(2)]
            szh = [kb.sb([128, T], F32, "gsz", st) for _ in range(2)]
            y2h = [kb.sb([128, T], BF16, "gy2", st) for _ in range(2)]
            hk = [Tok(), Tok()]
            y2k = [Tok(), Tok()]
            NS = 2

            def mk(shape, dt, name):
                return [kb.sb(shape, dt, name, st) for _ in range(NS)]
            ktok_, vtok_ = mk([128, 128], BF16, "ktok"), mk([128, 128], F32, "vtok")
            kf32 = mk([128, 128], F32, "kf32")
            W1, W2 = mk([128, 128], F32, "W1"), mk([128, 128], F32, "W2")
            Eb, ETb = mk([128, 128], F32, "Eb"), mk([128, 128], F32, "ETb")
            Pm = [mk([128, 128], F32, "Pm%d" % q_) for q_ in range(7)]
            PmT = [mk([128, 128], F32, "PmT%d" % q_) for q_ in range(7)]
            AT = mk([128, 128], BF16, "AT")
            Xs = [mk([128, 256], F32, "Xs%d" % q_) for q_ in range(2)]
            Cm = [mk([128, 128], F32, "Cm%d" % q_) for q_ in range(3)]
            Ym = [mk([128, 128], F32, "Ym%d" % q_) for q_ in range(6)]
            bmsk = kb.sb([128, 3, 128], F32, "bmsk", st)
            kb.dma("sp", bmsk[:], c_bmask[:, :, :], w=[mk_])
            cols = mk([128, 8], F32, "gcols")
            wTb, kdec, vnew = mk([128, 128], BF16, "wTb"), mk([128, 128], BF16, "kdec"), mk([128, 128], BF16, "vnew")
            avs, osb = mk([128, 128], F32, "avs"), mk([128, 128], F32, "osb")
            tk_ = [Tok() for _ in range(NS)]
            Ss = [kb.sb([128, 128], F32, "Sst", st) for _ in range(2)]
            Sbs = [kb.sb([128, 128], BF16, "Sbf", st) for _ in range(2)]
            sks = [Tok(), Tok()]
            ofw = mk([128, 128], F32, "ofw")
            ofk = [Tok() for _ in range(NS)]
            yb_ = mk([128, 128], BF16, "gyb")
            ybk = [Tok() for _ in range(NS)]
            ost = mk([128, 4], F32, "gost")
            def chain(h, hb, d, b, S, Sb, sk):
                order = list(range(0, NT)) if d == 0 else [1, 0] + list(range(NT - 1, 1, -1))
                m_le, m_gt = (0, 1) if d == 0 else (2, 3)
                kb.op("dve", lambda e: e.memset(S[:], 0.0), w=[sk])
                kb.op("dve", lambda e: e.memset(Sb[:], 0.0), w=[sk])
                for n_ in order:
                    tk = tk_[b]
                    ts = slice(n_ * 128, (n_ + 1) * 128)
                    gcol = gbs[:, n_, d * 8 + h:d * 8 + h + 1]
                    bcol = gbs[:, n_, 16 + d * 8 + h:16 + d * 8 + h + 1]
                    V = lambda fn, r=(), w=(): kb.op("dve", fn, r=[tk, mk_, hk[hb]] + list(r), w=[tk] + list(w))
                    A = lambda fn, r=(), w=(): kb.op("act", fn, r=[tk, mk_, hk[hb]] + list(r), w=[tk] + list(w))
                    P = lambda fn, r=(), w=(): kb.op("pe", fn, r=[tk, mk_, hk[hb], ctok] + list(r), w=list(w))
                    p1, t1 = kb.bank()
                    p1b = p1[:].bitcast(BF16)
                    P(lambda e: e.transpose(out=p1b[:, 0:128], in_=kT[hb][:, ts], identity=ident_b[:]), w=[t1])
                    P(lambda e: e.transpose(out=p1b[:, 128:256], in_=vT[hb][:, ts], identity=ident_b[:]), w=[t1])
                    V(lambda e: e.tensor_copy(out=kf32[b][:], in_=p1b[:, 0:128]), r=[t1])
                    A(lambda e: e.copy(out=vtok_[b][:], in_=p1b[:, 128:256]), r=[t1])
                    yield
                    V(lambda e: e.tensor_scalar(out=W1[b][:], in0=msk[:, m_le, :], scalar1=gcol, scalar2=None, op0=ALU.mult))
                    V(lambda e: e.tensor_scalar(out=W2[b][:], in0=msk[:, m_gt, :], scalar1=gcol, scalar2=None, op0=ALU.mult))
                    yield
                    p2, t2 = kb.bank()
                    P(lambda e: e.matmul(p2[:, 0:128], lhsT=W1[b][:], rhs=msk[:, m_gt, :], start=True, stop=True), w=[t2])
                    P(lambda e: e.matmul(p2[:, 128:256], lhsT=msk[:, m_gt, :], rhs=W1[b][:], start=True, stop=True), w=[t2])
                    P(lambda e: e.matmul(p2[:, 256:258], lhsT=W1[b][:], rhs=bc(ones_f[:, 0:1], [128, 2]), start=True, stop=True), w=[t2])
                    P(lambda e: e.matmul(p2[:, 258:260], lhsT=W2[b][:], rhs=bc(ones_f[:, 0:1], [128, 2]), start=True, stop=True), w=[t2])
                    A(lambda e: e.activation(out=Eb[b][:], in_=p2[:, 0:128], func=AF.Exp), r=[t2])
                    A(lambda e: e.activation(out=ETb[b][:], in_=p2[:, 128:256], func=AF.Exp), r=[t2])
                    yield
                    c_ = cols[b]
                    A(lambda e: e.activation(out=c_[:, 0:1], in_=p2[:, 256:257], func=AF.Exp), r=[t2])
                    A(lambda e: e.activation(out=c_[:, 1:2], in_=p2[:, 258:259], func=AF.Exp), r=[t2])
                    V(lambda e: e.tensor_tensor(out=c_[:, 2:3], in0=p2[:, 256:257], in1=c_[:, 1:2], op=ALU.bypass), r=[t2]) if False else None
                    V(lambda e: e.tensor_copy(out=c_[:, 2:3], in_=p2[:, 258:259]), r=[t2])
                    V(lambda e: e.tensor_tensor(out=c_[:, 2:3], in0=c_[:, 2:3], in1=p2[:, 256:257], op=ALU.add), r=[t2])
                    A(lambda e: e.activation(out=c_[:, 3:4], in_=c_[:, 2:3], func=AF.Exp))
                    V(lambda e: e.tensor_tensor(out=c_[:, 4:5], in0=c_[:, 0:1], in1=bcol, op=ALU.mult))
                    V(lambda e: e.tensor_scalar(out=c_[:, 5:6], in0=c_[:, 0:1], scalar1=HD ** -0.5, scalar2=None, op0=ALU.mult))
                    yield
                    p3, t3 = kb.bank()
                    P(lambda e: e.matmul(p3[:, 0:128], lhsT=kT[hb][:, ts], rhs=kT[hb][:, ts], start=True, stop=True), w=[t3])
                    P(lambda e: e.matmul(p3[:, 128:256], lhsT=kT[hb][:, ts], rhs=qT[hb][:, ts], start=True, stop=True), w=[t3])
                    yield
                    V(lambda e: e.tensor_tensor(out=Eb[b][:], in0=Eb[b][:], in1=msk[:, m_gt, :], op=ALU.mult))
                    V(lambda e: e.scalar_tensor_tensor(out=Pm[0][b][:], in0=p3[:, 0:128], scalar=bcol, in1=Eb[b][:], op0=ALU.mult, op1=ALU.mult), r=[t3])
                    V(lambda e: e.tensor_tensor(out=ETb[b][:], in0=ETb[b][:], in1=msk[:, m_le, :], op=ALU.mult))
                    V(lambda e: e.scalar_tensor_tensor(out=AT[b][:], in0=p3[:, 128:256], scalar=HD ** -0.5, in1=ETb[b][:], op0=ALU.mult, op1=ALU.mult), r=[t3])
                    yield
                    p4, t4 = kb.bank()
                    P(lambda e: e.transpose(out=p4[:, 0:128], in_=Pm[0][b][:], identity=ident_f[:]), w=[t4])
                    A(lambda e: e.copy(out=PmT[0][b][:], in_=p4[:, 0:128]), r=[t4])
                    yield
                    Ld, LdT = Pm[1][b], PmT[1][b]
                    V(lambda e: e.tensor_tensor(out=Ld[:], in0=Pm[0][b][:], in1=bmsk[:, 0, :], op=ALU.mult))
                    V(lambda e: e.tensor_tensor(out=LdT[:], in0=PmT[0][b][:], in1=bmsk[:, 0, :], op=ALU.mult))
                    C1, C1T, C2 = Cm[0][b], Cm[1][b], Cm[2][b]
                    V(lambda e: e.tensor_tensor(out=C1[:], in0=Pm[0][b][:], in1=bmsk[:, 1, :], op=ALU.mult))
                    V(lambda e: e.tensor_tensor(out=C1T[:], in0=PmT[0][b][:], in1=bmsk[:, 1, :], op=ALU.mult))
                    V(lambda e: e.tensor_tensor(out=C2[:], in0=Pm[0][b][:], in1=bmsk[:, 2, :], op=ALU.mult))
                    for q_ in range(1, 5):
                        yield
                        p5, t5 = kb.bank()
                        P(lambda e, q_=q_, p5=p5: e.matmul(p5[:, 0:128], lhsT=PmT[q_][b][:], rhs=Pm[q_][b][:], start=True, stop=True), w=[t5])
                        V(lambda e, q_=q_, p5=p5: e.tensor_copy(out=Pm[q_ + 1][b][:], in_=p5[:, 0:128]), r=[t5])
                        if q_ < 4:
                            P(lambda e, q_=q_, p5=p5: e.matmul(p5[:, 128:256], lhsT=Pm[q_][b][:], rhs=PmT[q_][b][:], start=True, stop=True), w=[t5])
                            A(lambda e, q_=q_, p5=p5: e.copy(out=PmT[q_ + 1][b][:], in_=p5[:, 128:256]), r=[t5])
                    yield
                    Ya, Yb = Ym[0][b], Ym[1][b]
                    V(lambda e: e.tensor_tensor(out=Ya[:], in0=Pm[5][b][:], in1=ident_f[:], op=ALU.add))
                    curY, nxtY = Ya, Yb
                    for q_ in range(4, 0, -1):
                        yield
                        p6, t6 = kb.bank()
                        P(lambda e, q_=q_, p6=p6, curY=curY: e.matmul(p6[:, 0:128], lhsT=PmT[q_][b][:], rhs=curY[:], start=True, stop=True), w=[t6])
                        op_ = ALU.add if q_ > 1 else ALU.subtract
                        V(lambda e, p6=p6, curY=curY, nxtY=nxtY, op_=op_: e.tensor_tensor(out=nxtY[:], in0=curY[:], in1=p6[:, 0:128], op=op_), r=[t6])
                        curY, nxtY = nxtY, curY
                    yield
                    Td = curY
                    TdT, Wm, T64, T64T, TT = Ym[2][b], Ym[3][b], Ym[4][b], Ym[5][b], nxtY
                    p6, t6 = kb.bank()
                    P(lambda e, p6=p6: e.transpose(out=p6[:, 0:128], in_=Td[:], identity=ident_f[:]), w=[t6])
                    A(lambda e, p6=p6: e.copy(out=TdT[:], in_=p6[:, 0:128]), r=[t6])

                    def merge(dst, base, lhs_in, rhs_in, lhs_out):
                        pa_, ta_ = kb.bank()
                        P(lambda e: e.matmul(pa_[:, 0:128], lhsT=lhs_in[:], rhs=rhs_in[:], start=True, stop=True), w=[ta_])
                        V(lambda e: e.tensor_copy(out=Wm[:], in_=pa_[:, 0:128]), r=[ta_])
                        pb_, tb2 = kb.bank()
                        P(lambda e: e.matmul(pb_[:, 0:128], lhsT=lhs_out[:], rhs=Wm[:], start=True, stop=True), w=[tb2])
                        V(lambda e: e.tensor_tensor(out=dst[:], in0=base[:], in1=pb_[:, 0:128], op=ALU.subtract), r=[tb2])
                    yield
                    merge(T64, Td, C1T, Td, TdT)
                    yield
                    merge(T64T, TdT, C1, TdT, Td)
                    yield
                    merge(TT, T64T, C2, T64T, T64)
                    yield
                    X0, X1 = Xs[0][b], Xs[1][b]
                    V(lambda e: e.tensor_scalar(out=X0[:, 0:128], in0=vtok_[b][:], scalar1=bcol, scalar2=None, op0=ALU.mult))
                    V(lambda e: e.tensor_scalar(out=X0[:, 128:256], in0=kf32[b][:], scalar1=c_[:, 4:5], scalar2=None, op0=ALU.mult))
                    p6, t6 = kb.bank()
                    P(lambda e, p6=p6: e.matmul(p6[:, 0:256], lhsT=TT[:], rhs=X0[:], start=True, stop=True), w=[t6])
                    V(lambda e, p6=p6: e.tensor_copy(out=X1[:], in_=p6[:, 0:256]), r=[t6])
                    cur = X1
                    yield
                    p7, t7 = kb.bank()
                    P(lambda e, cur=cur: e.transpose(out=p7[:, 0:128], in_=cur[:, 128:256], identity=ident_f[:]), w=[t7])
                    A(lambda e: e.copy(out=wTb[b][:], in_=p7[:, 0:128]), r=[t7])
                    V(lambda e: e.tensor_scalar(out=kdec[b][:], in0=kf32[b][:], scalar1=c_[:, 1:2], scalar2=None, op0=ALU.mult))
                    yield
                    p8, t8 = kb.bank()
                    P(lambda e: e.matmul(p8[:, 0:128], lhsT=wTb[b][:], rhs=Sb[:], start=True, stop=True), r=[sk], w=[t8])
                    P(lambda e: e.matmul(p8[:, 128:256], lhsT=qT[hb][:, ts], rhs=Sb[:], start=True, stop=True), r=[sk], w=[t8])
                    V(lambda e, cur=cur: e.tensor_tensor(out=vnew[b][:], in0=cur[:, 0:128], in1=p8[:, 0:128], op=ALU.subtract), r=[t8])
                    yield
                    p9, t9 = kb.bank()
                    P(lambda e: e.matmul(p9[:, 0:128], lhsT=AT[b][:], rhs=vnew[b][:], start=True, stop=True), w=[t9])
                    P(lambda e: e.matmul(p9[:, 128:256], lhsT=kdec[b][:], rhs=vnew[b][:], start=True, stop=True), w=[t9])
                    A(lambda e: e.copy(out=avs[b][:], in_=p9[:, 0:128]), r=[t9])
                    V(lambda e: e.scalar_tensor_tensor(out=osb[b][:], in0=p8[:, 128:256], scalar=c_[:, 5:6], in1=avs[b][:], op0=ALU.mult, op1=ALU.add), r=[t8])
                    kb.op("dve", lambda e: e.scalar_tensor_tensor(out=S[:], in0=S[:], scalar=c_[:, 3:4], in1=p9[:, 128:256], op0=ALU.mult, op1=ALU.add),
                          r=[tk, t9, sk], w=[sk])
                    kb.op("act", lambda e: e.copy(out=Sb[:], in_=S[:]), r=[sk], w=[sk])
                    yield
                    orow = o_d[n_ * 128:(n_ + 1) * 128, h * 128:(h + 1) * 128]
                    if d == 0:
                        kb.dma("sp", orow, osb[b][:], r=[tk], w=[o_tok[n_]])
                    else:
                        if last and n_ < 2:
                            continue
                        kb.dma("sp", ofw[b][:], orow, r=[o_tok[n_]], w=[ofk[b]])
                        V(lambda e: e.tensor_tensor(out=osb[b][:], in0=osb[b][:], in1=ofw[b][:], op=ALU.add), r=[ofk[b]])
                        A(lambda e: e.activation(out=avs[b][:], in_=osb[b][:], func=AF.Square, accum_out=ost[b][:, 0:1]))
                        V(lambda e: e.tensor_scalar(out=ost[b][:, 1:2], in0=ost[b][:, 0:1], scalar1=1.0 / HD, scalar2=EPS, op0=ALU.mult, op1=ALU.add))
                        A(lambda e: e.activation(out=ost[b][:, 2:3], in_=ost[b][:, 1:2], func=AF.Sqrt))
                        V(lambda e: e.reciprocal(out=ost[b][:, 3:4], in_=ost[b][:, 2:3]))
                        V(lambda e: e.scalar_tensor_tensor(out=yb_[b][:], in0=osb[b][:], scalar=ost[b][:, 3:4], in1=ngb[:], op0=ALU.mult, op1=ALU.mult), w=[ybk[b]])
                        pa, ta = kb.bank()
                        pab = pa[:].bitcast(BF16)
                        P(lambda e: e.transpose(out=pab[:, 0:128], in_=yb_[b][:], identity=ident_b[:]), r=[ybk[b]], w=[ta])
                        kb.op("dve", lambda e: e.tensor_tensor(out=y2h[hb][:, ts], in0=pab[:, 0:128], in1=szh[hb][:, ts], op=ALU.mult), r=[ta, hk[hb]], w=[y2k[hb]])
            for hp_ in range(NH_ // 2):
                for hb in range(2):
                    h = 2 * hp_ + hb
                    for (dst_, src_) in ((qT[hb], qkvT_d[h]), (kT[hb], qkvT_d[8 + h]), (vT[hb], qkvT_d[16 + h])):
                        kb.dma("sp", dst_[:], src_, r=[qkv_tok], w=[hk[hb]])
                    kb.dma("sp", szh[hb][:], sz_d[h], r=[sz_tok], w=[hk[hb]])
                for d in range(2):
                    alive = [chain(2 * hp_ + hb, hb, d, hb, Ss[hb], Sbs[hb], sks[hb]) for hb in range(2)]
                    while alive:
                        for g_ in list(alive):
                            try:
                                next(g_)
                            except StopIteration:
                                alive.remove(g_)
                for hb in range(2):
                    h = 2 * hp_ + hb
                    kb.dma("sp", y2_d[h], y2h[hb][:], r=[y2k[hb]], w=[y2_tok])
        stage("gdn_core")
        with kb.scope() as st:
            final_stage(i, last, gdn_out_w[j], st)
    try:
        for i in layers:
            kind, j = i % 3, i // 3
            last = (i == DEPTH - 1)
            first_layer[0] = (i == layers[0])
            if kind == 0:
                s5_layer(i, j, last)
            elif kind == 2:
                na_layer(i, j, last)
            elif kind == 1:
                gdn_layer(i, j, last)
            else:
                raise NotImplementedError
            if "hsnap_d" in DBG and not last:
                for ti in range(NT):
                    kb.dma("sp", hsnap_d[i, ti * 128:(ti + 1) * 128, :], h_d[ti * 128:(ti + 1) * 128, :], r=[h_tok[ti]], w=[snap_tok])
                kb.barrier()
    except StopBuild:
        kb.barrier()
        return kb
    if n_layers < DEPTH:
        with kb.scope() as st:
            bufs = [kb.sb([128, D], F32, "cp", st) for _ in range(2)]
            bk = [Tok(), Tok()]
            for ti in range(2, NT):
                b = ti % 2
                kb.dma("sp", bufs[b][:], h_d[ti * 128:(ti + 1) * 128, :], r=[h_tok[ti]], w=[bk[b]])
                kb.dma("sp", out[(ti - 2) * 128:(ti - 1) * 128, :], bufs[b][:], r=[bk[b]], w=[h_tok[ti]])
    kb.barrier()
    kb.es.close()
    return kb


def host_inputs(inp, b):
    f = np.float32
    A = lambda x: np.ascontiguousarray(x, dtype=f)
    m = {}
    m["hx"] = A(np.concatenate([inp["ctx"][b], inp["x"][b]], axis=0))
    cond = np.stack([inp["c"][b], inp["c_ctx"]], axis=0)
    m["condT"] = A(cond.reshape(2, 8, 128).transpose(2, 1, 0))
    m["ada_w"] = A(inp["ada_w"])
    m["ada_bT"] = A(inp["ada_b"].reshape(DEPTH, 24, 128).transpose(0, 2, 1))
    m["ada_bg"] = A(np.broadcast_to(inp["ada_b"][:, None, 2 * D:], (DEPTH, 128, D)))
    m["pre_gT"] = A(inp["pre_g"].reshape(DEPTH, 8, 128).transpose(0, 2, 1))
    m["post_gb"] = A(np.broadcast_to(inp["post_g"][:, None, :], (DEPTH, 128, D)))
    m["s5_in_w"] = A(inp["s5_in_w"])
    m["s5_glu_w"] = A(inp["s5_glu_w"])
    m["s5_out_w"] = A(inp["s5_out_w"])
    m["s5_lamT"] = A(np.stack([inp["s5_lam_re"], inp["s5_lam_im"]], axis=2).transpose(0, 1, 2, 4, 3))
    m["s5_ldt"] = A(np.broadcast_to(inp["s5_log_dt"][:, :, None, :], (2, 2, 64, 64)))
    bst = np.stack([inp["s5_b_re"], inp["s5_b_im"]], axis=2)
    m["s5_bT"] = A(bst.transpose(0, 1, 2, 4, 3, 5).reshape(2, 2, 2, 64, 1024))
    cst = np.stack([inp["s5_c_re"], inp["s5_c_im"]], axis=2)
    m["s5_cT"] = A(cst.transpose(0, 1, 2, 5, 3, 4).reshape(2, 2, 2, 64, 1024))
    m["s5_dT"] = A(inp["s5_d"].reshape(2, 8, 128).transpose(0, 2, 1))
    m["s5_gbT"] = A(inp["s5_glu_b"].reshape(2, 16, 128).transpose(0, 2, 1))
    m["gdn_in_w"] = A(inp["gdn_in_w"])
    m["gdn_out_w"] = A(inp["gdn_out_w"])
    m["gdn_prm"] = A(np.stack([inp["gdn_a_log"].reshape(1, 16), inp["gdn_dt_bias"].reshape(1, 16)], axis=2))
    m["gdn_cwT"] = A(inp["gdn_conv_w"].reshape(1, 5, 24, 128).transpose(0, 3, 2, 1))
    m["gdn_ngb"] = A(np.broadcast_to(inp["gdn_norm_g"][:, None, :], (1, 128, 128)))
    mi = np.arange(128)
    m["c_gmask"] = A(np.stack([mi[:, None] <= mi[None, :], mi[:, None] > mi[None, :], mi[:, None] >= mi[None, :], mi[:, None] < mi[None, :]], axis=1))
    m["c_bmask"] = A(np.stack([mi[:, None] // 32 == mi[None, :] // 32,
                               (mi[:, None] // 64 == mi[None, :] // 64) & (mi[:, None] // 32 != mi[None, :] // 32),
                               mi[:, None] // 64 != mi[None, :] // 64], axis=1))
    m["na_in_w"] = A(inp["na_in_w"])
    m["na_out_w"] = A(inp["na_out_w"])
    rpb = inp["na_rpb"]
    wq = np.arange(64)
    c0 = np.clip(wq - 8, 0, 48)
    wk_ = np.arange(64)
    inwin = (wk_[None, :] >= c0[:, None]) & (wk_[None, :] < c0[:, None] + 16)
    coff = np.clip(wk_[None, :] - wq[:, None] + 15, 0, 30)
    bt = np.where(inwin[None, None, None], rpb[:, :, :, coff], 0.0)
    bt = bt.reshape(1, 8, 2, 15, 64, 64).transpose(0, 2, 4, 1, 3, 5).reshape(1, 128, 8, 15, 64)
    m["na_biasT"] = A(bt)
    m["c_namask"] = A(np.tile(np.where(inwin, 0.0, -30000.0), (2, 1)))
    m["c_ident"] = np.eye(128, dtype=f)
    p = np.arange(128)
    m["c_bd"] = A((p[:, None] // 16) == (p[None, :] // 16))
    cp = np.stack([(p // 16) % 2 == 0, (p // 16) % 2 == 1, ((p // 16) % 2 == 0) & (p >= 96), ((p // 16) % 2 == 1) & (p >= 96)], axis=0)
    m["c_par"] = A(cp.T)
    m["c_colpar"] = A(np.broadcast_to(cp[None], (64, 4, 128)))
    return m


_CACHE = {}


def kernel(**inputs):
    inp = {k: np.asarray(v) for k, v in inputs.items()}
    if "nc" not in _CACHE:
        _CACHE["nc"] = build(DEPTH).nc
    nc = _CACHE["nc"]
    nb = inp["x"].shape[0]
    maps = [host_inputs(inp, b) for b in range(nb)]
    res = run_bass_kernel_spmd(nc, maps, core_ids=list(range(nb)))
    return np.stack([np.asarray(res.results[b]["out"]) for b in range(nb)], axis=0).astype(np.float32)
```

```python
import contextlib
import math
import numpy as np
import concourse.bass as bass
import concourse.mybir as mybir
from concourse.bass_utils import run_bass_kernel_spmd

F32 = mybir.dt.float32
BF16 = mybir.dt.bfloat16
AF = mybir.ActivationFunctionType
ALU = mybir.AluOpType
AX = mybir.AxisListType

D = 1024
SEQ = 4096
CTX = 256
T = SEQ + CTX
NT = T // 128
DEPTH = 4
EPS = 1e-6
L = 8
NK = T // L
TBLK = [(i * 512, min(512, T - i * 512)) for i in range((T + 511) // 512)]


DBG = set()
ONLY_GLU = False
OUT_CUT = 99


class Tok:
    __slots__ = ("w", "r")

    def __init__(self):
        self.w = None
        self.r = {}


class KB:
    def __init__(self):
        self.nc = bass.Bass("TRN2", target_bir_lowering=False)
        self.es = contextlib.ExitStack()
        nc = self.nc
        self.eng = {}
        for name, obj in (("pe", nc.tensor), ("act", nc.scalar), ("dve", nc.vector), ("pool", nc.gpsimd), ("sp", nc.sync)):
            sem = self.es.enter_context(nc.semaphore("sem_" + name))
            self.eng[name] = dict(e=obj, sem=sem, cnt=0, waited={}, name=name)
        self.ndma = 48
        self.dsem = [self.es.enter_context(nc.semaphore("dsem%d" % i)) for i in range(self.ndma)]
        self.dval = [0] * self.ndma
        self.dnext = 0
        self.uid = 0
        self.banks = []
        for i in range(8):
            t = self.es.enter_context(nc.psum_tensor("bank%d" % i, [128, 512], F32))
            self.banks.append((t, Tok()))
        self.bnext = 0
        self.ninst = 0
        self.bank_of = {}
        self.stale = set()
        self.keep = []

    def sb(self, shape, dtype, name=None, stack=None):
        self.uid += 1
        t = (stack or self.es).enter_context(self.nc.sbuf_tensor("%s_%d" % (name or "t", self.uid), list(shape), dtype))
        return t

    def dram(self, name, shape, dtype, kind="Internal"):
        if name in DBG:
            kind = "ExternalOutput"
        return self.nc.dram_tensor(name, list(shape), dtype, kind=kind).ap()

    def bank(self, idx=None):
        i = (self.bnext % 8) if idx is None else idx
        if idx is None:
            self.bnext += 1
        t, old = self.banks[i]
        tok = Tok()
        tok.w, tok.r = old.w, dict(old.r)
        self.banks[i] = (t, tok)
        self.bank_of[id(tok)] = i
        self.bank_of.pop(id(old), None)
        self.stale.add(id(old))
        self.keep.append(old)
        return t, tok

    def _wait(self, E, sem, val):
        key = id(sem)
        if E["waited"].get(key, 0) < val:
            E["e"].wait_ge(sem, val)
            E["waited"][key] = val

    def _deps(self, E, reads, writes):
        own = id(E["sem"])
        for t in reads:
            if t.w is not None:
                if E["name"] == "pe" and id(t.w[0]) == own:
                    continue
                self._wait(E, *t.w)
        for t in writes:
            if t.w is not None and not (E["name"] == "pe" and id(t.w[0]) == own):
                self._wait(E, *t.w)
            for sem, val in t.r.values():
                if E["name"] == "pe" and id(sem) == own:
                    continue
                self._wait(E, sem, val)

    def _mark(self, sem, val, reads, writes):
        for t in writes:
            t.w = (sem, val)
            t.r = {}
        for t in reads:
            t.r[id(sem)] = (sem, val)

    def op(self, eng, fn, r=(), w=()):
        for t in list(r) + list(w):
            assert id(t) not in self.stale, "stale PSUM bank token used (bank re-allocated before its last use)"
        E = self.eng[eng]
        self._deps(E, r, w)
        inst = fn(E["e"])
        E["cnt"] += 1
        inst.then_inc(E["sem"], 1)
        self._mark(E["sem"], E["cnt"], r, w)
        self.ninst += 1
        return inst

    def dma(self, q, out, in_, r=(), w=(), **kw):
        E = self.eng[q]
        i = self.dnext % self.ndma
        self.dnext += 1
        sem = self.dsem[i]
        self._wait(E, sem, self.dval[i])
        self._deps(E, r, w)
        inst = E["e"].dma_start(out=out, in_=in_, **kw)
        self.dval[i] += 16
        inst.then_inc(sem, 16)
        self._mark(sem, self.dval[i], r, w)
        self.ninst += 1

    def barrier(self):
        for E in self.eng.values():
            for Fn in self.eng.values():
                if Fn["cnt"] > 0:
                    self._wait(E, Fn["sem"], Fn["cnt"])
            for i in range(self.ndma):
                if self.dval[i] > 0:
                    self._wait(E, self.dsem[i], self.dval[i])

    @contextlib.contextmanager
    def scope(self):
        st = contextlib.ExitStack()
        yield st
        self.barrier()
        st.close()


def bc(ap, shape):
    return ap.to_broadcast(list(shape))


class StopBuild(Exception):
    pass


def build(n_layers=DEPTH, stop=None):
    layers = list(range(n_layers)) if isinstance(n_layers, int) else list(n_layers)
    n_layers = layers[-1] + 1
    first_layer = [True]
    kb = KB()
    nc = kb.nc

    def stage(name):
        if stop == name:
            raise StopBuild()

    def din(name, shape, dt=F32):
        return kb.dram(name, shape, dt, kind="ExternalInput")

    hx = din("hx", [T, D])
    condT = din("condT", [128, 8, 2])
    ada_w = din("ada_w", [DEPTH, D, 3 * D])
    ada_bT = din("ada_bT", [DEPTH, 128, 24])
    ada_bg = din("ada_bg", [DEPTH, 128, D])
    pre_gT = din("pre_gT", [DEPTH, 128, 8])
    post_gb = din("post_gb", [DEPTH, 128, D])
    s5_in_w = din("s5_in_w", [2, D, 2 * D])
    s5_glu_w = din("s5_glu_w", [2, D, 2 * D])
    s5_out_w = din("s5_out_w", [2, D, D])
    s5_lamT = din("s5_lamT", [2, 2, 2, 64, 64])
    s5_ldt = din("s5_ldt", [2, 2, 64, 64])
    s5_bT = din("s5_bT", [2, 2, 2, 64, 64 * 16])
    s5_cT = din("s5_cT", [2, 2, 2, 64, 64 * 16])
    s5_dT = din("s5_dT", [2, 128, 8])
    s5_gbT = din("s5_gbT", [2, 128, 16])
    gdn_in_w = din("gdn_in_w", [1, D, 4 * D + 32])
    gdn_out_w = din("gdn_out_w", [1, D, D])
    gdn_prm = din("gdn_prm", [1, 16, 2])
    gdn_cwT = din("gdn_cwT", [1, 128, 24, 5])
    gdn_ngb = din("gdn_ngb", [1, 128, 128])
    c_gmask = din("c_gmask", [128, 4, 128])
    c_bmask = din("c_bmask", [128, 3, 128])
    na_in_w = din("na_in_w", [1, D, 4 * D])
    na_out_w = din("na_out_w", [1, D, D])
    na_biasT = din("na_biasT", [1, 128, 8, 15, 64])
    c_namask = din("c_namask", [128, 64])
    c_ident = din("c_ident", [128, 128])
    c_bd = din("c_bd", [128, 128])
    c_par = din("c_par", [128, 4])
    c_colpar = din("c_colpar", [64, 4, 128])
    out = kb.dram("out", [SEQ, D], F32, kind="ExternalOutput")
    h_d = kb.dram("h_d", [T, D], F32)
    sz_d = kb.dram("sz_d", [8, 128, T], F32)
    u_d = kb.dram("u_d", [8, 128, T], BF16)
    u_tok = Tok()
    gl_d = kb.dram("gl_d", [8, 128, T], BF16)
    dx_d = kb.dram("dx_d", [2, 64, 2, 64, NK], F32)
    xin_d = kb.dram("xin_d", [2, 64, 2, 64, NK], BF16)
    kb_d = kb.dram("kb_d", [2, 128, 8, L, 128], BF16)
    q_d = kb.dram("q_d", [8, 128, T], BF16)
    k_d = kb.dram("k_d", [8, 128, T], BF16)
    v_d = kb.dram("v_d", [T, D], BF16)
    y2_d = kb.dram("y2_d", [8, 128, T], BF16)
    q_tok, k_tok, v_tok, y2_tok = Tok(), Tok(), Tok(), Tok()
    qkvT_d = kb.dram("qkvT_d", [24, 128, T], BF16)
    gb_d = kb.dram("gb_d", [T, 32], F32)
    o2_d = kb.dram("o2_d", [2, T, D], F32)
    o2_tok = [[Tok() for _ in range(NT)] for _ in range(2)]
    qkv_tok, gb_tok = Tok(), Tok()
    o_tok = [Tok() for _ in range(NT)]
    hsnap_d = kb.dram("hsnap_d", [3, T, D], F32) if "hsnap_d" in DBG else None
    snap_tok = Tok()
    h_tok = [Tok() for _ in range(NT)]
    sz_tok = Tok()
    gl_tok = Tok()
    dx_tok = [Tok(), Tok()]
    xin_tok = [Tok(), Tok()]
    kbd_tok = Tok()

    ident_f = kb.sb([128, 128], F32, "identf")
    ident_b = kb.sb([128, 128], BF16, "identb")
    bdmask = kb.sb([128, 128], F32, "bdmask")
    parmask = kb.sb([128, 4], F32, "parmask")
    colpar = kb.sb([64, 4, 128], F32, "colpar")
    halfpi = kb.sb([128, 1], F32, "halfpi")
    scT = kb.sb([128, 8, 2], F32, "scT")
    ctok = Tok()
    kb.dma("sp", ident_f[:], c_ident[:, :], w=[ctok])
    kb.dma("pool", ident_b[:], c_ident[:, :], w=[ctok])
    kb.dma("sp", bdmask[:], c_bd[:, :], w=[ctok])
    kb.dma("sp", parmask[:], c_par[:, :], w=[ctok])
    kb.dma("sp", colpar[:], c_colpar[:, :, :], w=[ctok])
    kb.op("dve", lambda e: e.memset(halfpi[:], math.pi / 2), w=[ctok])
    kb.dma("sp", scT[:], condT[:, :, :], w=[ctok])
    kb.op("act", lambda e: e.activation(out=scT[:], in_=scT[:], func=AF.Silu), r=[ctok], w=[ctok])
    kb.barrier()

    modT = kb.sb([128, 24, 2], F32, "modT")
    S1 = kb.sb([128, 8, 2], F32, "S1")
    GPG = kb.sb([128, 2, D], F32, "GPG")
    mod_tok = Tok()

    def adaln(i, st):
        wbuf = kb.sb([128, 8, 512], F32, "adaw", st)
        scRep = kb.sb([128, 8, 2, 128], F32, "scRep", st)
        kb.op("dve", lambda e: e.tensor_copy(out=scRep[:], in_=bc(scT[:].unsqueeze(3), [128, 8, 2, 128])), r=[ctok], w=[ctok])
        abT = kb.sb([128, 24], F32, "abT", st)
        abg = kb.sb([128, D], F32, "abg", st)
        pgb = kb.sb([128, D], F32, "pgb", st)
        pgT = kb.sb([128, 8], F32, "pgT", st)
        wt = Tok()
        ct = Tok()
        kb.dma("sp", abT[:], ada_bT[i], w=[ct])
        kb.dma("sp", abg[:], ada_bg[i], w=[ct])
        kb.dma("sp", pgb[:], post_gb[i], w=[ct])
        kb.dma("sp", pgT[:], pre_gT[i], w=[ct])
        wv = ada_w[i].rearrange("(kc p) n -> p kc n", p=128)
        for grp in range(6):
            kb.dma("sp", wbuf[:], wv[:, :, grp * 512:(grp + 1) * 512], w=[wt])
            for mm in range(4):
                m = grp * 4 + mm
                ps, pt = kb.bank()
                for kc in range(8):
                    kb.op("pe", lambda e, kc=kc, mm=mm, ps=ps: e.matmul(ps[:, 0:2], lhsT=wbuf[:, kc, mm * 128:(mm + 1) * 128],
                                                                          rhs=scT[:, kc, :], start=(kc == 0), stop=(kc == 7)),
                          r=[wt, ctok], w=[pt])
                kb.op("dve", lambda e, m=m, ps=ps: e.tensor_scalar(out=modT[:, m, :], in0=ps[:, 0:2], scalar1=abT[:, m:m + 1],
                                                                     scalar2=None, op0=ALU.add), r=[pt, ct], w=[mod_tok])
            if grp >= 4:
                for rr in range(2):
                    ps, pt = kb.bank()
                    for kc in range(8):
                        kb.op("pe", lambda e, kc=kc, rr=rr, ps=ps: e.matmul(ps[:, :], lhsT=scRep[:, kc, rr, :], rhs=wbuf[:, kc, :],
                                                                              start=(kc == 0), stop=(kc == 7)), r=[wt, ctok], w=[pt])
                    sl = slice((grp - 4) * 512, (grp - 3) * 512)
                    kb.op("dve", lambda e, rr=rr, ps=ps, sl=sl: e.tensor_tensor(out=GPG[:, rr, sl], in0=ps[:, :], in1=abg[:, sl], op=ALU.add),
                          r=[pt, ct], w=[mod_tok])
                    kb.op("dve", lambda e, rr=rr, sl=sl: e.tensor_tensor(out=GPG[:, rr, sl], in0=GPG[:, rr, sl], in1=pgb[:, sl], op=ALU.mult),
                          r=[ct, mod_tok], w=[mod_tok])
        kb.op("dve", lambda e: e.scalar_tensor_tensor(out=S1[:], in0=modT[:, 8:16, :], scalar=1.0,
                                                      in1=bc(pgT[:].unsqueeze(2), [128, 8, 2]), op0=ALU.add, op1=ALU.mult),
              r=[mod_tok, ct], w=[mod_tok])

    def prenorm(i, aT, a_toks, st):
        src = hx if first_layer[0] else h_d
        hts = [kb.sb([128, D], F32, "ht", st) for _ in range(2)]
        htk = [Tok(), Tok()]
        junk = kb.sb([128, D], BF16, "junk", st)
        ybs = [kb.sb([128, D], F32, "yb", st) for _ in range(2)]
        tmf = [kb.sb([128, 4, 128], F32, "tmf", st) for _ in range(2)]
        tmk = [Tok(), Tok()]
        ybk = [Tok(), Tok()]
        stat = kb.sb([128, 4], F32, "stat", st)
        jt = Tok()
        stt = Tok()
        for ti in range(NT):
            rr = 1 if ti < 2 else 0
            ht, hk = hts[ti % 2], htk[ti % 2]
            yb, yk = ybs[ti % 2], ybk[ti % 2]
            kb.dma("sp", ht[:], src[ti * 128:(ti + 1) * 128, :], r=[h_tok[ti]], w=[hk])
            kb.op("act", lambda e, ht=ht: e.activation(out=junk[:], in_=ht[:], func=AF.Square, accum_out=stat[:, 0:1]), r=[hk], w=[jt, stt])
            kb.op("dve", lambda e: e.tensor_scalar(out=stat[:, 1:2], in0=stat[:, 0:1], scalar1=1.0 / D, scalar2=EPS, op0=ALU.mult, op1=ALU.add),
                  r=[stt], w=[stt])
            kb.op("act", lambda e: e.activation(out=stat[:, 2:3], in_=stat[:, 1:2], func=AF.Sqrt), r=[stt], w=[stt])
            kb.op("dve", lambda e: e.reciprocal(out=stat[:, 3:4], in_=stat[:, 2:3]), r=[stt], w=[stt])
            kb.op("act", lambda e, ht=ht, yb=yb: e.activation(out=yb[:], in_=ht[:], func=AF.Copy, scale=stat[:, 3:4]), r=[hk, stt], w=[yk])
            for hf in range(2):
                ps, pt = kb.bank()
                for c4 in range(4):
                    c = hf * 4 + c4
                    kb.op("pe", lambda e, c=c, c4=c4, yb=yb, ps=ps: e.transpose(out=ps[:, c4 * 128:(c4 + 1) * 128], in_=yb[:, c * 128:(c + 1) * 128],
                                                                             identity=ident_f[:]), r=[yk, ctok], w=[pt])
                for c4 in range(4):
                    c = hf * 4 + c4
                    kb.op("act", lambda e, c=c, c4=c4, ps=ps, rr=rr: e.activation(out=aT[:, c, ti * 128:(ti + 1) * 128], in_=ps[:, c4 * 128:(c4 + 1) * 128],
                                                                               func=AF.Identity, scale=S1[:, c, rr:rr + 1], bias=modT[:, c, rr:rr + 1]),
                          r=[pt, mod_tok], w=[a_toks[ti]])

    def proj_bufs(st):
        return ([kb.sb([128, 8, 512], BF16, "wproj", st) for _ in range(2)], [Tok(), Tok()])

    def proj_fm(w_ap, ncols, aT, a_toks, consume, st, wb_=None, gsz=512):
        wv = w_ap.rearrange("(kc p) n -> p kc n", p=128)
        wbs, wks = wb_ if wb_ is not None else proj_bufs(st)
        ng = (ncols + gsz - 1) // gsz
        for g in range(ng):
            wb, wk = wbs[g % 2], wks[g % 2]
            c0 = g * gsz
            cn = min(gsz, ncols - c0)
            kb.dma("pool", wb[:, :, 0:cn], wv[:, :, c0:c0 + cn], w=[wk])
            for tb, (t0, n) in enumerate(TBLK):
                trs = a_toks[t0 // 128:(t0 + n) // 128]
                for mm in range((cn + 127) // 128):
                    mw = min(128, cn - mm * 128)
                    ps, pt = kb.bank()
                    for kc in range(8):
                        kb.op("pe", lambda e, kc=kc, mm=mm, mw=mw, ps=ps, wb=wb, t0=t0, n=n: e.matmul(
                            ps[0:mw, 0:n], lhsT=wb[:, kc, mm * 128:mm * 128 + mw], rhs=aT[:, kc, t0:t0 + n], start=(kc == 0), stop=(kc == 7)),
                            r=[wk] + trs, w=[pt])
                    consume(g * (gsz // 128) + mm, tb, t0, n, ps, pt)

    def outproj_block(i, y2T, y2k, t0, n, wout, wok, st_bufs, last):
        src = hx if first_layer[0] else h_d
        hold, holdk, tmp, tmpk, stat, stt, junk, jt = st_bufs
        for tt in range(n // 128):
            ti = (t0 // 128) + tt
            rr = 1 if ti < 2 else 0
            if last and rr == 1:
                continue
            p0, k0 = kb.bank()
            p1, k1 = kb.bank()
            for nh, (ps, pk) in enumerate(((p0, k0), (p1, k1))):
                for kc in range(8):
                    kb.op("pe", lambda e, kc=kc, ps=ps, nh=nh, tt=tt: e.matmul(ps[:, :], lhsT=y2T[:, kc, tt * 128:(tt + 1) * 128],
                                                                                 rhs=wout[:, kc, nh * 512:(nh + 1) * 512],
                                                                                 start=(kc == 0), stop=(kc == 7)), r=[y2k, wok], w=[pk])
            b = ti % 2
            if OUT_CUT < 2:
                continue
            kb.dma("sp", hold[b][:], src[ti * 128:(ti + 1) * 128, :], r=[h_tok[ti]], w=[holdk[b]])
            if OUT_CUT < 3:
                continue
            kb.op("act", lambda e: e.activation(out=junk[:, 0:512], in_=p0[:, :], func=AF.Square, accum_out=stat[:, 0:1]), r=[k0], w=[jt, stt])
            kb.op("act", lambda e: e.activation(out=junk[:, 512:1024], in_=p1[:, :], func=AF.Square, accum_out=stat[:, 1:2]), r=[k1], w=[jt, stt])
            if OUT_CUT < 4:
                continue
            kb.op("dve", lambda e: e.tensor_tensor(out=stat[:, 2:3], in0=stat[:, 0:1], in1=stat[:, 1:2], op=ALU.add), r=[stt], w=[stt])
            kb.op("dve", lambda e: e.tensor_scalar(out=stat[:, 3:4], in0=stat[:, 2:3], scalar1=1.0 / D, scalar2=EPS, op0=ALU.mult, op1=ALU.add),
                  r=[stt], w=[stt])
            kb.op("act", lambda e: e.activation(out=stat[:, 4:5], in_=stat[:, 3:4], func=AF.Sqrt), r=[stt], w=[stt])
            kb.op("dve", lambda e: e.reciprocal(out=stat[:, 5:6], in_=stat[:, 4:5]), r=[stt], w=[stt])
            if OUT_CUT < 5:
                continue
            for nh, (ps, pk) in enumerate(((p0, k0), (p1, k1))):
                sl = slice(nh * 512, (nh + 1) * 512)
                kb.op("dve", lambda e, ps=ps, sl=sl, b=b, rr=rr: e.scalar_tensor_tensor(out=tmp[b][:, sl], in0=ps[:, :], scalar=stat[:, 5:6],
                                                                                         in1=GPG[:, rr, sl], op0=ALU.mult, op1=ALU.mult),
                      r=[pk, stt, mod_tok], w=[tmpk[b]])
            if OUT_CUT < 6:
                continue
            kb.op("dve", lambda e, b=b: e.tensor_tensor(out=tmp[b][:], in0=tmp[b][:], in1=hold[b][:], op=ALU.add), r=[holdk[b]], w=[tmpk[b]])
            if OUT_CUT < 7:
                continue
            if last:
                kb.dma("sp", out[(ti - 2) * 128:(ti - 1) * 128, :], tmp[b][:], r=[tmpk[b]], w=[h_tok[ti]])
            else:
                kb.dma("sp", h_d[ti * 128:(ti + 1) * 128, :], tmp[b][:], r=[tmpk[b]], w=[h_tok[ti]])

    def outproj_bufs(st):
        hold = [kb.sb([128, D], F32, "hold", st) for _ in range(2)]
        tmp = [kb.sb([128, D], F32, "otmp", st) for _ in range(2)]
        return (hold, [Tok(), Tok()], tmp, [Tok(), Tok()], kb.sb([128, 8], F32, "ostat", st), Tok(), kb.sb([128, D], BF16, "ojunk", st), Tok())

    def s5_layer(i, j, last):
        with kb.scope() as st:
            adaln(i, st)
        if not ONLY_GLU:
            stage("adaln")
            with kb.scope() as st:
                aT = kb.sb([128, 8, T], BF16, "aT", st)
                a_toks = [Tok() for _ in range(NT)]
                prenorm(i, aT, a_toks, st)
                stage("prenorm")
                szs = [kb.sb([128, 512], BF16, "szs", st) for _ in range(2)]
                szf = [kb.sb([128, 512], F32, "szf", st) for _ in range(2)]
                szk = [Tok(), Tok()]
                cnt = [0]

                def consume(m, tb, t0, n, ps, pt):
                    b = cnt[0] % 2
                    cnt[0] += 1
                    if m < 8:
                        kb.op("dve", lambda e: e.tensor_copy(out=szs[b][:, 0:n], in_=ps[:, 0:n]), r=[pt], w=[szk[b]])
                        kb.dma("sp", u_d[m, :, t0:t0 + n], szs[b][:, 0:n], r=[szk[b]], w=[u_tok])
                    else:
                        kb.op("act", lambda e: e.activation(out=szf[b][:, 0:n], in_=ps[:, 0:n], func=AF.Silu), r=[pt], w=[szk[b]])
                        kb.dma("sp", sz_d[m - 8, :, t0:t0 + n], szf[b][:, 0:n], r=[szk[b]], w=[sz_tok])

                proj_fm(s5_in_w[j], 2 * D, aT, a_toks, consume, st)

            stage("proj")
            s5_scan(j)
            stage("scan")
        with kb.scope() as st:
            gw = kb.sb([128, 8, 2 * D], BF16, "gw", st)
            wout = kb.sb([128, 8, D], BF16, "wout", st)
            gbT = kb.sb([128, 16], F32, "gbT", st)
            wk = Tok()
            kb.dma("pool", gw[:], s5_glu_w[j].rearrange("(kc p) n -> p kc n", p=128), w=[wk])
            kb.dma("pool", wout[:], s5_out_w[j].rearrange("(kc p) n -> p kc n", p=128), w=[wk])
            kb.dma("sp", gbT[:], s5_gbT[j], w=[wk])
            stage("gluw")
            glb = [kb.sb([128, 8, 512], BF16, "glb", st) for _ in range(2)]
            szb = [kb.sb([128, 8, 512], F32, "szb", st) for _ in range(2)]
            y2b = [kb.sb([128, 8, 512], BF16, "y2b", st) for _ in range(2)]
            glk, szk2, y2k = [Tok(), Tok()], [Tok(), Tok()], [Tok(), Tok()]
            sg = [kb.sb([128, 512], F32, "sg", st) for _ in range(2)]
            sgk = [Tok(), Tok()]
            tt_ = [kb.sb([128, 512], F32, "gt", st) for _ in range(2)]
            ttk = [Tok(), Tok()]
            ob = outproj_bufs(st)
            for tb, (t0, n) in enumerate(TBLK):
                b = tb % 2
                kb.dma("sp", glb[b][:, :, 0:n], gl_d.rearrange("q p t -> p q t")[:, :, t0:t0 + n], r=[gl_tok], w=[glk[b]])
                kb.dma("sp", szb[b][:, :, 0:n], sz_d.rearrange("q p t -> p q t")[:, :, t0:t0 + n], r=[sz_tok], w=[szk2[b]])
                for m in range(8):
                    pa, ka = kb.bank()
                    pb, kbk = kb.bank()
                    for (ps, pk, off) in ((pa, ka, 0), (pb, kbk, D)):
                        for kc in range(8):
                            kb.op("pe", lambda e, ps=ps, kc=kc, off=off, m=m, b=b, n=n: e.matmul(
                                ps[:, 0:n], lhsT=gw[:, kc, off + m * 128:off + (m + 1) * 128], rhs=glb[b][:, kc, 0:n],
                                start=(kc == 0), stop=(kc == 7)), r=[wk, glk[b]], w=[pk])
                    s = m % 2
                    kb.op("act", lambda e, s=s, pb=pb, m=m, n=n: e.activation(out=sg[s][:, 0:n], in_=pb[:, 0:n], func=AF.Sigmoid,
                                                                              bias=gbT[:, 8 + m:9 + m]), r=[kbk, wk], w=[sgk[s]])
                    kb.op("dve", lambda e, s=s, pa=pa, m=m, n=n: e.scalar_tensor_tensor(out=tt_[s][:, 0:n], in0=pa[:, 0:n], scalar=gbT[:, m:m + 1],
                                                                                       in1=sg[s][:, 0:n], op0=ALU.add, op1=ALU.mult),
                          r=[ka, sgk[s], wk], w=[ttk[s]])
                    kb.op("pool", lambda e, s=s, m=m, b=b, n=n: e.tensor_tensor(out=y2b[b][:, m, 0:n], in0=tt_[s][:, 0:n], in1=szb[b][:, m, 0:n],
                                                                                 op=ALU.mult), r=[ttk[s], szk2[b]], w=[y2k[b]])
                if tb == 0:
                    stage("glu0")
                outproj_block(i, y2b[b], y2k[b], t0, n, wout, wk, ob, last)
                if tb == 0:
                    stage("out0")

    def s5_scan(j):
        NH = NK // 2
        with kb.scope() as st1:
            PR = [kb.sb([64, L + 1, 64], F32, "PR", st1) for _ in range(2)]
            PI = [kb.sb([64, L + 1, 64], F32, "PI", st1) for _ in range(2)]
            ptk = [Tok(), Tok()]
            for d in range(2):
                with kb.scope() as st:
                    g = Tok()

                    def t64(name):
                        return kb.sb([64, 64], F32, name, st)
                    lr, li, dt, mag, th, c, s, cc, ss, cs = [t64(x) for x in "lr li dt mag th c s cc ss cs".split()]
                    ar, ai, den, fr, fi, t1, t2, am1 = [t64(x) for x in "ar ai den fr fi t1 t2 am1".split()]
                    kb.dma("sp", lr[:], s5_lamT[j, d, 0], w=[g])
                    kb.dma("sp", li[:], s5_lamT[j, d, 1], w=[g])
                    kb.dma("sp", dt[:], s5_ldt[j, d], w=[g])
                    V = lambda fn: kb.op("dve", fn, r=[g], w=[g])
                    A = lambda fn: kb.op("act", fn, r=[g], w=[g])
                    TT = lambda o, a, b_, op: V(lambda e: e.tensor_tensor(out=o, in0=a, in1=b_, op=op))
                    A(lambda e: e.activation(out=dt[:], in_=dt[:], func=AF.Exp))
                    TT(mag[:], lr[:], dt[:], ALU.mult)
                    A(lambda e: e.activation(out=mag[:], in_=mag[:], func=AF.Exp))
                    TT(th[:], li[:], dt[:], ALU.mult)
                    A(lambda e: e.activation(out=s[:], in_=th[:], func=AF.Sin, scale=0.125))
                    A(lambda e: e.activation(out=c[:], in_=th[:], func=AF.Sin, scale=-0.125, bias=halfpi[0:64, 0:1]))
                    for _ in range(3):
                        TT(cc[:], c[:], c[:], ALU.mult)
                        TT(ss[:], s[:], s[:], ALU.mult)
                        TT(cs[:], c[:], s[:], ALU.mult)
                        TT(c[:], cc[:], ss[:], ALU.subtract)
                        V(lambda e: e.tensor_scalar(out=s[:], in0=cs[:], scalar1=2.0, scalar2=None, op0=ALU.mult))
                    TT(ar[:], mag[:], c[:], ALU.mult)
                    TT(ai[:], mag[:], s[:], ALU.mult)
                    TT(t1[:], lr[:], lr[:], ALU.mult)
                    TT(t2[:], li[:], li[:], ALU.mult)
                    TT(den[:], t1[:], t2[:], ALU.add)
                    V(lambda e: e.reciprocal(out=den[:], in_=den[:]))
                    V(lambda e: e.tensor_scalar(out=am1[:], in0=ar[:], scalar1=-1.0, scalar2=None, op0=ALU.add))
                    TT(t1[:], am1[:], lr[:], ALU.mult)
                    TT(t2[:], ai[:], li[:], ALU.mult)
                    TT(t1[:], t1[:], t2[:], ALU.add)
                    TT(fr[:], t1[:], den[:], ALU.mult)
                    TT(t1[:], ai[:], lr[:], ALU.mult)
                    TT(t2[:], am1[:], li[:], ALU.mult)
                    TT(t1[:], t1[:], t2[:], ALU.subtract)
                    TT(fi[:], t1[:], den[:], ALU.mult)

                    def cmul(orr, oi, xr, xi, yr, yi):
                        TT(t1[:], xr, yr, ALU.mult)
                        TT(t2[:], xi, yi, ALU.mult)
                        TT(orr, t1[:], t2[:], ALU.subtract)
                        TT(t1[:], xr, yi, ALU.mult)
                        TT(t2[:], xi, yr, ALU.mult)
                        TT(oi, t1[:], t2[:], ALU.add)
                    pr, pi = PR[d], PI[d]
                    V(lambda e: e.memset(pr[:, 0, :], 1.0))
                    V(lambda e: e.memset(pi[:, 0, :], 0.0))
                    for n_ in range(1, L + 1):
                        cmul(pr[:, n_, :], pi[:, n_, :], pr[:, n_ - 1, :], pi[:, n_ - 1, :], ar[:], ai[:])
                    PFR = kb.sb([64, L, 64], F32, "PFR", st)
                    PFI = kb.sb([64, L, 64], F32, "PFI", st)
                    for n_ in range(L):
                        cmul(PFR[:, n_, :], PFI[:, n_, :], pr[:, n_, :], pi[:, n_, :], fr[:], fi[:])
                    WTr = kb.sb([64, L, 64, 16], BF16, "WTr", st)
                    WTi = kb.sb([64, L, 64, 16], BF16, "WTi", st)
                    stA = contextlib.ExitStack()
                    br = kb.sb([64, 64, 16], F32, "br", stA)
                    bi = kb.sb([64, 64, 16], F32, "bi", stA)
                    kb.dma("sp", br[:], s5_bT[j, d, 0].rearrange("p (g c) -> p g c", c=16), w=[g])
                    kb.dma("sp", bi[:], s5_bT[j, d, 1].rearrange("p (g c) -> p g c", c=16), w=[g])
                    w1 = kb.sb([64, 64, 16], F32, "w1", stA)
                    w2 = kb.sb([64, 64, 16], F32, "w2", stA)
                    for jp in range(L):
                        n_ = (L - 1 - jp) if d == 0 else jp
                        pfr = bc(PFR[:, n_, :].unsqueeze(2), [64, 64, 16])
                        pfi = bc(PFI[:, n_, :].unsqueeze(2), [64, 64, 16])
                        TT(w1[:], br[:], pfr, ALU.mult)
                        TT(w2[:], bi[:], pfi, ALU.mult)
                        TT(WTr[:, jp], w1[:], w2[:], ALU.subtract)
                        TT(w1[:], bi[:], pfr, ALU.mult)
                        TT(w2[:], br[:], pfi, ALU.mult)
                        TT(WTi[:, jp], w1[:], w2[:], ALU.add)
                    stage("gen%d" % d)
                    kb.barrier()
                    stA.close()
                    stB = contextlib.ExitStack()
                    crb = kb.sb([64, 64 * 16], BF16, "crb", stB)
                    cib = kb.sb([64, 64 * 16], BF16, "cib", stB)
                    kb.dma("pool", crb[:], s5_cT[j, d, 0], w=[g])
                    kb.dma("pool", cib[:], s5_cT[j, d, 1], w=[g])
                    A(lambda e: e.mul(out=cib[:], in_=cib[:], mul=-1.0))
                    KBs = kb.sb([128, 8, L, 128], BF16, "KBs", stB)
                    dT = kb.sb([128, 8], F32, "dT", stB)
                    kb.dma("sp", dT[:], s5_dT[j], w=[g])
                    kt = Tok()
                    for q in range(8):
                        for tau in range(L):
                            jp = (L - 1 - tau) if d == 0 else tau
                            ps, pt = kb.bank()
                            kb.op("pe", lambda e, ps=ps, jp=jp, q=q: e.matmul(ps[:, 0:128], lhsT=WTr[:, jp, q * 8:(q + 1) * 8, :].rearrange("p g c -> p (g c)"), rhs=crb[:, q * 128:(q + 1) * 128],
                                                                               start=True, stop=False), r=[g], w=[pt])
                            kb.op("pe", lambda e, ps=ps, jp=jp, q=q: e.matmul(ps[:, 0:128], lhsT=WTi[:, jp, q * 8:(q + 1) * 8, :].rearrange("p g c -> p (g c)"), rhs=cib[:, q * 128:(q + 1) * 128],
                                                                               start=False, stop=True), r=[g], w=[pt])
                            kb.op("dve", lambda e, ps=ps, q=q, tau=tau: e.tensor_tensor(out=KBs[:, q, tau, :], in0=ps[:, 0:128], in1=bdmask[:], op=ALU.mult),
                                  r=[pt, ctok], w=[kt])
                        if d == 0:
                            kb.op("dve", lambda e, q=q: e.scalar_tensor_tensor(out=KBs[:, q, 0, :], in0=ident_f[:], scalar=dT[:, q:q + 1], in1=KBs[:, q, 0, :],
                                                                               op0=ALU.mult, op1=ALU.add), r=[g, ctok, kt], w=[kt])
                    kb.dma("sp", kb_d[d], KBs[:], r=[kt], w=[kbd_tok])
                    kb.barrier()
                    stB.close()
                    stage("kblk%d" % d)
                    WPs = [kb.sb([128, 4, 2, L, 64], BF16, "WP", st) for _ in range(2)]
                    WPk = [Tok(), Tok()]
                    dXs = [kb.sb([64, 2, 8, NK], F32, "dXs", st) for _ in range(2)]
                    dXk = [Tok(), Tok()]
                    ev = 0
                    uqs = [kb.sb([128, T], BF16, "uq", st) for _ in range(2)]
                    uqk = [Tok(), Tok()]
                    for q in range(8):
                        WP, wpk = WPs[q % 2], WPk[q % 2]
                        dX, dxk = dXs[q % 2], dXk[q % 2]
                        uq, uk = uqs[q % 2], uqk[q % 2]
                        kb.dma("sp", uq[:], u_d[q], r=[u_tok], w=[uk])
                        uv = uq[:].rearrange("p (k j) -> p k j", j=L)
                        ps, pt = kb.bank()
                        psb = ps[:].bitcast(BF16)
                        for ri, WTx in enumerate((WTr, WTi)):
                            for jp in range(L):
                                o = (ri * L + jp) * 64
                                kb.op("pe", lambda e, psb=psb, o=o, WTx=WTx, jp=jp, q=q: e.transpose(out=psb[:, o:o + 64], in_=WTx[:, jp, q * 8:(q + 1) * 8, :].rearrange("p g c -> p (g c)"),
                                                                                                     identity=ident_b[0:64, 0:64]), r=[g, ctok], w=[pt])
                        for par in range(4):
                            kb.op("dve", lambda e, psb=psb, par=par, WP=WP: e.tensor_scalar(out=WP[:, par].rearrange("p a b c -> p (a b c)"), in0=psb[:, 0:2 * L * 64],
                                                                                             scalar1=parmask[:, par:par + 1], scalar2=None, op0=ALU.mult),
                                  r=[pt, ctok], w=[wpk])
                        for pp in range(4):
                            for par in range(2):
                                gl = 2 * pp + par
                                for ri in range(2):
                                    for hf in range(2):
                                        ps, pt = kb.bank()
                                        for jp in range(L):
                                            kb.op("pe", lambda e, ps=ps, WP=WP, par=par, ri=ri, jp=jp, pp=pp, q=q, hf=hf: e.matmul(
                                                ps[0:64, 0:NH], lhsT=(WP[32 * pp:32 * pp + 32, par, ri, jp, :] if pp < 3 else WP[64:128, 2 + par, ri, jp, :]),
                                                rhs=(uv[32 * pp:32 * pp + 32, hf * NH:(hf + 1) * NH, jp] if pp < 3 else uv[64:128, hf * NH:(hf + 1) * NH, jp]),
                                                start=(jp == 0), stop=(jp == L - 1)),
                                                r=[wpk, uk], w=[pt])
                                        dst = dX[:, ri, gl, hf * NH:(hf + 1) * NH]
                                        if ev % 2 == 0:
                                            kb.op("act", lambda e, ps=ps, dst=dst: e.copy(out=dst, in_=ps[0:64, 0:NH]), r=[pt], w=[dxk])
                                        else:
                                            kb.op("dve", lambda e, ps=ps, dst=dst: e.tensor_copy(out=dst, in_=ps[0:64, 0:NH]), r=[pt], w=[dxk])
                                        ev += 1
                        kb.dma("sp", dx_d[d, :, :, q * 8:(q + 1) * 8, :], dX[:], r=[dxk], w=[dx_tok[d]])
            stage("dx")
            SEG = 32
            ctx_k = CTX // L
            segs_f = [(0, ctx_k)] + [(k0, min(SEG, NK - k0)) for k0 in range(ctx_k, NK, SEG)]
            with kb.scope() as st:
                def rec(d):
                    E = "dve" if d == 0 else "pool"
                    X = kb.sb([64, 2, 64], F32, "X", st)
                    t1 = kb.sb([64, 2, 64], F32, "rt1", st)
                    t2 = kb.sb([64, 2, 64], F32, "rt2", st)
                    AR2 = kb.sb([64, 2, 64], F32, "AR2", st)
                    AIn = kb.sb([64, 64], F32, "AIn", st)
                    AIp = kb.sb([64, 64], F32, "AIp", st)
                    xk, tk1, tk2, ak = Tok(), Tok(), Tok(), Tok()
                    kb.op(E, lambda e, X=X: e.memset(X[:], 0.0), w=[xk])
                    for h_ in range(2):
                        kb.op(E, lambda e, h_=h_, AR2=AR2, d=d: e.tensor_copy(out=AR2[:, h_, :], in_=PR[d][:, L, :]), w=[ak])
                    kb.op(E, lambda e, AIp=AIp, d=d: e.tensor_copy(out=AIp[:], in_=PI[d][:, L, :]), w=[ak])
                    kb.op(E, lambda e, AIn=AIn, d=d: e.tensor_scalar(out=AIn[:], in0=PI[d][:, L, :], scalar1=-1.0, scalar2=None, op0=ALU.mult), w=[ak])
                    dsegs = [kb.sb([64, 2, 64, SEG], F32, "dseg", st) for _ in range(2)]
                    xsegs = [kb.sb([64, 2, 64, SEG], BF16, "xseg", st) for _ in range(2)]
                    dsk, xsk = [Tok(), Tok()], [Tok(), Tok()]
                    if d == 0:
                        order = [(k0, n, False) for (k0, n) in segs_f]
                    else:
                        csegs = [(k0, n) for (k0, n) in segs_f if k0 < ctx_k]
                        lsegs = [(k0, n) for (k0, n) in segs_f if k0 >= ctx_k]
                        order = [(k0, n, True) for (k0, n) in reversed(csegs)] + [(k0, n, True) for (k0, n) in reversed(lsegs)]
                    for si, (k0, n, rev) in enumerate(order):
                        b = si % 2
                        ds, xs = dsegs[b], xsegs[b]
                        kb.dma("sp", ds[:, :, :, 0:n], dx_d[d, :, :, :, k0:k0 + n], r=[dx_tok[d]], w=[dsk[b]])
                        ks = range(n - 1, -1, -1) if rev else range(n)
                        for kk in ks:
                            kb.op("act", lambda e, xs=xs, kk=kk, X=X: e.copy(out=xs[:, :, :, kk], in_=X[:]), r=[xk], w=[xsk[b]])
                            kb.op(E, lambda e, t1=t1, X=X, AR2=AR2: e.tensor_tensor(out=t1[:], in0=X[:], in1=AR2[:], op=ALU.mult), r=[xk, ak], w=[tk1])
                            kb.op(E, lambda e, t2=t2, X=X, AIn=AIn: e.tensor_tensor(out=t2[:, 0, :], in0=X[:, 1, :], in1=AIn[:], op=ALU.mult), r=[xk, ak], w=[tk2])
                            kb.op(E, lambda e, t2=t2, X=X, AIp=AIp: e.tensor_tensor(out=t2[:, 1, :], in0=X[:, 0, :], in1=AIp[:], op=ALU.mult), r=[xk, ak], w=[tk2])
                            kb.op(E, lambda e, t1=t1, t2=t2: e.tensor_tensor(out=t1[:], in0=t1[:], in1=t2[:], op=ALU.add), r=[tk2], w=[tk1])
                            kb.op(E, lambda e, t1=t1, X=X, ds=ds, kk=kk: e.tensor_tensor(out=X[:], in0=t1[:], in1=ds[:, :, :, kk], op=ALU.add),
                                  r=[tk1, dsk[b]], w=[xk])
                            yield
                        kb.dma("sp", xin_d[d, :, :, :, k0:k0 + n], xs[:, :, :, 0:n], r=[xsk[b]], w=[xin_tok[d]])
                alive = [rec(0), rec(1)]
                while alive:
                    for g_ in list(alive):
                        try:
                            next(g_)
                        except StopIteration:
                            alive.remove(g_)
            stage("rec")
            with kb.scope() as st:
                PRm = [kb.sb([64, L, 64], F32, "PRm", st) for _ in range(2)]
                PIm = [kb.sb([64, L, 64], F32, "PIm", st) for _ in range(2)]
                pmk = Tok()
                for d in range(2):
                    for r_ in range(L):
                        m_ = r_ + 1 if d == 0 else L - r_
                        kb.op("dve", lambda e, d=d, r_=r_, m_=m_: e.tensor_copy(out=PRm[d][:, r_, :], in_=PR[d][:, m_, :]), w=[pmk])
                        kb.op("dve", lambda e, d=d, r_=r_, m_=m_: e.tensor_copy(out=PIm[d][:, r_, :], in_=PI[d][:, m_, :]), w=[pmk])
                cr = [kb.sb([64, 64, 16], F32, "cr", st) for _ in range(2)]
                ci = [kb.sb([64, 64, 16], F32, "ci", st) for _ in range(2)]
                for d in range(2):
                    kb.dma("sp", cr[d][:], s5_cT[j, d, 0].rearrange("p (g c) -> p g c", c=16), w=[pmk])
                    kb.dma("sp", ci[d][:], s5_cT[j, d, 1].rearrange("p (g c) -> p g c", c=16), w=[pmk])
                m1 = kb.sb([64, L, 8, 16], F32, "m1", st)
                m2 = kb.sb([64, L, 8, 16], F32, "m2", st)
                mr = kb.sb([64, L, 8, 16], F32, "mr", st)
                mi = kb.sb([64, L, 8, 16], F32, "mi", st)
                mk = Tok()
                MX = [kb.sb([64, 2, 2, 4, L, 128], BF16, "MX", st) for _ in range(1)]
                mxk = [Tok(), Tok()]
                XQ = [kb.sb([64, 2, 2, 8, NH], BF16, "XQ", st) for _ in range(2)]
                xqk = [Tok(), Tok()]
                KQ = [kb.sb([128, 2, L, 128], BF16, "KQ", st) for _ in range(2)]
                kqk = [Tok(), Tok()]
                glq = [kb.sb([128, NK, L], BF16, "glq", st) for _ in range(2)]
                glk = [Tok(), Tok()]
                yx = [kb.sb([128, NH], F32, "yx", st) for _ in range(2)]
                y2 = [kb.sb([128, NH], F32, "yy", st) for _ in range(2)]
                ysg = [kb.sb([128, NH], F32, "ysg", st) for _ in range(2)]
                yk = [Tok(), Tok()]
                uqs = [kb.sb([128, T], BF16, "uq3", st) for _ in range(2)]
                uqk = [Tok(), Tok()]
                it = 0
                for q in range(8):
                    MXq, mxq = MX[0], mxk[0]
                    KQq, kqq = KQ[q % 2], kqk[q % 2]
                    gq, gqk = glq[q % 2], glk[q % 2]
                    for d in range(2):
                        kb.dma("sp", KQq[:, d], kb_d[d, :, q], r=[kbd_tok], w=[kqq])
                    uq, uk = uqs[q % 2], uqk[q % 2]
                    kb.dma("sp", uq[:], u_d[q], r=[u_tok], w=[uk])
                    uv = uq[:].rearrange("p (k j) -> p k j", j=L)
                    for d in range(2):
                        crq = bc(cr[d][:, q * 8:(q + 1) * 8, :].unsqueeze(1), [64, L, 8, 16])
                        ciq = bc(ci[d][:, q * 8:(q + 1) * 8, :].unsqueeze(1), [64, L, 8, 16])
                        prq = bc(PRm[d][:, :, q * 8:(q + 1) * 8].unsqueeze(3), [64, L, 8, 16])
                        piq = bc(PIm[d][:, :, q * 8:(q + 1) * 8].unsqueeze(3), [64, L, 8, 16])
                        V = lambda fn: kb.op("dve", fn, r=[pmk, mk], w=[mk])
                        V(lambda e: e.tensor_tensor(out=m1[:], in0=crq, in1=prq, op=ALU.mult))
                        V(lambda e: e.tensor_tensor(out=m2[:], in0=ciq, in1=piq, op=ALU.mult))
                        V(lambda e: e.tensor_tensor(out=mr[:], in0=m1[:], in1=m2[:], op=ALU.subtract))
                        V(lambda e: e.tensor_tensor(out=m1[:], in0=crq, in1=piq, op=ALU.mult))
                        V(lambda e: e.tensor_tensor(out=m2[:], in0=ciq, in1=prq, op=ALU.mult))
                        V(lambda e: e.scalar_tensor_tensor(out=mi[:], in0=m1[:], scalar=-1.0, in1=m2[:], op0=ALU.mult, op1=ALU.subtract))
                        for ri, src in enumerate((mr, mi)):
                            for par in range(4):
                                kb.op("dve", lambda e, d=d, ri=ri, par=par, src=src, MXq=MXq: e.tensor_tensor(
                                    out=MXq[:, d, ri, par], in0=src[:].rearrange("p r g c -> p r (g c)"),
                                    in1=bc(colpar[:, par:par + 1, :], [64, L, 128]), op=ALU.mult), r=[mk, ctok], w=[mxq])
                    for hf in range(2):
                        XQh, xqh = XQ[hf], xqk[hf]
                        for d in range(2):
                            kb.dma("sp", XQh[:, d], xin_d[d, :, :, q * 8:(q + 1) * 8, hf * NH:(hf + 1) * NH], r=xin_tok, w=[xqh])
                        for r_ in range(L):
                            ps, pt = kb.bank()
                            first = [True]

                            def MM(lhsT, rhs, outp, extra_r):
                                stt_ = first[0]
                                first[0] = False
                                kb.op("pe", lambda e: e.matmul(outp, lhsT=lhsT, rhs=rhs, start=stt_, stop=False, skip_group_check=True), r=extra_r, w=[pt])
                            for tau in range(0, r_ + 1):
                                MM(KQq[:, 0, tau, :], uv[:, hf * NH:(hf + 1) * NH, r_ - tau], ps[:, 0:NH], [kqq, uk])
                            for tau in range(0, L - r_):
                                MM(KQq[:, 1, tau, :], uv[:, hf * NH:(hf + 1) * NH, r_ + tau], ps[:, 0:NH], [kqq, uk])
                            for d in range(2):
                                for pp in range(4):
                                    for par in range(2):
                                        for ri in range(2):
                                            if pp < 3:
                                                MM(MXq[:, d, ri, par, r_, 32 * pp:32 * pp + 32], XQh[:, d, ri, 2 * pp + par, :], ps[32 * pp:32 * pp + 32, 0:NH], [mxq, xqh])
                                            else:
                                                MM(MXq[:, d, ri, 2 + par, r_, 64:128], XQh[:, d, ri, 2 * pp + par, :], ps[64:128, 0:NH], [mxq, xqh])
                            b = it % 2
                            it += 1
                            kb.op("act", lambda e, b=b, ps=ps: e.copy(out=yx[b][:], in_=ps[:, 0:NH]), r=[pt], w=[yk[b]])
                            kb.op("pool", lambda e, b=b: e.tensor_tensor(out=y2[b][:], in0=yx[b][:], in1=yx[b][:], op=ALU.mult), r=[yk[b]], w=[yk[b]])
                            kb.op("dve", lambda e, b=b: e.tensor_scalar(out=y2[b][:], in0=y2[b][:], scalar1=0.044715, scalar2=1.0, op0=ALU.mult, op1=ALU.add),
                                  r=[yk[b]], w=[yk[b]])
                            kb.op("pool", lambda e, b=b: e.tensor_tensor(out=y2[b][:], in0=y2[b][:], in1=yx[b][:], op=ALU.mult), r=[yk[b]], w=[yk[b]])
                            kb.op("act", lambda e, b=b: e.activation(out=ysg[b][:], in_=y2[b][:], func=AF.Sigmoid, scale=1.5957691216057308), r=[yk[b]], w=[yk[b]])
                            kb.op("dve", lambda e, b=b, gq=gq, hf=hf, r_=r_: e.tensor_tensor(out=gq[:, hf * NH:(hf + 1) * NH, r_], in0=yx[b][:], in1=ysg[b][:], op=ALU.mult),
                                  r=[yk[b]], w=[gqk])
                    kb.dma("sp", gl_d[q], gq[:].rearrange("p k j -> p (k j)"), r=[gqk], w=[gl_tok])


    def stash_consume(dst_list, st):
        szs = [kb.sb([128, 512], BF16, "stg", st) for _ in range(3)]
        szf = [kb.sb([128, 512], F32, "stgf", st) for _ in range(3)]
        szk = [Tok() for _ in range(3)]
        cnt = [0]

        def consume(m, tb, t0, n, ps, pt):
            dst, dtok, func = dst_list[m]
            b = cnt[0] % 3
            cnt[0] += 1
            if func is None:
                kb.op("dve", lambda e: e.tensor_copy(out=szs[b][0:ps_rows(m), 0:n], in_=ps[0:ps_rows(m), 0:n]), r=[pt], w=[szk[b]])
            else:
                kb.op("act", lambda e: e.activation(out=szf[b][0:ps_rows(m), 0:n], in_=ps[0:ps_rows(m), 0:n], func=func), r=[pt], w=[szk[b]])
                kb.dma("sp", dst[0:ps_rows(m), t0:t0 + n], szf[b][0:ps_rows(m), 0:n], r=[szk[b]], w=[dtok])
                return
            kb.dma("sp", dst[0:ps_rows(m), t0:t0 + n], szs[b][0:ps_rows(m), 0:n], r=[szk[b]], w=[dtok])

        def ps_rows(m):
            return dst_list[m][0].shape[0]
        return consume

    def final_stage(i, last, wout_ap, st):
        wout = kb.sb([128, 8, D], BF16, "wout", st)
        wk = Tok()
        kb.dma("pool", wout[:], wout_ap.rearrange("(kc p) n -> p kc n", p=128), w=[wk])
        y2b = [kb.sb([128, 8, 512], BF16, "y2b", st) for _ in range(2)]
        y2k = [Tok(), Tok()]
        ob = outproj_bufs(st)
        for tb, (t0, n) in enumerate(TBLK):
            b = tb % 2
            kb.dma("sp", y2b[b][:, :, 0:n], y2_d.rearrange("q p t -> p q t")[:, :, t0:t0 + n], r=[y2_tok], w=[y2k[b]])
            outproj_block(i, y2b[b], y2k[b], t0, n, wout, wk, ob, last)

    def na_layer(i, j, last):
        with kb.scope() as st:
            adaln(i, st)
        with kb.scope() as st:
            aT = kb.sb([128, 8, T], BF16, "aT", st)
            a_toks = [Tok() for _ in range(NT)]
            prenorm(i, aT, a_toks, st)
            dl = [(q_d[m], q_tok, None) for m in range(8)] + [(k_d[m], k_tok, None) for m in range(8)]
            dl += [None] * 8 + [(sz_d[m], sz_tok, AF.Silu) for m in range(8)]
            cons = stash_consume(dl, st)
            proj_fm(na_in_w[j][:, 0:2 * D], 2 * D, aT, a_toks, cons, st)
            proj_fm(na_in_w[j][:, 3 * D:4 * D], D, aT, a_toks, lambda m, *a: cons(m + 24, *a), st)
            wv = kb.sb([128, 8, D], BF16, "wv", st)
            wvk = Tok()
            kb.dma("pool", wv[:], na_in_w[j].rearrange("(kc p) n -> p kc n", p=128)[:, :, 2 * D:3 * D], w=[wvk])
            vst = [kb.sb([128, D], BF16, "vst", st) for _ in range(2)]
            vsk = [Tok(), Tok()]
            for ti in range(NT):
                b = ti % 2
                for nh in range(2):
                    ps, pt = kb.bank()
                    for kc in range(8):
                        kb.op("pe", lambda e, ps=ps, kc=kc, ti=ti, nh=nh: e.matmul(ps[:, :], lhsT=aT[:, kc, ti * 128:(ti + 1) * 128],
                                                                                     rhs=wv[:, kc, nh * 512:(nh + 1) * 512], start=(kc == 0), stop=(kc == 7)),
                              r=[wvk, a_toks[ti]], w=[pt])
                    if nh == 0:
                        kb.op("act", lambda e, ps=ps, b=b: e.copy(out=vst[b][:, 0:512], in_=ps[:, :]), r=[pt], w=[vsk[b]])
                    else:
                        kb.op("dve", lambda e, ps=ps, b=b: e.tensor_copy(out=vst[b][:, 512:1024], in_=ps[:, :]), r=[pt], w=[vsk[b]])
                kb.dma("sp", v_d[ti * 128:(ti + 1) * 128, :], vst[b][:], r=[vsk[b]], w=[v_tok])
        stage("na_proj")
        with kb.scope() as st:
            biasm = kb.sb([128, 8, 15, 64], BF16, "biasm", st)
            bstage = kb.sb([128, 15, 64], F32, "bstage", st)
            mask = kb.sb([128, 64], F32, "namask", st)
            bk = Tok()
            kb.dma("sp", mask[:], c_namask[:, :], w=[bk])
            for ch in range(8):
                kb.dma("sp", bstage[:], na_biasT[j, :, ch], w=[bk])
                kb.op("dve", lambda e, ch=ch: e.tensor_tensor(out=biasm[:, ch], in0=bstage[:], in1=bc(mask[:].unsqueeze(1), [128, 15, 64]), op=ALU.add),
                      r=[bk], w=[bk])
            qs = [kb.sb([128, T], BF16, "qs", st) for _ in range(2)]
            ks = [kb.sb([128, T], BF16, "ks", st) for _ in range(2)]
            szc = [kb.sb([128, T], F32, "szc", st) for _ in range(2)]
            vA = [kb.sb([128, 32, 128], BF16, "vA", st) for _ in range(2)]
            vB = [kb.sb([128, 32, 128], BF16, "vB", st) for _ in range(2)]
            vC = [kb.sb([128, 2, 128], BF16, "vC", st) for _ in range(2)]
            y2c = [kb.sb([128, T], BF16, "y2c", st) for _ in range(2)]
            lk = [Tok(), Tok()]
            y2k = [Tok(), Tok()]
            NB = 3
            sc = [kb.sb([128, 768], F32, "sc", st) for _ in range(NB)]
            pb = [kb.sb([128, 768], BF16, "pb", st) for _ in range(NB)]
            pT = [kb.sb([128, 768], BF16, "pT", st) for _ in range(NB)]
            sts = [kb.sb([128, 4], F32, "nst", st) for _ in range(NB)]
            sck = [Tok() for _ in range(NB)]
            pbk = [Tok() for _ in range(NB)]
            ptk = [Tok() for _ in range(NB)]
            stk = [Tok() for _ in range(NB)]
            vd3 = v_d.rearrange("t (c d) -> t c d", d=128)

            def softmax_pv(b, ps_list, width, vtiles, out_ap_rows, ncol, y2dst, szsrc, deps, y2tok):
                for (pap, ptok, c0, w_, bias) in ps_list:
                    if bias is not None:
                        kb.op("dve", lambda e, pap=pap, c0=c0, w_=w_, bias=bias: e.scalar_tensor_tensor(
                            out=sc[b][:, c0:c0 + w_], in0=pap, scalar=0.125, in1=bias, op0=ALU.mult, op1=ALU.add), r=[ptok, bk], w=[sck[b]])
                    else:
                        kb.op("act", lambda e, pap=pap, c0=c0, w_=w_: e.activation(out=sc[b][:, c0:c0 + w_], in_=pap, func=AF.Copy, scale=0.125),
                              r=[ptok], w=[sck[b]])
                yield
                kb.op("dve", lambda e: e.reduce_max(out=sts[b][:, 0:1], in_=sc[b][:, 0:width], axis=AX.X), r=[sck[b]], w=[stk[b]])
                kb.op("dve", lambda e: e.tensor_scalar(out=sts[b][:, 1:2], in0=sts[b][:, 0:1], scalar1=-1.0, scalar2=None, op0=ALU.mult), r=[stk[b]], w=[stk[b]])
                yield
                kb.op("act", lambda e: e.activation(out=pb[b][:, 0:width], in_=sc[b][:, 0:width], func=AF.Exp, bias=sts[b][:, 1:2], accum_out=sts[b][:, 2:3]),
                      r=[sck[b], stk[b]], w=[pbk[b], stk[b]])
                yield
                kb.op("dve", lambda e: e.reciprocal(out=sts[b][:, 3:4], in_=sts[b][:, 2:3]), r=[stk[b]], w=[stk[b]])
                kb.op("dve", lambda e: e.tensor_scalar(out=pb[b][:, 0:width], in0=pb[b][:, 0:width], scalar1=sts[b][:, 3:4], scalar2=None, op0=ALU.mult),
                      r=[stk[b]], w=[pbk[b]])
                yield
                ps, pt = kb.bank(4 * b + 2)
                psb = ps[:].bitcast(BF16)
                nkt = width // 128
                for kt in range(nkt):
                    kb.op("pe", lambda e, kt=kt: e.transpose(out=psb[:, kt * 128:(kt + 1) * 128], in_=pb[b][:, kt * 128:(kt + 1) * 128], identity=ident_b[:]),
                          r=[pbk[b], ctok], w=[pt])
                yield
                kb.op("act", lambda e: e.copy(out=pT[b][:, 0:width], in_=psb[:, 0:width]), r=[pt], w=[ptk[b]])
                yield
                ops, opt = kb.bank(4 * b + 3)
                for kt in range(nkt):
                    for (hb, c0q) in out_ap_rows:
                        kb.op("pe", lambda e, kt=kt, hb=hb, c0q=c0q: e.matmul(ops[hb:hb + 64, 0:ncol], lhsT=vtiles[kt][:, hb:hb + 64],
                                                                               rhs=pT[b][:, kt * 128 + c0q:kt * 128 + c0q + ncol],
                                                                               start=(kt == 0), stop=(kt == nkt - 1)), r=[ptk[b]] + deps, w=[opt])
                yield
                rows = slice(min(h_ for h_, _ in out_ap_rows), max(h_ for h_, _ in out_ap_rows) + 64)
                kb.op("dve", lambda e: e.tensor_tensor(out=y2dst[rows], in0=ops[rows, 0:ncol], in1=szsrc[rows], op=ALU.mult), r=[opt] + deps, w=[y2tok])

            def unit_ctx(b, b2, hp, qt):
                q_, k_, sz_, y2_ = qs[b2], ks[b2], szc[b2], y2c[b2]
                hb = 64 * hp
                ps, pt = kb.bank(4 * b)
                kb.op("pe", lambda e: e.matmul(ps[:, 0:256], lhsT=q_[hb:hb + 64, qt * 128:(qt + 1) * 128], rhs=k_[hb:hb + 64, 0:256],
                                               start=True, stop=True), r=[lk[b2]], w=[pt])
                yield
                t0 = qt * 128
                yield from softmax_pv(b, [(ps[:, 0:256], pt, 0, 256, None)], 256, [vC[b2][:, 0, :], vC[b2][:, 1, :]], [(hb, 0)], 128,
                                      y2_[:, t0:t0 + 128], sz_[:, t0:t0 + 128], [lk[b2]], y2k[b2])

            def unit_row(b, b2, ch, r_):
                q_, k_, sz_, y2_ = qs[b2], ks[b2], szc[b2], y2c[b2]
                r0 = min(max(r_ - 4, 0), 56)
                ro0 = r0 - r_ + 7
                tq = CTX + 64 * r_
                tk = CTX + 64 * r0
                pw, ptw = kb.bank(4 * b)
                pc, ptc = kb.bank(4 * b + 1)
                for hp in range(2):
                    hb = 64 * hp
                    kb.op("pe", lambda e, hb=hb: e.matmul(pw[hb:hb + 64, :], lhsT=q_[hb:hb + 64, tq:tq + 64], rhs=k_[hb:hb + 64, tk:tk + 512],
                                                          start=True, stop=True), r=[lk[b2]], w=[ptw])
                    kb.op("pe", lambda e, hb=hb: e.matmul(pc[hb:hb + 64, 0:256], lhsT=q_[hb:hb + 64, tq:tq + 64], rhs=k_[hb:hb + 64, 0:256],
                                                          start=True, stop=True), r=[lk[b2]], w=[ptc])
                yield
                bias = biasm[:, ch, ro0:ro0 + 8, :].rearrange("p a b -> p (a b)")
                if r0 % 2 == 0:
                    vt = [vA[b2][:, r0 // 2 + kt, :] for kt in range(4)]
                else:
                    vt = [vB[b2][:, (r0 - 1) // 2 + kt, :] for kt in range(4)]
                vt += [vC[b2][:, 0, :], vC[b2][:, 1, :]]
                yield from softmax_pv(b, [(pw[:, :], ptw, 0, 512, bias), (pc[:, 0:256], ptc, 512, 256, None)], 768, vt, [(0, 0), (64, 64)], 64,
                                      y2_[:, tq:tq + 64], sz_[:, tq:tq + 64], [lk[b2]], y2k[b2])

            for ch in range(8):
                b2 = ch % 2
                kb.dma("sp", qs[b2][:], q_d[ch], r=[q_tok], w=[lk[b2]])
                kb.dma("sp", ks[b2][:], k_d[ch], r=[k_tok], w=[lk[b2]])
                kb.dma("sp", szc[b2][:], sz_d[ch], r=[sz_tok], w=[lk[b2]])
                kb.dma("sp", vA[b2][:], vd3[CTX:T, ch, :].rearrange("(m p) d -> p m d", p=128), r=[v_tok], w=[lk[b2]])
                kb.dma("sp", vB[b2][:, 0:31, :], vd3[CTX + 64:T - 64, ch, :].rearrange("(m p) d -> p m d", p=128), r=[v_tok], w=[lk[b2]])
                kb.dma("sp", vC[b2][:], vd3[0:CTX, ch, :].rearrange("(m p) d -> p m d", p=128), r=[v_tok], w=[lk[b2]])
                pending = [("c", hp, qt) for hp in range(2) for qt in range(2)] + [("r", r_) for r_ in range(64)]
                active = {}
                while pending or active:
                    for sl_ in range(2):
                        if sl_ not in active and pending:
                            u = pending.pop(0)
                            active[sl_] = unit_ctx(sl_, b2, u[1], u[2]) if u[0] == "c" else unit_row(sl_, b2, ch, u[1])
                        if sl_ in active:
                            try:
                                next(active[sl_])
                            except StopIteration:
                                del active[sl_]
                kb.dma("sp", y2_d[ch], y2c[b2][:], r=[y2k[b2]], w=[y2_tok])
        stage("na_attn")
        with kb.scope() as st:
            final_stage(i, last, na_out_w[j], st)

    def gdn_layer(i, j, last):
        HD = 128
        NH_ = 8
        with kb.scope() as st:
            adaln(i, st)
        with kb.scope() as st:
            aT = kb.sb([128, 8, T], BF16, "aT", st)
            a_toks = [Tok() for _ in range(NT)]
            prenorm(i, aT, a_toks, st)
            cons = stash_consume([None] * 24 + [(sz_d[m], sz_tok, AF.Silu) for m in range(8)], st)
            wb_ = proj_bufs(st)
            proj_fm(gdn_in_w[j][:, 3 * D:4 * D], D, aT, a_toks, lambda m, *a: cons(m + 24, *a), st, wb_)
            stG = contextlib.ExitStack()
            grow = kb.sb([16, T], F32, "grow", stG)
            brow = kb.sb([16, T], F32, "brow", stG)
            gk, bk_ = Tok(), Tok()
            proj_fm(gdn_in_w[j][:, 4 * D:4 * D + 16], 16, aT, a_toks,
                    lambda m, tb, t0, n, ps, pt: kb.op("act", lambda e: e.copy(out=grow[:, t0:t0 + n], in_=ps[0:16, 0:n]), r=[pt], w=[gk]), st, wb_)
            proj_fm(gdn_in_w[j][:, 4 * D + 16:4 * D + 32], 16, aT, a_toks,
                    lambda m, tb, t0, n, ps, pt: kb.op("act", lambda e: e.activation(out=brow[:, t0:t0 + n], in_=ps[0:16, 0:n], func=AF.Sigmoid), r=[pt], w=[bk_]), st, wb_)
            st = stG
            prm = kb.sb([16, 4], F32, "gprm", st)
            kb.dma("sp", prm[:, 0:2], gdn_prm[j], w=[gk])
            kb.op("dve", lambda e: e.memset(prm[:, 3:4], 1.0), w=[gk])
            kb.op("act", lambda e: e.activation(out=prm[:, 2:3], in_=prm[:, 0:1], func=AF.Exp), r=[gk], w=[gk])
            kb.op("dve", lambda e: e.tensor_scalar(out=prm[:, 2:3], in0=prm[:, 2:3], scalar1=-1.0, scalar2=None, op0=ALU.mult), r=[gk], w=[gk])
            kb.op("act", lambda e: e.activation(out=grow[:], in_=grow[:], func=AF.Exp, bias=prm[:, 1:2]), r=[gk], w=[gk])
            kb.op("act", lambda e: e.activation(out=grow[:], in_=grow[:], func=AF.Ln, bias=prm[:, 3:4]), r=[gk], w=[gk])
            kb.op("dve", lambda e: e.tensor_scalar(out=grow[:], in0=grow[:], scalar1=prm[:, 2:3], scalar2=None, op0=ALU.mult), r=[gk], w=[gk])
            gbs = kb.sb([128, NT, 32], F32, "gbs", st)
            gbk = Tok()
            for ti in range(NT):
                ps, pt = kb.bank()
                kb.op("pe", lambda e, ps=ps, ti=ti: e.transpose(out=ps[:, 0:16], in_=grow[:, ti * 128:(ti + 1) * 128], identity=ident_f[0:16, 0:16]), r=[gk, ctok], w=[pt])
                kb.op("pe", lambda e, ps=ps, ti=ti: e.transpose(out=ps[:, 16:32], in_=brow[:, ti * 128:(ti + 1) * 128], identity=ident_f[0:16, 0:16]), r=[bk_, ctok], w=[pt])
                kb.op("dve", lambda e, ps=ps, ti=ti: e.tensor_copy(out=gbs[:, ti, :], in_=ps[:, 0:32]), r=[pt], w=[gbk])
            kb.dma("sp", gb_d.rearrange("(n p) c -> p n c", p=128), gbs[:], r=[gbk], w=[gb_tok])
            kb.barrier()
            stG.close()
            st = contextlib.ExitStack()
            xr = [kb.sb([128, T], F32, "xrow", st) for _ in range(2)]
            xk = [Tok() for _ in range(2)]
            yr = kb.sb([128, T], F32, "yrow", st)
            sq = kb.sb([128, T], BF16, "sqrow", st)
            ykk = Tok()
            cw = kb.sb([128, 24, 5], F32, "convw", st)
            cwk = Tok()
            kb.dma("sp", cw[:], gdn_cwT[j], w=[cwk])
            ones_b = kb.sb([128, 128], BF16, "ones_b", st)
            epsc = kb.sb([128, 1], F32, "epsc", st)
            kb.op("dve", lambda e: e.memset(ones_b[:], 1.0), w=[cwk])
            kb.op("dve", lambda e: e.memset(epsc[:], EPS), w=[cwk])
            stg = [kb.sb([128, 512], BF16, "cstg", st) for _ in range(2)]
            stgk = [Tok(), Tok()]
            rnb = [kb.sb([128, 512], F32, "rnb", st) for _ in range(2)]
            rnk = [Tok(), Tok()]
            sc_ = [0]

            def qkv_consume(m, tb, t0, n, ps, pt):
                mm = m % 2
                if (tb + m) % 2 == 0:
                    kb.op("act", lambda e: e.copy(out=xr[mm][:, t0:t0 + n], in_=ps[:, 0:n]), r=[pt], w=[xk[mm]])
                else:
                    kb.op("dve", lambda e: e.tensor_copy(out=xr[mm][:, t0:t0 + n], in_=ps[:, 0:n]), r=[pt], w=[xk[mm]])
                if tb != len(TBLK) - 1:
                    return
                x = xr[mm]
                for (a0, a1) in ((0, CTX), (CTX, T)):
                    kb.op("dve", lambda e: e.tensor_scalar(out=yr[:, a0:a1], in0=x[:, a0:a1], scalar1=cw[:, m, 2:3], scalar2=None, op0=ALU.mult),
                          r=[xk[mm], cwk], w=[ykk])
                    for s_ in (-2, -1, 1, 2):
                        lo, hi = max(a0, a0 - s_), min(a1, a1 - s_)
                        kb.op("dve", lambda e, lo=lo, hi=hi, s_=s_: e.scalar_tensor_tensor(out=yr[:, lo:hi], in0=x[:, lo + s_:hi + s_], scalar=cw[:, m, 2 + s_:3 + s_],
                                                                                          in1=yr[:, lo:hi], op0=ALU.mult, op1=ALU.add), r=[xk[mm], cwk, ykk], w=[ykk])
                kb.op("act", lambda e: e.activation(out=yr[:], in_=yr[:], func=AF.Silu), r=[ykk], w=[ykk])
                dstd = qkvT_d[m]
                if m < 16:
                    kb.op("act", lambda e: e.activation(out=sq[:], in_=yr[:], func=AF.Square), r=[ykk], w=[ykk])
                for tb2, (u0, n2) in enumerate(TBLK):
                    b = sc_[0] % 2
                    sc_[0] += 1
                    if m < 16:
                        p2, pt2 = kb.bank()
                        kb.op("pe", lambda e, p2=p2, u0=u0, n2=n2: e.matmul(p2[:, 0:n2], lhsT=ones_b[:], rhs=sq[:, u0:u0 + n2], start=True, stop=True), r=[ykk, cwk], w=[pt2])
                        kb.op("act", lambda e, p2=p2, n2=n2, b=b: e.activation(out=rnb[b][:, 0:n2], in_=p2[:, 0:n2], func=AF.Sqrt, bias=epsc[:, 0:1]), r=[pt2, cwk], w=[rnk[b]])
                        kb.op("dve", lambda e, n2=n2, b=b: e.reciprocal(out=rnb[b][:, 0:n2], in_=rnb[b][:, 0:n2]), r=[rnk[b]], w=[rnk[b]])
                        kb.op("dve", lambda e, u0=u0, n2=n2, b=b: e.tensor_tensor(out=stg[b][:, 0:n2], in0=yr[:, u0:u0 + n2], in1=rnb[b][:, 0:n2], op=ALU.mult),
                              r=[rnk[b], ykk], w=[stgk[b]])
                    else:
                        kb.op("act", lambda e, u0=u0, n2=n2, b=b: e.copy(out=stg[b][:, 0:n2], in_=yr[:, u0:u0 + n2]), r=[ykk], w=[stgk[b]])
                    kb.dma("sp", dstd[:, u0:u0 + n2], stg[b][:, 0:n2], r=[stgk[b]], w=[qkv_tok])

            proj_fm(gdn_in_w[j][:, 0:3 * D], 3 * D, aT, a_toks, qkv_consume, st, wb_, gsz=256)
            kb.barrier()
            st.close()
        stage("gdn_proj")
        with kb.scope() as st:
            msk = kb.sb([128, 4, 128], F32, "gmsk", st)
            ones_f = kb.sb([128, 1], F32, "ones_f", st)
            ngb = kb.sb([128, 128], F32, "ngb", st)
            mk_ = Tok()
            kb.dma("sp", msk[:], c_gmask[:, :, :], w=[mk_])
            kb.dma("sp", ngb[:], gdn_ngb[j], w=[mk_])
            kb.op("dve", lambda e: e.memset(ones_f[:], 1.0), w=[mk_])
            gbs = kb.sb([128, NT, 32], F32, "gbs2", st)
            kb.dma("sp", gbs[:], gb_d.rearrange("(n p) c -> p n c", p=128), r=[gb_tok], w=[mk_])
            qT = [kb.sb([128, T], BF16, "gqT", st) for _ in range(2)]
            kT = [kb.sb([128, T], BF16, "gkT", st) for _ in range(2)]
            vT = [kb.sb([128, T], BF16, "gvT", st) for _ in range(2)]
            szh = [kb.sb([128, T], F32, "gsz", st) for _ in range(2)]
            y2h = [kb.sb([128, T], BF16, "gy2", st) for _ in range(2)]
            hk = [Tok(), Tok()]
            y2k = [Tok(), Tok()]
            NS = 4

            def mk(shape, dt, name):
                return [kb.sb(shape, dt, name, st) for _ in range(NS)]
            ktok_, vtok_ = mk([128, 128], BF16, "ktok"), mk([128, 128], F32, "vtok")
            kf32 = mk([128, 128], F32, "kf32")
            W1, W2 = mk([128, 128], F32, "W1"), mk([128, 128], F32, "W2")
            Eb, ETb = mk([128, 128], F32, "Eb"), mk([128, 128], F32, "ETb")
            Pm = [mk([128, 128], F32, "Pm%d" % q_) for q_ in range(7)]
            PmT = [mk([128, 128], F32, "PmT%d" % q_) for q_ in range(7)]
            AT = mk([128, 128], BF16, "AT")
            Xs = [mk([128, 256], F32, "Xs%d" % q_) for q_ in range(2)]
            Cm = [mk([128, 128], F32, "Cm%d" % q_) for q_ in range(3)]
            Ym = [mk([128, 128], F32, "Ym%d" % q_) for q_ in range(6)]
            bmsk = kb.sb([128, 3, 128], F32, "bmsk", st)
            kb.dma("sp", bmsk[:], c_bmask[:, :, :], w=[mk_])
            cols = mk([128, 8], F32, "gcols")
            wTb, kdec, vnew = mk([128, 128], BF16, "wTb"), mk([128, 128], BF16, "kdec"), mk([128, 128], BF16, "vnew")
            avs, osb = mk([128, 128], F32, "avs"), mk([128, 128], F32, "osb")
            tk_ = [Tok() for _ in range(NS)]
            Ss = [kb.sb([128, 128], F32, "Sst", st) for _ in range(4)]
            Sbs = [kb.sb([128, 128], BF16, "Sbf", st) for _ in range(4)]
            sks = [Tok() for _ in range(4)]
            cfw = [kb.sb([128, 128], F32, "cfw", st) for _ in range(2)]
            crv = [kb.sb([128, 128], F32, "crv", st) for _ in range(2)]
            cjk = [kb.sb([128, 128], F32, "cjk", st) for _ in range(2)]
            cyb = [kb.sb([128, 128], BF16, "cyb", st) for _ in range(2)]
            cst = [kb.sb([128, 4], F32, "cst", st) for _ in range(2)]
            ck = [Tok(), Tok()]
            ofw = mk([128, 128], F32, "ofw")
            ofk = [Tok() for _ in range(NS)]
            yb_ = mk([128, 128], BF16, "gyb")
            ybk = [Tok() for _ in range(NS)]
            ost = mk([128, 4], F32, "gost")
            def chain(h, hb, d, b, S, Sb, sk):
                order = list(range(0, NT)) if d == 0 else [1, 0] + list(range(NT - 1, 1, -1))
                m_le, m_gt = (0, 1) if d == 0 else (2, 3)
                kb.op("dve", lambda e: e.memset(S[:], 0.0), w=[sk])
                kb.op("dve", lambda e: e.memset(Sb[:], 0.0), w=[sk])
                for n_ in order:
                    tk = tk_[b]
                    ts = slice(n_ * 128, (n_ + 1) * 128)
                    gcol = gbs[:, n_, d * 8 + h:d * 8 + h + 1]
                    bcol = gbs[:, n_, 16 + d * 8 + h:16 + d * 8 + h + 1]
                    V = lambda fn, r=(), w=(): kb.op("dve", fn, r=[tk, mk_, hk[hb]] + list(r), w=[tk] + list(w))
                    A = lambda fn, r=(), w=(): kb.op("act", fn, r=[tk, mk_, hk[hb]] + list(r), w=[tk] + list(w))
                    P = lambda fn, r=(), w=(): kb.op("pe", fn, r=[tk, mk_, hk[hb], ctok] + list(r), w=list(w))
                    p1, t1 = kb.bank()
                    p1b = p1[:].bitcast(BF16)
                    P(lambda e: e.transpose(out=p1b[:, 0:128], in_=kT[hb][:, ts], identity=ident_b[:]), w=[t1])
                    P(lambda e: e.transpose(out=p1b[:, 128:256], in_=vT[hb][:, ts], identity=ident_b[:]), w=[t1])
                    V(lambda e: e.tensor_copy(out=kf32[b][:], in_=p1b[:, 0:128]), r=[t1])
                    A(lambda e: e.copy(out=vtok_[b][:], in_=p1b[:, 128:256]), r=[t1])
                    yield
                    V(lambda e: e.tensor_scalar(out=W1[b][:], in0=msk[:, m_le, :], scalar1=gcol, scalar2=None, op0=ALU.mult))
                    V(lambda e: e.tensor_scalar(out=W2[b][:], in0=msk[:, m_gt, :], scalar1=gcol, scalar2=None, op0=ALU.mult))
                    yield
                    p2, t2 = kb.bank()
                    P(lambda e: e.matmul(p2[:, 0:128], lhsT=W1[b][:], rhs=msk[:, m_gt, :], start=True, stop=True), w=[t2])
                    P(lambda e: e.matmul(p2[:, 128:256], lhsT=msk[:, m_gt, :], rhs=W1[b][:], start=True, stop=True), w=[t2])
                    P(lambda e: e.matmul(p2[:, 256:258], lhsT=W1[b][:], rhs=bc(ones_f[:, 0:1], [128, 2]), start=True, stop=True), w=[t2])
                    P(lambda e: e.matmul(p2[:, 258:260], lhsT=W2[b][:], rhs=bc(ones_f[:, 0:1], [128, 2]), start=True, stop=True), w=[t2])
                    A(lambda e: e.activation(out=Eb[b][:], in_=p2[:, 0:128], func=AF.Exp), r=[t2])
                    A(lambda e: e.activation(out=ETb[b][:], in_=p2[:, 128:256], func=AF.Exp), r=[t2])
                    yield
                    c_ = cols[b]
                    A(lambda e: e.activation(out=c_[:, 0:1], in_=p2[:, 256:257], func=AF.Exp), r=[t2])
                    A(lambda e: e.activation(out=c_[:, 1:2], in_=p2[:, 258:259], func=AF.Exp), r=[t2])
                    V(lambda e: e.tensor_tensor(out=c_[:, 2:3], in0=p2[:, 256:257], in1=c_[:, 1:2], op=ALU.bypass), r=[t2]) if False else None
                    V(lambda e: e.tensor_copy(out=c_[:, 2:3], in_=p2[:, 258:259]), r=[t2])
                    V(lambda e: e.tensor_tensor(out=c_[:, 2:3], in0=c_[:, 2:3], in1=p2[:, 256:257], op=ALU.add), r=[t2])
                    A(lambda e: e.activation(out=c_[:, 3:4], in_=c_[:, 2:3], func=AF.Exp))
                    V(lambda e: e.tensor_tensor(out=c_[:, 4:5], in0=c_[:, 0:1], in1=bcol, op=ALU.mult))
                    V(lambda e: e.tensor_scalar(out=c_[:, 5:6], in0=c_[:, 0:1], scalar1=HD ** -0.5, scalar2=None, op0=ALU.mult))
                    yield
                    p3, t3 = kb.bank()
                    P(lambda e: e.matmul(p3[:, 0:128], lhsT=kT[hb][:, ts], rhs=kT[hb][:, ts], start=True, stop=True), w=[t3])
                    P(lambda e: e.matmul(p3[:, 128:256], lhsT=kT[hb][:, ts], rhs=qT[hb][:, ts], start=True, stop=True), w=[t3])
                    yield
                    V(lambda e: e.tensor_tensor(out=Eb[b][:], in0=Eb[b][:], in1=msk[:, m_gt, :], op=ALU.mult))
                    V(lambda e: e.scalar_tensor_tensor(out=Pm[0][b][:], in0=p3[:, 0:128], scalar=bcol, in1=Eb[b][:], op0=ALU.mult, op1=ALU.mult), r=[t3])
                    V(lambda e: e.tensor_tensor(out=ETb[b][:], in0=ETb[b][:], in1=msk[:, m_le, :], op=ALU.mult))
                    V(lambda e: e.scalar_tensor_tensor(out=AT[b][:], in0=p3[:, 128:256], scalar=HD ** -0.5, in1=ETb[b][:], op0=ALU.mult, op1=ALU.mult), r=[t3])
                    yield
                    p4, t4 = kb.bank()
                    P(lambda e: e.transpose(out=p4[:, 0:128], in_=Pm[0][b][:], identity=ident_f[:]), w=[t4])
                    A(lambda e: e.copy(out=PmT[0][b][:], in_=p4[:, 0:128]), r=[t4])
                    yield
                    Ld, LdT = Pm[1][b], PmT[1][b]
                    V(lambda e: e.tensor_tensor(out=Ld[:], in0=Pm[0][b][:], in1=bmsk[:, 0, :], op=ALU.mult))
                    V(lambda e: e.tensor_tensor(out=LdT[:], in0=PmT[0][b][:], in1=bmsk[:, 0, :], op=ALU.mult))
                    C1, C1T, C2 = Cm[0][b], Cm[1][b], Cm[2][b]
                    V(lambda e: e.tensor_tensor(out=C1[:], in0=Pm[0][b][:], in1=bmsk[:, 1, :], op=ALU.mult))
                    V(lambda e: e.tensor_tensor(out=C1T[:], in0=PmT[0][b][:], in1=bmsk[:, 1, :], op=ALU.mult))
                    V(lambda e: e.tensor_tensor(out=C2[:], in0=Pm[0][b][:], in1=bmsk[:, 2, :], op=ALU.mult))
                    for q_ in range(1, 5):
                        yield
                        p5, t5 = kb.bank()
                        P(lambda e, q_=q_, p5=p5: e.matmul(p5[:, 0:128], lhsT=PmT[q_][b][:], rhs=Pm[q_][b][:], start=True, stop=True), w=[t5])
                        V(lambda e, q_=q_, p5=p5: e.tensor_copy(out=Pm[q_ + 1][b][:], in_=p5[:, 0:128]), r=[t5])
                        if q_ < 4:
                            P(lambda e, q_=q_, p5=p5: e.matmul(p5[:, 128:256], lhsT=Pm[q_][b][:], rhs=PmT[q_][b][:], start=True, stop=True), w=[t5])
                            A(lambda e, q_=q_, p5=p5: e.copy(out=PmT[q_ + 1][b][:], in_=p5[:, 128:256]), r=[t5])
                    yield
                    Ya, Yb = Ym[0][b], Ym[1][b]
                    V(lambda e: e.tensor_tensor(out=Ya[:], in0=Pm[5][b][:], in1=ident_f[:], op=ALU.add))
                    curY, nxtY = Ya, Yb
                    for q_ in range(4, 0, -1):
                        yield
                        p6, t6 = kb.bank()
                        P(lambda e, q_=q_, p6=p6, curY=curY: e.matmul(p6[:, 0:128], lhsT=PmT[q_][b][:], rhs=curY[:], start=True, stop=True), w=[t6])
                        op_ = ALU.add if q_ > 1 else ALU.subtract
                        V(lambda e, p6=p6, curY=curY, nxtY=nxtY, op_=op_: e.tensor_tensor(out=nxtY[:], in0=curY[:], in1=p6[:, 0:128], op=op_), r=[t6])
                        curY, nxtY = nxtY, curY
                    yield
                    Td = curY
                    TdT, Wm, T64, T64T, TT = Ym[2][b], Ym[3][b], Ym[4][b], Ym[5][b], nxtY
                    p6, t6 = kb.bank()
                    P(lambda e, p6=p6: e.transpose(out=p6[:, 0:128], in_=Td[:], identity=ident_f[:]), w=[t6])
                    A(lambda e, p6=p6: e.copy(out=TdT[:], in_=p6[:, 0:128]), r=[t6])

                    def merge(dst, base, lhs_in, rhs_in, lhs_out):
                        pa_, ta_ = kb.bank()
                        P(lambda e: e.matmul(pa_[:, 0:128], lhsT=lhs_in[:], rhs=rhs_in[:], start=True, stop=True), w=[ta_])
                        V(lambda e: e.tensor_copy(out=Wm[:], in_=pa_[:, 0:128]), r=[ta_])
                        pb_, tb2 = kb.bank()
                        P(lambda e: e.matmul(pb_[:, 0:128], lhsT=lhs_out[:], rhs=Wm[:], start=True, stop=True), w=[tb2])
                        V(lambda e: e.tensor_tensor(out=dst[:], in0=base[:], in1=pb_[:, 0:128], op=ALU.subtract), r=[tb2])
                    yield
                    merge(T64, Td, C1T, Td, TdT)
                    yield
                    merge(T64T, TdT, C1, TdT, Td)
                    yield
                    merge(TT, T64T, C2, T64T, T64)
                    yield
                    X0, X1 = Xs[0][b], Xs[1][b]
                    V(lambda e: e.tensor_scalar(out=X0[:, 0:128], in0=vtok_[b][:], scalar1=bcol, scalar2=None, op0=ALU.mult))
                    V(lambda e: e.tensor_scalar(out=X0[:, 128:256], in0=kf32[b][:], scalar1=c_[:, 4:5], scalar2=None, op0=ALU.mult))
                    p6, t6 = kb.bank()
                    P(lambda e, p6=p6: e.matmul(p6[:, 0:256], lhsT=TT[:], rhs=X0[:], start=True, stop=True), w=[t6])
                    V(lambda e, p6=p6: e.tensor_copy(out=X1[:], in_=p6[:, 0:256]), r=[t6])
                    cur = X1
                    yield
                    p7, t7 = kb.bank()
                    P(lambda e, cur=cur: e.transpose(out=p7[:, 0:128], in_=cur[:, 128:256], identity=ident_f[:]), w=[t7])
                    A(lambda e: e.copy(out=wTb[b][:], in_=p7[:, 0:128]), r=[t7])
                    V(lambda e: e.tensor_scalar(out=kdec[b][:], in0=kf32[b][:], scalar1=c_[:, 1:2], scalar2=None, op0=ALU.mult))
                    yield
                    p8, t8 = kb.bank()
                    P(lambda e: e.matmul(p8[:, 0:128], lhsT=wTb[b][:], rhs=Sb[:], start=True, stop=True), r=[sk], w=[t8])
                    P(lambda e: e.matmul(p8[:, 128:256], lhsT=qT[hb][:, ts], rhs=Sb[:], start=True, stop=True), r=[sk], w=[t8])
                    V(lambda e, cur=cur: e.tensor_tensor(out=vnew[b][:], in0=cur[:, 0:128], in1=p8[:, 0:128], op=ALU.subtract), r=[t8])
                    yield
                    p9, t9 = kb.bank()
                    P(lambda e: e.matmul(p9[:, 0:128], lhsT=AT[b][:], rhs=vnew[b][:], start=True, stop=True), w=[t9])
                    P(lambda e: e.matmul(p9[:, 128:256], lhsT=kdec[b][:], rhs=vnew[b][:], start=True, stop=True), w=[t9])
                    A(lambda e: e.copy(out=avs[b][:], in_=p9[:, 0:128]), r=[t9])
                    V(lambda e: e.scalar_tensor_tensor(out=osb[b][:], in0=p8[:, 128:256], scalar=c_[:, 5:6], in1=avs[b][:], op0=ALU.mult, op1=ALU.add), r=[t8])
                    kb.op("dve", lambda e: e.scalar_tensor_tensor(out=S[:], in0=S[:], scalar=c_[:, 3:4], in1=p9[:, 128:256], op0=ALU.mult, op1=ALU.add),
                          r=[tk, t9, sk], w=[sk])
                    kb.op("act", lambda e: e.copy(out=Sb[:], in_=S[:]), r=[sk], w=[sk])
                    yield
                    kb.dma("sp", o2_d[d, n_ * 128:(n_ + 1) * 128, h * 128:(h + 1) * 128], osb[b][:], r=[tk], w=[o2_tok[d][n_]])
            for hp_ in range(NH_ // 2):
                for hb in range(2):
                    h = 2 * hp_ + hb
                    for (dst_, src_) in ((qT[hb], qkvT_d[h]), (kT[hb], qkvT_d[8 + h]), (vT[hb], qkvT_d[16 + h])):
                        kb.dma("sp", dst_[:], src_, r=[qkv_tok], w=[hk[hb]])
                    kb.dma("sp", szh[hb][:], sz_d[h], r=[sz_tok], w=[hk[hb]])
                alive = [chain(2 * hp_ + hb, hb, d, 2 * d + hb, Ss[2 * d + hb], Sbs[2 * d + hb], sks[2 * d + hb]) for d in range(2) for hb in range(2)]
                while alive:
                    for g_ in list(alive):
                        try:
                            next(g_)
                        except StopIteration:
                            alive.remove(g_)
                for hb in range(2):
                    h = 2 * hp_ + hb
                    for n_ in range(NT):
                        c = n_ % 2
                        ts = slice(n_ * 128, (n_ + 1) * 128)
                        CV = lambda fn, r=(), w=(): kb.op("dve", fn, r=[ck[c], mk_] + list(r), w=[ck[c]] + list(w))
                        CA = lambda fn, r=(), w=(): kb.op("act", fn, r=[ck[c], mk_] + list(r), w=[ck[c]] + list(w))
                        kb.dma("sp", cfw[c][:], o2_d[0, n_ * 128:(n_ + 1) * 128, h * 128:(h + 1) * 128], r=[o2_tok[0][n_]], w=[ck[c]])
                        kb.dma("sp", crv[c][:], o2_d[1, n_ * 128:(n_ + 1) * 128, h * 128:(h + 1) * 128], r=[o2_tok[1][n_]], w=[ck[c]])
                        CV(lambda e: e.tensor_tensor(out=cfw[c][:], in0=cfw[c][:], in1=crv[c][:], op=ALU.add))
                        CA(lambda e: e.activation(out=cjk[c][:], in_=cfw[c][:], func=AF.Square, accum_out=cst[c][:, 0:1]))
                        CV(lambda e: e.tensor_scalar(out=cst[c][:, 1:2], in0=cst[c][:, 0:1], scalar1=1.0 / HD, scalar2=EPS, op0=ALU.mult, op1=ALU.add))
                        CA(lambda e: e.activation(out=cst[c][:, 2:3], in_=cst[c][:, 1:2], func=AF.Sqrt))
                        CV(lambda e: e.reciprocal(out=cst[c][:, 3:4], in_=cst[c][:, 2:3]))
                        CV(lambda e: e.scalar_tensor_tensor(out=cyb[c][:], in0=cfw[c][:], scalar=cst[c][:, 3:4], in1=ngb[:], op0=ALU.mult, op1=ALU.mult))
                        pa, ta = kb.bank()
                        pab = pa[:].bitcast(BF16)
                        kb.op("pe", lambda e: e.transpose(out=pab[:, 0:128], in_=cyb[c][:], identity=ident_b[:]), r=[ck[c], ctok], w=[ta])
                        kb.op("dve", lambda e: e.tensor_tensor(out=y2h[hb][:, ts], in0=pab[:, 0:128], in1=szh[hb][:, ts], op=ALU.mult), r=[ta, hk[hb]], w=[y2k[hb]])
                for hb in range(2):
                    h = 2 * hp_ + hb
                    kb.dma("sp", y2_d[h], y2h[hb][:], r=[y2k[hb]], w=[y2_tok])
        stage("gdn_core")
        with kb.scope() as st:
            final_stage(i, last, gdn_out_w[j], st)
    try:
        for i in layers:
            kind, j = i % 3, i // 3
            last = (i == DEPTH - 1)
            first_layer[0] = (i == layers[0])
            if kind == 0:
                s5_layer(i, j, last)
            elif kind == 2:
                na_layer(i, j, last)
            elif kind == 1:
                gdn_layer(i, j, last)
            else:
                raise NotImplementedError
            if "hsnap_d" in DBG and not last:
                for ti in range(NT):
                    kb.dma("sp", hsnap_d[i, ti * 128:(ti + 1) * 128, :], h_d[ti * 128:(ti + 1) * 128, :], r=[h_tok[ti]], w=[snap_tok])
                kb.barrier()
    except StopBuild:
        kb.barrier()
        return kb
    if n_layers < DEPTH:
        with kb.scope() as st:
            bufs = [kb.sb([128, D], F32, "cp", st) for _ in range(2)]
            bk = [Tok(), Tok()]
            for ti in range(2, NT):
                b = ti % 2
                kb.dma("sp", bufs[b][:], h_d[ti * 128:(ti + 1) * 128, :], r=[h_tok[ti]], w=[bk[b]])
                kb.dma("sp", out[(ti - 2) * 128:(ti - 1) * 128, :], bufs[b][:], r=[bk[b]], w=[h_tok[ti]])
    kb.barrier()
    kb.es.close()
    return kb


def host_inputs(inp, b):
    f = np.float32
    A = lambda x: np.ascontiguousarray(x, dtype=f)
    m = {}
    m["hx"] = A(np.concatenate([inp["ctx"][b], inp["x"][b]], axis=0))
    cond = np.stack([inp["c"][b], inp["c_ctx"]], axis=0)
    m["condT"] = A(cond.reshape(2, 8, 128).transpose(2, 1, 0))
    m["ada_w"] = A(inp["ada_w"])
    m["ada_bT"] = A(inp["ada_b"].reshape(DEPTH, 24, 128).transpose(0, 2, 1))
    m["ada_bg"] = A(np.broadcast_to(inp["ada_b"][:, None, 2 * D:], (DEPTH, 128, D)))
    m["pre_gT"] = A(inp["pre_g"].reshape(DEPTH, 8, 128).transpose(0, 2, 1))
    m["post_gb"] = A(np.broadcast_to(inp["post_g"][:, None, :], (DEPTH, 128, D)))
    m["s5_in_w"] = A(inp["s5_in_w"])
    m["s5_glu_w"] = A(inp["s5_glu_w"])
    m["s5_out_w"] = A(inp["s5_out_w"])
    m["s5_lamT"] = A(np.stack([inp["s5_lam_re"], inp["s5_lam_im"]], axis=2).transpose(0, 1, 2, 4, 3))
    m["s5_ldt"] = A(np.broadcast_to(inp["s5_log_dt"][:, :, None, :], (2, 2, 64, 64)))
    bst = np.stack([inp["s5_b_re"], inp["s5_b_im"]], axis=2)
    m["s5_bT"] = A(bst.transpose(0, 1, 2, 4, 3, 5).reshape(2, 2, 2, 64, 1024))
    cst = np.stack([inp["s5_c_re"], inp["s5_c_im"]], axis=2)
    m["s5_cT"] = A(cst.transpose(0, 1, 2, 5, 3, 4).reshape(2, 2, 2, 64, 1024))
    m["s5_dT"] = A(inp["s5_d"].reshape(2, 8, 128).transpose(0, 2, 1))
    m["s5_gbT"] = A(inp["s5_glu_b"].reshape(2, 16, 128).transpose(0, 2, 1))
    m["gdn_in_w"] = A(inp["gdn_in_w"])
    m["gdn_out_w"] = A(inp["gdn_out_w"])
    m["gdn_prm"] = A(np.stack([inp["gdn_a_log"].reshape(1, 16), inp["gdn_dt_bias"].reshape(1, 16)], axis=2))
    m["gdn_cwT"] = A(inp["gdn_conv_w"].reshape(1, 5, 24, 128).transpose(0, 3, 2, 1))
    m["gdn_ngb"] = A(np.broadcast_to(inp["gdn_norm_g"][:, None, :], (1, 128, 128)))
    mi = np.arange(128)
    m["c_gmask"] = A(np.stack([mi[:, None] <= mi[None, :], mi[:, None] > mi[None, :], mi[:, None] >= mi[None, :], mi[:, None] < mi[None, :]], axis=1))
    m["c_bmask"] = A(np.stack([mi[:, None] // 32 == mi[None, :] // 32,
                               (mi[:, None] // 64 == mi[None, :] // 64) & (mi[:, None] // 32 != mi[None, :] // 32),
                               mi[:, None] // 64 != mi[None, :] // 64], axis=1))
    m["na_in_w"] = A(inp["na_in_w"])
    m["na_out_w"] = A(inp["na_out_w"])
    rpb = inp["na_rpb"]
    wq = np.arange(64)
    c0 = np.clip(wq - 8, 0, 48)
    wk_ = np.arange(64)
    inwin = (wk_[None, :] >= c0[:, None]) & (wk_[None, :] < c0[:, None] + 16)
    coff = np.clip(wk_[None, :] - wq[:, None] + 15, 0, 30)
    bt = np.where(inwin[None, None, None], rpb[:, :, :, coff], 0.0)
    bt = bt.reshape(1, 8, 2, 15, 64, 64).transpose(0, 2, 4, 1, 3, 5).reshape(1, 128, 8, 15, 64)
    m["na_biasT"] = A(bt)
    m["c_namask"] = A(np.tile(np.where(inwin, 0.0, -30000.0), (2, 1)))
    m["c_ident"] = np.eye(128, dtype=f)
    p = np.arange(128)
    m["c_bd"] = A((p[:, None] // 16) == (p[None, :] // 16))
    cp = np.stack([(p // 16) % 2 == 0, (p // 16) % 2 == 1, ((p // 16) % 2 == 0) & (p >= 96), ((p // 16) % 2 == 1) & (p >= 96)], axis=0)
    m["c_par"] = A(cp.T)
    m["c_colpar"] = A(np.broadcast_to(cp[None], (64, 4, 128)))
    return m


_CACHE = {}


def kernel(**inputs):
    inp = {k: np.asarray(v) for k, v in inputs.items()}
    if "nc" not in _CACHE:
        _CACHE["nc"] = build(DEPTH).nc
    nc = _CACHE["nc"]
    nb = inp["x"].shape[0]
    maps = [host_inputs(inp, b) for b in range(nb)]
    res = run_bass_kernel_spmd(nc, maps, core_ids=list(range(nb)))
    return np.stack([np.asarray(res.results[b]["out"]) for b in range(nb)], axis=0).astype(np.float32)
```

```python
import contextlib
import math
import numpy as np
import concourse.bass as bass
import concourse.mybir as mybir
from concourse.bass_utils import run_bass_kernel_spmd

F32 = mybir.dt.float32
BF16 = mybir.dt.bfloat16
AF = mybir.ActivationFunctionType
ALU = mybir.AluOpType
AX = mybir.AxisListType

D = 1024
SEQ = 4096
CTX = 256
T = SEQ + CTX
NT = T // 128
DEPTH = 4
EPS = 1e-6
L = 8
NK = T // L
TBLK = [(i * 512, min(512, T - i * 512)) for i in range((T + 511) // 512)]


DBG = set()
ONLY_GLU = False
OUT_CUT = 99


class Tok:
    __slots__ = ("w", "r")

    def __init__(self):
        self.w = None
        self.r = {}


class KB:
    def __init__(self):
        self.nc = bass.Bass("TRN2", target_bir_lowering=False)
        self.es = contextlib.ExitStack()
        nc = self.nc
        self.eng = {}
        for name, obj in (("pe", nc.tensor), ("act", nc.scalar), ("dve", nc.vector), ("pool", nc.gpsimd), ("sp", nc.sync)):
            sem = self.es.enter_context(nc.semaphore("sem_" + name))
            self.eng[name] = dict(e=obj, sem=sem, cnt=0, waited={}, name=name)
        self.ndma = 48
        self.dsem = [self.es.enter_context(nc.semaphore("dsem%d" % i)) for i in range(self.ndma)]
        self.dval = [0] * self.ndma
        self.dnext = 0
        self.psem = [self.es.enter_context(nc.semaphore("psem%d" % i)) for i in range(48)]
        self.pnext = 0
        self.uid = 0
        self.banks = []
        for i in range(8):
            t = self.es.enter_context(nc.psum_tensor("bank%d" % i, [128, 512], F32))
            self.banks.append((t, Tok()))
        self.bnext = 0
        self.ninst = 0
        self.bank_of = {}
        self.stale = set()
        self.keep = []

    def sb(self, shape, dtype, name=None, stack=None):
        self.uid += 1
        t = (stack or self.es).enter_context(self.nc.sbuf_tensor("%s_%d" % (name or "t", self.uid), list(shape), dtype))
        return t

    def dram(self, name, shape, dtype, kind="Internal"):
        if name in DBG:
            kind = "ExternalOutput"
        return self.nc.dram_tensor(name, list(shape), dtype, kind=kind).ap()

    def bank(self, idx=None):
        i = (self.bnext % 8) if idx is None else idx
        if idx is None:
            self.bnext += 1
        t, old = self.banks[i]
        tok = Tok()
        tok.w, tok.r = old.w, dict(old.r)
        self.banks[i] = (t, tok)
        self.bank_of[id(tok)] = i
        self.bank_of.pop(id(old), None)
        self.stale.add(id(old))
        self.keep.append(old)
        return t, tok

    def _wait(self, E, sem, val):
        key = id(sem)
        if E["waited"].get(key, 0) < val:
            E["e"].wait_ge(sem, val)
            E["waited"][key] = val

    def _deps(self, E, reads, writes):
        own = id(E["sem"])
        for t in reads:
            if t.w is not None:
                if E["name"] == "pe" and id(t.w[0]) == own:
                    continue
                self._wait(E, *t.w)
        for t in writes:
            if t.w is not None and not (E["name"] == "pe" and id(t.w[0]) == own):
                self._wait(E, *t.w)
            for sem, val in t.r.values():
                if E["name"] == "pe" and id(sem) == own:
                    continue
                self._wait(E, sem, val)

    def _mark(self, sem, val, reads, writes):
        for t in writes:
            t.w = (sem, val)
            t.r = {}
        for t in reads:
            t.r[id(sem)] = (sem, val)

    def op(self, eng, fn, r=(), w=()):
        for t in list(r) + list(w):
            assert id(t) not in self.stale, "stale PSUM bank token used (bank re-allocated before its last use)"
        E = self.eng[eng]
        self._deps(E, r, w)
        inst = fn(E["e"])
        E["cnt"] += 1
        inst.then_inc(E["sem"], 1)
        self._mark(E["sem"], E["cnt"], r, w)
        self.ninst += 1
        return inst

    def dma(self, q, out, in_, r=(), w=(), **kw):
        E = self.eng[q]
        if q == "pool":
            sem = self.psem[self.pnext]
            self.pnext += 1
            self._deps(E, r, w)
            inst = E["e"].dma_start(out=out, in_=in_, **kw)
            inst.then_inc(sem, 16)
            self._mark(sem, 16, r, w)
            self.ninst += 1
            return
        i = self.dnext % self.ndma
        self.dnext += 1
        sem = self.dsem[i]
        self._wait(E, sem, self.dval[i])
        self._deps(E, r, w)
        inst = E["e"].dma_start(out=out, in_=in_, **kw)
        self.dval[i] += 16
        inst.then_inc(sem, 16)
        self._mark(sem, self.dval[i], r, w)
        self.ninst += 1

    def barrier(self):
        for E in self.eng.values():
            for Fn in self.eng.values():
                if Fn["cnt"] > 0:
                    self._wait(E, Fn["sem"], Fn["cnt"])
            for i in range(self.ndma):
                if self.dval[i] > 0:
                    self._wait(E, self.dsem[i], self.dval[i])
            for i in range(self.pnext):
                self._wait(E, self.psem[i], 16)

    @contextlib.contextmanager
    def scope(self):
        st = contextlib.ExitStack()
        yield st
        self.barrier()
        st.close()


def bc(ap, shape):
    return ap.to_broadcast(list(shape))


class StopBuild(Exception):
    pass


def build(n_layers=DEPTH, stop=None):
    layers = list(range(n_layers)) if isinstance(n_layers, int) else list(n_layers)
    n_layers = layers[-1] + 1
    first_layer = [True]
    kb = KB()
    nc = kb.nc

    def stage(name):
        if stop == name:
            raise StopBuild()

    def din(name, shape, dt=F32):
        return kb.dram(name, shape, dt, kind="ExternalInput")

    hx = din("hx", [T, D])
    condT = din("condT", [128, 8, 2])
    ada_w = din("ada_w", [DEPTH, D, 3 * D])
    ada_bT = din("ada_bT", [DEPTH, 128, 24])
    ada_bg = din("ada_bg", [DEPTH, 128, D])
    pre_gT = din("pre_gT", [DEPTH, 128, 8])
    post_gb = din("post_gb", [DEPTH, 128, D])
    s5_in_w = din("s5_in_w", [2, D, 2 * D])
    s5_glu_w = din("s5_glu_w", [2, D, 2 * D])
    s5_out_w = din("s5_out_w", [2, D, D])
    s5_lamT = din("s5_lamT", [2, 2, 2, 64, 64])
    s5_ldt = din("s5_ldt", [2, 2, 64, 64])
    s5_bT = din("s5_bT", [2, 2, 2, 64, 64 * 16])
    s5_cT = din("s5_cT", [2, 2, 2, 64, 64 * 16])
    s5_dT = din("s5_dT", [2, 128, 8])
    s5_gbT = din("s5_gbT", [2, 128, 16])
    gdn_in_w = din("gdn_in_w", [1, D, 4 * D + 32])
    gdn_out_w = din("gdn_out_w", [1, D, D])
    gdn_prm = din("gdn_prm", [1, 16, 2])
    gdn_cwT = din("gdn_cwT", [1, 128, 24, 5])
    gdn_ngb = din("gdn_ngb", [1, 128, 128])
    c_gmask = din("c_gmask", [128, 4, 128])
    c_bmask = din("c_bmask", [128, 3, 128])
    na_in_w = din("na_in_w", [1, D, 4 * D])
    na_out_w = din("na_out_w", [1, D, D])
    na_biasT = din("na_biasT", [1, 128, 8, 15, 64])
    c_namask = din("c_namask", [128, 64])
    c_ident = din("c_ident", [128, 128])
    c_bd = din("c_bd", [128, 128])
    c_par = din("c_par", [128, 4])
    c_colpar = din("c_colpar", [64, 4, 128])
    out = kb.dram("out", [SEQ, D], F32, kind="ExternalOutput")
    h_d = kb.dram("h_d", [T, D], F32)
    sz_d = kb.dram("sz_d", [8, 128, T], F32)
    u_d = kb.dram("u_d", [8, 128, T], BF16)
    u_tok = Tok()
    gl_d = kb.dram("gl_d", [8, 128, T], BF16)
    dx_d = kb.dram("dx_d", [2, 64, 2, 64, NK], F32)
    xin_d = kb.dram("xin_d", [2, 64, 2, 64, NK], BF16)
    kb_d = kb.dram("kb_d", [2, 128, 8, L, 128], BF16)
    q_d = kb.dram("q_d", [8, 128, T], BF16)
    k_d = kb.dram("k_d", [8, 128, T], BF16)
    v_d = kb.dram("v_d", [T, D], BF16)
    y2_d = kb.dram("y2_d", [8, 128, T], BF16)
    q_tok, k_tok, v_tok, y2_tok = Tok(), Tok(), Tok(), Tok()
    qkvT_d = kb.dram("qkvT_d", [24, 128, T], BF16)
    gb_d = kb.dram("gb_d", [T, 32], F32)
    o2_d = kb.dram("o2_d", [2, T, D], F32)
    o2_tok = [[Tok() for _ in range(NT)] for _ in range(2)]
    qkv_tok, gb_tok = Tok(), Tok()
    o_tok = [Tok() for _ in range(NT)]
    hsnap_d = kb.dram("hsnap_d", [3, T, D], F32) if "hsnap_d" in DBG else None
    snap_tok = Tok()
    h_tok = [Tok() for _ in range(NT)]
    sz_tok = Tok()
    gl_tok = Tok()
    dx_tok = [Tok(), Tok()]
    xin_tok = [Tok(), Tok()]
    kbd_tok = Tok()

    ident_f = kb.sb([128, 128], F32, "identf")
    ident_b = kb.sb([128, 128], BF16, "identb")
    bdmask = kb.sb([128, 128], F32, "bdmask")
    parmask = kb.sb([128, 4], F32, "parmask")
    colpar = kb.sb([64, 4, 128], F32, "colpar")
    halfpi = kb.sb([128, 1], F32, "halfpi")
    scT = kb.sb([128, 8, 2], F32, "scT")
    ctok = Tok()
    kb.dma("sp", ident_f[:], c_ident[:, :], w=[ctok])
    kb.dma("pool", ident_b[:], c_ident[:, :], w=[ctok])
    kb.dma("sp", bdmask[:], c_bd[:, :], w=[ctok])
    kb.dma("sp", parmask[:], c_par[:, :], w=[ctok])
    kb.dma("sp", colpar[:], c_colpar[:, :, :], w=[ctok])
    kb.op("dve", lambda e: e.memset(halfpi[:], math.pi / 2), w=[ctok])
    kb.dma("sp", scT[:], condT[:, :, :], w=[ctok])
    kb.op("act", lambda e: e.activation(out=scT[:], in_=scT[:], func=AF.Silu), r=[ctok], w=[ctok])
    kb.barrier()

    modT = kb.sb([128, 24, 2], F32, "modT")
    S1 = kb.sb([128, 8, 2], F32, "S1")
    GPG = kb.sb([128, 2, D], F32, "GPG")
    mod_tok = Tok()

    def adaln(i, st):
        wbuf = kb.sb([128, 8, 512], F32, "adaw", st)
        scRep = kb.sb([128, 8, 2, 128], F32, "scRep", st)
        kb.op("dve", lambda e: e.tensor_copy(out=scRep[:], in_=bc(scT[:].unsqueeze(3), [128, 8, 2, 128])), r=[ctok], w=[ctok])
        abT = kb.sb([128, 24], F32, "abT", st)
        abg = kb.sb([128, D], F32, "abg", st)
        pgb = kb.sb([128, D], F32, "pgb", st)
        pgT = kb.sb([128, 8], F32, "pgT", st)
        wt = Tok()
        ct = Tok()
        kb.dma("sp", abT[:], ada_bT[i], w=[ct])
        kb.dma("sp", abg[:], ada_bg[i], w=[ct])
        kb.dma("sp", pgb[:], post_gb[i], w=[ct])
        kb.dma("sp", pgT[:], pre_gT[i], w=[ct])
        wv = ada_w[i].rearrange("(kc p) n -> p kc n", p=128)
        for grp in range(6):
            kb.dma("sp", wbuf[:], wv[:, :, grp * 512:(grp + 1) * 512], w=[wt])
            for mm in range(4):
                m = grp * 4 + mm
                ps, pt = kb.bank()
                for kc in range(8):
                    kb.op("pe", lambda e, kc=kc, mm=mm, ps=ps: e.matmul(ps[:, 0:2], lhsT=wbuf[:, kc, mm * 128:(mm + 1) * 128],
                                                                          rhs=scT[:, kc, :], start=(kc == 0), stop=(kc == 7)),
                          r=[wt, ctok], w=[pt])
                kb.op("dve", lambda e, m=m, ps=ps: e.tensor_scalar(out=modT[:, m, :], in0=ps[:, 0:2], scalar1=abT[:, m:m + 1],
                                                                     scalar2=None, op0=ALU.add), r=[pt, ct], w=[mod_tok])
            if grp >= 4:
                for rr in range(2):
                    ps, pt = kb.bank()
                    for kc in range(8):
                        kb.op("pe", lambda e, kc=kc, rr=rr, ps=ps: e.matmul(ps[:, :], lhsT=scRep[:, kc, rr, :], rhs=wbuf[:, kc, :],
                                                                              start=(kc == 0), stop=(kc == 7)), r=[wt, ctok], w=[pt])
                    sl = slice((grp - 4) * 512, (grp - 3) * 512)
                    kb.op("dve", lambda e, rr=rr, ps=ps, sl=sl: e.tensor_tensor(out=GPG[:, rr, sl], in0=ps[:, :], in1=abg[:, sl], op=ALU.add),
                          r=[pt, ct], w=[mod_tok])
                    kb.op("dve", lambda e, rr=rr, sl=sl: e.tensor_tensor(out=GPG[:, rr, sl], in0=GPG[:, rr, sl], in1=pgb[:, sl], op=ALU.mult),
                          r=[ct, mod_tok], w=[mod_tok])
        kb.op("dve", lambda e: e.scalar_tensor_tensor(out=S1[:], in0=modT[:, 8:16, :], scalar=1.0,
                                                      in1=bc(pgT[:].unsqueeze(2), [128, 8, 2]), op0=ALU.add, op1=ALU.mult),
              r=[mod_tok, ct], w=[mod_tok])

    def prenorm(i, aT, a_toks, st):
        src = hx if first_layer[0] else h_d
        hts = [kb.sb([128, D], F32, "ht", st) for _ in range(2)]
        htk = [Tok(), Tok()]
        junk = kb.sb([128, D], BF16, "junk", st)
        ybs = [kb.sb([128, D], F32, "yb", st) for _ in range(2)]
        tmf = [kb.sb([128, 4, 128], F32, "tmf", st) for _ in range(2)]
        tmk = [Tok(), Tok()]
        ybk = [Tok(), Tok()]
        stat = kb.sb([128, 4], F32, "stat", st)
        jt = Tok()
        stt = Tok()
        for ti in range(NT):
            rr = 1 if ti < 2 else 0
            ht, hk = hts[ti % 2], htk[ti % 2]
            yb, yk = ybs[ti % 2], ybk[ti % 2]
            kb.dma("sp", ht[:], src[ti * 128:(ti + 1) * 128, :], r=[h_tok[ti]], w=[hk])
            kb.op("act", lambda e, ht=ht: e.activation(out=junk[:], in_=ht[:], func=AF.Square, accum_out=stat[:, 0:1]), r=[hk], w=[jt, stt])
            kb.op("dve", lambda e: e.tensor_scalar(out=stat[:, 1:2], in0=stat[:, 0:1], scalar1=1.0 / D, scalar2=EPS, op0=ALU.mult, op1=ALU.add),
                  r=[stt], w=[stt])
            kb.op("act", lambda e: e.activation(out=stat[:, 2:3], in_=stat[:, 1:2], func=AF.Sqrt), r=[stt], w=[stt])
            kb.op("dve", lambda e: e.reciprocal(out=stat[:, 3:4], in_=stat[:, 2:3]), r=[stt], w=[stt])
            kb.op("act", lambda e, ht=ht, yb=yb: e.activation(out=yb[:], in_=ht[:], func=AF.Copy, scale=stat[:, 3:4]), r=[hk, stt], w=[yk])
            for hf in range(2):
                ps, pt = kb.bank()
                for c4 in range(4):
                    c = hf * 4 + c4
                    kb.op("pe", lambda e, c=c, c4=c4, yb=yb, ps=ps: e.transpose(out=ps[:, c4 * 128:(c4 + 1) * 128], in_=yb[:, c * 128:(c + 1) * 128],
                                                                             identity=ident_f[:]), r=[yk, ctok], w=[pt])
                for c4 in range(4):
                    c = hf * 4 + c4
                    kb.op("act", lambda e, c=c, c4=c4, ps=ps, rr=rr: e.activation(out=aT[:, c, ti * 128:(ti + 1) * 128], in_=ps[:, c4 * 128:(c4 + 1) * 128],
                                                                               func=AF.Identity, scale=S1[:, c, rr:rr + 1], bias=modT[:, c, rr:rr + 1]),
                          r=[pt, mod_tok], w=[a_toks[ti]])

    def proj_bufs(st):
        return ([kb.sb([128, 8, 512], BF16, "wproj", st) for _ in range(2)], [Tok(), Tok()])

    def proj_fm(w_ap, ncols, aT, a_toks, consume, st, wb_=None, gsz=512):
        wv = w_ap.rearrange("(kc p) n -> p kc n", p=128)
        wbs, wks = wb_ if wb_ is not None else proj_bufs(st)
        ng = (ncols + gsz - 1) // gsz
        for g in range(ng):
            wb, wk = wbs[g % 2], wks[g % 2]
            c0 = g * gsz
            cn = min(gsz, ncols - c0)
            kb.dma("pool", wb[:, :, 0:cn], wv[:, :, c0:c0 + cn], w=[wk])
            for tb, (t0, n) in enumerate(TBLK):
                trs = a_toks[t0 // 128:(t0 + n) // 128]
                for mm in range((cn + 127) // 128):
                    mw = min(128, cn - mm * 128)
                    ps, pt = kb.bank()
                    for kc in range(8):
                        kb.op("pe", lambda e, kc=kc, mm=mm, mw=mw, ps=ps, wb=wb, t0=t0, n=n: e.matmul(
                            ps[0:mw, 0:n], lhsT=wb[:, kc, mm * 128:mm * 128 + mw], rhs=aT[:, kc, t0:t0 + n], start=(kc == 0), stop=(kc == 7)),
                            r=[wk] + trs, w=[pt])
                    consume(g * (gsz // 128) + mm, tb, t0, n, ps, pt)

    def outproj_block(i, y2T, y2k, t0, n, wout, wok, st_bufs, last):
        src = hx if first_layer[0] else h_d
        hold, holdk, tmp, tmpk, stat, stt, junk, jt = st_bufs
        for tt in range(n // 128):
            ti = (t0 // 128) + tt
            rr = 1 if ti < 2 else 0
            if last and rr == 1:
                continue
            p0, k0 = kb.bank()
            p1, k1 = kb.bank()
            for nh, (ps, pk) in enumerate(((p0, k0), (p1, k1))):
                for kc in range(8):
                    kb.op("pe", lambda e, kc=kc, ps=ps, nh=nh, tt=tt: e.matmul(ps[:, :], lhsT=y2T[:, kc, tt * 128:(tt + 1) * 128],
                                                                                 rhs=wout[:, kc, nh * 512:(nh + 1) * 512],
                                                                                 start=(kc == 0), stop=(kc == 7)), r=[y2k, wok], w=[pk])
            b = ti % 2
            if OUT_CUT < 2:
                continue
            kb.dma("sp", hold[b][:], src[ti * 128:(ti + 1) * 128, :], r=[h_tok[ti]], w=[holdk[b]])
            if OUT_CUT < 3:
                continue
            kb.op("act", lambda e: e.activation(out=junk[:, 0:512], in_=p0[:, :], func=AF.Square, accum_out=stat[:, 0:1]), r=[k0], w=[jt, stt])
            kb.op("act", lambda e: e.activation(out=junk[:, 512:1024], in_=p1[:, :], func=AF.Square, accum_out=stat[:, 1:2]), r=[k1], w=[jt, stt])
            if OUT_CUT < 4:
                continue
            kb.op("dve", lambda e: e.tensor_tensor(out=stat[:, 2:3], in0=stat[:, 0:1], in1=stat[:, 1:2], op=ALU.add), r=[stt], w=[stt])
            kb.op("dve", lambda e: e.tensor_scalar(out=stat[:, 3:4], in0=stat[:, 2:3], scalar1=1.0 / D, scalar2=EPS, op0=ALU.mult, op1=ALU.add),
                  r=[stt], w=[stt])
            kb.op("act", lambda e: e.activation(out=stat[:, 4:5], in_=stat[:, 3:4], func=AF.Sqrt), r=[stt], w=[stt])
            kb.op("dve", lambda e: e.reciprocal(out=stat[:, 5:6], in_=stat[:, 4:5]), r=[stt], w=[stt])
            if OUT_CUT < 5:
                continue
            for nh, (ps, pk) in enumerate(((p0, k0), (p1, k1))):
                sl = slice(nh * 512, (nh + 1) * 512)
                kb.op("dve", lambda e, ps=ps, sl=sl, b=b, rr=rr: e.scalar_tensor_tensor(out=tmp[b][:, sl], in0=ps[:, :], scalar=stat[:, 5:6],
                                                                                         in1=GPG[:, rr, sl], op0=ALU.mult, op1=ALU.mult),
                      r=[pk, stt, mod_tok], w=[tmpk[b]])
            if OUT_CUT < 6:
                continue
            kb.op("dve", lambda e, b=b: e.tensor_tensor(out=tmp[b][:], in0=tmp[b][:], in1=hold[b][:], op=ALU.add), r=[holdk[b]], w=[tmpk[b]])
            if OUT_CUT < 7:
                continue
            if last:
                kb.dma("sp", out[(ti - 2) * 128:(ti - 1) * 128, :], tmp[b][:], r=[tmpk[b]], w=[h_tok[ti]])
            else:
                kb.dma("sp", h_d[ti * 128:(ti + 1) * 128, :], tmp[b][:], r=[tmpk[b]], w=[h_tok[ti]])

    def outproj_bufs(st):
        hold = [kb.sb([128, D], F32, "hold", st) for _ in range(2)]
        tmp = [kb.sb([128, D], F32, "otmp", st) for _ in range(2)]
        return (hold, [Tok(), Tok()], tmp, [Tok(), Tok()], kb.sb([128, 8], F32, "ostat", st), Tok(), kb.sb([128, D], BF16, "ojunk", st), Tok())

    def s5_layer(i, j, last):
        with kb.scope() as st:
            adaln(i, st)
        if not ONLY_GLU:
            stage("adaln")
            with kb.scope() as st:
                aT = kb.sb([128, 8, T], BF16, "aT", st)
                a_toks = [Tok() for _ in range(NT)]
                prenorm(i, aT, a_toks, st)
                stage("prenorm")
                szs = [kb.sb([128, 512], BF16, "szs", st) for _ in range(2)]
                szf = [kb.sb([128, 512], F32, "szf", st) for _ in range(2)]
                szk = [Tok(), Tok()]
                cnt = [0]

                def consume(m, tb, t0, n, ps, pt):
                    b = cnt[0] % 2
                    cnt[0] += 1
                    if m < 8:
                        kb.op("dve", lambda e: e.tensor_copy(out=szs[b][:, 0:n], in_=ps[:, 0:n]), r=[pt], w=[szk[b]])
                        kb.dma("sp", u_d[m, :, t0:t0 + n], szs[b][:, 0:n], r=[szk[b]], w=[u_tok])
                    else:
                        kb.op("act", lambda e: e.activation(out=szf[b][:, 0:n], in_=ps[:, 0:n], func=AF.Silu), r=[pt], w=[szk[b]])
                        kb.dma("sp", sz_d[m - 8, :, t0:t0 + n], szf[b][:, 0:n], r=[szk[b]], w=[sz_tok])

                proj_fm(s5_in_w[j], 2 * D, aT, a_toks, consume, st)

            stage("proj")
            s5_scan(j)
            stage("scan")
        with kb.scope() as st:
            gw = kb.sb([128, 8, 2 * D], BF16, "gw", st)
            wout = kb.sb([128, 8, D], BF16, "wout", st)
            gbT = kb.sb([128, 16], F32, "gbT", st)
            wk = Tok()
            kb.dma("pool", gw[:], s5_glu_w[j].rearrange("(kc p) n -> p kc n", p=128), w=[wk])
            kb.dma("pool", wout[:], s5_out_w[j].rearrange("(kc p) n -> p kc n", p=128), w=[wk])
            kb.dma("sp", gbT[:], s5_gbT[j], w=[wk])
            stage("gluw")
            glb = [kb.sb([128, 8, 512], BF16, "glb", st) for _ in range(2)]
            szb = [kb.sb([128, 8, 512], F32, "szb", st) for _ in range(2)]
            y2b = [kb.sb([128, 8, 512], BF16, "y2b", st) for _ in range(2)]
            glk, szk2, y2k = [Tok(), Tok()], [Tok(), Tok()], [Tok(), Tok()]
            sg = [kb.sb([128, 512], F32, "sg", st) for _ in range(2)]
            sgk = [Tok(), Tok()]
            tt_ = [kb.sb([128, 512], F32, "gt", st) for _ in range(2)]
            ttk = [Tok(), Tok()]
            ob = outproj_bufs(st)
            for tb, (t0, n) in enumerate(TBLK):
                b = tb % 2
                kb.dma("sp", glb[b][:, :, 0:n], gl_d.rearrange("q p t -> p q t")[:, :, t0:t0 + n], r=[gl_tok], w=[glk[b]])
                kb.dma("sp", szb[b][:, :, 0:n], sz_d.rearrange("q p t -> p q t")[:, :, t0:t0 + n], r=[sz_tok], w=[szk2[b]])
                for m in range(8):
                    pa, ka = kb.bank()
                    pb, kbk = kb.bank()
                    for (ps, pk, off) in ((pa, ka, 0), (pb, kbk, D)):
                        for kc in range(8):
                            kb.op("pe", lambda e, ps=ps, kc=kc, off=off, m=m, b=b, n=n: e.matmul(
                                ps[:, 0:n], lhsT=gw[:, kc, off + m * 128:off + (m + 1) * 128], rhs=glb[b][:, kc, 0:n],
                                start=(kc == 0), stop=(kc == 7)), r=[wk, glk[b]], w=[pk])
                    s = m % 2
                    kb.op("act", lambda e, s=s, pb=pb, m=m, n=n: e.activation(out=sg[s][:, 0:n], in_=pb[:, 0:n], func=AF.Sigmoid,
                                                                              bias=gbT[:, 8 + m:9 + m]), r=[kbk, wk], w=[sgk[s]])
                    kb.op("dve", lambda e, s=s, pa=pa, m=m, n=n: e.scalar_tensor_tensor(out=tt_[s][:, 0:n], in0=pa[:, 0:n], scalar=gbT[:, m:m + 1],
                                                                                       in1=sg[s][:, 0:n], op0=ALU.add, op1=ALU.mult),
                          r=[ka, sgk[s], wk], w=[ttk[s]])
                    kb.op("pool", lambda e, s=s, m=m, b=b, n=n: e.tensor_tensor(out=y2b[b][:, m, 0:n], in0=tt_[s][:, 0:n], in1=szb[b][:, m, 0:n],
                                                                                 op=ALU.mult), r=[ttk[s], szk2[b]], w=[y2k[b]])
                if tb == 0:
                    stage("glu0")
                outproj_block(i, y2b[b], y2k[b], t0, n, wout, wk, ob, last)
                if tb == 0:
                    stage("out0")

    def s5_scan(j):
        NH = NK // 2
        with kb.scope() as st1:
            PR = [kb.sb([64, L + 1, 64], F32, "PR", st1) for _ in range(2)]
            PI = [kb.sb([64, L + 1, 64], F32, "PI", st1) for _ in range(2)]
            ptk = [Tok(), Tok()]
            for d in range(2):
                with kb.scope() as st:
                    g = Tok()

                    def t64(name):
                        return kb.sb([64, 64], F32, name, st)
                    lr, li, dt, mag, th, c, s, cc, ss, cs = [t64(x) for x in "lr li dt mag th c s cc ss cs".split()]
                    ar, ai, den, fr, fi, t1, t2, am1 = [t64(x) for x in "ar ai den fr fi t1 t2 am1".split()]
                    kb.dma("sp", lr[:], s5_lamT[j, d, 0], w=[g])
                    kb.dma("sp", li[:], s5_lamT[j, d, 1], w=[g])
                    kb.dma("sp", dt[:], s5_ldt[j, d], w=[g])
                    V = lambda fn: kb.op("dve", fn, r=[g], w=[g])
                    A = lambda fn: kb.op("act", fn, r=[g], w=[g])
                    TT = lambda o, a, b_, op: V(lambda e: e.tensor_tensor(out=o, in0=a, in1=b_, op=op))
                    A(lambda e: e.activation(out=dt[:], in_=dt[:], func=AF.Exp))
                    TT(mag[:], lr[:], dt[:], ALU.mult)
                    A(lambda e: e.activation(out=mag[:], in_=mag[:], func=AF.Exp))
                    TT(th[:], li[:], dt[:], ALU.mult)
                    A(lambda e: e.activation(out=s[:], in_=th[:], func=AF.Sin, scale=0.125))
                    A(lambda e: e.activation(out=c[:], in_=th[:], func=AF.Sin, scale=-0.125, bias=halfpi[0:64, 0:1]))
                    for _ in range(3):
                        TT(cc[:], c[:], c[:], ALU.mult)
                        TT(ss[:], s[:], s[:], ALU.mult)
                        TT(cs[:], c[:], s[:], ALU.mult)
                        TT(c[:], cc[:], ss[:], ALU.subtract)
                        V(lambda e: e.tensor_scalar(out=s[:], in0=cs[:], scalar1=2.0, scalar2=None, op0=ALU.mult))
                    TT(ar[:], mag[:], c[:], ALU.mult)
                    TT(ai[:], mag[:], s[:], ALU.mult)
                    TT(t1[:], lr[:], lr[:], ALU.mult)
                    TT(t2[:], li[:], li[:], ALU.mult)
                    TT(den[:], t1[:], t2[:], ALU.add)
                    V(lambda e: e.reciprocal(out=den[:], in_=den[:]))
                    V(lambda e: e.tensor_scalar(out=am1[:], in0=ar[:], scalar1=-1.0, scalar2=None, op0=ALU.add))
                    TT(t1[:], am1[:], lr[:], ALU.mult)
                    TT(t2[:], ai[:], li[:], ALU.mult)
                    TT(t1[:], t1[:], t2[:], ALU.add)
                    TT(fr[:], t1[:], den[:], ALU.mult)
                    TT(t1[:], ai[:], lr[:], ALU.mult)
                    TT(t2[:], am1[:], li[:], ALU.mult)
                    TT(t1[:], t1[:], t2[:], ALU.subtract)
                    TT(fi[:], t1[:], den[:], ALU.mult)

                    def cmul(orr, oi, xr, xi, yr, yi):
                        TT(t1[:], xr, yr, ALU.mult)
                        TT(t2[:], xi, yi, ALU.mult)
                        TT(orr, t1[:], t2[:], ALU.subtract)
                        TT(t1[:], xr, yi, ALU.mult)
                        TT(t2[:], xi, yr, ALU.mult)
                        TT(oi, t1[:], t2[:], ALU.add)
                    pr, pi = PR[d], PI[d]
                    V(lambda e: e.memset(pr[:, 0, :], 1.0))
                    V(lambda e: e.memset(pi[:, 0, :], 0.0))
                    for n_ in range(1, L + 1):
                        cmul(pr[:, n_, :], pi[:, n_, :], pr[:, n_ - 1, :], pi[:, n_ - 1, :], ar[:], ai[:])
                    PFR = kb.sb([64, L, 64], F32, "PFR", st)
                    PFI = kb.sb([64, L, 64], F32, "PFI", st)
                    for n_ in range(L):
                        cmul(PFR[:, n_, :], PFI[:, n_, :], pr[:, n_, :], pi[:, n_, :], fr[:], fi[:])
                    WTr = kb.sb([64, L, 64, 16], BF16, "WTr", st)
                    WTi = kb.sb([64, L, 64, 16], BF16, "WTi", st)
                    stA = contextlib.ExitStack()
                    br = kb.sb([64, 64, 16], F32, "br", stA)
                    bi = kb.sb([64, 64, 16], F32, "bi", stA)
                    kb.dma("sp", br[:], s5_bT[j, d, 0].rearrange("p (g c) -> p g c", c=16), w=[g])
                    kb.dma("sp", bi[:], s5_bT[j, d, 1].rearrange("p (g c) -> p g c", c=16), w=[g])
                    w1 = kb.sb([64, 64, 16], F32, "w1", stA)
                    w2 = kb.sb([64, 64, 16], F32, "w2", stA)
                    for jp in range(L):
                        n_ = (L - 1 - jp) if d == 0 else jp
                        pfr = bc(PFR[:, n_, :].unsqueeze(2), [64, 64, 16])
                        pfi = bc(PFI[:, n_, :].unsqueeze(2), [64, 64, 16])
                        TT(w1[:], br[:], pfr, ALU.mult)
                        TT(w2[:], bi[:], pfi, ALU.mult)
                        TT(WTr[:, jp], w1[:], w2[:], ALU.subtract)
                        TT(w1[:], bi[:], pfr, ALU.mult)
                        TT(w2[:], br[:], pfi, ALU.mult)
                        TT(WTi[:, jp], w1[:], w2[:], ALU.add)
                    stage("gen%d" % d)
                    kb.barrier()
                    stA.close()
                    stB = contextlib.ExitStack()
                    crb = kb.sb([64, 64 * 16], BF16, "crb", stB)
                    cib = kb.sb([64, 64 * 16], BF16, "cib", stB)
                    kb.dma("pool", crb[:], s5_cT[j, d, 0], w=[g])
                    kb.dma("pool", cib[:], s5_cT[j, d, 1], w=[g])
                    A(lambda e: e.mul(out=cib[:], in_=cib[:], mul=-1.0))
                    KBs = kb.sb([128, 8, L, 128], BF16, "KBs", stB)
                    dT = kb.sb([128, 8], F32, "dT", stB)
                    kb.dma("sp", dT[:], s5_dT[j], w=[g])
                    kt = Tok()
                    for q in range(8):
                        for tau in range(L):
                            jp = (L - 1 - tau) if d == 0 else tau
                            ps, pt = kb.bank()
                            kb.op("pe", lambda e, ps=ps, jp=jp, q=q: e.matmul(ps[:, 0:128], lhsT=WTr[:, jp, q * 8:(q + 1) * 8, :].rearrange("p g c -> p (g c)"), rhs=crb[:, q * 128:(q + 1) * 128],
                                                                               start=True, stop=False), r=[g], w=[pt])
                            kb.op("pe", lambda e, ps=ps, jp=jp, q=q: e.matmul(ps[:, 0:128], lhsT=WTi[:, jp, q * 8:(q + 1) * 8, :].rearrange("p g c -> p (g c)"), rhs=cib[:, q * 128:(q + 1) * 128],
                                                                               start=False, stop=True), r=[g], w=[pt])
                            kb.op("dve", lambda e, ps=ps, q=q, tau=tau: e.tensor_tensor(out=KBs[:, q, tau, :], in0=ps[:, 0:128], in1=bdmask[:], op=ALU.mult),
                                  r=[pt, ctok], w=[kt])
                        if d == 0:
                            kb.op("dve", lambda e, q=q: e.scalar_tensor_tensor(out=KBs[:, q, 0, :], in0=ident_f[:], scalar=dT[:, q:q + 1], in1=KBs[:, q, 0, :],
                                                                               op0=ALU.mult, op1=ALU.add), r=[g, ctok, kt], w=[kt])
                    kb.dma("sp", kb_d[d], KBs[:], r=[kt], w=[kbd_tok])
                    kb.barrier()
                    stB.close()
                    stage("kblk%d" % d)
                    WPs = [kb.sb([128, 4, 2, L, 64], BF16, "WP", st) for _ in range(2)]
                    WPk = [Tok(), Tok()]
                    dXs = [kb.sb([64, 2, 8, NK], F32, "dXs", st) for _ in range(2)]
                    dXk = [Tok(), Tok()]
                    ev = 0
                    uqs = [kb.sb([128, T], BF16, "uq", st) for _ in range(2)]
                    uqk = [Tok(), Tok()]
                    for q in range(8):
                        WP, wpk = WPs[q % 2], WPk[q % 2]
                        dX, dxk = dXs[q % 2], dXk[q % 2]
                        uq, uk = uqs[q % 2], uqk[q % 2]
                        kb.dma("sp", uq[:], u_d[q], r=[u_tok], w=[uk])
                        uv = uq[:].rearrange("p (k j) -> p k j", j=L)
                        ps, pt = kb.bank()
                        psb = ps[:].bitcast(BF16)
                        for ri, WTx in enumerate((WTr, WTi)):
                            for jp in range(L):
                                o = (ri * L + jp) * 64
                                kb.op("pe", lambda e, psb=psb, o=o, WTx=WTx, jp=jp, q=q: e.transpose(out=psb[:, o:o + 64], in_=WTx[:, jp, q * 8:(q + 1) * 8, :].rearrange("p g c -> p (g c)"),
                                                                                                     identity=ident_b[0:64, 0:64]), r=[g, ctok], w=[pt])
                        for par in range(4):
                            kb.op("dve", lambda e, psb=psb, par=par, WP=WP: e.tensor_scalar(out=WP[:, par].rearrange("p a b c -> p (a b c)"), in0=psb[:, 0:2 * L * 64],
                                                                                             scalar1=parmask[:, par:par + 1], scalar2=None, op0=ALU.mult),
                                  r=[pt, ctok], w=[wpk])
                        for pp in range(4):
                            for par in range(2):
                                gl = 2 * pp + par
                                for ri in range(2):
                                    for hf in range(2):
                                        ps, pt = kb.bank()
                                        for jp in range(L):
                                            kb.op("pe", lambda e, ps=ps, WP=WP, par=par, ri=ri, jp=jp, pp=pp, q=q, hf=hf: e.matmul(
                                                ps[0:64, 0:NH], lhsT=(WP[32 * pp:32 * pp + 32, par, ri, jp, :] if pp < 3 else WP[64:128, 2 + par, ri, jp, :]),
                                                rhs=(uv[32 * pp:32 * pp + 32, hf * NH:(hf + 1) * NH, jp] if pp < 3 else uv[64:128, hf * NH:(hf + 1) * NH, jp]),
                                                start=(jp == 0), stop=(jp == L - 1)),
                                                r=[wpk, uk], w=[pt])
                                        dst = dX[:, ri, gl, hf * NH:(hf + 1) * NH]
                                        if ev % 2 == 0:
                                            kb.op("act", lambda e, ps=ps, dst=dst: e.copy(out=dst, in_=ps[0:64, 0:NH]), r=[pt], w=[dxk])
                                        else:
                                            kb.op("dve", lambda e, ps=ps, dst=dst: e.tensor_copy(out=dst, in_=ps[0:64, 0:NH]), r=[pt], w=[dxk])
                                        ev += 1
                        kb.dma("sp", dx_d[d, :, :, q * 8:(q + 1) * 8, :], dX[:], r=[dxk], w=[dx_tok[d]])
            stage("dx")
            SEG = 32
            ctx_k = CTX // L
            segs_f = [(0, ctx_k)] + [(k0, min(SEG, NK - k0)) for k0 in range(ctx_k, NK, SEG)]
            with kb.scope() as st:
                def rec(d):
                    E = "dve" if d == 0 else "pool"
                    X = kb.sb([64, 2, 64], F32, "X", st)
                    t1 = kb.sb([64, 2, 64], F32, "rt1", st)
                    t2 = kb.sb([64, 2, 64], F32, "rt2", st)
                    AR2 = kb.sb([64, 2, 64], F32, "AR2", st)
                    AIn = kb.sb([64, 64], F32, "AIn", st)
                    AIp = kb.sb([64, 64], F32, "AIp", st)
                    xk, tk1, tk2, ak = Tok(), Tok(), Tok(), Tok()
                    kb.op(E, lambda e, X=X: e.memset(X[:], 0.0), w=[xk])
                    for h_ in range(2):
                        kb.op(E, lambda e, h_=h_, AR2=AR2, d=d: e.tensor_copy(out=AR2[:, h_, :], in_=PR[d][:, L, :]), w=[ak])
                    kb.op(E, lambda e, AIp=AIp, d=d: e.tensor_copy(out=AIp[:], in_=PI[d][:, L, :]), w=[ak])
                    kb.op(E, lambda e, AIn=AIn, d=d: e.tensor_scalar(out=AIn[:], in0=PI[d][:, L, :], scalar1=-1.0, scalar2=None, op0=ALU.mult), w=[ak])
                    dsegs = [kb.sb([64, 2, 64, SEG], F32, "dseg", st) for _ in range(2)]
                    xsegs = [kb.sb([64, 2, 64, SEG], BF16, "xseg", st) for _ in range(2)]
                    dsk, xsk = [Tok(), Tok()], [Tok(), Tok()]
                    if d == 0:
                        order = [(k0, n, False) for (k0, n) in segs_f]
                    else:
                        csegs = [(k0, n) for (k0, n) in segs_f if k0 < ctx_k]
                        lsegs = [(k0, n) for (k0, n) in segs_f if k0 >= ctx_k]
                        order = [(k0, n, True) for (k0, n) in reversed(csegs)] + [(k0, n, True) for (k0, n) in reversed(lsegs)]
                    for si, (k0, n, rev) in enumerate(order):
                        b = si % 2
                        ds, xs = dsegs[b], xsegs[b]
                        kb.dma("sp", ds[:, :, :, 0:n], dx_d[d, :, :, :, k0:k0 + n], r=[dx_tok[d]], w=[dsk[b]])
                        ks = range(n - 1, -1, -1) if rev else range(n)
                        for kk in ks:
                            kb.op("act", lambda e, xs=xs, kk=kk, X=X: e.copy(out=xs[:, :, :, kk], in_=X[:]), r=[xk], w=[xsk[b]])
                            kb.op(E, lambda e, t1=t1, X=X, AR2=AR2: e.tensor_tensor(out=t1[:], in0=X[:], in1=AR2[:], op=ALU.mult), r=[xk, ak], w=[tk1])
                            kb.op(E, lambda e, t2=t2, X=X, AIn=AIn: e.tensor_tensor(out=t2[:, 0, :], in0=X[:, 1, :], in1=AIn[:], op=ALU.mult), r=[xk, ak], w=[tk2])
                            kb.op(E, lambda e, t2=t2, X=X, AIp=AIp: e.tensor_tensor(out=t2[:, 1, :], in0=X[:, 0, :], in1=AIp[:], op=ALU.mult), r=[xk, ak], w=[tk2])
                            kb.op(E, lambda e, t1=t1, t2=t2: e.tensor_tensor(out=t1[:], in0=t1[:], in1=t2[:], op=ALU.add), r=[tk2], w=[tk1])
                            kb.op(E, lambda e, t1=t1, X=X, ds=ds, kk=kk: e.tensor_tensor(out=X[:], in0=t1[:], in1=ds[:, :, :, kk], op=ALU.add),
                                  r=[tk1, dsk[b]], w=[xk])
                            yield
                        kb.dma("sp", xin_d[d, :, :, :, k0:k0 + n], xs[:, :, :, 0:n], r=[xsk[b]], w=[xin_tok[d]])
                alive = [rec(0), rec(1)]
                while alive:
                    for g_ in list(alive):
                        try:
                            next(g_)
                        except StopIteration:
                            alive.remove(g_)
            stage("rec")
            with kb.scope() as st:
                PRm = [kb.sb([64, L, 64], F32, "PRm", st) for _ in range(2)]
                PIm = [kb.sb([64, L, 64], F32, "PIm", st) for _ in range(2)]
                pmk = Tok()
                for d in range(2):
                    for r_ in range(L):
                        m_ = r_ + 1 if d == 0 else L - r_
                        kb.op("dve", lambda e, d=d, r_=r_, m_=m_: e.tensor_copy(out=PRm[d][:, r_, :], in_=PR[d][:, m_, :]), w=[pmk])
                        kb.op("dve", lambda e, d=d, r_=r_, m_=m_: e.tensor_copy(out=PIm[d][:, r_, :], in_=PI[d][:, m_, :]), w=[pmk])
                cr = [kb.sb([64, 64, 16], F32, "cr", st) for _ in range(2)]
                ci = [kb.sb([64, 64, 16], F32, "ci", st) for _ in range(2)]
                for d in range(2):
                    kb.dma("sp", cr[d][:], s5_cT[j, d, 0].rearrange("p (g c) -> p g c", c=16), w=[pmk])
                    kb.dma("sp", ci[d][:], s5_cT[j, d, 1].rearrange("p (g c) -> p g c", c=16), w=[pmk])
                m1 = kb.sb([64, L, 8, 16], F32, "m1", st)
                m2 = kb.sb([64, L, 8, 16], F32, "m2", st)
                mr = kb.sb([64, L, 8, 16], F32, "mr", st)
                mi = kb.sb([64, L, 8, 16], F32, "mi", st)
                mk = Tok()
                MX = [kb.sb([64, 2, 2, 4, L, 128], BF16, "MX", st) for _ in range(1)]
                mxk = [Tok(), Tok()]
                XQ = [kb.sb([64, 2, 2, 8, NH], BF16, "XQ", st) for _ in range(2)]
                xqk = [Tok(), Tok()]
                KQ = [kb.sb([128, 2, L, 128], BF16, "KQ", st) for _ in range(2)]
                kqk = [Tok(), Tok()]
                glq = [kb.sb([128, NK, L], BF16, "glq", st) for _ in range(2)]
                glk = [Tok(), Tok()]
                yx = [kb.sb([128, NH], F32, "yx", st) for _ in range(2)]
                y2 = [kb.sb([128, NH], F32, "yy", st) for _ in range(2)]
                ysg = [kb.sb([128, NH], F32, "ysg", st) for _ in range(2)]
                yk = [Tok(), Tok()]
                uqs = [kb.sb([128, T], BF16, "uq3", st) for _ in range(2)]
                uqk = [Tok(), Tok()]
                it = 0
                for q in range(8):
                    MXq, mxq = MX[0], mxk[0]
                    KQq, kqq = KQ[q % 2], kqk[q % 2]
                    gq, gqk = glq[q % 2], glk[q % 2]
                    for d in range(2):
                        kb.dma("sp", KQq[:, d], kb_d[d, :, q], r=[kbd_tok], w=[kqq])
                    uq, uk = uqs[q % 2], uqk[q % 2]
                    kb.dma("sp", uq[:], u_d[q], r=[u_tok], w=[uk])
                    uv = uq[:].rearrange("p (k j) -> p k j", j=L)
                    for d in range(2):
                        crq = bc(cr[d][:, q * 8:(q + 1) * 8, :].unsqueeze(1), [64, L, 8, 16])
                        ciq = bc(ci[d][:, q * 8:(q + 1) * 8, :].unsqueeze(1), [64, L, 8, 16])
                        prq = bc(PRm[d][:, :, q * 8:(q + 1) * 8].unsqueeze(3), [64, L, 8, 16])
                        piq = bc(PIm[d][:, :, q * 8:(q + 1) * 8].unsqueeze(3), [64, L, 8, 16])
                        V = lambda fn: kb.op("dve", fn, r=[pmk, mk], w=[mk])
                        V(lambda e: e.tensor_tensor(out=m1[:], in0=crq, in1=prq, op=ALU.mult))
                        V(lambda e: e.tensor_tensor(out=m2[:], in0=ciq, in1=piq, op=ALU.mult))
                        V(lambda e: e.tensor_tensor(out=mr[:], in0=m1[:], in1=m2[:], op=ALU.subtract))
                        V(lambda e: e.tensor_tensor(out=m1[:], in0=crq, in1=piq, op=ALU.mult))
                        V(lambda e: e.tensor_tensor(out=m2[:], in0=ciq, in1=prq, op=ALU.mult))
                        V(lambda e: e.scalar_tensor_tensor(out=mi[:], in0=m1[:], scalar=-1.0, in1=m2[:], op0=ALU.mult, op1=ALU.subtract))
                        for ri, src in enumerate((mr, mi)):
                            for par in range(4):
                                kb.op("dve", lambda e, d=d, ri=ri, par=par, src=src, MXq=MXq: e.tensor_tensor(
                                    out=MXq[:, d, ri, par], in0=src[:].rearrange("p r g c -> p r (g c)"),
                                    in1=bc(colpar[:, par:par + 1, :], [64, L, 128]), op=ALU.mult), r=[mk, ctok], w=[mxq])
                    for hf in range(2):
                        XQh, xqh = XQ[hf], xqk[hf]
                        for d in range(2):
                            kb.dma("sp", XQh[:, d], xin_d[d, :, :, q * 8:(q + 1) * 8, hf * NH:(hf + 1) * NH], r=xin_tok, w=[xqh])
                        for r_ in range(L):
                            ps, pt = kb.bank()
                            first = [True]

                            def MM(lhsT, rhs, outp, extra_r):
                                stt_ = first[0]
                                first[0] = False
                                kb.op("pe", lambda e: e.matmul(outp, lhsT=lhsT, rhs=rhs, start=stt_, stop=False, skip_group_check=True), r=extra_r, w=[pt])
                            for tau in range(0, r_ + 1):
                                MM(KQq[:, 0, tau, :], uv[:, hf * NH:(hf + 1) * NH, r_ - tau], ps[:, 0:NH], [kqq, uk])
                            for tau in range(0, L - r_):
                                MM(KQq[:, 1, tau, :], uv[:, hf * NH:(hf + 1) * NH, r_ + tau], ps[:, 0:NH], [kqq, uk])
                            for d in range(2):
                                for pp in range(4):
                                    for par in range(2):
                                        for ri in range(2):
                                            if pp < 3:
                                                MM(MXq[:, d, ri, par, r_, 32 * pp:32 * pp + 32], XQh[:, d, ri, 2 * pp + par, :], ps[32 * pp:32 * pp + 32, 0:NH], [mxq, xqh])
                                            else:
                                                MM(MXq[:, d, ri, 2 + par, r_, 64:128], XQh[:, d, ri, 2 * pp + par, :], ps[64:128, 0:NH], [mxq, xqh])
                            b = it % 2
                            it += 1
                            kb.op("act", lambda e, b=b, ps=ps: e.copy(out=yx[b][:], in_=ps[:, 0:NH]), r=[pt], w=[yk[b]])
                            kb.op("pool", lambda e, b=b: e.tensor_tensor(out=y2[b][:], in0=yx[b][:], in1=yx[b][:], op=ALU.mult), r=[yk[b]], w=[yk[b]])
                            kb.op("dve", lambda e, b=b: e.tensor_scalar(out=y2[b][:], in0=y2[b][:], scalar1=0.044715, scalar2=1.0, op0=ALU.mult, op1=ALU.add),
                                  r=[yk[b]], w=[yk[b]])
                            kb.op("pool", lambda e, b=b: e.tensor_tensor(out=y2[b][:], in0=y2[b][:], in1=yx[b][:], op=ALU.mult), r=[yk[b]], w=[yk[b]])
                            kb.op("act", lambda e, b=b: e.activation(out=ysg[b][:], in_=y2[b][:], func=AF.Sigmoid, scale=1.5957691216057308), r=[yk[b]], w=[yk[b]])
                            kb.op("dve", lambda e, b=b, gq=gq, hf=hf, r_=r_: e.tensor_tensor(out=gq[:, hf * NH:(hf + 1) * NH, r_], in0=yx[b][:], in1=ysg[b][:], op=ALU.mult),
                                  r=[yk[b]], w=[gqk])
                    kb.dma("sp", gl_d[q], gq[:].rearrange("p k j -> p (k j)"), r=[gqk], w=[gl_tok])


    def stash_consume(dst_list, st):
        szs = [kb.sb([128, 512], BF16, "stg", st) for _ in range(3)]
        szf = [kb.sb([128, 512], F32, "stgf", st) for _ in range(3)]
        szk = [Tok() for _ in range(3)]
        cnt = [0]

        def consume(m, tb, t0, n, ps, pt):
            dst, dtok, func = dst_list[m]
            b = cnt[0] % 3
            cnt[0] += 1
            if func is None:
                kb.op("dve", lambda e: e.tensor_copy(out=szs[b][0:ps_rows(m), 0:n], in_=ps[0:ps_rows(m), 0:n]), r=[pt], w=[szk[b]])
            else:
                kb.op("act", lambda e: e.activation(out=szf[b][0:ps_rows(m), 0:n], in_=ps[0:ps_rows(m), 0:n], func=func), r=[pt], w=[szk[b]])
                kb.dma("sp", dst[0:ps_rows(m), t0:t0 + n], szf[b][0:ps_rows(m), 0:n], r=[szk[b]], w=[dtok])
                return
            kb.dma("sp", dst[0:ps_rows(m), t0:t0 + n], szs[b][0:ps_rows(m), 0:n], r=[szk[b]], w=[dtok])

        def ps_rows(m):
            return dst_list[m][0].shape[0]
        return consume

    def final_stage(i, last, wout_ap, st):
        wout = kb.sb([128, 8, D], BF16, "wout", st)
        wk = Tok()
        kb.dma("pool", wout[:], wout_ap.rearrange("(kc p) n -> p kc n", p=128), w=[wk])
        y2b = [kb.sb([128, 8, 512], BF16, "y2b", st) for _ in range(2)]
        y2k = [Tok(), Tok()]
        ob = outproj_bufs(st)
        for tb, (t0, n) in enumerate(TBLK):
            b = tb % 2
            kb.dma("sp", y2b[b][:, :, 0:n], y2_d.rearrange("q p t -> p q t")[:, :, t0:t0 + n], r=[y2_tok], w=[y2k[b]])
            outproj_block(i, y2b[b], y2k[b], t0, n, wout, wk, ob, last)

    def na_layer(i, j, last):
        with kb.scope() as st:
            adaln(i, st)
        with kb.scope() as st:
            aT = kb.sb([128, 8, T], BF16, "aT", st)
            a_toks = [Tok() for _ in range(NT)]
            prenorm(i, aT, a_toks, st)
            dl = [(q_d[m], q_tok, None) for m in range(8)] + [(k_d[m], k_tok, None) for m in range(8)]
            dl += [None] * 8 + [(sz_d[m], sz_tok, AF.Silu) for m in range(8)]
            cons = stash_consume(dl, st)
            proj_fm(na_in_w[j][:, 0:2 * D], 2 * D, aT, a_toks, cons, st)
            proj_fm(na_in_w[j][:, 3 * D:4 * D], D, aT, a_toks, lambda m, *a: cons(m + 24, *a), st)
            wv = kb.sb([128, 8, D], BF16, "wv", st)
            wvk = Tok()
            kb.dma("pool", wv[:], na_in_w[j].rearrange("(kc p) n -> p kc n", p=128)[:, :, 2 * D:3 * D], w=[wvk])
            vst = [kb.sb([128, D], BF16, "vst", st) for _ in range(2)]
            vsk = [Tok(), Tok()]
            for ti in range(NT):
                b = ti % 2
                for nh in range(2):
                    ps, pt = kb.bank()
                    for kc in range(8):
                        kb.op("pe", lambda e, ps=ps, kc=kc, ti=ti, nh=nh: e.matmul(ps[:, :], lhsT=aT[:, kc, ti * 128:(ti + 1) * 128],
                                                                                     rhs=wv[:, kc, nh * 512:(nh + 1) * 512], start=(kc == 0), stop=(kc == 7)),
                              r=[wvk, a_toks[ti]], w=[pt])
                    if nh == 0:
                        kb.op("act", lambda e, ps=ps, b=b: e.copy(out=vst[b][:, 0:512], in_=ps[:, :]), r=[pt], w=[vsk[b]])
                    else:
                        kb.op("dve", lambda e, ps=ps, b=b: e.tensor_copy(out=vst[b][:, 512:1024], in_=ps[:, :]), r=[pt], w=[vsk[b]])
                kb.dma("sp", v_d[ti * 128:(ti + 1) * 128, :], vst[b][:], r=[vsk[b]], w=[v_tok])
        stage("na_proj")
        with kb.scope() as st:
            biasm = kb.sb([128, 8, 15, 64], BF16, "biasm", st)
            bstage = kb.sb([128, 15, 64], F32, "bstage", st)
            mask = kb.sb([128, 64], F32, "namask", st)
            bk = Tok()
            kb.dma("sp", mask[:], c_namask[:, :], w=[bk])
            for ch in range(8):
                kb.dma("sp", bstage[:], na_biasT[j, :, ch], w=[bk])
                kb.op("dve", lambda e, ch=ch: e.tensor_tensor(out=biasm[:, ch], in0=bstage[:], in1=bc(mask[:].unsqueeze(1), [128, 15, 64]), op=ALU.add),
                      r=[bk], w=[bk])
            qs = [kb.sb([128, T], BF16, "qs", st) for _ in range(2)]
            ks = [kb.sb([128, T], BF16, "ks", st) for _ in range(2)]
            szc = [kb.sb([128, T], F32, "szc", st) for _ in range(2)]
            vA = [kb.sb([128, 32, 128], BF16, "vA", st) for _ in range(2)]
            vB = [kb.sb([128, 32, 128], BF16, "vB", st) for _ in range(2)]
            vC = [kb.sb([128, 2, 128], BF16, "vC", st) for _ in range(2)]
            y2c = [kb.sb([128, T], BF16, "y2c", st) for _ in range(2)]
            lk = [Tok(), Tok()]
            y2k = [Tok(), Tok()]
            NB = 3
            sc = [kb.sb([128, 768], F32, "sc", st) for _ in range(NB)]
            pb = [kb.sb([128, 768], BF16, "pb", st) for _ in range(NB)]
            pT = [kb.sb([128, 768], BF16, "pT", st) for _ in range(NB)]
            sts = [kb.sb([128, 4], F32, "nst", st) for _ in range(NB)]
            sck = [Tok() for _ in range(NB)]
            pbk = [Tok() for _ in range(NB)]
            ptk = [Tok() for _ in range(NB)]
            stk = [Tok() for _ in range(NB)]
            vd3 = v_d.rearrange("t (c d) -> t c d", d=128)

            def softmax_pv(b, ps_list, width, vtiles, out_ap_rows, ncol, y2dst, szsrc, deps, y2tok):
                for (pap, ptok, c0, w_, bias) in ps_list:
                    if bias is not None:
                        kb.op("dve", lambda e, pap=pap, c0=c0, w_=w_, bias=bias: e.scalar_tensor_tensor(
                            out=sc[b][:, c0:c0 + w_], in0=pap, scalar=0.125, in1=bias, op0=ALU.mult, op1=ALU.add), r=[ptok, bk], w=[sck[b]])
                    else:
                        kb.op("act", lambda e, pap=pap, c0=c0, w_=w_: e.activation(out=sc[b][:, c0:c0 + w_], in_=pap, func=AF.Copy, scale=0.125),
                              r=[ptok], w=[sck[b]])
                yield
                kb.op("dve", lambda e: e.reduce_max(out=sts[b][:, 0:1], in_=sc[b][:, 0:width], axis=AX.X), r=[sck[b]], w=[stk[b]])
                kb.op("dve", lambda e: e.tensor_scalar(out=sts[b][:, 1:2], in0=sts[b][:, 0:1], scalar1=-1.0, scalar2=None, op0=ALU.mult), r=[stk[b]], w=[stk[b]])
                yield
                kb.op("act", lambda e: e.activation(out=pb[b][:, 0:width], in_=sc[b][:, 0:width], func=AF.Exp, bias=sts[b][:, 1:2], accum_out=sts[b][:, 2:3]),
                      r=[sck[b], stk[b]], w=[pbk[b], stk[b]])
                yield
                kb.op("dve", lambda e: e.reciprocal(out=sts[b][:, 3:4], in_=sts[b][:, 2:3]), r=[stk[b]], w=[stk[b]])
                kb.op("dve", lambda e: e.tensor_scalar(out=pb[b][:, 0:width], in0=pb[b][:, 0:width], scalar1=sts[b][:, 3:4], scalar2=None, op0=ALU.mult),
                      r=[stk[b]], w=[pbk[b]])
                yield
                ps, pt = kb.bank(4 * b + 2)
                psb = ps[:].bitcast(BF16)
                nkt = width // 128
                for kt in range(nkt):
                    kb.op("pe", lambda e, kt=kt: e.transpose(out=psb[:, kt * 128:(kt + 1) * 128], in_=pb[b][:, kt * 128:(kt + 1) * 128], identity=ident_b[:]),
                          r=[pbk[b], ctok], w=[pt])
                yield
                kb.op("act", lambda e: e.copy(out=pT[b][:, 0:width], in_=psb[:, 0:width]), r=[pt], w=[ptk[b]])
                yield
                ops, opt = kb.bank(4 * b + 3)
                for kt in range(nkt):
                    for (hb, c0q) in out_ap_rows:
                        kb.op("pe", lambda e, kt=kt, hb=hb, c0q=c0q: e.matmul(ops[hb:hb + 64, 0:ncol], lhsT=vtiles[kt][:, hb:hb + 64],
                                                                               rhs=pT[b][:, kt * 128 + c0q:kt * 128 + c0q + ncol],
                                                                               start=(kt == 0), stop=(kt == nkt - 1)), r=[ptk[b]] + deps, w=[opt])
                yield
                rows = slice(min(h_ for h_, _ in out_ap_rows), max(h_ for h_, _ in out_ap_rows) + 64)
                kb.op("dve", lambda e: e.tensor_tensor(out=y2dst[rows], in0=ops[rows, 0:ncol], in1=szsrc[rows], op=ALU.mult), r=[opt] + deps, w=[y2tok])

            def unit_ctx(b, b2, hp, qt):
                q_, k_, sz_, y2_ = qs[b2], ks[b2], szc[b2], y2c[b2]
                hb = 64 * hp
                ps, pt = kb.bank(4 * b)
                kb.op("pe", lambda e: e.matmul(ps[:, 0:256], lhsT=q_[hb:hb + 64, qt * 128:(qt + 1) * 128], rhs=k_[hb:hb + 64, 0:256],
                                               start=True, stop=True), r=[lk[b2]], w=[pt])
                yield
                t0 = qt * 128
                yield from softmax_pv(b, [(ps[:, 0:256], pt, 0, 256, None)], 256, [vC[b2][:, 0, :], vC[b2][:, 1, :]], [(hb, 0)], 128,
                                      y2_[:, t0:t0 + 128], sz_[:, t0:t0 + 128], [lk[b2]], y2k[b2])

            def unit_row(b, b2, ch, r_):
                q_, k_, sz_, y2_ = qs[b2], ks[b2], szc[b2], y2c[b2]
                r0 = min(max(r_ - 4, 0), 56)
                ro0 = r0 - r_ + 7
                tq = CTX + 64 * r_
                tk = CTX + 64 * r0
                pw, ptw = kb.bank(4 * b)
                pc, ptc = kb.bank(4 * b + 1)
                for hp in range(2):
                    hb = 64 * hp
                    kb.op("pe", lambda e, hb=hb: e.matmul(pw[hb:hb + 64, :], lhsT=q_[hb:hb + 64, tq:tq + 64], rhs=k_[hb:hb + 64, tk:tk + 512],
                                                          start=True, stop=True), r=[lk[b2]], w=[ptw])
                    kb.op("pe", lambda e, hb=hb: e.matmul(pc[hb:hb + 64, 0:256], lhsT=q_[hb:hb + 64, tq:tq + 64], rhs=k_[hb:hb + 64, 0:256],
                                                          start=True, stop=True), r=[lk[b2]], w=[ptc])
                yield
                bias = biasm[:, ch, ro0:ro0 + 8, :].rearrange("p a b -> p (a b)")
                if r0 % 2 == 0:
                    vt = [vA[b2][:, r0 // 2 + kt, :] for kt in range(4)]
                else:
                    vt = [vB[b2][:, (r0 - 1) // 2 + kt, :] for kt in range(4)]
                vt += [vC[b2][:, 0, :], vC[b2][:, 1, :]]
                yield from softmax_pv(b, [(pw[:, :], ptw, 0, 512, bias), (pc[:, 0:256], ptc, 512, 256, None)], 768, vt, [(0, 0), (64, 64)], 64,
                                      y2_[:, tq:tq + 64], sz_[:, tq:tq + 64], [lk[b2]], y2k[b2])

            for ch in range(8):
                b2 = ch % 2
                kb.dma("sp", qs[b2][:], q_d[ch], r=[q_tok], w=[lk[b2]])
                kb.dma("sp", ks[b2][:], k_d[ch], r=[k_tok], w=[lk[b2]])
                kb.dma("sp", szc[b2][:], sz_d[ch], r=[sz_tok], w=[lk[b2]])
                kb.dma("sp", vA[b2][:], vd3[CTX:T, ch, :].rearrange("(m p) d -> p m d", p=128), r=[v_tok], w=[lk[b2]])
                kb.dma("sp", vB[b2][:, 0:31, :], vd3[CTX + 64:T - 64, ch, :].rearrange("(m p) d -> p m d", p=128), r=[v_tok], w=[lk[b2]])
                kb.dma("sp", vC[b2][:], vd3[0:CTX, ch, :].rearrange("(m p) d -> p m d", p=128), r=[v_tok], w=[lk[b2]])
                pending = [("c", hp, qt) for hp in range(2) for qt in range(2)] + [("r", r_) for r_ in range(64)]
                active = {}
                while pending or active:
                    for sl_ in range(2):
                        if sl_ not in active and pending:
                            u = pending.pop(0)
                            active[sl_] = unit_ctx(sl_, b2, u[1], u[2]) if u[0] == "c" else unit_row(sl_, b2, ch, u[1])
                        if sl_ in active:
                            try:
                                next(active[sl_])
                            except StopIteration:
                                del active[sl_]
                kb.dma("sp", y2_d[ch], y2c[b2][:], r=[y2k[b2]], w=[y2_tok])
        stage("na_attn")
        with kb.scope() as st:
            final_stage(i, last, na_out_w[j], st)

    def gdn_layer(i, j, last):
        HD = 128
        NH_ = 8
        with kb.scope() as st:
            adaln(i, st)
        with kb.scope() as st:
            aT = kb.sb([128, 8, T], BF16, "aT", st)
            a_toks = [Tok() for _ in range(NT)]
            prenorm(i, aT, a_toks, st)
            cons = stash_consume([None] * 24 + [(sz_d[m], sz_tok, AF.Silu) for m in range(8)], st)
            wb_ = proj_bufs(st)
            proj_fm(gdn_in_w[j][:, 3 * D:4 * D], D, aT, a_toks, lambda m, *a: cons(m + 24, *a), st, wb_)
            stG = contextlib.ExitStack()
            grow = kb.sb([16, T], F32, "grow", stG)
            brow = kb.sb([16, T], F32, "brow", stG)
            gk, bk_ = Tok(), Tok()
            proj_fm(gdn_in_w[j][:, 4 * D:4 * D + 16], 16, aT, a_toks,
                    lambda m, tb, t0, n, ps, pt: kb.op("act", lambda e: e.copy(out=grow[:, t0:t0 + n], in_=ps[0:16, 0:n]), r=[pt], w=[gk]), st, wb_)
            proj_fm(gdn_in_w[j][:, 4 * D + 16:4 * D + 32], 16, aT, a_toks,
                    lambda m, tb, t0, n, ps, pt: kb.op("act", lambda e: e.activation(out=brow[:, t0:t0 + n], in_=ps[0:16, 0:n], func=AF.Sigmoid), r=[pt], w=[bk_]), st, wb_)
            st = stG
            prm = kb.sb([16, 4], F32, "gprm", st)
            kb.dma("sp", prm[:, 0:2], gdn_prm[j], w=[gk])
            kb.op("dve", lambda e: e.memset(prm[:, 3:4], 1.0), w=[gk])
            kb.op("act", lambda e: e.activation(out=prm[:, 2:3], in_=prm[:, 0:1], func=AF.Exp), r=[gk], w=[gk])
            kb.op("dve", lambda e: e.tensor_scalar(out=prm[:, 2:3], in0=prm[:, 2:3], scalar1=-1.0, scalar2=None, op0=ALU.mult), r=[gk], w=[gk])
            kb.op("act", lambda e: e.activation(out=grow[:], in_=grow[:], func=AF.Exp, bias=prm[:, 1:2]), r=[gk], w=[gk])
            kb.op("act", lambda e: e.activation(out=grow[:], in_=grow[:], func=AF.Ln, bias=prm[:, 3:4]), r=[gk], w=[gk])
            kb.op("dve", lambda e: e.tensor_scalar(out=grow[:], in0=grow[:], scalar1=prm[:, 2:3], scalar2=None, op0=ALU.mult), r=[gk], w=[gk])
            gbs = kb.sb([128, NT, 32], F32, "gbs", st)
            gbk = Tok()
            for ti in range(NT):
                ps, pt = kb.bank()
                kb.op("pe", lambda e, ps=ps, ti=ti: e.transpose(out=ps[:, 0:16], in_=grow[:, ti * 128:(ti + 1) * 128], identity=ident_f[0:16, 0:16]), r=[gk, ctok], w=[pt])
                kb.op("pe", lambda e, ps=ps, ti=ti: e.transpose(out=ps[:, 16:32], in_=brow[:, ti * 128:(ti + 1) * 128], identity=ident_f[0:16, 0:16]), r=[bk_, ctok], w=[pt])
                kb.op("dve", lambda e, ps=ps, ti=ti: e.tensor_copy(out=gbs[:, ti, :], in_=ps[:, 0:32]), r=[pt], w=[gbk])
            kb.dma("sp", gb_d.rearrange("(n p) c -> p n c", p=128), gbs[:], r=[gbk], w=[gb_tok])
            kb.barrier()
            stG.close()
            st = contextlib.ExitStack()
            xr = [kb.sb([128, T], F32, "xrow", st) for _ in range(2)]
            xk = [Tok() for _ in range(2)]
            yr = kb.sb([128, T], F32, "yrow", st)
            sq = kb.sb([128, T], BF16, "sqrow", st)
            ykk = Tok()
            cw = kb.sb([128, 24, 5], F32, "convw", st)
            cwk = Tok()
            kb.dma("sp", cw[:], gdn_cwT[j], w=[cwk])
            ones_b = kb.sb([128, 128], BF16, "ones_b", st)
            epsc = kb.sb([128, 1], F32, "epsc", st)
            kb.op("dve", lambda e: e.memset(ones_b[:], 1.0), w=[cwk])
            kb.op("dve", lambda e: e.memset(epsc[:], EPS), w=[cwk])
            stg = [kb.sb([128, 512], BF16, "cstg", st) for _ in range(2)]
            stgk = [Tok(), Tok()]
            rnb = [kb.sb([128, 512], F32, "rnb", st) for _ in range(2)]
            rnk = [Tok(), Tok()]
            sc_ = [0]

            def qkv_consume(m, tb, t0, n, ps, pt):
                mm = m % 2
                if (tb + m) % 2 == 0:
                    kb.op("act", lambda e: e.copy(out=xr[mm][:, t0:t0 + n], in_=ps[:, 0:n]), r=[pt], w=[xk[mm]])
                else:
                    kb.op("dve", lambda e: e.tensor_copy(out=xr[mm][:, t0:t0 + n], in_=ps[:, 0:n]), r=[pt], w=[xk[mm]])
                if tb != len(TBLK) - 1:
                    return
                x = xr[mm]
                for (a0, a1) in ((0, CTX), (CTX, T)):
                    kb.op("dve", lambda e: e.tensor_scalar(out=yr[:, a0:a1], in0=x[:, a0:a1], scalar1=cw[:, m, 2:3], scalar2=None, op0=ALU.mult),
                          r=[xk[mm], cwk], w=[ykk])
                    for s_ in (-2, -1, 1, 2):
                        lo, hi = max(a0, a0 - s_), min(a1, a1 - s_)
                        kb.op("dve", lambda e, lo=lo, hi=hi, s_=s_: e.scalar_tensor_tensor(out=yr[:, lo:hi], in0=x[:, lo + s_:hi + s_], scalar=cw[:, m, 2 + s_:3 + s_],
                                                                                          in1=yr[:, lo:hi], op0=ALU.mult, op1=ALU.add), r=[xk[mm], cwk, ykk], w=[ykk])
                kb.op("act", lambda e: e.activation(out=yr[:], in_=yr[:], func=AF.Silu), r=[ykk], w=[ykk])
                dstd = qkvT_d[m]
                if m < 16:
                    kb.op("act", lambda e: e.activation(out=sq[:], in_=yr[:], func=AF.Square), r=[ykk], w=[ykk])
                for tb2, (u0, n2) in enumerate(TBLK):
                    b = sc_[0] % 2
                    sc_[0] += 1
                    if m < 16:
                        p2, pt2 = kb.bank()
                        kb.op("pe", lambda e, p2=p2, u0=u0, n2=n2: e.matmul(p2[:, 0:n2], lhsT=ones_b[:], rhs=sq[:, u0:u0 + n2], start=True, stop=True), r=[ykk, cwk], w=[pt2])
                        kb.op("act", lambda e, p2=p2, n2=n2, b=b: e.activation(out=rnb[b][:, 0:n2], in_=p2[:, 0:n2], func=AF.Sqrt, bias=epsc[:, 0:1]), r=[pt2, cwk], w=[rnk[b]])
                        kb.op("dve", lambda e, n2=n2, b=b: e.reciprocal(out=rnb[b][:, 0:n2], in_=rnb[b][:, 0:n2]), r=[rnk[b]], w=[rnk[b]])
                        kb.op("dve", lambda e, u0=u0, n2=n2, b=b: e.tensor_tensor(out=stg[b][:, 0:n2], in0=yr[:, u0:u0 + n2], in1=rnb[b][:, 0:n2], op=ALU.mult),
                              r=[rnk[b], ykk], w=[stgk[b]])
                    else:
                        kb.op("act", lambda e, u0=u0, n2=n2, b=b: e.copy(out=stg[b][:, 0:n2], in_=yr[:, u0:u0 + n2]), r=[ykk], w=[stgk[b]])
                    kb.dma("sp", dstd[:, u0:u0 + n2], stg[b][:, 0:n2], r=[stgk[b]], w=[qkv_tok])

            proj_fm(gdn_in_w[j][:, 0:3 * D], 3 * D, aT, a_toks, qkv_consume, st, wb_, gsz=256)
            kb.barrier()
            st.close()
        stage("gdn_proj")
        with kb.scope() as st:
            msk = kb.sb([128, 4, 128], F32, "gmsk", st)
            ones_f = kb.sb([128, 1], F32, "ones_f", st)
            ngb = kb.sb([128, 128], F32, "ngb", st)
            mk_ = Tok()
            kb.dma("sp", msk[:], c_gmask[:, :, :], w=[mk_])
            kb.dma("sp", ngb[:], gdn_ngb[j], w=[mk_])
            kb.op("dve", lambda e: e.memset(ones_f[:], 1.0), w=[mk_])
            gbs = kb.sb([128, NT, 32], F32, "gbs2", st)
            kb.dma("sp", gbs[:], gb_d.rearrange("(n p) c -> p n c", p=128), r=[gb_tok], w=[mk_])
            qT = [kb.sb([128, T], BF16, "gqT", st) for _ in range(2)]
            kT = [kb.sb([128, T], BF16, "gkT", st) for _ in range(2)]
            vT = [kb.sb([128, T], BF16, "gvT", st) for _ in range(2)]
            szh = [kb.sb([128, T], F32, "gsz", st) for _ in range(2)]
            y2h = [kb.sb([128, T], BF16, "gy2", st) for _ in range(2)]
            hk = [Tok(), Tok()]
            y2k = [Tok(), Tok()]
            NS = 4

            def mk(shape, dt, name):
                return [kb.sb(shape, dt, name, st) for _ in range(NS)]
            ktok_, vtok_ = mk([128, 128], BF16, "ktok"), mk([128, 128], F32, "vtok")
            kf32 = mk([128, 128], F32, "kf32")
            W1, W2 = mk([128, 128], F32, "W1"), mk([128, 128], F32, "W2")
            Eb, ETb = mk([128, 128], F32, "Eb"), mk([128, 128], F32, "ETb")
            Pm = [mk([128, 128], F32, "Pm%d" % q_) for q_ in range(7)]
            PmT = [mk([128, 128], F32, "PmT%d" % q_) for q_ in range(7)]
            AT = mk([128, 128], BF16, "AT")
            Xs = [mk([128, 256], F32, "Xs%d" % q_) for q_ in range(2)]
            Cm = [mk([128, 128], F32, "Cm%d" % q_) for q_ in range(3)]
            Ym = [mk([128, 128], F32, "Ym%d" % q_) for q_ in range(6)]
            bmsk = kb.sb([128, 3, 128], F32, "bmsk", st)
            kb.dma("sp", bmsk[:], c_bmask[:, :, :], w=[mk_])
            cols = mk([128, 8], F32, "gcols")
            wTb, kdec, vnew = mk([128, 128], BF16, "wTb"), mk([128, 128], BF16, "kdec"), mk([128, 128], BF16, "vnew")
            avs, osb = mk([128, 128], F32, "avs"), mk([128, 128], F32, "osb")
            tk_ = [Tok() for _ in range(NS)]
            Ss = [kb.sb([128, 128], F32, "Sst", st) for _ in range(4)]
            Sbs = [kb.sb([128, 128], BF16, "Sbf", st) for _ in range(4)]
            sks = [Tok() for _ in range(4)]
            cfw = [kb.sb([128, 128], F32, "cfw", st) for _ in range(2)]
            crv = [kb.sb([128, 128], F32, "crv", st) for _ in range(2)]
            cjk = [kb.sb([128, 128], F32, "cjk", st) for _ in range(2)]
            cyb = [kb.sb([128, 128], BF16, "cyb", st) for _ in range(2)]
            cst = [kb.sb([128, 4], F32, "cst", st) for _ in range(2)]
            ck = [Tok(), Tok()]
            ofw = mk([128, 128], F32, "ofw")
            ofk = [Tok() for _ in range(NS)]
            yb_ = mk([128, 128], BF16, "gyb")
            ybk = [Tok() for _ in range(NS)]
            ost = mk([128, 4], F32, "gost")
            def chain(h, hb, d, b, S, Sb, sk):
                order = list(range(0, NT)) if d == 0 else [1, 0] + list(range(NT - 1, 1, -1))
                m_le, m_gt = (0, 1) if d == 0 else (2, 3)
                kb.op("dve", lambda e: e.memset(S[:], 0.0), w=[sk])
                kb.op("dve", lambda e: e.memset(Sb[:], 0.0), w=[sk])
                for n_ in order:
                    tk = tk_[b]
                    ts = slice(n_ * 128, (n_ + 1) * 128)
                    gcol = gbs[:, n_, d * 8 + h:d * 8 + h + 1]
                    bcol = gbs[:, n_, 16 + d * 8 + h:16 + d * 8 + h + 1]
                    V = lambda fn, r=(), w=(): kb.op("dve", fn, r=[tk, mk_, hk[hb]] + list(r), w=[tk] + list(w))
                    A = lambda fn, r=(), w=(): kb.op("act", fn, r=[tk, mk_, hk[hb]] + list(r), w=[tk] + list(w))
                    P = lambda fn, r=(), w=(): kb.op("pe", fn, r=[tk, mk_, hk[hb], ctok] + list(r), w=list(w))
                    p1, t1 = kb.bank()
                    p1b = p1[:].bitcast(BF16)
                    P(lambda e: e.transpose(out=p1b[:, 0:128], in_=kT[hb][:, ts], identity=ident_b[:]), w=[t1])
                    P(lambda e: e.transpose(out=p1b[:, 128:256], in_=vT[hb][:, ts], identity=ident_b[:]), w=[t1])
                    V(lambda e: e.tensor_copy(out=kf32[b][:], in_=p1b[:, 0:128]), r=[t1])
                    A(lambda e: e.copy(out=vtok_[b][:], in_=p1b[:, 128:256]), r=[t1])
                    yield
                    V(lambda e: e.tensor_scalar(out=W1[b][:], in0=msk[:, m_le, :], scalar1=gcol, scalar2=None, op0=ALU.mult))
                    V(lambda e: e.tensor_scalar(out=W2[b][:], in0=msk[:, m_gt, :], scalar1=gcol, scalar2=None, op0=ALU.mult))
                    yield
                    p2, t2 = kb.bank()
                    P(lambda e: e.matmul(p2[:, 0:128], lhsT=W1[b][:], rhs=msk[:, m_gt, :], start=True, stop=True), w=[t2])
                    P(lambda e: e.matmul(p2[:, 128:256], lhsT=msk[:, m_gt, :], rhs=W1[b][:], start=True, stop=True), w=[t2])
                    P(lambda e: e.matmul(p2[:, 256:258], lhsT=W1[b][:], rhs=bc(ones_f[:, 0:1], [128, 2]), start=True, stop=True), w=[t2])
                    P(lambda e: e.matmul(p2[:, 258:260], lhsT=W2[b][:], rhs=bc(ones_f[:, 0:1], [128, 2]), start=True, stop=True), w=[t2])
                    A(lambda e: e.activation(out=Eb[b][:], in_=p2[:, 0:128], func=AF.Exp), r=[t2])
                    A(lambda e: e.activation(out=ETb[b][:], in_=p2[:, 128:256], func=AF.Exp), r=[t2])
                    yield
                    c_ = cols[b]
                    A(lambda e: e.activation(out=c_[:, 0:1], in_=p2[:, 256:257], func=AF.Exp), r=[t2])
                    A(lambda e: e.activation(out=c_[:, 1:2], in_=p2[:, 258:259], func=AF.Exp), r=[t2])
                    V(lambda e: e.tensor_tensor(out=c_[:, 2:3], in0=p2[:, 256:257], in1=c_[:, 1:2], op=ALU.bypass), r=[t2]) if False else None
                    V(lambda e: e.tensor_copy(out=c_[:, 2:3], in_=p2[:, 258:259]), r=[t2])
                    V(lambda e: e.tensor_tensor(out=c_[:, 2:3], in0=c_[:, 2:3], in1=p2[:, 256:257], op=ALU.add), r=[t2])
                    A(lambda e: e.activation(out=c_[:, 3:4], in_=c_[:, 2:3], func=AF.Exp))
                    V(lambda e: e.tensor_tensor(out=c_[:, 4:5], in0=c_[:, 0:1], in1=bcol, op=ALU.mult))
                    V(lambda e: e.tensor_scalar(out=c_[:, 5:6], in0=c_[:, 0:1], scalar1=HD ** -0.5, scalar2=None, op0=ALU.mult))
                    yield
                    p3, t3 = kb.bank()
                    P(lambda e: e.matmul(p3[:, 0:128], lhsT=kT[hb][:, ts], rhs=kT[hb][:, ts], start=True, stop=True), w=[t3])
                    P(lambda e: e.matmul(p3[:, 128:256], lhsT=kT[hb][:, ts], rhs=qT[hb][:, ts], start=True, stop=True), w=[t3])
                    yield
                    V(lambda e: e.tensor_tensor(out=Eb[b][:], in0=Eb[b][:], in1=msk[:, m_gt, :], op=ALU.mult))
                    V(lambda e: e.scalar_tensor_tensor(out=Pm[0][b][:], in0=p3[:, 0:128], scalar=bcol, in1=Eb[b][:], op0=ALU.mult, op1=ALU.mult), r=[t3])
                    V(lambda e: e.tensor_tensor(out=ETb[b][:], in0=ETb[b][:], in1=msk[:, m_le, :], op=ALU.mult))
                    V(lambda e: e.scalar_tensor_tensor(out=AT[b][:], in0=p3[:, 128:256], scalar=HD ** -0.5, in1=ETb[b][:], op0=ALU.mult, op1=ALU.mult), r=[t3])
                    yield
                    p4, t4 = kb.bank()
                    P(lambda e: e.transpose(out=p4[:, 0:128], in_=Pm[0][b][:], identity=ident_f[:]), w=[t4])
                    A(lambda e: e.copy(out=PmT[0][b][:], in_=p4[:, 0:128]), r=[t4])
                    yield
                    Ld, LdT = Pm[1][b], PmT[1][b]
                    V(lambda e: e.tensor_tensor(out=Ld[:], in0=Pm[0][b][:], in1=bmsk[:, 0, :], op=ALU.mult))
                    V(lambda e: e.tensor_tensor(out=LdT[:], in0=PmT[0][b][:], in1=bmsk[:, 0, :], op=ALU.mult))
                    C1, C1T, C2 = Cm[0][b], Cm[1][b], Cm[2][b]
                    V(lambda e: e.tensor_tensor(out=C1[:], in0=Pm[0][b][:], in1=bmsk[:, 1, :], op=ALU.mult))
                    V(lambda e: e.tensor_tensor(out=C1T[:], in0=PmT[0][b][:], in1=bmsk[:, 1, :], op=ALU.mult))
                    V(lambda e: e.tensor_tensor(out=C2[:], in0=Pm[0][b][:], in1=bmsk[:, 2, :], op=ALU.mult))
                    for q_ in range(1, 5):
                        yield
                        p5, t5 = kb.bank()
                        P(lambda e, q_=q_, p5=p5: e.matmul(p5[:, 0:128], lhsT=PmT[q_][b][:], rhs=Pm[q_][b][:], start=True, stop=True), w=[t5])
                        V(lambda e, q_=q_, p5=p5: e.tensor_copy(out=Pm[q_ + 1][b][:], in_=p5[:, 0:128]), r=[t5])
                        if q_ < 4:
                            P(lambda e, q_=q_, p5=p5: e.matmul(p5[:, 128:256], lhsT=Pm[q_][b][:], rhs=PmT[q_][b][:], start=True, stop=True), w=[t5])
                            A(lambda e, q_=q_, p5=p5: e.copy(out=PmT[q_ + 1][b][:], in_=p5[:, 128:256]), r=[t5])
                    yield
                    Ya, Yb = Ym[0][b], Ym[1][b]
                    V(lambda e: e.tensor_tensor(out=Ya[:], in0=Pm[5][b][:], in1=ident_f[:], op=ALU.add))
                    curY, nxtY = Ya, Yb
                    for q_ in range(4, 0, -1):
                        yield
                        p6, t6 = kb.bank()
                        P(lambda e, q_=q_, p6=p6, curY=curY: e.matmul(p6[:, 0:128], lhsT=PmT[q_][b][:], rhs=curY[:], start=True, stop=True), w=[t6])
                        op_ = ALU.add if q_ > 1 else ALU.subtract
                        V(lambda e, p6=p6, curY=curY, nxtY=nxtY, op_=op_: e.tensor_tensor(out=nxtY[:], in0=curY[:], in1=p6[:, 0:128], op=op_), r=[t6])
                        curY, nxtY = nxtY, curY
                    yield
                    Td = curY
                    TdT, Wm, T64, T64T, TT = Ym[2][b], Ym[3][b], Ym[4][b], Ym[5][b], nxtY
                    p6, t6 = kb.bank()
                    P(lambda e, p6=p6: e.transpose(out=p6[:, 0:128], in_=Td[:], identity=ident_f[:]), w=[t6])
                    A(lambda e, p6=p6: e.copy(out=TdT[:], in_=p6[:, 0:128]), r=[t6])

                    def merge(dst, base, lhs_in, rhs_in, lhs_out):
                        pa_, ta_ = kb.bank()
                        P(lambda e: e.matmul(pa_[:, 0:128], lhsT=lhs_in[:], rhs=rhs_in[:], start=True, stop=True), w=[ta_])
                        V(lambda e: e.tensor_copy(out=Wm[:], in_=pa_[:, 0:128]), r=[ta_])
                        pb_, tb2 = kb.bank()
                        P(lambda e: e.matmul(pb_[:, 0:128], lhsT=lhs_out[:], rhs=Wm[:], start=True, stop=True), w=[tb2])
                        V(lambda e: e.tensor_tensor(out=dst[:], in0=base[:], in1=pb_[:, 0:128], op=ALU.subtract), r=[tb2])
                    yield
                    merge(T64, Td, C1T, Td, TdT)
                    yield
                    merge(T64T, TdT, C1, TdT, Td)
                    yield
                    merge(TT, T64T, C2, T64T, T64)
                    yield
                    X0, X1 = Xs[0][b], Xs[1][b]
                    V(lambda e: e.tensor_scalar(out=X0[:, 0:128], in0=vtok_[b][:], scalar1=bcol, scalar2=None, op0=ALU.mult))
                    V(lambda e: e.tensor_scalar(out=X0[:, 128:256], in0=kf32[b][:], scalar1=c_[:, 4:5], scalar2=None, op0=ALU.mult))
                    p6, t6 = kb.bank()
                    P(lambda e, p6=p6: e.matmul(p6[:, 0:256], lhsT=TT[:], rhs=X0[:], start=True, stop=True), w=[t6])
                    V(lambda e, p6=p6: e.tensor_copy(out=X1[:], in_=p6[:, 0:256]), r=[t6])
                    cur = X1
                    yield
                    p7, t7 = kb.bank()
                    P(lambda e, cur=cur: e.transpose(out=p7[:, 0:128], in_=cur[:, 128:256], identity=ident_f[:]), w=[t7])
                    A(lambda e: e.copy(out=wTb[b][:], in_=p7[:, 0:128]), r=[t7])
                    V(lambda e: e.tensor_scalar(out=kdec[b][:], in0=kf32[b][:], scalar1=c_[:, 1:2], scalar2=None, op0=ALU.mult))
                    yield
                    p8, t8 = kb.bank()
                    P(lambda e: e.matmul(p8[:, 0:128], lhsT=wTb[b][:], rhs=Sb[:], start=True, stop=True), r=[sk], w=[t8])
                    P(lambda e: e.matmul(p8[:, 128:256], lhsT=qT[hb][:, ts], rhs=Sb[:], start=True, stop=True), r=[sk], w=[t8])
                    V(lambda e, cur=cur: e.tensor_tensor(out=vnew[b][:], in0=cur[:, 0:128], in1=p8[:, 0:128], op=ALU.subtract), r=[t8])
                    yield
                    p9, t9 = kb.bank()
                    P(lambda e: e.matmul(p9[:, 0:128], lhsT=AT[b][:], rhs=vnew[b][:], start=True, stop=True), w=[t9])
                    P(lambda e: e.matmul(p9[:, 128:256], lhsT=kdec[b][:], rhs=vnew[b][:], start=True, stop=True), w=[t9])
                    A(lambda e: e.copy(out=avs[b][:], in_=p9[:, 0:128]), r=[t9])
                    V(lambda e: e.scalar_tensor_tensor(out=osb[b][:], in0=p8[:, 128:256], scalar=c_[:, 5:6], in1=avs[b][:], op0=ALU.mult, op1=ALU.add), r=[t8])
                    kb.op("dve", lambda e: e.scalar_tensor_tensor(out=S[:], in0=S[:], scalar=c_[:, 3:4], in1=p9[:, 128:256], op0=ALU.mult, op1=ALU.add),
                          r=[tk, t9, sk], w=[sk])
                    kb.op("act", lambda e: e.copy(out=Sb[:], in_=S[:]), r=[sk], w=[sk])
                    yield
                    kb.dma("sp", o2_d[d, n_ * 128:(n_ + 1) * 128, h * 128:(h + 1) * 128], osb[b][:], r=[tk], w=[o2_tok[d][n_]])
            for hp_ in range(NH_ // 2):
                for hb in range(2):
                    h = 2 * hp_ + hb
                    for (dst_, src_) in ((qT[hb], qkvT_d[h]), (kT[hb], qkvT_d[8 + h]), (vT[hb], qkvT_d[16 + h])):
                        kb.dma("sp", dst_[:], src_, r=[qkv_tok], w=[hk[hb]])
                    kb.dma("sp", szh[hb][:], sz_d[h], r=[sz_tok], w=[hk[hb]])
                alive = [chain(2 * hp_ + hb, hb, d, 2 * d + hb, Ss[2 * d + hb], Sbs[2 * d + hb], sks[2 * d + hb]) for d in range(2) for hb in range(2)]
                while alive:
                    for g_ in list(alive):
                        try:
                            next(g_)
                        except StopIteration:
                            alive.remove(g_)
                for hb in range(2):
                    h = 2 * hp_ + hb
                    for n_ in range(NT):
                        c = n_ % 2
                        ts = slice(n_ * 128, (n_ + 1) * 128)
                        CV = lambda fn, r=(), w=(): kb.op("dve", fn, r=[ck[c], mk_] + list(r), w=[ck[c]] + list(w))
                        CA = lambda fn, r=(), w=(): kb.op("act", fn, r=[ck[c], mk_] + list(r), w=[ck[c]] + list(w))
                        kb.dma("sp", cfw[c][:], o2_d[0, n_ * 128:(n_ + 1) * 128, h * 128:(h + 1) * 128], r=[o2_tok[0][n_]], w=[ck[c]])
                        kb.dma("sp", crv[c][:], o2_d[1, n_ * 128:(n_ + 1) * 128, h * 128:(h + 1) * 128], r=[o2_tok[1][n_]], w=[ck[c]])
                        CV(lambda e: e.tensor_tensor(out=cfw[c][:], in0=cfw[c][:], in1=crv[c][:], op=ALU.add))
                        CA(lambda e: e.activation(out=cjk[c][:], in_=cfw[c][:], func=AF.Square, accum_out=cst[c][:, 0:1]))
                        CV(lambda e: e.tensor_scalar(out=cst[c][:, 1:2], in0=cst[c][:, 0:1], scalar1=1.0 / HD, scalar2=EPS, op0=ALU.mult, op1=ALU.add))
                        CA(lambda e: e.activation(out=cst[c][:, 2:3], in_=cst[c][:, 1:2], func=AF.Sqrt))
                        CV(lambda e: e.reciprocal(out=cst[c][:, 3:4], in_=cst[c][:, 2:3]))
                        CV(lambda e: e.scalar_tensor_tensor(out=cyb[c][:], in0=cfw[c][:], scalar=cst[c][:, 3:4], in1=ngb[:], op0=ALU.mult, op1=ALU.mult))
                        pa, ta = kb.bank()
                        pab = pa[:].bitcast(BF16)
                        kb.op("pe", lambda e: e.transpose(out=pab[:, 0:128], in_=cyb[c][:], identity=ident_b[:]), r=[ck[c], ctok], w=[ta])
                        kb.op("dve", lambda e: e.tensor_tensor(out=y2h[hb][:, ts], in0=pab[:, 0:128], in1=szh[hb][:, ts], op=ALU.mult), r=[ta, hk[hb]], w=[y2k[hb]])
                for hb in range(2):
                    h = 2 * hp_ + hb
                    kb.dma("sp", y2_d[h], y2h[hb][:], r=[y2k[hb]], w=[y2_tok])
        stage("gdn_core")
        with kb.scope() as st:
            final_stage(i, last, gdn_out_w[j], st)
    try:
        for i in layers:
            kind, j = i % 3, i // 3
            last = (i == DEPTH - 1)
            first_layer[0] = (i == layers[0])
            if kind == 0:
                s5_layer(i, j, last)
            elif kind == 2:
                na_layer(i, j, last)
            elif kind == 1:
                gdn_layer(i, j, last)
            else:
                raise NotImplementedError
            if "hsnap_d" in DBG and not last:
                for ti in range(NT):
                    kb.dma("sp", hsnap_d[i, ti * 128:(ti + 1) * 128, :], h_d[ti * 128:(ti + 1) * 128, :], r=[h_tok[ti]], w=[snap_tok])
                kb.barrier()
    except StopBuild:
        kb.barrier()
        return kb
    if n_layers < DEPTH:
        with kb.scope() as st:
            bufs = [kb.sb([128, D], F32, "cp", st) for _ in range(2)]
            bk = [Tok(), Tok()]
            for ti in range(2, NT):
                b = ti % 2
                kb.dma("sp", bufs[b][:], h_d[ti * 128:(ti + 1) * 128, :], r=[h_tok[ti]], w=[bk[b]])
                kb.dma("sp", out[(ti - 2) * 128:(ti - 1) * 128, :], bufs[b][:], r=[bk[b]], w=[h_tok[ti]])
    kb.barrier()
    kb.es.close()
    return kb


def host_inputs(inp, b):
    f = np.float32
    A = lambda x: np.ascontiguousarray(x, dtype=f)
    m = {}
    m["hx"] = A(np.concatenate([inp["ctx"][b], inp["x"][b]], axis=0))
    cond = np.stack([inp["c"][b], inp["c_ctx"]], axis=0)
    m["condT"] = A(cond.reshape(2, 8, 128).transpose(2, 1, 0))
    m["ada_w"] = A(inp["ada_w"])
    m["ada_bT"] = A(inp["ada_b"].reshape(DEPTH, 24, 128).transpose(0, 2, 1))
    m["ada_bg"] = A(np.broadcast_to(inp["ada_b"][:, None, 2 * D:], (DEPTH, 128, D)))
    m["pre_gT"] = A(inp["pre_g"].reshape(DEPTH, 8, 128).transpose(0, 2, 1))
    m["post_gb"] = A(np.broadcast_to(inp["post_g"][:, None, :], (DEPTH, 128, D)))
    m["s5_in_w"] = A(inp["s5_in_w"])
    m["s5_glu_w"] = A(inp["s5_glu_w"])
    m["s5_out_w"] = A(inp["s5_out_w"])
    m["s5_lamT"] = A(np.stack([inp["s5_lam_re"], inp["s5_lam_im"]], axis=2).transpose(0, 1, 2, 4, 3))
    m["s5_ldt"] = A(np.broadcast_to(inp["s5_log_dt"][:, :, None, :], (2, 2, 64, 64)))
    bst = np.stack([inp["s5_b_re"], inp["s5_b_im"]], axis=2)
    m["s5_bT"] = A(bst.transpose(0, 1, 2, 4, 3, 5).reshape(2, 2, 2, 64, 1024))
    cst = np.stack([inp["s5_c_re"], inp["s5_c_im"]], axis=2)
    m["s5_cT"] = A(cst.transpose(0, 1, 2, 5, 3, 4).reshape(2, 2, 2, 64, 1024))
    m["s5_dT"] = A(inp["s5_d"].reshape(2, 8, 128).transpose(0, 2, 1))
    m["s5_gbT"] = A(inp["s5_glu_b"].reshape(2, 16, 128).transpose(0, 2, 1))
    m["gdn_in_w"] = A(inp["gdn_in_w"])
    m["gdn_out_w"] = A(inp["gdn_out_w"])
    m["gdn_prm"] = A(np.stack([inp["gdn_a_log"].reshape(1, 16), inp["gdn_dt_bias"].reshape(1, 16)], axis=2))
    m["gdn_cwT"] = A(inp["gdn_conv_w"].reshape(1, 5, 24, 128).transpose(0, 3, 2, 1))
    m["gdn_ngb"] = A(np.broadcast_to(inp["gdn_norm_g"][:, None, :], (1, 128, 128)))
    mi = np.arange(128)
    m["c_gmask"] = A(np.stack([mi[:, None] <= mi[None, :], mi[:, None] > mi[None, :], mi[:, None] >= mi[None, :], mi[:, None] < mi[None, :]], axis=1))
    m["c_bmask"] = A(np.stack([mi[:, None] // 32 == mi[None, :] // 32,
                               (mi[:, None] // 64 == mi[None, :] // 64) & (mi[:, None] // 32 != mi[None, :] // 32),
                               mi[:, None] // 64 != mi[None, :] // 64], axis=1))
    m["na_in_w"] = A(inp["na_in_w"])
    m["na_out_w"] = A(inp["na_out_w"])
    rpb = inp["na_rpb"]
    wq = np.arange(64)
    c0 = np.clip(wq - 8, 0, 48)
    wk_ = np.arange(64)
    inwin = (wk_[None, :] >= c0[:, None]) & (wk_[None, :] < c0[:, None] + 16)
    coff = np.clip(wk_[None, :] - wq[:, None] + 15, 0, 30)
    bt = np.where(inwin[None, None, None], rpb[:, :, :, coff], 0.0)
    bt = bt.reshape(1, 8, 2, 15, 64, 64).transpose(0, 2, 4, 1, 3, 5).reshape(1, 128, 8, 15, 64)
    m["na_biasT"] = A(bt)
    m["c_namask"] = A(np.tile(np.where(inwin, 0.0, -30000.0), (2, 1)))
    m["c_ident"] = np.eye(128, dtype=f)
    p = np.arange(128)
    m["c_bd"] = A((p[:, None] // 16) == (p[None, :] // 16))
    cp = np.stack([(p // 16) % 2 == 0, (p // 16) % 2 == 1, ((p // 16) % 2 == 0) & (p >= 96), ((p // 16) % 2 == 1) & (p >= 96)], axis=0)
    m["c_par"] = A(cp.T)
    m["c_colpar"] = A(np.broadcast_to(cp[None], (64, 4, 128)))
    return m


_CACHE = {}


def kernel(**inputs):
    inp = {k: np.asarray(v) for k, v in inputs.items()}
    if "nc" not in _CACHE:
        _CACHE["nc"] = build(DEPTH).nc
    nc = _CACHE["nc"]
    nb = inp["x"].shape[0]
    maps = [host_inputs(inp, b) for b in range(nb)]
    res = run_bass_kernel_spmd(nc, maps, core_ids=list(range(nb)))
    return np.stack([np.asarray(res.results[b]["out"]) for b in range(nb)], axis=0).astype(np.float32)
```

```python
import contextlib
import math
import numpy as np
import concourse.bass as bass
import concourse.mybir as mybir
from concourse.bass_utils import run_bass_kernel_spmd

F32 = mybir.dt.float32
BF16 = mybir.dt.bfloat16
AF = mybir.ActivationFunctionType
ALU = mybir.AluOpType
AX = mybir.AxisListType

D = 1024
SEQ = 4096
CTX = 256
T = SEQ + CTX
NT = T // 128
DEPTH = 4
EPS = 1e-6
L = 8
NK = T // L
TBLK = [(i * 512, min(512, T - i * 512)) for i in range((T + 511) // 512)]


DBG = set()
ONLY_GLU = False
OUT_CUT = 99


class Tok:
    __slots__ = ("w", "r")

    def __init__(self):
        self.w = None
        self.r = {}


class KB:
    def __init__(self):
        self.nc = bass.Bass("TRN2", target_bir_lowering=False)
        self.es = contextlib.ExitStack()
        nc = self.nc
        self.eng = {}
        for name, obj in (("pe", nc.tensor), ("act", nc.scalar), ("dve", nc.vector), ("pool", nc.gpsimd), ("sp", nc.sync)):
            sem = self.es.enter_context(nc.semaphore("sem_" + name))
            self.eng[name] = dict(e=obj, sem=sem, cnt=0, waited={}, name=name)
        self.ndma = 48
        self.dsem = [self.es.enter_context(nc.semaphore("dsem%d" % i)) for i in range(self.ndma)]
        self.dval = [0] * self.ndma
        self.dnext = 0
        self.psem = [self.es.enter_context(nc.semaphore("psem%d" % i)) for i in range(48)]
        self.pnext = 0
        self.uid = 0
        self.banks = []
        for i in range(8):
            t = self.es.enter_context(nc.psum_tensor("bank%d" % i, [128, 512], F32))
            self.banks.append((t, Tok()))
        self.bnext = 0
        self.ninst = 0
        self.bank_of = {}
        self.stale = set()
        self.keep = []

    def sb(self, shape, dtype, name=None, stack=None):
        self.uid += 1
        t = (stack or self.es).enter_context(self.nc.sbuf_tensor("%s_%d" % (name or "t", self.uid), list(shape), dtype))
        return t

    def dram(self, name, shape, dtype, kind="Internal"):
        if name in DBG:
            kind = "ExternalOutput"
        return self.nc.dram_tensor(name, list(shape), dtype, kind=kind).ap()

    def bank(self, idx=None):
        i = (self.bnext % 8) if idx is None else idx
        if idx is None:
            self.bnext += 1
        t, old = self.banks[i]
        tok = Tok()
        tok.w, tok.r = old.w, dict(old.r)
        self.banks[i] = (t, tok)
        self.bank_of[id(tok)] = i
        self.bank_of.pop(id(old), None)
        self.stale.add(id(old))
        self.keep.append(old)
        return t, tok

    def _wait(self, E, sem, val):
        key = id(sem)
        if E["waited"].get(key, 0) < val:
            E["e"].wait_ge(sem, val)
            E["waited"][key] = val

    def _deps(self, E, reads, writes):
        own = id(E["sem"])
        for t in reads:
            if t.w is not None:
                if E["name"] == "pe" and id(t.w[0]) == own:
                    continue
                self._wait(E, *t.w)
        for t in writes:
            if t.w is not None and not (E["name"] == "pe" and id(t.w[0]) == own):
                self._wait(E, *t.w)
            for sem, val in t.r.values():
                if E["name"] == "pe" and id(sem) == own:
                    continue
                self._wait(E, sem, val)

    def _mark(self, sem, val, reads, writes):
        for t in writes:
            t.w = (sem, val)
            t.r = {}
        for t in reads:
            t.r[id(sem)] = (sem, val)

    def op(self, eng, fn, r=(), w=()):
        for t in list(r) + list(w):
            assert id(t) not in self.stale, "stale PSUM bank token used (bank re-allocated before its last use)"
        E = self.eng[eng]
        self._deps(E, r, w)
        inst = fn(E["e"])
        E["cnt"] += 1
        inst.then_inc(E["sem"], 1)
        self._mark(E["sem"], E["cnt"], r, w)
        self.ninst += 1
        return inst

    def dma(self, q, out, in_, r=(), w=(), **kw):
        E = self.eng[q]
        if q == "pool":
            sem = self.psem[self.pnext]
            self.pnext += 1
            self._deps(E, r, w)
            inst = E["e"].dma_start(out=out, in_=in_, **kw)
            inst.then_inc(sem, 16)
            self._mark(sem, 16, r, w)
            self.ninst += 1
            return
        i = self.dnext % self.ndma
        self.dnext += 1
        sem = self.dsem[i]
        self._wait(E, sem, self.dval[i])
        self._deps(E, r, w)
        inst = E["e"].dma_start(out=out, in_=in_, **kw)
        self.dval[i] += 16
        inst.then_inc(sem, 16)
        self._mark(sem, self.dval[i], r, w)
        self.ninst += 1

    def barrier(self):
        for E in self.eng.values():
            for Fn in self.eng.values():
                if Fn["cnt"] > 0:
                    self._wait(E, Fn["sem"], Fn["cnt"])
            for i in range(self.ndma):
                if self.dval[i] > 0:
                    self._wait(E, self.dsem[i], self.dval[i])
            for i in range(self.pnext):
                self._wait(E, self.psem[i], 16)

    @contextlib.contextmanager
    def scope(self):
        st = contextlib.ExitStack()
        yield st
        self.barrier()
        st.close()


def bc(ap, shape):
    return ap.to_broadcast(list(shape))


class StopBuild(Exception):
    pass


def build(n_layers=DEPTH, stop=None):
    layers = list(range(n_layers)) if isinstance(n_layers, int) else list(n_layers)
    n_layers = layers[-1] + 1
    first_layer = [True]
    kb = KB()
    nc = kb.nc

    def stage(name):
        if stop == name:
            raise StopBuild()

    def din(name, shape, dt=F32):
        return kb.dram(name, shape, dt, kind="ExternalInput")

    hx = din("hx", [T, D])
    condT = din("condT", [128, 8, 2])
    ada_w = din("ada_w", [DEPTH, D, 3 * D])
    ada_bT = din("ada_bT", [DEPTH, 128, 24])
    ada_bg = din("ada_bg", [DEPTH, 128, D])
    pre_gT = din("pre_gT", [DEPTH, 128, 8])
    post_gb = din("post_gb", [DEPTH, 128, D])
    s5_in_w = din("s5_in_w", [2, D, 2 * D])
    s5_glu_w = din("s5_glu_w", [2, D, 2 * D])
    s5_out_w = din("s5_out_w", [2, D, D])
    s5_lamT = din("s5_lamT", [2, 2, 2, 64, 64])
    s5_ldt = din("s5_ldt", [2, 2, 64, 64])
    s5_bT = din("s5_bT", [2, 2, 2, 64, 64 * 16])
    s5_cT = din("s5_cT", [2, 2, 2, 64, 64 * 16])
    s5_dT = din("s5_dT", [2, 128, 8])
    s5_gbT = din("s5_gbT", [2, 128, 16])
    gdn_in_w = din("gdn_in_w", [1, D, 4 * D + 32])
    gdn_out_w = din("gdn_out_w", [1, D, D])
    gdn_prm = din("gdn_prm", [1, 16, 2])
    gdn_cwT = din("gdn_cwT", [1, 128, 24, 5])
    gdn_ngb = din("gdn_ngb", [1, 128, 128])
    c_gmask = din("c_gmask", [128, 4, 128])
    c_bmask = din("c_bmask", [128, 3, 128])
    na_in_w = din("na_in_w", [1, D, 4 * D])
    na_out_w = din("na_out_w", [1, D, D])
    na_biasT = din("na_biasT", [1, 128, 8, 15, 64])
    c_namask = din("c_namask", [128, 64])
    c_ident = din("c_ident", [128, 128])
    c_bd = din("c_bd", [128, 128])
    c_par = din("c_par", [128, 4])
    c_colpar = din("c_colpar", [64, 4, 128])
    out = kb.dram("out", [SEQ, D], F32, kind="ExternalOutput")
    h_d = kb.dram("h_d", [T, D], F32)
    sz_d = kb.dram("sz_d", [8, 128, T], F32)
    u_d = kb.dram("u_d", [8, 128, T], BF16)
    u_tok = Tok()
    gl_d = kb.dram("gl_d", [8, 128, T], BF16)
    dx_d = kb.dram("dx_d", [2, 64, 2, 64, NK], F32)
    xin_d = kb.dram("xin_d", [2, 64, 2, 64, NK], BF16)
    kb_d = kb.dram("kb_d", [2, 128, 8, L, 128], BF16)
    q_d = kb.dram("q_d", [8, 128, T], BF16)
    k_d = kb.dram("k_d", [8, 128, T], BF16)
    v_d = kb.dram("v_d", [T, D], BF16)
    y2_d = kb.dram("y2_d", [8, 128, T], BF16)
    q_tok, k_tok, v_tok, y2_tok = Tok(), Tok(), Tok(), Tok()
    qkvT_d = kb.dram("qkvT_d", [24, 128, T], BF16)
    gb_d = kb.dram("gb_d", [T, 32], F32)
    o2_d = kb.dram("o2_d", [2, T, D], F32)
    o2_tok = [[Tok() for _ in range(NT)] for _ in range(2)]
    qkv_tok, gb_tok = Tok(), Tok()
    o_tok = [Tok() for _ in range(NT)]
    hsnap_d = kb.dram("hsnap_d", [3, T, D], F32) if "hsnap_d" in DBG else None
    snap_tok = Tok()
    h_tok = [Tok() for _ in range(NT)]
    sz_tok = Tok()
    gl_tok = Tok()
    dx_tok = [Tok(), Tok()]
    xin_tok = [Tok(), Tok()]
    kbd_tok = Tok()

    ident_f = kb.sb([128, 128], F32, "identf")
    ident_b = kb.sb([128, 128], BF16, "identb")
    bdmask = kb.sb([128, 128], F32, "bdmask")
    parmask = kb.sb([128, 4], F32, "parmask")
    colpar = kb.sb([64, 4, 128], F32, "colpar")
    halfpi = kb.sb([128, 1], F32, "halfpi")
    scT = kb.sb([128, 8, 2], F32, "scT")
    ctok = Tok()
    kb.dma("sp", ident_f[:], c_ident[:, :], w=[ctok])
    kb.dma("pool", ident_b[:], c_ident[:, :], w=[ctok])
    kb.dma("sp", bdmask[:], c_bd[:, :], w=[ctok])
    kb.dma("sp", parmask[:], c_par[:, :], w=[ctok])
    kb.dma("sp", colpar[:], c_colpar[:, :, :], w=[ctok])
    kb.op("dve", lambda e: e.memset(halfpi[:], math.pi / 2), w=[ctok])
    kb.dma("sp", scT[:], condT[:, :, :], w=[ctok])
    kb.op("act", lambda e: e.activation(out=scT[:], in_=scT[:], func=AF.Silu), r=[ctok], w=[ctok])
    kb.barrier()

    modT = kb.sb([128, 24, 2], F32, "modT")
    S1 = kb.sb([128, 8, 2], F32, "S1")
    GPG = kb.sb([128, 2, D], F32, "GPG")
    mod_tok = Tok()

    def adaln(i, st):
        wbuf = kb.sb([128, 8, 512], F32, "adaw", st)
        scRep = kb.sb([128, 8, 2, 128], F32, "scRep", st)
        kb.op("dve", lambda e: e.tensor_copy(out=scRep[:], in_=bc(scT[:].unsqueeze(3), [128, 8, 2, 128])), r=[ctok], w=[ctok])
        abT = kb.sb([128, 24], F32, "abT", st)
        abg = kb.sb([128, D], F32, "abg", st)
        pgb = kb.sb([128, D], F32, "pgb", st)
        pgT = kb.sb([128, 8], F32, "pgT", st)
        wt = Tok()
        ct = Tok()
        kb.dma("sp", abT[:], ada_bT[i], w=[ct])
        kb.dma("sp", abg[:], ada_bg[i], w=[ct])
        kb.dma("sp", pgb[:], post_gb[i], w=[ct])
        kb.dma("sp", pgT[:], pre_gT[i], w=[ct])
        wv = ada_w[i].rearrange("(kc p) n -> p kc n", p=128)
        for grp in range(6):
            kb.dma("sp", wbuf[:], wv[:, :, grp * 512:(grp + 1) * 512], w=[wt])
            for mm in range(4):
                m = grp * 4 + mm
                ps, pt = kb.bank()
                for kc in range(8):
                    kb.op("pe", lambda e, kc=kc, mm=mm, ps=ps: e.matmul(ps[:, 0:2], lhsT=wbuf[:, kc, mm * 128:(mm + 1) * 128],
                                                                          rhs=scT[:, kc, :], start=(kc == 0), stop=(kc == 7)),
                          r=[wt, ctok], w=[pt])
                kb.op("dve", lambda e, m=m, ps=ps: e.tensor_scalar(out=modT[:, m, :], in0=ps[:, 0:2], scalar1=abT[:, m:m + 1],
                                                                     scalar2=None, op0=ALU.add), r=[pt, ct], w=[mod_tok])
            if grp >= 4:
                for rr in range(2):
                    ps, pt = kb.bank()
                    for kc in range(8):
                        kb.op("pe", lambda e, kc=kc, rr=rr, ps=ps: e.matmul(ps[:, :], lhsT=scRep[:, kc, rr, :], rhs=wbuf[:, kc, :],
                                                                              start=(kc == 0), stop=(kc == 7)), r=[wt, ctok], w=[pt])
                    sl = slice((grp - 4) * 512, (grp - 3) * 512)
                    kb.op("dve", lambda e, rr=rr, ps=ps, sl=sl: e.tensor_tensor(out=GPG[:, rr, sl], in0=ps[:, :], in1=abg[:, sl], op=ALU.add),
                          r=[pt, ct], w=[mod_tok])
                    kb.op("dve", lambda e, rr=rr, sl=sl: e.tensor_tensor(out=GPG[:, rr, sl], in0=GPG[:, rr, sl], in1=pgb[:, sl], op=ALU.mult),
                          r=[ct, mod_tok], w=[mod_tok])
        kb.op("dve", lambda e: e.scalar_tensor_tensor(out=S1[:], in0=modT[:, 8:16, :], scalar=1.0,
                                                      in1=bc(pgT[:].unsqueeze(2), [128, 8, 2]), op0=ALU.add, op1=ALU.mult),
              r=[mod_tok, ct], w=[mod_tok])

    def prenorm(i, aT, a_toks, st):
        src = hx if first_layer[0] else h_d
        hts = [kb.sb([128, D], F32, "ht", st) for _ in range(2)]
        htk = [Tok(), Tok()]
        junk = kb.sb([128, D], BF16, "junk", st)
        ybs = [kb.sb([128, D], F32, "yb", st) for _ in range(2)]
        tmf = [kb.sb([128, 4, 128], F32, "tmf", st) for _ in range(2)]
        tmk = [Tok(), Tok()]
        ybk = [Tok(), Tok()]
        stat = kb.sb([128, 4], F32, "stat", st)
        jt = Tok()
        stt = Tok()
        for ti in range(NT):
            rr = 1 if ti < 2 else 0
            ht, hk = hts[ti % 2], htk[ti % 2]
            yb, yk = ybs[ti % 2], ybk[ti % 2]
            kb.dma("sp", ht[:], src[ti * 128:(ti + 1) * 128, :], r=[h_tok[ti]], w=[hk])
            kb.op("act", lambda e, ht=ht: e.activation(out=junk[:], in_=ht[:], func=AF.Square, accum_out=stat[:, 0:1]), r=[hk], w=[jt, stt])
            kb.op("dve", lambda e: e.tensor_scalar(out=stat[:, 1:2], in0=stat[:, 0:1], scalar1=1.0 / D, scalar2=EPS, op0=ALU.mult, op1=ALU.add),
                  r=[stt], w=[stt])
            kb.op("act", lambda e: e.activation(out=stat[:, 2:3], in_=stat[:, 1:2], func=AF.Sqrt), r=[stt], w=[stt])
            kb.op("dve", lambda e: e.reciprocal(out=stat[:, 3:4], in_=stat[:, 2:3]), r=[stt], w=[stt])
            kb.op("act", lambda e, ht=ht, yb=yb: e.activation(out=yb[:], in_=ht[:], func=AF.Copy, scale=stat[:, 3:4]), r=[hk, stt], w=[yk])
            for hf in range(2):
                ps, pt = kb.bank()
                for c4 in range(4):
                    c = hf * 4 + c4
                    kb.op("pe", lambda e, c=c, c4=c4, yb=yb, ps=ps: e.transpose(out=ps[:, c4 * 128:(c4 + 1) * 128], in_=yb[:, c * 128:(c + 1) * 128],
                                                                             identity=ident_f[:]), r=[yk, ctok], w=[pt])
                for c4 in range(4):
                    c = hf * 4 + c4
                    kb.op("act", lambda e, c=c, c4=c4, ps=ps, rr=rr: e.activation(out=aT[:, c, ti * 128:(ti + 1) * 128], in_=ps[:, c4 * 128:(c4 + 1) * 128],
                                                                               func=AF.Identity, scale=S1[:, c, rr:rr + 1], bias=modT[:, c, rr:rr + 1]),
                          r=[pt, mod_tok], w=[a_toks[ti]])

    def proj_bufs(st):
        return ([kb.sb([128, 8, 512], BF16, "wproj", st) for _ in range(2)], [Tok(), Tok()])

    def proj_fm(w_ap, ncols, aT, a_toks, consume, st, wb_=None, gsz=512):
        wv = w_ap.rearrange("(kc p) n -> p kc n", p=128)
        wbs, wks = wb_ if wb_ is not None else proj_bufs(st)
        ng = (ncols + gsz - 1) // gsz
        for g in range(ng):
            wb, wk = wbs[g % 2], wks[g % 2]
            c0 = g * gsz
            cn = min(gsz, ncols - c0)
            kb.dma("pool", wb[:, :, 0:cn], wv[:, :, c0:c0 + cn], w=[wk])
            for tb, (t0, n) in enumerate(TBLK):
                trs = a_toks[t0 // 128:(t0 + n) // 128]
                for mm in range((cn + 127) // 128):
                    mw = min(128, cn - mm * 128)
                    ps, pt = kb.bank()
                    for kc in range(8):
                        kb.op("pe", lambda e, kc=kc, mm=mm, mw=mw, ps=ps, wb=wb, t0=t0, n=n: e.matmul(
                            ps[0:mw, 0:n], lhsT=wb[:, kc, mm * 128:mm * 128 + mw], rhs=aT[:, kc, t0:t0 + n], start=(kc == 0), stop=(kc == 7)),
                            r=[wk] + trs, w=[pt])
                    consume(g * (gsz // 128) + mm, tb, t0, n, ps, pt)

    def outproj_block(i, y2T, y2k, t0, n, wout, wok, st_bufs, last):
        src = hx if first_layer[0] else h_d
        hold, holdk, tmp, tmpk, stat, stt, junk, jt = st_bufs
        for tt in range(n // 128):
            ti = (t0 // 128) + tt
            rr = 1 if ti < 2 else 0
            if last and rr == 1:
                continue
            p0, k0 = kb.bank()
            p1, k1 = kb.bank()
            for nh, (ps, pk) in enumerate(((p0, k0), (p1, k1))):
                for kc in range(8):
                    kb.op("pe", lambda e, kc=kc, ps=ps, nh=nh, tt=tt: e.matmul(ps[:, :], lhsT=y2T[:, kc, tt * 128:(tt + 1) * 128],
                                                                                 rhs=wout[:, kc, nh * 512:(nh + 1) * 512],
                                                                                 start=(kc == 0), stop=(kc == 7)), r=[y2k, wok], w=[pk])
            b = ti % 2
            if OUT_CUT < 2:
                continue
            kb.dma("sp", hold[b][:], src[ti * 128:(ti + 1) * 128, :], r=[h_tok[ti]], w=[holdk[b]])
            if OUT_CUT < 3:
                continue
            kb.op("act", lambda e: e.activation(out=junk[:, 0:512], in_=p0[:, :], func=AF.Square, accum_out=stat[:, 0:1]), r=[k0], w=[jt, stt])
            kb.op("act", lambda e: e.activation(out=junk[:, 512:1024], in_=p1[:, :], func=AF.Square, accum_out=stat[:, 1:2]), r=[k1], w=[jt, stt])
            if OUT_CUT < 4:
                continue
            kb.op("dve", lambda e: e.tensor_tensor(out=stat[:, 2:3], in0=stat[:, 0:1], in1=stat[:, 1:2], op=ALU.add), r=[stt], w=[stt])
            kb.op("dve", lambda e: e.tensor_scalar(out=stat[:, 3:4], in0=stat[:, 2:3], scalar1=1.0 / D, scalar2=EPS, op0=ALU.mult, op1=ALU.add),
                  r=[stt], w=[stt])
            kb.op("act", lambda e: e.activation(out=stat[:, 4:5], in_=stat[:, 3:4], func=AF.Sqrt), r=[stt], w=[stt])
            kb.op("dve", lambda e: e.reciprocal(out=stat[:, 5:6], in_=stat[:, 4:5]), r=[stt], w=[stt])
            if OUT_CUT < 5:
                continue
            for nh, (ps, pk) in enumerate(((p0, k0), (p1, k1))):
                sl = slice(nh * 512, (nh + 1) * 512)
                kb.op("dve", lambda e, ps=ps, sl=sl, b=b, rr=rr: e.scalar_tensor_tensor(out=tmp[b][:, sl], in0=ps[:, :], scalar=stat[:, 5:6],
                                                                                         in1=GPG[:, rr, sl], op0=ALU.mult, op1=ALU.mult),
                      r=[pk, stt, mod_tok], w=[tmpk[b]])
            if OUT_CUT < 6:
                continue
            kb.op("dve", lambda e, b=b: e.tensor_tensor(out=tmp[b][:], in0=tmp[b][:], in1=hold[b][:], op=ALU.add), r=[holdk[b]], w=[tmpk[b]])
            if OUT_CUT < 7:
                continue
            if last:
                kb.dma("sp", out[(ti - 2) * 128:(ti - 1) * 128, :], tmp[b][:], r=[tmpk[b]], w=[h_tok[ti]])
            else:
                kb.dma("sp", h_d[ti * 128:(ti + 1) * 128, :], tmp[b][:], r=[tmpk[b]], w=[h_tok[ti]])

    def outproj_bufs(st):
        hold = [kb.sb([128, D], F32, "hold", st) for _ in range(2)]
        tmp = [kb.sb([128, D], F32, "otmp", st) for _ in range(2)]
        return (hold, [Tok(), Tok()], tmp, [Tok(), Tok()], kb.sb([128, 8], F32, "ostat", st), Tok(), kb.sb([128, D], BF16, "ojunk", st), Tok())

    def s5_layer(i, j, last):
        with kb.scope() as st:
            adaln(i, st)
        if not ONLY_GLU:
            stage("adaln")
            with kb.scope() as st:
                aT = kb.sb([128, 8, T], BF16, "aT", st)
                a_toks = [Tok() for _ in range(NT)]
                prenorm(i, aT, a_toks, st)
                stage("prenorm")
                szs = [kb.sb([128, 512], BF16, "szs", st) for _ in range(2)]
                szf = [kb.sb([128, 512], F32, "szf", st) for _ in range(2)]
                szk = [Tok(), Tok()]
                cnt = [0]

                def consume(m, tb, t0, n, ps, pt):
                    b = cnt[0] % 2
                    cnt[0] += 1
                    if m < 8:
                        kb.op("dve", lambda e: e.tensor_copy(out=szs[b][:, 0:n], in_=ps[:, 0:n]), r=[pt], w=[szk[b]])
                        kb.dma("sp", u_d[m, :, t0:t0 + n], szs[b][:, 0:n], r=[szk[b]], w=[u_tok])
                    else:
                        kb.op("act", lambda e: e.activation(out=szf[b][:, 0:n], in_=ps[:, 0:n], func=AF.Silu), r=[pt], w=[szk[b]])
                        kb.dma("sp", sz_d[m - 8, :, t0:t0 + n], szf[b][:, 0:n], r=[szk[b]], w=[sz_tok])

                proj_fm(s5_in_w[j], 2 * D, aT, a_toks, consume, st)

            stage("proj")
            s5_scan(j)
            stage("scan")
        with kb.scope() as st:
            gw = kb.sb([128, 8, 2 * D], BF16, "gw", st)
            wout = kb.sb([128, 8, D], BF16, "wout", st)
            gbT = kb.sb([128, 16], F32, "gbT", st)
            wk = Tok()
            kb.dma("pool", gw[:], s5_glu_w[j].rearrange("(kc p) n -> p kc n", p=128), w=[wk])
            kb.dma("pool", wout[:], s5_out_w[j].rearrange("(kc p) n -> p kc n", p=128), w=[wk])
            kb.dma("sp", gbT[:], s5_gbT[j], w=[wk])
            stage("gluw")
            glb = [kb.sb([128, 8, 512], BF16, "glb", st) for _ in range(2)]
            szb = [kb.sb([128, 8, 512], F32, "szb", st) for _ in range(2)]
            y2b = [kb.sb([128, 8, 512], BF16, "y2b", st) for _ in range(2)]
            glk, szk2, y2k = [Tok(), Tok()], [Tok(), Tok()], [Tok(), Tok()]
            sg = [kb.sb([128, 512], F32, "sg", st) for _ in range(2)]
            sgk = [Tok(), Tok()]
            tt_ = [kb.sb([128, 512], F32, "gt", st) for _ in range(2)]
            ttk = [Tok(), Tok()]
            ob = outproj_bufs(st)
            for tb, (t0, n) in enumerate(TBLK):
                b = tb % 2
                kb.dma("sp", glb[b][:, :, 0:n], gl_d.rearrange("q p t -> p q t")[:, :, t0:t0 + n], r=[gl_tok], w=[glk[b]])
                kb.dma("sp", szb[b][:, :, 0:n], sz_d.rearrange("q p t -> p q t")[:, :, t0:t0 + n], r=[sz_tok], w=[szk2[b]])
                for m in range(8):
                    pa, ka = kb.bank()
                    pb, kbk = kb.bank()
                    for (ps, pk, off) in ((pa, ka, 0), (pb, kbk, D)):
                        for kc in range(8):
                            kb.op("pe", lambda e, ps=ps, kc=kc, off=off, m=m, b=b, n=n: e.matmul(
                                ps[:, 0:n], lhsT=gw[:, kc, off + m * 128:off + (m + 1) * 128], rhs=glb[b][:, kc, 0:n],
                                start=(kc == 0), stop=(kc == 7)), r=[wk, glk[b]], w=[pk])
                    s = m % 2
                    kb.op("act", lambda e, s=s, pb=pb, m=m, n=n: e.activation(out=sg[s][:, 0:n], in_=pb[:, 0:n], func=AF.Sigmoid,
                                                                              bias=gbT[:, 8 + m:9 + m]), r=[kbk, wk], w=[sgk[s]])
                    kb.op("dve", lambda e, s=s, pa=pa, m=m, n=n: e.scalar_tensor_tensor(out=tt_[s][:, 0:n], in0=pa[:, 0:n], scalar=gbT[:, m:m + 1],
                                                                                       in1=sg[s][:, 0:n], op0=ALU.add, op1=ALU.mult),
                          r=[ka, sgk[s], wk], w=[ttk[s]])
                    kb.op("pool", lambda e, s=s, m=m, b=b, n=n: e.tensor_tensor(out=y2b[b][:, m, 0:n], in0=tt_[s][:, 0:n], in1=szb[b][:, m, 0:n],
                                                                                 op=ALU.mult), r=[ttk[s], szk2[b]], w=[y2k[b]])
                if tb == 0:
                    stage("glu0")
                outproj_block(i, y2b[b], y2k[b], t0, n, wout, wk, ob, last)
                if tb == 0:
                    stage("out0")

    def s5_scan(j):
        NH = NK // 2
        with kb.scope() as st1:
            PR = [kb.sb([64, L + 1, 64], F32, "PR", st1) for _ in range(2)]
            PI = [kb.sb([64, L + 1, 64], F32, "PI", st1) for _ in range(2)]
            ptk = [Tok(), Tok()]
            for d in range(2):
                with kb.scope() as st:
                    g = Tok()

                    def t64(name):
                        return kb.sb([64, 64], F32, name, st)
                    lr, li, dt, mag, th, c, s, cc, ss, cs = [t64(x) for x in "lr li dt mag th c s cc ss cs".split()]
                    ar, ai, den, fr, fi, t1, t2, am1 = [t64(x) for x in "ar ai den fr fi t1 t2 am1".split()]
                    kb.dma("sp", lr[:], s5_lamT[j, d, 0], w=[g])
                    kb.dma("sp", li[:], s5_lamT[j, d, 1], w=[g])
                    kb.dma("sp", dt[:], s5_ldt[j, d], w=[g])
                    V = lambda fn: kb.op("dve", fn, r=[g], w=[g])
                    A = lambda fn: kb.op("act", fn, r=[g], w=[g])
                    TT = lambda o, a, b_, op: V(lambda e: e.tensor_tensor(out=o, in0=a, in1=b_, op=op))
                    A(lambda e: e.activation(out=dt[:], in_=dt[:], func=AF.Exp))
                    TT(mag[:], lr[:], dt[:], ALU.mult)
                    A(lambda e: e.activation(out=mag[:], in_=mag[:], func=AF.Exp))
                    TT(th[:], li[:], dt[:], ALU.mult)
                    A(lambda e: e.activation(out=s[:], in_=th[:], func=AF.Sin, scale=0.125))
                    A(lambda e: e.activation(out=c[:], in_=th[:], func=AF.Sin, scale=-0.125, bias=halfpi[0:64, 0:1]))
                    for _ in range(3):
                        TT(cc[:], c[:], c[:], ALU.mult)
                        TT(ss[:], s[:], s[:], ALU.mult)
                        TT(cs[:], c[:], s[:], ALU.mult)
                        TT(c[:], cc[:], ss[:], ALU.subtract)
                        V(lambda e: e.tensor_scalar(out=s[:], in0=cs[:], scalar1=2.0, scalar2=None, op0=ALU.mult))
                    TT(ar[:], mag[:], c[:], ALU.mult)
                    TT(ai[:], mag[:], s[:], ALU.mult)
                    TT(t1[:], lr[:], lr[:], ALU.mult)
                    TT(t2[:], li[:], li[:], ALU.mult)
                    TT(den[:], t1[:], t2[:], ALU.add)
                    V(lambda e: e.reciprocal(out=den[:], in_=den[:]))
                    V(lambda e: e.tensor_scalar(out=am1[:], in0=ar[:], scalar1=-1.0, scalar2=None, op0=ALU.add))
                    TT(t1[:], am1[:], lr[:], ALU.mult)
                    TT(t2[:], ai[:], li[:], ALU.mult)
                    TT(t1[:], t1[:], t2[:], ALU.add)
                    TT(fr[:], t1[:], den[:], ALU.mult)
                    TT(t1[:], ai[:], lr[:], ALU.mult)
                    TT(t2[:], am1[:], li[:], ALU.mult)
                    TT(t1[:], t1[:], t2[:], ALU.subtract)
                    TT(fi[:], t1[:], den[:], ALU.mult)

                    def cmul(orr, oi, xr, xi, yr, yi):
                        TT(t1[:], xr, yr, ALU.mult)
                        TT(t2[:], xi, yi, ALU.mult)
                        TT(orr, t1[:], t2[:], ALU.subtract)
                        TT(t1[:], xr, yi, ALU.mult)
                        TT(t2[:], xi, yr, ALU.mult)
                        TT(oi, t1[:], t2[:], ALU.add)
                    pr, pi = PR[d], PI[d]
                    V(lambda e: e.memset(pr[:, 0, :], 1.0))
                    V(lambda e: e.memset(pi[:, 0, :], 0.0))
                    for n_ in range(1, L + 1):
                        cmul(pr[:, n_, :], pi[:, n_, :], pr[:, n_ - 1, :], pi[:, n_ - 1, :], ar[:], ai[:])
                    PFR = kb.sb([64, L, 64], F32, "PFR", st)
                    PFI = kb.sb([64, L, 64], F32, "PFI", st)
                    for n_ in range(L):
                        cmul(PFR[:, n_, :], PFI[:, n_, :], pr[:, n_, :], pi[:, n_, :], fr[:], fi[:])
                    WTr = kb.sb([64, L, 64, 16], BF16, "WTr", st)
                    WTi = kb.sb([64, L, 64, 16], BF16, "WTi", st)
                    stA = contextlib.ExitStack()
                    br = kb.sb([64, 64, 16], F32, "br", stA)
                    bi = kb.sb([64, 64, 16], F32, "bi", stA)
                    kb.dma("sp", br[:], s5_bT[j, d, 0].rearrange("p (g c) -> p g c", c=16), w=[g])
                    kb.dma("sp", bi[:], s5_bT[j, d, 1].rearrange("p (g c) -> p g c", c=16), w=[g])
                    w1 = kb.sb([64, 64, 16], F32, "w1", stA)
                    w2 = kb.sb([64, 64, 16], F32, "w2", stA)
                    for jp in range(L):
                        n_ = (L - 1 - jp) if d == 0 else jp
                        pfr = bc(PFR[:, n_, :].unsqueeze(2), [64, 64, 16])
                        pfi = bc(PFI[:, n_, :].unsqueeze(2), [64, 64, 16])
                        TT(w1[:], br[:], pfr, ALU.mult)
                        TT(w2[:], bi[:], pfi, ALU.mult)
                        TT(WTr[:, jp], w1[:], w2[:], ALU.subtract)
                        TT(w1[:], bi[:], pfr, ALU.mult)
                        TT(w2[:], br[:], pfi, ALU.mult)
                        TT(WTi[:, jp], w1[:], w2[:], ALU.add)
                    stage("gen%d" % d)
                    kb.barrier()
                    stA.close()
                    stB = contextlib.ExitStack()
                    crb = kb.sb([64, 64 * 16], BF16, "crb", stB)
                    cib = kb.sb([64, 64 * 16], BF16, "cib", stB)
                    kb.dma("pool", crb[:], s5_cT[j, d, 0], w=[g])
                    kb.dma("pool", cib[:], s5_cT[j, d, 1], w=[g])
                    A(lambda e: e.mul(out=cib[:], in_=cib[:], mul=-1.0))
                    KBs = kb.sb([128, 8, L, 128], BF16, "KBs", stB)
                    dT = kb.sb([128, 8], F32, "dT", stB)
                    kb.dma("sp", dT[:], s5_dT[j], w=[g])
                    kt = Tok()
                    for q in range(8):
                        for tau in range(L):
                            jp = (L - 1 - tau) if d == 0 else tau
                            ps, pt = kb.bank()
                            kb.op("pe", lambda e, ps=ps, jp=jp, q=q: e.matmul(ps[:, 0:128], lhsT=WTr[:, jp, q * 8:(q + 1) * 8, :].rearrange("p g c -> p (g c)"), rhs=crb[:, q * 128:(q + 1) * 128],
                                                                               start=True, stop=False), r=[g], w=[pt])
                            kb.op("pe", lambda e, ps=ps, jp=jp, q=q: e.matmul(ps[:, 0:128], lhsT=WTi[:, jp, q * 8:(q + 1) * 8, :].rearrange("p g c -> p (g c)"), rhs=cib[:, q * 128:(q + 1) * 128],
                                                                               start=False, stop=True), r=[g], w=[pt])
                            kb.op("dve", lambda e, ps=ps, q=q, tau=tau: e.tensor_tensor(out=KBs[:, q, tau, :], in0=ps[:, 0:128], in1=bdmask[:], op=ALU.mult),
                                  r=[pt, ctok], w=[kt])
                        if d == 0:
                            kb.op("dve", lambda e, q=q: e.scalar_tensor_tensor(out=KBs[:, q, 0, :], in0=ident_f[:], scalar=dT[:, q:q + 1], in1=KBs[:, q, 0, :],
                                                                               op0=ALU.mult, op1=ALU.add), r=[g, ctok, kt], w=[kt])
                    kb.dma("sp", kb_d[d], KBs[:], r=[kt], w=[kbd_tok])
                    kb.barrier()
                    stB.close()
                    stage("kblk%d" % d)
                    WPs = [kb.sb([128, 4, 2, L, 64], BF16, "WP", st) for _ in range(2)]
                    WPk = [Tok(), Tok()]
                    dXs = [kb.sb([64, 2, 8, NK], F32, "dXs", st) for _ in range(2)]
                    dXk = [Tok(), Tok()]
                    ev = 0
                    uqs = [kb.sb([128, T], BF16, "uq", st) for _ in range(2)]
                    uqk = [Tok(), Tok()]
                    for q in range(8):
                        WP, wpk = WPs[q % 2], WPk[q % 2]
                        dX, dxk = dXs[q % 2], dXk[q % 2]
                        uq, uk = uqs[q % 2], uqk[q % 2]
                        kb.dma("sp", uq[:], u_d[q], r=[u_tok], w=[uk])
                        uv = uq[:].rearrange("p (k j) -> p k j", j=L)
                        ps, pt = kb.bank()
                        psb = ps[:].bitcast(BF16)
                        for ri, WTx in enumerate((WTr, WTi)):
                            for jp in range(L):
                                o = (ri * L + jp) * 64
                                kb.op("pe", lambda e, psb=psb, o=o, WTx=WTx, jp=jp, q=q: e.transpose(out=psb[:, o:o + 64], in_=WTx[:, jp, q * 8:(q + 1) * 8, :].rearrange("p g c -> p (g c)"),
                                                                                                     identity=ident_b[0:64, 0:64]), r=[g, ctok], w=[pt])
                        for par in range(4):
                            kb.op("dve", lambda e, psb=psb, par=par, WP=WP: e.tensor_scalar(out=WP[:, par].rearrange("p a b c -> p (a b c)"), in0=psb[:, 0:2 * L * 64],
                                                                                             scalar1=parmask[:, par:par + 1], scalar2=None, op0=ALU.mult),
                                  r=[pt, ctok], w=[wpk])
                        for pp in range(4):
                            for par in range(2):
                                gl = 2 * pp + par
                                for ri in range(2):
                                    for hf in range(2):
                                        ps, pt = kb.bank()
                                        for jp in range(L):
                                            kb.op("pe", lambda e, ps=ps, WP=WP, par=par, ri=ri, jp=jp, pp=pp, q=q, hf=hf: e.matmul(
                                                ps[0:64, 0:NH], lhsT=(WP[32 * pp:32 * pp + 32, par, ri, jp, :] if pp < 3 else WP[64:128, 2 + par, ri, jp, :]),
                                                rhs=(uv[32 * pp:32 * pp + 32, hf * NH:(hf + 1) * NH, jp] if pp < 3 else uv[64:128, hf * NH:(hf + 1) * NH, jp]),
                                                start=(jp == 0), stop=(jp == L - 1)),
                                                r=[wpk, uk], w=[pt])
                                        dst = dX[:, ri, gl, hf * NH:(hf + 1) * NH]
                                        if ev % 2 == 0:
                                            kb.op("act", lambda e, ps=ps, dst=dst: e.copy(out=dst, in_=ps[0:64, 0:NH]), r=[pt], w=[dxk])
                                        else:
                                            kb.op("dve", lambda e, ps=ps, dst=dst: e.tensor_copy(out=dst, in_=ps[0:64, 0:NH]), r=[pt], w=[dxk])
                                        ev += 1
                        kb.dma("sp", dx_d[d, :, :, q * 8:(q + 1) * 8, :], dX[:], r=[dxk], w=[dx_tok[d]])
            stage("dx")
            SEG = 32
            ctx_k = CTX // L
            segs_f = [(0, ctx_k)] + [(k0, min(SEG, NK - k0)) for k0 in range(ctx_k, NK, SEG)]
            with kb.scope() as st:
                def rec(d):
                    E = "dve" if d == 0 else "pool"
                    X = kb.sb([64, 2, 64], F32, "X", st)
                    t1 = kb.sb([64, 2, 64], F32, "rt1", st)
                    t2 = kb.sb([64, 2, 64], F32, "rt2", st)
                    AR2 = kb.sb([64, 2, 64], F32, "AR2", st)
                    AIn = kb.sb([64, 64], F32, "AIn", st)
                    AIp = kb.sb([64, 64], F32, "AIp", st)
                    xk, tk1, tk2, ak = Tok(), Tok(), Tok(), Tok()
                    kb.op(E, lambda e, X=X: e.memset(X[:], 0.0), w=[xk])
                    for h_ in range(2):
                        kb.op(E, lambda e, h_=h_, AR2=AR2, d=d: e.tensor_copy(out=AR2[:, h_, :], in_=PR[d][:, L, :]), w=[ak])
                    kb.op(E, lambda e, AIp=AIp, d=d: e.tensor_copy(out=AIp[:], in_=PI[d][:, L, :]), w=[ak])
                    kb.op(E, lambda e, AIn=AIn, d=d: e.tensor_scalar(out=AIn[:], in0=PI[d][:, L, :], scalar1=-1.0, scalar2=None, op0=ALU.mult), w=[ak])
                    dsegs = [kb.sb([64, 2, 64, SEG], F32, "dseg", st) for _ in range(2)]
                    xsegs = [kb.sb([64, 2, 64, SEG], BF16, "xseg", st) for _ in range(2)]
                    dsk, xsk = [Tok(), Tok()], [Tok(), Tok()]
                    if d == 0:
                        order = [(k0, n, False) for (k0, n) in segs_f]
                    else:
                        csegs = [(k0, n) for (k0, n) in segs_f if k0 < ctx_k]
                        lsegs = [(k0, n) for (k0, n) in segs_f if k0 >= ctx_k]
                        order = [(k0, n, True) for (k0, n) in reversed(csegs)] + [(k0, n, True) for (k0, n) in reversed(lsegs)]
                    for si, (k0, n, rev) in enumerate(order):
                        b = si % 2
                        ds, xs = dsegs[b], xsegs[b]
                        kb.dma("sp", ds[:, :, :, 0:n], dx_d[d, :, :, :, k0:k0 + n], r=[dx_tok[d]], w=[dsk[b]])
                        ks = range(n - 1, -1, -1) if rev else range(n)
                        for kk in ks:
                            kb.op("act", lambda e, xs=xs, kk=kk, X=X: e.copy(out=xs[:, :, :, kk], in_=X[:]), r=[xk], w=[xsk[b]])
                            kb.op(E, lambda e, t1=t1, X=X, AR2=AR2: e.tensor_tensor(out=t1[:], in0=X[:], in1=AR2[:], op=ALU.mult), r=[xk, ak], w=[tk1])
                            kb.op(E, lambda e, t2=t2, X=X, AIn=AIn: e.tensor_tensor(out=t2[:, 0, :], in0=X[:, 1, :], in1=AIn[:], op=ALU.mult), r=[xk, ak], w=[tk2])
                            kb.op(E, lambda e, t2=t2, X=X, AIp=AIp: e.tensor_tensor(out=t2[:, 1, :], in0=X[:, 0, :], in1=AIp[:], op=ALU.mult), r=[xk, ak], w=[tk2])
                            kb.op(E, lambda e, t1=t1, t2=t2: e.tensor_tensor(out=t1[:], in0=t1[:], in1=t2[:], op=ALU.add), r=[tk2], w=[tk1])
                            kb.op(E, lambda e, t1=t1, X=X, ds=ds, kk=kk: e.tensor_tensor(out=X[:], in0=t1[:], in1=ds[:, :, :, kk], op=ALU.add),
                                  r=[tk1, dsk[b]], w=[xk])
                            yield
                        kb.dma("sp", xin_d[d, :, :, :, k0:k0 + n], xs[:, :, :, 0:n], r=[xsk[b]], w=[xin_tok[d]])
                alive = [rec(0), rec(1)]
                while alive:
                    for g_ in list(alive):
                        try:
                            next(g_)
                        except StopIteration:
                            alive.remove(g_)
            stage("rec")
            with kb.scope() as st:
                PRm = [kb.sb([64, L, 64], F32, "PRm", st) for _ in range(2)]
                PIm = [kb.sb([64, L, 64], F32, "PIm", st) for _ in range(2)]
                pmk = Tok()
                for d in range(2):
                    for r_ in range(L):
                        m_ = r_ + 1 if d == 0 else L - r_
                        kb.op("dve", lambda e, d=d, r_=r_, m_=m_: e.tensor_copy(out=PRm[d][:, r_, :], in_=PR[d][:, m_, :]), w=[pmk])
                        kb.op("dve", lambda e, d=d, r_=r_, m_=m_: e.tensor_copy(out=PIm[d][:, r_, :], in_=PI[d][:, m_, :]), w=[pmk])
                cr = [kb.sb([64, 64, 16], F32, "cr", st) for _ in range(2)]
                ci = [kb.sb([64, 64, 16], F32, "ci", st) for _ in range(2)]
                for d in range(2):
                    kb.dma("sp", cr[d][:], s5_cT[j, d, 0].rearrange("p (g c) -> p g c", c=16), w=[pmk])
                    kb.dma("sp", ci[d][:], s5_cT[j, d, 1].rearrange("p (g c) -> p g c", c=16), w=[pmk])
                m1 = kb.sb([64, L, 8, 16], F32, "m1", st)
                m2 = kb.sb([64, L, 8, 16], F32, "m2", st)
                mr = kb.sb([64, L, 8, 16], F32, "mr", st)
                mi = kb.sb([64, L, 8, 16], F32, "mi", st)
                mk = Tok()
                MX = [kb.sb([64, 2, 2, 4, L, 128], BF16, "MX", st) for _ in range(1)]
                mxk = [Tok(), Tok()]
                XQ = [kb.sb([64, 2, 2, 8, NH], BF16, "XQ", st) for _ in range(2)]
                xqk = [Tok(), Tok()]
                KQ = [kb.sb([128, 2, L, 128], BF16, "KQ", st) for _ in range(2)]
                kqk = [Tok(), Tok()]
                glq = [kb.sb([128, NK, L], BF16, "glq", st) for _ in range(2)]
                glk = [Tok(), Tok()]
                yx = [kb.sb([128, NH], F32, "yx", st) for _ in range(2)]
                y2 = [kb.sb([128, NH], F32, "yy", st) for _ in range(2)]
                ysg = [kb.sb([128, NH], F32, "ysg", st) for _ in range(2)]
                yk = [Tok(), Tok()]
                uqs = [kb.sb([128, T], BF16, "uq3", st) for _ in range(2)]
                uqk = [Tok(), Tok()]
                it = 0
                for q in range(8):
                    MXq, mxq = MX[0], mxk[0]
                    KQq, kqq = KQ[q % 2], kqk[q % 2]
                    gq, gqk = glq[q % 2], glk[q % 2]
                    for d in range(2):
                        kb.dma("sp", KQq[:, d], kb_d[d, :, q], r=[kbd_tok], w=[kqq])
                    uq, uk = uqs[q % 2], uqk[q % 2]
                    kb.dma("sp", uq[:], u_d[q], r=[u_tok], w=[uk])
                    uv = uq[:].rearrange("p (k j) -> p k j", j=L)
                    for d in range(2):
                        crq = bc(cr[d][:, q * 8:(q + 1) * 8, :].unsqueeze(1), [64, L, 8, 16])
                        ciq = bc(ci[d][:, q * 8:(q + 1) * 8, :].unsqueeze(1), [64, L, 8, 16])
                        prq = bc(PRm[d][:, :, q * 8:(q + 1) * 8].unsqueeze(3), [64, L, 8, 16])
                        piq = bc(PIm[d][:, :, q * 8:(q + 1) * 8].unsqueeze(3), [64, L, 8, 16])
                        V = lambda fn: kb.op("dve", fn, r=[pmk, mk], w=[mk])
                        V(lambda e: e.tensor_tensor(out=m1[:], in0=crq, in1=prq, op=ALU.mult))
                        V(lambda e: e.tensor_tensor(out=m2[:], in0=ciq, in1=piq, op=ALU.mult))
                        V(lambda e: e.tensor_tensor(out=mr[:], in0=m1[:], in1=m2[:], op=ALU.subtract))
                        V(lambda e: e.tensor_tensor(out=m1[:], in0=crq, in1=piq, op=ALU.mult))
                        V(lambda e: e.tensor_tensor(out=m2[:], in0=ciq, in1=prq, op=ALU.mult))
                        V(lambda e: e.scalar_tensor_tensor(out=mi[:], in0=m1[:], scalar=-1.0, in1=m2[:], op0=ALU.mult, op1=ALU.subtract))
                        for ri, src in enumerate((mr, mi)):
                            for par in range(4):
                                kb.op("dve", lambda e, d=d, ri=ri, par=par, src=src, MXq=MXq: e.tensor_tensor(
                                    out=MXq[:, d, ri, par], in0=src[:].rearrange("p r g c -> p r (g c)"),
                                    in1=bc(colpar[:, par:par + 1, :], [64, L, 128]), op=ALU.mult), r=[mk, ctok], w=[mxq])
                    for hf in range(2):
                        XQh, xqh = XQ[hf], xqk[hf]
                        for d in range(2):
                            kb.dma("sp", XQh[:, d], xin_d[d, :, :, q * 8:(q + 1) * 8, hf * NH:(hf + 1) * NH], r=xin_tok, w=[xqh])
                        for r_ in range(L):
                            ps, pt = kb.bank()
                            first = [True]

                            def MM(lhsT, rhs, outp, extra_r):
                                stt_ = first[0]
                                first[0] = False
                                kb.op("pe", lambda e: e.matmul(outp, lhsT=lhsT, rhs=rhs, start=stt_, stop=False, skip_group_check=True), r=extra_r, w=[pt])
                            for tau in range(0, r_ + 1):
                                MM(KQq[:, 0, tau, :], uv[:, hf * NH:(hf + 1) * NH, r_ - tau], ps[:, 0:NH], [kqq, uk])
                            for tau in range(0, L - r_):
                                MM(KQq[:, 1, tau, :], uv[:, hf * NH:(hf + 1) * NH, r_ + tau], ps[:, 0:NH], [kqq, uk])
                            for d in range(2):
                                for pp in range(4):
                                    for par in range(2):
                                        for ri in range(2):
                                            if pp < 3:
                                                MM(MXq[:, d, ri, par, r_, 32 * pp:32 * pp + 32], XQh[:, d, ri, 2 * pp + par, :], ps[32 * pp:32 * pp + 32, 0:NH], [mxq, xqh])
                                            else:
                                                MM(MXq[:, d, ri, 2 + par, r_, 64:128], XQh[:, d, ri, 2 * pp + par, :], ps[64:128, 0:NH], [mxq, xqh])
                            b = it % 2
                            it += 1
                            kb.op("act", lambda e, b=b, ps=ps: e.copy(out=yx[b][:], in_=ps[:, 0:NH]), r=[pt], w=[yk[b]])
                            kb.op("pool", lambda e, b=b: e.tensor_tensor(out=y2[b][:], in0=yx[b][:], in1=yx[b][:], op=ALU.mult), r=[yk[b]], w=[yk[b]])
                            kb.op("dve", lambda e, b=b: e.tensor_scalar(out=y2[b][:], in0=y2[b][:], scalar1=0.044715, scalar2=1.0, op0=ALU.mult, op1=ALU.add),
                                  r=[yk[b]], w=[yk[b]])
                            kb.op("pool", lambda e, b=b: e.tensor_tensor(out=y2[b][:], in0=y2[b][:], in1=yx[b][:], op=ALU.mult), r=[yk[b]], w=[yk[b]])
                            kb.op("act", lambda e, b=b: e.activation(out=ysg[b][:], in_=y2[b][:], func=AF.Sigmoid, scale=1.5957691216057308), r=[yk[b]], w=[yk[b]])
                            kb.op("dve", lambda e, b=b, gq=gq, hf=hf, r_=r_: e.tensor_tensor(out=gq[:, hf * NH:(hf + 1) * NH, r_], in0=yx[b][:], in1=ysg[b][:], op=ALU.mult),
                                  r=[yk[b]], w=[gqk])
                    kb.dma("sp", gl_d[q], gq[:].rearrange("p k j -> p (k j)"), r=[gqk], w=[gl_tok])


    def stash_consume(dst_list, st):
        szs = [kb.sb([128, 512], BF16, "stg", st) for _ in range(3)]
        szf = [kb.sb([128, 512], F32, "stgf", st) for _ in range(3)]
        szk = [Tok() for _ in range(3)]
        cnt = [0]

        def consume(m, tb, t0, n, ps, pt):
            dst, dtok, func = dst_list[m]
            b = cnt[0] % 3
            cnt[0] += 1
            if func is None:
                kb.op("dve", lambda e: e.tensor_copy(out=szs[b][0:ps_rows(m), 0:n], in_=ps[0:ps_rows(m), 0:n]), r=[pt], w=[szk[b]])
            else:
                kb.op("act", lambda e: e.activation(out=szf[b][0:ps_rows(m), 0:n], in_=ps[0:ps_rows(m), 0:n], func=func), r=[pt], w=[szk[b]])
                kb.dma("sp", dst[0:ps_rows(m), t0:t0 + n], szf[b][0:ps_rows(m), 0:n], r=[szk[b]], w=[dtok])
                return
            kb.dma("sp", dst[0:ps_rows(m), t0:t0 + n], szs[b][0:ps_rows(m), 0:n], r=[szk[b]], w=[dtok])

        def ps_rows(m):
            return dst_list[m][0].shape[0]
        return consume

    def final_stage(i, last, wout_ap, st):
        wout = kb.sb([128, 8, D], BF16, "wout", st)
        wk = Tok()
        kb.dma("pool", wout[:], wout_ap.rearrange("(kc p) n -> p kc n", p=128), w=[wk])
        y2b = [kb.sb([128, 8, 512], BF16, "y2b", st) for _ in range(2)]
        y2k = [Tok(), Tok()]
        ob = outproj_bufs(st)
        for tb, (t0, n) in enumerate(TBLK):
            b = tb % 2
            kb.dma("sp", y2b[b][:, :, 0:n], y2_d.rearrange("q p t -> p q t")[:, :, t0:t0 + n], r=[y2_tok], w=[y2k[b]])
            outproj_block(i, y2b[b], y2k[b], t0, n, wout, wk, ob, last)

    def na_layer(i, j, last):
        with kb.scope() as st:
            adaln(i, st)
        with kb.scope() as st:
            aT = kb.sb([128, 8, T], BF16, "aT", st)
            a_toks = [Tok() for _ in range(NT)]
            prenorm(i, aT, a_toks, st)
            dl = [(q_d[m], q_tok, None) for m in range(8)] + [(k_d[m], k_tok, None) for m in range(8)]
            dl += [None] * 8 + [(sz_d[m], sz_tok, AF.Silu) for m in range(8)]
            cons = stash_consume(dl, st)
            proj_fm(na_in_w[j][:, 0:2 * D], 2 * D, aT, a_toks, cons, st)
            proj_fm(na_in_w[j][:, 3 * D:4 * D], D, aT, a_toks, lambda m, *a: cons(m + 24, *a), st)
            wv = kb.sb([128, 8, D], BF16, "wv", st)
            wvk = Tok()
            kb.dma("pool", wv[:], na_in_w[j].rearrange("(kc p) n -> p kc n", p=128)[:, :, 2 * D:3 * D], w=[wvk])
            vst = [kb.sb([128, D], BF16, "vst", st) for _ in range(2)]
            vsk = [Tok(), Tok()]
            for ti in range(NT):
                b = ti % 2
                for nh in range(2):
                    ps, pt = kb.bank()
                    for kc in range(8):
                        kb.op("pe", lambda e, ps=ps, kc=kc, ti=ti, nh=nh: e.matmul(ps[:, :], lhsT=aT[:, kc, ti * 128:(ti + 1) * 128],
                                                                                     rhs=wv[:, kc, nh * 512:(nh + 1) * 512], start=(kc == 0), stop=(kc == 7)),
                              r=[wvk, a_toks[ti]], w=[pt])
                    if nh == 0:
                        kb.op("act", lambda e, ps=ps, b=b: e.copy(out=vst[b][:, 0:512], in_=ps[:, :]), r=[pt], w=[vsk[b]])
                    else:
                        kb.op("dve", lambda e, ps=ps, b=b: e.tensor_copy(out=vst[b][:, 512:1024], in_=ps[:, :]), r=[pt], w=[vsk[b]])
                kb.dma("sp", v_d[ti * 128:(ti + 1) * 128, :], vst[b][:], r=[vsk[b]], w=[v_tok])
        stage("na_proj")
        with kb.scope() as st:
            biasm = kb.sb([128, 8, 15, 64], BF16, "biasm", st)
            bstage = kb.sb([128, 15, 64], F32, "bstage", st)
            mask = kb.sb([128, 64], F32, "namask", st)
            bk = Tok()
            kb.dma("sp", mask[:], c_namask[:, :], w=[bk])
            for ch in range(8):
                kb.dma("sp", bstage[:], na_biasT[j, :, ch], w=[bk])
                kb.op("dve", lambda e, ch=ch: e.tensor_tensor(out=biasm[:, ch], in0=bstage[:], in1=bc(mask[:].unsqueeze(1), [128, 15, 64]), op=ALU.add),
                      r=[bk], w=[bk])
            qs = [kb.sb([128, T], BF16, "qs", st) for _ in range(2)]
            ks = [kb.sb([128, T], BF16, "ks", st) for _ in range(2)]
            szc = [kb.sb([128, T], F32, "szc", st) for _ in range(2)]
            vA = [kb.sb([128, 32, 128], BF16, "vA", st) for _ in range(2)]
            vB = [kb.sb([128, 32, 128], BF16, "vB", st) for _ in range(2)]
            vC = [kb.sb([128, 2, 128], BF16, "vC", st) for _ in range(2)]
            y2c = [kb.sb([128, T], BF16, "y2c", st) for _ in range(2)]
            lk = [Tok(), Tok()]
            y2k = [Tok(), Tok()]
            NB = 3
            sc = [kb.sb([128, 768], F32, "sc", st) for _ in range(NB)]
            pb = [kb.sb([128, 768], BF16, "pb", st) for _ in range(NB)]
            pT = [kb.sb([128, 768], BF16, "pT", st) for _ in range(NB)]
            sts = [kb.sb([128, 4], F32, "nst", st) for _ in range(NB)]
            sck = [Tok() for _ in range(NB)]
            pbk = [Tok() for _ in range(NB)]
            ptk = [Tok() for _ in range(NB)]
            stk = [Tok() for _ in range(NB)]
            vd3 = v_d.rearrange("t (c d) -> t c d", d=128)

            def softmax_pv(b, ps_list, width, vtiles, out_ap_rows, ncol, y2dst, szsrc, deps, y2tok):
                for (pap, ptok, c0, w_, bias) in ps_list:
                    if bias is not None:
                        kb.op("dve", lambda e, pap=pap, c0=c0, w_=w_, bias=bias: e.scalar_tensor_tensor(
                            out=sc[b][:, c0:c0 + w_], in0=pap, scalar=0.125, in1=bias, op0=ALU.mult, op1=ALU.add), r=[ptok, bk], w=[sck[b]])
                    else:
                        kb.op("act", lambda e, pap=pap, c0=c0, w_=w_: e.activation(out=sc[b][:, c0:c0 + w_], in_=pap, func=AF.Copy, scale=0.125),
                              r=[ptok], w=[sck[b]])
                yield
                kb.op("dve", lambda e: e.reduce_max(out=sts[b][:, 0:1], in_=sc[b][:, 0:width], axis=AX.X), r=[sck[b]], w=[stk[b]])
                kb.op("dve", lambda e: e.tensor_scalar(out=sts[b][:, 1:2], in0=sts[b][:, 0:1], scalar1=-1.0, scalar2=None, op0=ALU.mult), r=[stk[b]], w=[stk[b]])
                yield
                kb.op("act", lambda e: e.activation(out=pb[b][:, 0:width], in_=sc[b][:, 0:width], func=AF.Exp, bias=sts[b][:, 1:2], accum_out=sts[b][:, 2:3]),
                      r=[sck[b], stk[b]], w=[pbk[b], stk[b]])
                yield
                kb.op("dve", lambda e: e.reciprocal(out=sts[b][:, 3:4], in_=sts[b][:, 2:3]), r=[stk[b]], w=[stk[b]])
                kb.op("dve", lambda e: e.tensor_scalar(out=pb[b][:, 0:width], in0=pb[b][:, 0:width], scalar1=sts[b][:, 3:4], scalar2=None, op0=ALU.mult),
                      r=[stk[b]], w=[pbk[b]])
                yield
                ps, pt = kb.bank(4 * b + 2)
                psb = ps[:].bitcast(BF16)
                nkt = width // 128
                for kt in range(nkt):
                    kb.op("pe", lambda e, kt=kt: e.transpose(out=psb[:, kt * 128:(kt + 1) * 128], in_=pb[b][:, kt * 128:(kt + 1) * 128], identity=ident_b[:]),
                          r=[pbk[b], ctok], w=[pt])
                yield
                kb.op("act", lambda e: e.copy(out=pT[b][:, 0:width], in_=psb[:, 0:width]), r=[pt], w=[ptk[b]])
                yield
                ops, opt = kb.bank(4 * b + 3)
                for kt in range(nkt):
                    for (hb, c0q) in out_ap_rows:
                        kb.op("pe", lambda e, kt=kt, hb=hb, c0q=c0q: e.matmul(ops[hb:hb + 64, 0:ncol], lhsT=vtiles[kt][:, hb:hb + 64],
                                                                               rhs=pT[b][:, kt * 128 + c0q:kt * 128 + c0q + ncol],
                                                                               start=(kt == 0), stop=(kt == nkt - 1)), r=[ptk[b]] + deps, w=[opt])
                yield
                rows = slice(min(h_ for h_, _ in out_ap_rows), max(h_ for h_, _ in out_ap_rows) + 64)
                kb.op("dve", lambda e: e.tensor_tensor(out=y2dst[rows], in0=ops[rows, 0:ncol], in1=szsrc[rows], op=ALU.mult), r=[opt] + deps, w=[y2tok])

            def unit_ctx(b, b2, hp, qt):
                q_, k_, sz_, y2_ = qs[b2], ks[b2], szc[b2], y2c[b2]
                hb = 64 * hp
                ps, pt = kb.bank(4 * b)
                kb.op("pe", lambda e: e.matmul(ps[:, 0:256], lhsT=q_[hb:hb + 64, qt * 128:(qt + 1) * 128], rhs=k_[hb:hb + 64, 0:256],
                                               start=True, stop=True), r=[lk[b2]], w=[pt])
                yield
                t0 = qt * 128
                yield from softmax_pv(b, [(ps[:, 0:256], pt, 0, 256, None)], 256, [vC[b2][:, 0, :], vC[b2][:, 1, :]], [(hb, 0)], 128,
                                      y2_[:, t0:t0 + 128], sz_[:, t0:t0 + 128], [lk[b2]], y2k[b2])

            def unit_row(b, b2, ch, r_):
                q_, k_, sz_, y2_ = qs[b2], ks[b2], szc[b2], y2c[b2]
                r0 = min(max(r_ - 4, 0), 56)
                ro0 = r0 - r_ + 7
                tq = CTX + 64 * r_
                tk = CTX + 64 * r0
                pw, ptw = kb.bank(4 * b)
                pc, ptc = kb.bank(4 * b + 1)
                for hp in range(2):
                    hb = 64 * hp
                    kb.op("pe", lambda e, hb=hb: e.matmul(pw[hb:hb + 64, :], lhsT=q_[hb:hb + 64, tq:tq + 64], rhs=k_[hb:hb + 64, tk:tk + 512],
                                                          start=True, stop=True), r=[lk[b2]], w=[ptw])
                    kb.op("pe", lambda e, hb=hb: e.matmul(pc[hb:hb + 64, 0:256], lhsT=q_[hb:hb + 64, tq:tq + 64], rhs=k_[hb:hb + 64, 0:256],
                                                          start=True, stop=True), r=[lk[b2]], w=[ptc])
                yield
                bias = biasm[:, ch, ro0:ro0 + 8, :].rearrange("p a b -> p (a b)")
                if r0 % 2 == 0:
                    vt = [vA[b2][:, r0 // 2 + kt, :] for kt in range(4)]
                else:
                    vt = [vB[b2][:, (r0 - 1) // 2 + kt, :] for kt in range(4)]
                vt += [vC[b2][:, 0, :], vC[b2][:, 1, :]]
                yield from softmax_pv(b, [(pw[:, :], ptw, 0, 512, bias), (pc[:, 0:256], ptc, 512, 256, None)], 768, vt, [(0, 0), (64, 64)], 64,
                                      y2_[:, tq:tq + 64], sz_[:, tq:tq + 64], [lk[b2]], y2k[b2])

            for ch in range(8):
                b2 = ch % 2
                kb.dma("sp", qs[b2][:], q_d[ch], r=[q_tok], w=[lk[b2]])
                kb.dma("sp", ks[b2][:], k_d[ch], r=[k_tok], w=[lk[b2]])
                kb.dma("sp", szc[b2][:], sz_d[ch], r=[sz_tok], w=[lk[b2]])
                kb.dma("sp", vA[b2][:], vd3[CTX:T, ch, :].rearrange("(m p) d -> p m d", p=128), r=[v_tok], w=[lk[b2]])
                kb.dma("sp", vB[b2][:, 0:31, :], vd3[CTX + 64:T - 64, ch, :].rearrange("(m p) d -> p m d", p=128), r=[v_tok], w=[lk[b2]])
                kb.dma("sp", vC[b2][:], vd3[0:CTX, ch, :].rearrange("(m p) d -> p m d", p=128), r=[v_tok], w=[lk[b2]])
                pending = [("c", hp, qt) for hp in range(2) for qt in range(2)] + [("r", r_) for r_ in range(64)]
                active = {}
                while pending or active:
                    for sl_ in range(2):
                        if sl_ not in active and pending:
                            u = pending.pop(0)
                            active[sl_] = unit_ctx(sl_, b2, u[1], u[2]) if u[0] == "c" else unit_row(sl_, b2, ch, u[1])
                        if sl_ in active:
                            try:
                                next(active[sl_])
                            except StopIteration:
                                del active[sl_]
                kb.dma("sp", y2_d[ch], y2c[b2][:], r=[y2k[b2]], w=[y2_tok])
        stage("na_attn")
        with kb.scope() as st:
            final_stage(i, last, na_out_w[j], st)

    def gdn_layer(i, j, last):
        HD = 128
        NH_ = 8
        with kb.scope() as st:
            adaln(i, st)
        with kb.scope() as st:
            aT = kb.sb([128, 8, T], BF16, "aT", st)
            a_toks = [Tok() for _ in range(NT)]
            prenorm(i, aT, a_toks, st)
            cons = stash_consume([None] * 24 + [(sz_d[m], sz_tok, AF.Silu) for m in range(8)], st)
            wb_ = proj_bufs(st)
            proj_fm(gdn_in_w[j][:, 3 * D:4 * D], D, aT, a_toks, lambda m, *a: cons(m + 24, *a), st, wb_)
            stG = contextlib.ExitStack()
            grow = kb.sb([16, T], F32, "grow", stG)
            brow = kb.sb([16, T], F32, "brow", stG)
            gk, bk_ = Tok(), Tok()
            proj_fm(gdn_in_w[j][:, 4 * D:4 * D + 16], 16, aT, a_toks,
                    lambda m, tb, t0, n, ps, pt: kb.op("act", lambda e: e.copy(out=grow[:, t0:t0 + n], in_=ps[0:16, 0:n]), r=[pt], w=[gk]), st, wb_)
            proj_fm(gdn_in_w[j][:, 4 * D + 16:4 * D + 32], 16, aT, a_toks,
                    lambda m, tb, t0, n, ps, pt: kb.op("act", lambda e: e.activation(out=brow[:, t0:t0 + n], in_=ps[0:16, 0:n], func=AF.Sigmoid), r=[pt], w=[bk_]), st, wb_)
            st = stG
            prm = kb.sb([16, 4], F32, "gprm", st)
            kb.dma("sp", prm[:, 0:2], gdn_prm[j], w=[gk])
            kb.op("dve", lambda e: e.memset(prm[:, 3:4], 1.0), w=[gk])
            kb.op("act", lambda e: e.activation(out=prm[:, 2:3], in_=prm[:, 0:1], func=AF.Exp), r=[gk], w=[gk])
            kb.op("dve", lambda e: e.tensor_scalar(out=prm[:, 2:3], in0=prm[:, 2:3], scalar1=-1.0, scalar2=None, op0=ALU.mult), r=[gk], w=[gk])
            kb.op("act", lambda e: e.activation(out=grow[:], in_=grow[:], func=AF.Exp, bias=prm[:, 1:2]), r=[gk], w=[gk])
            kb.op("act", lambda e: e.activation(out=grow[:], in_=grow[:], func=AF.Ln, bias=prm[:, 3:4]), r=[gk], w=[gk])
            kb.op("dve", lambda e: e.tensor_scalar(out=grow[:], in0=grow[:], scalar1=prm[:, 2:3], scalar2=None, op0=ALU.mult), r=[gk], w=[gk])
            gbs = kb.sb([128, NT, 32], F32, "gbs", st)
            gbk = Tok()
            for ti in range(NT):
                ps, pt = kb.bank()
                kb.op("pe", lambda e, ps=ps, ti=ti: e.transpose(out=ps[:, 0:16], in_=grow[:, ti * 128:(ti + 1) * 128], identity=ident_f[0:16, 0:16]), r=[gk, ctok], w=[pt])
                kb.op("pe", lambda e, ps=ps, ti=ti: e.transpose(out=ps[:, 16:32], in_=brow[:, ti * 128:(ti + 1) * 128], identity=ident_f[0:16, 0:16]), r=[bk_, ctok], w=[pt])
                kb.op("dve", lambda e, ps=ps, ti=ti: e.tensor_copy(out=gbs[:, ti, :], in_=ps[:, 0:32]), r=[pt], w=[gbk])
            kb.dma("sp", gb_d.rearrange("(n p) c -> p n c", p=128), gbs[:], r=[gbk], w=[gb_tok])
            kb.barrier()
            stG.close()
            st = contextlib.ExitStack()
            xr = [kb.sb([128, T], F32, "xrow", st) for _ in range(2)]
            xk = [Tok() for _ in range(2)]
            yr = kb.sb([128, T], F32, "yrow", st)
            sq = kb.sb([128, T], BF16, "sqrow", st)
            ykk = Tok()
            cw = kb.sb([128, 24, 5], F32, "convw", st)
            cwk = Tok()
            kb.dma("sp", cw[:], gdn_cwT[j], w=[cwk])
            ones_b = kb.sb([128, 128], BF16, "ones_b", st)
            epsc = kb.sb([128, 1], F32, "epsc", st)
            kb.op("dve", lambda e: e.memset(ones_b[:], 1.0), w=[cwk])
            kb.op("dve", lambda e: e.memset(epsc[:], EPS), w=[cwk])
            stg = [kb.sb([128, 512], BF16, "cstg", st) for _ in range(2)]
            stgk = [Tok(), Tok()]
            rnb = [kb.sb([128, 512], F32, "rnb", st) for _ in range(2)]
            rnk = [Tok(), Tok()]
            sc_ = [0]

            def qkv_consume(m, tb, t0, n, ps, pt):
                mm = m % 2
                if (tb + m) % 2 == 0:
                    kb.op("act", lambda e: e.copy(out=xr[mm][:, t0:t0 + n], in_=ps[:, 0:n]), r=[pt], w=[xk[mm]])
                else:
                    kb.op("dve", lambda e: e.tensor_copy(out=xr[mm][:, t0:t0 + n], in_=ps[:, 0:n]), r=[pt], w=[xk[mm]])
                if tb != len(TBLK) - 1:
                    return
                x = xr[mm]
                for (a0, a1) in ((0, CTX), (CTX, T)):
                    kb.op("dve", lambda e: e.tensor_scalar(out=yr[:, a0:a1], in0=x[:, a0:a1], scalar1=cw[:, m, 2:3], scalar2=None, op0=ALU.mult),
                          r=[xk[mm], cwk], w=[ykk])
                    for s_ in (-2, -1, 1, 2):
                        lo, hi = max(a0, a0 - s_), min(a1, a1 - s_)
                        kb.op("dve", lambda e, lo=lo, hi=hi, s_=s_: e.scalar_tensor_tensor(out=yr[:, lo:hi], in0=x[:, lo + s_:hi + s_], scalar=cw[:, m, 2 + s_:3 + s_],
                                                                                          in1=yr[:, lo:hi], op0=ALU.mult, op1=ALU.add), r=[xk[mm], cwk, ykk], w=[ykk])
                kb.op("act", lambda e: e.activation(out=yr[:], in_=yr[:], func=AF.Silu), r=[ykk], w=[ykk])
                dstd = qkvT_d[m]
                if m < 16:
                    kb.op("act", lambda e: e.activation(out=sq[:], in_=yr[:], func=AF.Square), r=[ykk], w=[ykk])
                for tb2, (u0, n2) in enumerate(TBLK):
                    b = sc_[0] % 2
                    sc_[0] += 1
                    if m < 16:
                        p2, pt2 = kb.bank()
                        kb.op("pe", lambda e, p2=p2, u0=u0, n2=n2: e.matmul(p2[:, 0:n2], lhsT=ones_b[:], rhs=sq[:, u0:u0 + n2], start=True, stop=True), r=[ykk, cwk], w=[pt2])
                        kb.op("act", lambda e, p2=p2, n2=n2, b=b: e.activation(out=rnb[b][:, 0:n2], in_=p2[:, 0:n2], func=AF.Sqrt, bias=epsc[:, 0:1]), r=[pt2, cwk], w=[rnk[b]])
                        kb.op("dve", lambda e, n2=n2, b=b: e.reciprocal(out=rnb[b][:, 0:n2], in_=rnb[b][:, 0:n2]), r=[rnk[b]], w=[rnk[b]])
                        kb.op("dve", lambda e, u0=u0, n2=n2, b=b: e.tensor_tensor(out=stg[b][:, 0:n2], in0=yr[:, u0:u0 + n2], in1=rnb[b][:, 0:n2], op=ALU.mult),
                              r=[rnk[b], ykk], w=[stgk[b]])
                    else:
                        kb.op("act", lambda e, u0=u0, n2=n2, b=b: e.copy(out=stg[b][:, 0:n2], in_=yr[:, u0:u0 + n2]), r=[ykk], w=[stgk[b]])
                    kb.dma("sp", dstd[:, u0:u0 + n2], stg[b][:, 0:n2], r=[stgk[b]], w=[qkv_tok])

            proj_fm(gdn_in_w[j][:, 0:3 * D], 3 * D, aT, a_toks, qkv_consume, st, wb_, gsz=256)
            kb.barrier()
            st.close()
        stage("gdn_proj")
        with kb.scope() as st:
            msk = kb.sb([128, 4, 128], F32, "gmsk", st)
            ones_f = kb.sb([128, 1], F32, "ones_f", st)
            ngb = kb.sb([128, 128], F32, "ngb", st)
            mk_ = Tok()
            kb.dma("sp", msk[:], c_gmask[:, :, :], w=[mk_])
            kb.dma("sp", ngb[:], gdn_ngb[j], w=[mk_])
            kb.op("dve", lambda e: e.memset(ones_f[:], 1.0), w=[mk_])
            gbs = kb.sb([128, NT, 32], F32, "gbs2", st)
            kb.dma("sp", gbs[:], gb_d.rearrange("(n p) c -> p n c", p=128), r=[gb_tok], w=[mk_])
            qT = [kb.sb([128, T], BF16, "gqT", st) for _ in range(2)]
            kT = [kb.sb([128, T], BF16, "gkT", st) for _ in range(2)]
            vT = [kb.sb([128, T], BF16, "gvT", st) for _ in range(2)]
            szh = [kb.sb([128, T], F32, "gsz", st) for _ in range(2)]
            y2h = [kb.sb([128, T], BF16, "gy2", st) for _ in range(2)]
            hk = [Tok(), Tok()]
            y2k = [Tok(), Tok()]
            NS = 4

            def mk(shape, dt, name):
                return [kb.sb(shape, dt, name, st) for _ in range(NS)]
            ktok_, vtok_ = mk([128, 128], BF16, "ktok"), mk([128, 128], F32, "vtok")
            kf32 = mk([128, 128], F32, "kf32")
            W1, W2 = mk([128, 128], F32, "W1"), mk([128, 128], F32, "W2")
            Eb, ETb = mk([128, 128], F32, "Eb"), mk([128, 128], F32, "ETb")
            Pm = [mk([128, 128], F32, "Pm%d" % q_) for q_ in range(7)]
            PmT = [mk([128, 128], F32, "PmT%d" % q_) for q_ in range(7)]
            AT = mk([128, 128], BF16, "AT")
            Xs = [mk([128, 256], F32, "Xs%d" % q_) for q_ in range(2)]
            Cm = [mk([128, 128], F32, "Cm%d" % q_) for q_ in range(3)]
            Ym = [mk([128, 128], F32, "Ym%d" % q_) for q_ in range(6)]
            bmsk = kb.sb([128, 3, 128], F32, "bmsk", st)
            kb.dma("sp", bmsk[:], c_bmask[:, :, :], w=[mk_])
            cols = mk([128, 8], F32, "gcols")
            wTb, kdec, vnew = mk([128, 128], BF16, "wTb"), mk([128, 128], BF16, "kdec"), mk([128, 128], BF16, "vnew")
            avs, osb = mk([128, 128], F32, "avs"), mk([128, 128], F32, "osb")
            tk_ = [Tok() for _ in range(NS)]
            Ss = [kb.sb([128, 128], F32, "Sst", st) for _ in range(4)]
            Sbs = [kb.sb([128, 128], BF16, "Sbf", st) for _ in range(4)]
            sks = [Tok() for _ in range(4)]
            cfw = [kb.sb([128, 128], F32, "cfw", st) for _ in range(2)]
            crv = [kb.sb([128, 128], F32, "crv", st) for _ in range(2)]
            cjk = [kb.sb([128, 128], F32, "cjk", st) for _ in range(2)]
            cyb = [kb.sb([128, 128], BF16, "cyb", st) for _ in range(2)]
            cst = [kb.sb([128, 4], F32, "cst", st) for _ in range(2)]
            ck = [Tok(), Tok()]
            ofw = mk([128, 128], F32, "ofw")
            ofk = [Tok() for _ in range(NS)]
            yb_ = mk([128, 128], BF16, "gyb")
            ybk = [Tok() for _ in range(NS)]
            ost = mk([128, 4], F32, "gost")
            def chain(h, hb, d, b, S, Sb, sk):
                order = list(range(0, NT)) if d == 0 else [1, 0] + list(range(NT - 1, 1, -1))
                m_le, m_gt = (0, 1) if d == 0 else (2, 3)
                kb.op("dve", lambda e: e.memset(S[:], 0.0), w=[sk])
                kb.op("dve", lambda e: e.memset(Sb[:], 0.0), w=[sk])
                for n_ in order:
                    tk = tk_[b]
                    ts = slice(n_ * 128, (n_ + 1) * 128)
                    gcol = gbs[:, n_, d * 8 + h:d * 8 + h + 1]
                    bcol = gbs[:, n_, 16 + d * 8 + h:16 + d * 8 + h + 1]
                    V = lambda fn, r=(), w=(): kb.op("dve", fn, r=[tk, mk_, hk[hb]] + list(r), w=[tk] + list(w))
                    A = lambda fn, r=(), w=(): kb.op("act", fn, r=[tk, mk_, hk[hb]] + list(r), w=[tk] + list(w))
                    P = lambda fn, r=(), w=(): kb.op("pe", fn, r=[tk, mk_, hk[hb], ctok] + list(r), w=list(w))
                    p1, t1 = kb.bank()
                    p1b = p1[:].bitcast(BF16)
                    P(lambda e: e.transpose(out=p1b[:, 0:128], in_=kT[hb][:, ts], identity=ident_b[:]), w=[t1])
                    P(lambda e: e.transpose(out=p1b[:, 128:256], in_=vT[hb][:, ts], identity=ident_b[:]), w=[t1])
                    V(lambda e: e.tensor_copy(out=kf32[b][:], in_=p1b[:, 0:128]), r=[t1])
                    A(lambda e: e.copy(out=vtok_[b][:], in_=p1b[:, 128:256]), r=[t1])
                    yield
                    V(lambda e: e.tensor_scalar(out=W1[b][:], in0=msk[:, m_le, :], scalar1=gcol, scalar2=None, op0=ALU.mult))
                    V(lambda e: e.tensor_scalar(out=W2[b][:], in0=msk[:, m_gt, :], scalar1=gcol, scalar2=None, op0=ALU.mult))
                    yield
                    p2, t2 = kb.bank()
                    P(lambda e: e.matmul(p2[:, 0:128], lhsT=W1[b][:], rhs=msk[:, m_gt, :], start=True, stop=True), w=[t2])
                    P(lambda e: e.matmul(p2[:, 128:256], lhsT=msk[:, m_gt, :], rhs=W1[b][:], start=True, stop=True), w=[t2])
                    P(lambda e: e.matmul(p2[:, 256:258], lhsT=W1[b][:], rhs=bc(ones_f[:, 0:1], [128, 2]), start=True, stop=True), w=[t2])
                    P(lambda e: e.matmul(p2[:, 258:260], lhsT=W2[b][:], rhs=bc(ones_f[:, 0:1], [128, 2]), start=True, stop=True), w=[t2])
                    A(lambda e: e.activation(out=Eb[b][:], in_=p2[:, 0:128], func=AF.Exp), r=[t2])
                    A(lambda e: e.activation(out=ETb[b][:], in_=p2[:, 128:256], func=AF.Exp), r=[t2])
                    yield
                    c_ = cols[b]
                    A(lambda e: e.activation(out=c_[:, 0:1], in_=p2[:, 256:257], func=AF.Exp), r=[t2])
                    A(lambda e: e.activation(out=c_[:, 1:2], in_=p2[:, 258:259], func=AF.Exp), r=[t2])
                    V(lambda e: e.tensor_tensor(out=c_[:, 2:3], in0=p2[:, 256:257], in1=c_[:, 1:2], op=ALU.bypass), r=[t2]) if False else None
                    V(lambda e: e.tensor_copy(out=c_[:, 2:3], in_=p2[:, 258:259]), r=[t2])
                    V(lambda e: e.tensor_tensor(out=c_[:, 2:3], in0=c_[:, 2:3], in1=p2[:, 256:257], op=ALU.add), r=[t2])
                    A(lambda e: e.activation(out=c_[:, 3:4], in_=c_[:, 2:3], func=AF.Exp))
                    V(lambda e: e.tensor_tensor(out=c_[:, 4:5], in0=c_[:, 0:1], in1=bcol, op=ALU.mult))
                    V(lambda e: e.tensor_scalar(out=c_[:, 5:6], in0=c_[:, 0:1], scalar1=HD ** -0.5, scalar2=None, op0=ALU.mult))
                    yield
                    p3, t3 = kb.bank()
                    P(lambda e: e.matmul(p3[:, 0:128], lhsT=kT[hb][:, ts], rhs=kT[hb][:, ts], start=True, stop=True), w=[t3])
                    P(lambda e: e.matmul(p3[:, 128:256], lhsT=kT[hb][:, ts], rhs=qT[hb][:, ts], start=True, stop=True), w=[t3])
                    yield
                    V(lambda e: e.tensor_tensor(out=Eb[b][:], in0=Eb[b][:], in1=msk[:, m_gt, :], op=ALU.mult))
                    V(lambda e: e.scalar_tensor_tensor(out=Pm[0][b][:], in0=p3[:, 0:128], scalar=bcol, in1=Eb[b][:], op0=ALU.mult, op1=ALU.mult), r=[t3])
                    V(lambda e: e.tensor_tensor(out=ETb[b][:], in0=ETb[b][:], in1=msk[:, m_le, :], op=ALU.mult))
                    V(lambda e: e.scalar_tensor_tensor(out=AT[b][:], in0=p3[:, 128:256], scalar=HD ** -0.5, in1=ETb[b][:], op0=ALU.mult, op1=ALU.mult), r=[t3])
                    yield
                    p4, t4 = kb.bank()
                    P(lambda e: e.transpose(out=p4[:, 0:128], in_=Pm[0][b][:], identity=ident_f[:]), w=[t4])
                    A(lambda e: e.copy(out=PmT[0][b][:], in_=p4[:, 0:128]), r=[t4])
                    yield
                    Ld, LdT = Pm[1][b], PmT[1][b]
                    V(lambda e: e.tensor_tensor(out=Ld[:], in0=Pm[0][b][:], in1=bmsk[:, 0, :], op=ALU.mult))
                    V(lambda e: e.tensor_tensor(out=LdT[:], in0=PmT[0][b][:], in1=bmsk[:, 0, :], op=ALU.mult))
                    C1, C1T, C2 = Cm[0][b], Cm[1][b], Cm[2][b]
                    V(lambda e: e.tensor_tensor(out=C1[:], in0=Pm[0][b][:], in1=bmsk[:, 1, :], op=ALU.mult))
                    V(lambda e: e.tensor_tensor(out=C1T[:], in0=PmT[0][b][:], in1=bmsk[:, 1, :], op=ALU.mult))
                    V(lambda e: e.tensor_tensor(out=C2[:], in0=Pm[0][b][:], in1=bmsk[:, 2, :], op=ALU.mult))
                    for q_ in range(1, 5):
                        yield
                        p5, t5 = kb.bank()
                        P(lambda e, q_=q_, p5=p5: e.matmul(p5[:, 0:128], lhsT=PmT[q_][b][:], rhs=Pm[q_][b][:], start=True, stop=True), w=[t5])
                        A(lambda e, q_=q_, p5=p5: e.copy(out=Pm[q_ + 1][b][:], in_=p5[:, 0:128]), r=[t5])
                        if q_ < 4:
                            P(lambda e, q_=q_, p5=p5: e.matmul(p5[:, 128:256], lhsT=Pm[q_][b][:], rhs=PmT[q_][b][:], start=True, stop=True), w=[t5])
                            A(lambda e, q_=q_, p5=p5: e.copy(out=PmT[q_ + 1][b][:], in_=p5[:, 128:256]), r=[t5])
                    yield
                    Ya, Yb = Ym[0][b], Ym[1][b]
                    V(lambda e: e.tensor_tensor(out=Ya[:], in0=Pm[5][b][:], in1=ident_f[:], op=ALU.add))
                    curY, nxtY = Ya, Yb
                    for q_ in range(4, 0, -1):
                        yield
                        p6, t6 = kb.bank()
                        P(lambda e, q_=q_, p6=p6, curY=curY: e.matmul(p6[:, 0:128], lhsT=PmT[q_][b][:], rhs=curY[:], start=True, stop=True), w=[t6])
                        op_ = ALU.add if q_ > 1 else ALU.subtract
                        V(lambda e, p6=p6, curY=curY, nxtY=nxtY, op_=op_: e.tensor_tensor(out=nxtY[:], in0=curY[:], in1=p6[:, 0:128], op=op_), r=[t6])
                        curY, nxtY = nxtY, curY
                    yield
                    Td = curY
                    TdT, Wm, T64, T64T, TT = Ym[2][b], Ym[3][b], Ym[4][b], Ym[5][b], nxtY
                    p6, t6 = kb.bank()
                    P(lambda e, p6=p6: e.transpose(out=p6[:, 0:128], in_=Td[:], identity=ident_f[:]), w=[t6])
                    A(lambda e, p6=p6: e.copy(out=TdT[:], in_=p6[:, 0:128]), r=[t6])

                    def merge(dst, base, lhs_in, rhs_in, lhs_out):
                        pa_, ta_ = kb.bank()
                        P(lambda e: e.matmul(pa_[:, 0:128], lhsT=lhs_in[:], rhs=rhs_in[:], start=True, stop=True), w=[ta_])
                        A(lambda e: e.copy(out=Wm[:], in_=pa_[:, 0:128]), r=[ta_])
                        pb_, tb2 = kb.bank()
                        P(lambda e: e.matmul(pb_[:, 0:128], lhsT=lhs_out[:], rhs=Wm[:], start=True, stop=True), w=[tb2])
                        V(lambda e: e.tensor_tensor(out=dst[:], in0=base[:], in1=pb_[:, 0:128], op=ALU.subtract), r=[tb2])
                    yield
                    merge(T64, Td, C1T, Td, TdT)
                    yield
                    merge(T64T, TdT, C1, TdT, Td)
                    yield
                    merge(TT, T64T, C2, T64T, T64)
                    yield
                    X0, X1 = Xs[0][b], Xs[1][b]
                    V(lambda e: e.tensor_scalar(out=X0[:, 0:128], in0=vtok_[b][:], scalar1=bcol, scalar2=None, op0=ALU.mult))
                    V(lambda e: e.tensor_scalar(out=X0[:, 128:256], in0=kf32[b][:], scalar1=c_[:, 4:5], scalar2=None, op0=ALU.mult))
                    p6, t6 = kb.bank()
                    P(lambda e, p6=p6: e.matmul(p6[:, 0:256], lhsT=TT[:], rhs=X0[:], start=True, stop=True), w=[t6])
                    A(lambda e, p6=p6: e.copy(out=X1[:], in_=p6[:, 0:256]), r=[t6])
                    cur = X1
                    yield
                    p7, t7 = kb.bank()
                    P(lambda e, cur=cur: e.transpose(out=p7[:, 0:128], in_=cur[:, 128:256], identity=ident_f[:]), w=[t7])
                    A(lambda e: e.copy(out=wTb[b][:], in_=p7[:, 0:128]), r=[t7])
                    V(lambda e: e.tensor_scalar(out=kdec[b][:], in0=kf32[b][:], scalar1=c_[:, 1:2], scalar2=None, op0=ALU.mult))
                    yield
                    p8, t8 = kb.bank()
                    P(lambda e: e.matmul(p8[:, 0:128], lhsT=wTb[b][:], rhs=Sb[:], start=True, stop=True), r=[sk], w=[t8])
                    P(lambda e: e.matmul(p8[:, 128:256], lhsT=qT[hb][:, ts], rhs=Sb[:], start=True, stop=True), r=[sk], w=[t8])
                    V(lambda e, cur=cur: e.tensor_tensor(out=vnew[b][:], in0=cur[:, 0:128], in1=p8[:, 0:128], op=ALU.subtract), r=[t8])
                    yield
                    p9, t9 = kb.bank()
                    P(lambda e: e.matmul(p9[:, 0:128], lhsT=AT[b][:], rhs=vnew[b][:], start=True, stop=True), w=[t9])
                    P(lambda e: e.matmul(p9[:, 128:256], lhsT=kdec[b][:], rhs=vnew[b][:], start=True, stop=True), w=[t9])
                    A(lambda e: e.copy(out=avs[b][:], in_=p9[:, 0:128]), r=[t9])
                    V(lambda e: e.scalar_tensor_tensor(out=osb[b][:], in0=p8[:, 128:256], scalar=c_[:, 5:6], in1=avs[b][:], op0=ALU.mult, op1=ALU.add), r=[t8])
                    kb.op("dve", lambda e: e.scalar_tensor_tensor(out=S[:], in0=S[:], scalar=c_[:, 3:4], in1=p9[:, 128:256], op0=ALU.mult, op1=ALU.add),
                          r=[tk, t9, sk], w=[sk])
                    kb.op("act", lambda e: e.copy(out=Sb[:], in_=S[:]), r=[sk], w=[sk])
                    yield
                    kb.dma("sp", o2_d[d, n_ * 128:(n_ + 1) * 128, h * 128:(h + 1) * 128], osb[b][:], r=[tk], w=[o2_tok[d][n_]])
            for hp_ in range(NH_ // 2):
                for hb in range(2):
                    h = 2 * hp_ + hb
                    for (dst_, src_) in ((qT[hb], qkvT_d[h]), (kT[hb], qkvT_d[8 + h]), (vT[hb], qkvT_d[16 + h])):
                        kb.dma("sp", dst_[:], src_, r=[qkv_tok], w=[hk[hb]])
                    kb.dma("sp", szh[hb][:], sz_d[h], r=[sz_tok], w=[hk[hb]])
                alive = [chain(2 * hp_ + hb, hb, d, 2 * d + hb, Ss[2 * d + hb], Sbs[2 * d + hb], sks[2 * d + hb]) for d in range(2) for hb in range(2)]
                while alive:
                    for g_ in list(alive):
                        try:
                            next(g_)
                        except StopIteration:
                            alive.remove(g_)
                for hb in range(2):
                    h = 2 * hp_ + hb
                    for n_ in range(NT):
                        c = n_ % 2
                        ts = slice(n_ * 128, (n_ + 1) * 128)
                        CV = lambda fn, r=(), w=(): kb.op("dve", fn, r=[ck[c], mk_] + list(r), w=[ck[c]] + list(w))
                        CA = lambda fn, r=(), w=(): kb.op("act", fn, r=[ck[c], mk_] + list(r), w=[ck[c]] + list(w))
                        kb.dma("sp", cfw[c][:], o2_d[0, n_ * 128:(n_ + 1) * 128, h * 128:(h + 1) * 128], r=[o2_tok[0][n_]], w=[ck[c]])
                        kb.dma("sp", crv[c][:], o2_d[1, n_ * 128:(n_ + 1) * 128, h * 128:(h + 1) * 128], r=[o2_tok[1][n_]], w=[ck[c]])
                        CV(lambda e: e.tensor_tensor(out=cfw[c][:], in0=cfw[c][:], in1=crv[c][:], op=ALU.add))
                        CA(lambda e: e.activation(out=cjk[c][:], in_=cfw[c][:], func=AF.Square, accum_out=cst[c][:, 0:1]))
                        CV(lambda e: e.tensor_scalar(out=cst[c][:, 1:2], in0=cst[c][:, 0:1], scalar1=1.0 / HD, scalar2=EPS, op0=ALU.mult, op1=ALU.add))
                        CA(lambda e: e.activation(out=cst[c][:, 2:3], in_=cst[c][:, 1:2], func=AF.Sqrt))
                        CV(lambda e: e.reciprocal(out=cst[c][:, 3:4], in_=cst[c][:, 2:3]))
                        CV(lambda e: e.scalar_tensor_tensor(out=cyb[c][:], in0=cfw[c][:], scalar=cst[c][:, 3:4], in1=ngb[:], op0=ALU.mult, op1=ALU.mult))
                        pa, ta = kb.bank()
                        pab = pa[:].bitcast(BF16)
                        kb.op("pe", lambda e: e.transpose(out=pab[:, 0:128], in_=cyb[c][:], identity=ident_b[:]), r=[ck[c], ctok], w=[ta])
                        kb.op("dve", lambda e: e.tensor_tensor(out=y2h[hb][:, ts], in0=pab[:, 0:128], in1=szh[hb][:, ts], op=ALU.mult), r=[ta, hk[hb]], w=[y2k[hb]])
                for hb in range(2):
                    h = 2 * hp_ + hb
                    kb.dma("sp", y2_d[h], y2h[hb][:], r=[y2k[hb]], w=[y2_tok])
        stage("gdn_core")
        with kb.scope() as st:
            final_stage(i, last, gdn_out_w[j], st)
    try:
        for i in layers:
            kind, j = i % 3, i // 3
            last = (i == DEPTH - 1)
            first_layer[0] = (i == layers[0])
            if kind == 0:
                s5_layer(i, j, last)
            elif kind == 2:
                na_layer(i, j, last)
            elif kind == 1:
                gdn_layer(i, j, last)
            else:
                raise NotImplementedError
            if "hsnap_d" in DBG and not last:
                for ti in range(NT):
                    kb.dma("sp", hsnap_d[i, ti * 128:(ti + 1) * 128, :], h_d[ti * 128:(ti + 1) * 128, :], r=[h_tok[ti]], w=[snap_tok])
                kb.barrier()
    except StopBuild:
        kb.barrier()
        return kb
    if n_layers < DEPTH:
        with kb.scope() as st:
            bufs = [kb.sb([128, D], F32, "cp", st) for _ in range(2)]
            bk = [Tok(), Tok()]
            for ti in range(2, NT):
                b = ti % 2
                kb.dma("sp", bufs[b][:], h_d[ti * 128:(ti + 1) * 128, :], r=[h_tok[ti]], w=[bk[b]])
                kb.dma("sp", out[(ti - 2) * 128:(ti - 1) * 128, :], bufs[b][:], r=[bk[b]], w=[h_tok[ti]])
    kb.barrier()
    kb.es.close()
    return kb


def host_inputs(inp, b):
    f = np.float32
    A = lambda x: np.ascontiguousarray(x, dtype=f)
    m = {}
    m["hx"] = A(np.concatenate([inp["ctx"][b], inp["x"][b]], axis=0))
    cond = np.stack([inp["c"][b], inp["c_ctx"]], axis=0)
    m["condT"] = A(cond.reshape(2, 8, 128).transpose(2, 1, 0))
    m["ada_w"] = A(inp["ada_w"])
    m["ada_bT"] = A(inp["ada_b"].reshape(DEPTH, 24, 128).transpose(0, 2, 1))
    m["ada_bg"] = A(np.broadcast_to(inp["ada_b"][:, None, 2 * D:], (DEPTH, 128, D)))
    m["pre_gT"] = A(inp["pre_g"].reshape(DEPTH, 8, 128).transpose(0, 2, 1))
    m["post_gb"] = A(np.broadcast_to(inp["post_g"][:, None, :], (DEPTH, 128, D)))
    m["s5_in_w"] = A(inp["s5_in_w"])
    m["s5_glu_w"] = A(inp["s5_glu_w"])
    m["s5_out_w"] = A(inp["s5_out_w"])
    m["s5_lamT"] = A(np.stack([inp["s5_lam_re"], inp["s5_lam_im"]], axis=2).transpose(0, 1, 2, 4, 3))
    m["s5_ldt"] = A(np.broadcast_to(inp["s5_log_dt"][:, :, None, :], (2, 2, 64, 64)))
    bst = np.stack([inp["s5_b_re"], inp["s5_b_im"]], axis=2)
    m["s5_bT"] = A(bst.transpose(0, 1, 2, 4, 3, 5).reshape(2, 2, 2, 64, 1024))
    cst = np.stack([inp["s5_c_re"], inp["s5_c_im"]], axis=2)
    m["s5_cT"] = A(cst.transpose(0, 1, 2, 5, 3, 4).reshape(2, 2, 2, 64, 1024))
    m["s5_dT"] = A(inp["s5_d"].reshape(2, 8, 128).transpose(0, 2, 1))
    m["s5_gbT"] = A(inp["s5_glu_b"].reshape(2, 16, 128).transpose(0, 2, 1))
    m["gdn_in_w"] = A(inp["gdn_in_w"])
    m["gdn_out_w"] = A(inp["gdn_out_w"])
    m["gdn_prm"] = A(np.stack([inp["gdn_a_log"].reshape(1, 16), inp["gdn_dt_bias"].reshape(1, 16)], axis=2))
    m["gdn_cwT"] = A(inp["gdn_conv_w"].reshape(1, 5, 24, 128).transpose(0, 3, 2, 1))
    m["gdn_ngb"] = A(np.broadcast_to(inp["gdn_norm_g"][:, None, :], (1, 128, 128)))
    mi = np.arange(128)
    m["c_gmask"] = A(np.stack([mi[:, None] <= mi[None, :], mi[:, None] > mi[None, :], mi[:, None] >= mi[None, :], mi[:, None] < mi[None, :]], axis=1))
    m["c_bmask"] = A(np.stack([mi[:, None] // 32 == mi[None, :] // 32,
                               (mi[:, None] // 64 == mi[None, :] // 64) & (mi[:, None] // 32 != mi[None, :] // 32),
                               mi[:, None] // 64 != mi[None, :] // 64], axis=1))
    m["na_in_w"] = A(inp["na_in_w"])
    m["na_out_w"] = A(inp["na_out_w"])
    rpb = inp["na_rpb"]
    wq = np.arange(64)
    c0 = np.clip(wq - 8, 0, 48)
    wk_ = np.arange(64)
    inwin = (wk_[None, :] >= c0[:, None]) & (wk_[None, :] < c0[:, None] + 16)
    coff = np.clip(wk_[None, :] - wq[:, None] + 15, 0, 30)
    bt = np.where(inwin[None, None, None], rpb[:, :, :, coff], 0.0)
    bt = bt.reshape(1, 8, 2, 15, 64, 64).transpose(0, 2, 4, 1, 3, 5).reshape(1, 128, 8, 15, 64)
    m["na_biasT"] = A(bt)
    m["c_namask"] = A(np.tile(np.where(inwin, 0.0, -30000.0), (2, 1)))
    m["c_ident"] = np.eye(128, dtype=f)
    p = np.arange(128)
    m["c_bd"] = A((p[:, None] // 16) == (p[None, :] // 16))
    cp = np.stack([(p // 16) % 2 == 0, (p // 16) % 2 == 1, ((p // 16) % 2 == 0) & (p >= 96), ((p // 16) % 2 == 1) & (p >= 96)], axis=0)
    m["c_par"] = A(cp.T)
    m["c_colpar"] = A(np.broadcast_to(cp[None], (64, 4, 128)))
    return m


_CACHE = {}


def kernel(**inputs):
    inp = {k: np.asarray(v) for k, v in inputs.items()}
    if "nc" not in _CACHE:
        _CACHE["nc"] = build(DEPTH).nc
    nc = _CACHE["nc"]
    nb = inp["x"].shape[0]
    maps = [host_inputs(inp, b) for b in range(nb)]
    res = run_bass_kernel_spmd(nc, maps, core_ids=list(range(nb)))
    return np.stack([np.asarray(res.results[b]["out"]) for b in range(nb)], axis=0).astype(np.float32)
```
